# Optimizing a Trainium2 kernel written in Bass

```python
import math
import numpy as np
import jax
import jax.numpy as jnp
from jax import lax

D_MODEL = 1024
BATCH = 8
SEQ = 4096
DEPTH = 2

POOL_WIDTH = D_MODEL // 2
POOL_GROUPS = 4
POOL_WINDOWS = (2, 4, 8, 16)
POOL_GROUP_CH = POOL_WIDTH // POOL_GROUPS
SSM_WIDTH = D_MODEL // 2
SSM_GROUP_CH = 16
SSM_GROUPS = SSM_WIDTH // SSM_GROUP_CH
SSM_STATE = 64
NSA_HEADS = 8
NSA_KV_HEADS = 2
NSA_GROUP = NSA_HEADS // NSA_KV_HEADS
HEAD_DIM = 64
NSA_WIDTH = NSA_HEADS * HEAD_DIM
KV_WIDTH = NSA_KV_HEADS * HEAD_DIM
CMP_BLOCK = 32
CMP_STRIDE = 16
CMP_HIDDEN = 2 * HEAD_DIM
SEL_BLOCK = 64
SEL_TOPK = 16
WINDOW = 512
Q_BLOCK = 128
SEL_Q_BLOCK = 64
ROPE_THETA = 500000.0
ROPE_DIMS = HEAD_DIM // 4
NEG = -1e30
FORCE_SCORE = 1e6
D_FF = 4 * D_MODEL
ALPHA = (2 * DEPTH) ** 0.25
BETA = (8 * DEPTH) ** -0.25
LN_EPS = 1e-5
IN_SPLITS = (POOL_WIDTH, SSM_WIDTH, NSA_WIDTH, 6 * KV_WIDTH, 3 * NSA_HEADS, 3 * D_MODEL)
IN_WIDTH = sum(IN_SPLITS)

kernel_name = "hybrid_pool_s5_nsa_deepnorm"


def layer_norm(x, g, b):
    xf = x.astype(jnp.float32)
    mu = jnp.mean(xf, axis=-1, keepdims=True)
    var = jnp.mean(jnp.square(xf - mu), axis=-1, keepdims=True)
    y = (xf - mu) * lax.rsqrt(var + LN_EPS)
    return (y * g.astype(jnp.float32) + b.astype(jnp.float32)).astype(x.dtype)


def rope_tables(L):
    pos = jnp.arange(L, dtype=jnp.float32)
    inv_freq = ROPE_THETA ** (-jnp.arange(0, ROPE_DIMS, 2, dtype=jnp.float32) / ROPE_DIMS)
    ang = pos[:, None] * inv_freq[None, :]
    return jnp.cos(ang), jnp.sin(ang)


def partial_rope(x, cos, sin):
    half = ROPE_DIMS // 2
    c = cos[:, None, :].astype(x.dtype)
    s = sin[:, None, :].astype(x.dtype)
    x1 = x[..., :half]
    x2 = x[..., half:ROPE_DIMS]
    return jnp.concatenate([x1 * c - x2 * s, x1 * s + x2 * c, x[..., ROPE_DIMS:]], axis=-1)


def pool_mixer(u, w_pool, pool_scale):
    B, L, _ = u.shape
    uf = u.astype(jnp.float32)
    csum = jnp.cumsum(uf, axis=1)
    t = jnp.arange(L)
    means = []
    for gi, w in enumerate(POOL_WINDOWS):
        cs = csum[..., gi * POOL_GROUP_CH:(gi + 1) * POOL_GROUP_CH]
        lag = jnp.pad(cs[:, :L - w], ((0, 0), (w, 0), (0, 0)))
        cnt = jnp.minimum(t + 1, w).astype(jnp.float32)[None, :, None]
        means.append((cs - lag) / cnt)
    z = (jnp.concatenate(means, axis=-1) - uf).astype(u.dtype)
    z = z.reshape(B, L, POOL_GROUPS, POOL_GROUP_CH)
    y = jnp.einsum('blgc,gcd->blgd', z, w_pool).reshape(B, L, POOL_WIDTH)
    return y * pool_scale


def s5_mixer(u, lam_re, lam_im, log_dt, b_re, b_im, c_re, c_im, d_skip, w_glu, b_glu):
    B, L, _ = u.shape
    f32 = jnp.float32
    uf = u.astype(f32).reshape(B, L, SSM_GROUPS, SSM_GROUP_CH)
    lr = lam_re.astype(f32)
    li = lam_im.astype(f32)
    step = jnp.exp(log_dt.astype(f32))[:, None]
    mag = jnp.exp(lr * step)
    ab_re = mag * jnp.cos(li * step)
    ab_im = mag * jnp.sin(li * step)
    den = lr * lr + li * li
    n_re = ab_re - 1.0
    n_im = ab_im
    k_re = (n_re * lr + n_im * li) / den
    k_im = (n_im * lr - n_re * li) / den
    br = b_re.astype(f32)
    bi = b_im.astype(f32)
    bb_re = k_re[..., None] * br - k_im[..., None] * bi
    bb_im = k_re[..., None] * bi + k_im[..., None] * br
    bu_re = jnp.einsum('gpc,blgc->blgp', bb_re, uf)
    bu_im = jnp.einsum('gpc,blgc->blgp', bb_im, uf)
    a_re = jnp.broadcast_to(ab_re[None, None], (1, L) + ab_re.shape)
    a_im = jnp.broadcast_to(ab_im[None, None], (1, L) + ab_im.shape)

    def combine(e1, e2):
        a1r, a1i, b1r, b1i = e1
        a2r, a2i, b2r, b2i = e2
        return (a1r * a2r - a1i * a2i,
                a1r * a2i + a1i * a2r,
                a2r * b1r - a2i * b1i + b2r,
                a2r * b1i + a2i * b1r + b2i)

    _, _, h_re, h_im = lax.associative_scan(combine, (a_re, a_im, bu_re, bu_im), axis=1)
    y = (jnp.einsum('gcp,blgp->blgc', c_re.astype(f32), h_re)
         - jnp.einsum('gcp,blgp->blgc', c_im.astype(f32), h_im)
         + d_skip.astype(f32) * uf)
    z = jax.nn.gelu(y.reshape(B, L, SSM_WIDTH))
    out = z * jax.nn.sigmoid(z @ w_glu.astype(f32) + b_glu.astype(f32))
    return out.astype(u.dtype)


def overlap_matrix(n_cmp, n_sel):
    cs = np.arange(n_cmp)[:, None] * CMP_STRIDE
    ss = np.arange(n_sel)[None, :] * SEL_BLOCK
    ov = np.minimum(cs + CMP_BLOCK, ss + SEL_BLOCK) - np.maximum(cs, ss)
    return jnp.asarray(np.maximum(ov, 0) / CMP_STRIDE, dtype=jnp.float32)


def nsa_mixer(q, kc, vc, ks, vs, kw, vw, gate_logits, pe_k, pe_v, wk1, wk2, wv1, wv2):
    B, L = q.shape[0], q.shape[1]
    dtype = q.dtype
    f32 = jnp.float32
    scale = 1.0 / math.sqrt(HEAD_DIM)
    K, G = NSA_KV_HEADS, NSA_GROUP
    qg = q.reshape(B, L, K, G, HEAD_DIM)
    t = jnp.arange(L)

    n_cmp = (L - CMP_BLOCK) // CMP_STRIDE + 1
    idx = np.arange(n_cmp)[:, None] * CMP_STRIDE + np.arange(CMP_BLOCK)[None, :]

    def compress(src, pe, w1, w2):
        blk = src[:, idx] + pe[None, None, :, None, :]
        flat = blk.transpose(0, 1, 3, 2, 4).reshape(B, n_cmp, K, CMP_BLOCK * HEAD_DIM)
        return jax.nn.gelu(flat @ w1) @ w2

    k_cmp = compress(kc, pe_k, wk1, wk2)
    v_cmp = compress(vc, pe_v, wv1, wv2)
    s_cmp = jnp.einsum('blkgd,bnkd->bkgln', qg, k_cmp).astype(f32) * scale
    blk_end = jnp.arange(n_cmp) * CMP_STRIDE + CMP_BLOCK - 1
    cmp_valid = blk_end[None, :] <= t[:, None]
    p_cmp = jax.nn.softmax(jnp.where(cmp_valid, s_cmp, NEG), axis=-1)
    p_cmp = jnp.where(cmp_valid, p_cmp, 0.0)
    o_cmp = jnp.einsum('bkgln,bnkd->blkgd', p_cmp.astype(dtype), v_cmp)

    n_sel = L // SEL_BLOCK
    topk = min(SEL_TOPK, n_sel)
    imp = jnp.einsum('bkgln,ns->bkls', p_cmp, overlap_matrix(n_cmp, n_sel))
    j = jnp.arange(n_sel)[None, :]
    cur = (t // SEL_BLOCK)[:, None]
    forced = (j == 0) | (j == cur) | (j == cur - 1)
    sel_valid = j * SEL_BLOCK <= t[:, None]
    score = jnp.where(forced, FORCE_SCORE, jnp.where(sel_valid, imp, -FORCE_SCORE))
    _, sel_idx = lax.top_k(score, topk)

    ks_blk = ks.transpose(0, 2, 1, 3).reshape(B, K, n_sel, SEL_BLOCK, HEAD_DIM)
    vs_blk = vs.transpose(0, 2, 1, 3).reshape(B, K, n_sel, SEL_BLOCK, HEAD_DIM)
    nq = L // SEL_Q_BLOCK
    q_ch = qg.reshape(B, nq, SEL_Q_BLOCK, K, G, HEAD_DIM).transpose(1, 0, 2, 3, 4, 5)
    i_ch = sel_idx.reshape(B, K, nq, SEL_Q_BLOCK, topk).transpose(2, 0, 1, 3, 4)
    t_ch = t.reshape(nq, SEL_Q_BLOCK)
    b_ix = jnp.arange(B)[:, None, None, None]
    k_ix = jnp.arange(K)[None, :, None, None]

    def sel_chunk(args):
        qc, ic, tc = args
        kg = ks_blk[b_ix, k_ix, ic]
        vg = vs_blk[b_ix, k_ix, ic]
        s = jnp.einsum('bqkgd,bkqsjd->bkgqsj', qc, kg).astype(f32) * scale
        kpos = ic[..., None] * SEL_BLOCK + jnp.arange(SEL_BLOCK)
        mask = (kpos <= tc[None, None, :, None, None])[:, :, None]
        s = jnp.where(mask, s, NEG).reshape(B, K, G, SEL_Q_BLOCK, topk * SEL_BLOCK)
        p = jax.nn.softmax(s, axis=-1).reshape(B, K, G, SEL_Q_BLOCK, topk, SEL_BLOCK)
        return jnp.einsum('bkgqsj,bkqsjd->bqkgd', p.astype(dtype), vg)

    o_sel = lax.map(sel_chunk, (q_ch, i_ch, t_ch))
    o_sel = o_sel.transpose(1, 0, 2, 3, 4, 5).reshape(B, L, K, G, HEAD_DIM)

    kw_pad = jnp.pad(kw, ((0, 0), (WINDOW, 0), (0, 0), (0, 0)))
    vw_pad = jnp.pad(vw, ((0, 0), (WINDOW, 0), (0, 0), (0, 0)))
    nwq = L // Q_BLOCK
    qw_ch = qg.reshape(B, nwq, Q_BLOCK, K, G, HEAD_DIM).transpose(1, 0, 2, 3, 4, 5)

    def win_chunk(args):
        qc, c = args
        start = c * Q_BLOCK
        kb = lax.dynamic_slice_in_dim(kw_pad, start, WINDOW + Q_BLOCK, axis=1)
        vb = lax.dynamic_slice_in_dim(vw_pad, start, WINDOW + Q_BLOCK, axis=1)
        s = jnp.einsum('bqkgd,bskd->bkgqs', qc, kb).astype(f32) * scale
        tq = start + jnp.arange(Q_BLOCK)
        sk = start - WINDOW + jnp.arange(WINDOW + Q_BLOCK)
        diff = tq[:, None] - sk[None, :]
        mask = (sk[None, :] >= 0) & (diff >= 0) & (diff < WINDOW)
        p = jax.nn.softmax(jnp.where(mask, s, NEG), axis=-1)
        return jnp.einsum('bkgqs,bskd->bqkgd', p.astype(dtype), vb)

    o_win = lax.map(win_chunk, (qw_ch, jnp.arange(nwq)))
    o_win = o_win.transpose(1, 0, 2, 3, 4, 5).reshape(B, L, K, G, HEAD_DIM)

    g = jax.nn.sigmoid(gate_logits.astype(f32)).astype(dtype).reshape(B, L, K, G, 3)
    o = g[..., 0:1] * o_cmp + g[..., 1:2] * o_sel + g[..., 2:3] * o_win
    return o.reshape(B, L, NSA_WIDTH)


def hybrid_layer(x, cos, sin, w_in, w_pool, pool_scale, ssm_lam_re, ssm_lam_im, ssm_log_dt,
                 ssm_b_re, ssm_b_im, ssm_c_re, ssm_c_im, ssm_d, w_glu, b_glu,
                 cmp_pe_k, cmp_pe_v, cmp_wk1, cmp_wk2, cmp_wv1, cmp_wv2,
                 w_up_pool, w_up_ssm, w_up_nsa, w_out, ln1_g, ln1_b, w_ff1, w_ff2, ln2_g, ln2_b):
    B, L, _ = x.shape
    proj = x @ w_in
    offs = [int(o) for o in np.cumsum(IN_SPLITS)[:-1]]
    u_pool, u_ssm, q, kv, g_nsa, g_merge = jnp.split(proj, offs, axis=-1)
    kc, vc, ks, vs, kw, vw = [a.reshape(B, L, NSA_KV_HEADS, HEAD_DIM) for a in jnp.split(kv, 6, axis=-1)]
    q = partial_rope(q.reshape(B, L, NSA_HEADS, HEAD_DIM), cos, sin)
    kc = partial_rope(kc, cos, sin)
    ks = partial_rope(ks, cos, sin)
    kw = partial_rope(kw, cos, sin)

    y_pool = pool_mixer(u_pool, w_pool, pool_scale)
    y_ssm = s5_mixer(u_ssm, ssm_lam_re, ssm_lam_im, ssm_log_dt, ssm_b_re, ssm_b_im,
                     ssm_c_re, ssm_c_im, ssm_d, w_glu, b_glu)
    y_nsa = nsa_mixer(q, kc, vc, ks, vs, kw, vw, g_nsa.reshape(B, L, NSA_HEADS, 3),
                      cmp_pe_k, cmp_pe_v, cmp_wk1, cmp_wk2, cmp_wv1, cmp_wv2)

    gates = jax.nn.sigmoid(g_merge.astype(jnp.float32)).astype(x.dtype).reshape(B, L, 3, D_MODEL)
    merged = (gates[:, :, 0] * (y_pool @ w_up_pool)
              + gates[:, :, 1] * (y_ssm @ w_up_ssm)
              + gates[:, :, 2] * (y_nsa @ w_up_nsa))
    x = layer_norm(ALPHA * x + merged @ w_out, ln1_g, ln1_b)
    h = jnp.square(jax.nn.relu(x @ w_ff1)) @ w_ff2
    return layer_norm(ALPHA * x + h, ln2_g, ln2_b)


def setup_inputs(seed: int = 0) -> dict:
    key = jax.random.key(seed)
    k = jax.random.split(key, 40)
    f32 = jnp.float32

    def nrm(kk, shape, scale):
        return jax.random.normal(kk, shape, f32) * scale

    Dp = DEPTH
    n = jnp.arange(SSM_STATE, dtype=f32)
    return {
        "x": nrm(k[0], (BATCH, SEQ, D_MODEL), 1.0),
        "ln_in_g": 1.0 + nrm(k[1], (D_MODEL,), 0.02),
        "ln_in_b": nrm(k[2], (D_MODEL,), 0.02),
        "w_in": nrm(k[3], (Dp, D_MODEL, IN_WIDTH), D_MODEL ** -0.5),
        "w_pool": nrm(k[4], (Dp, POOL_GROUPS, POOL_GROUP_CH, POOL_GROUP_CH), POOL_GROUP_CH ** -0.5),
        "pool_scale": 1.0 + nrm(k[5], (Dp, POOL_WIDTH), 0.02),
        "ssm_lam_re": -0.5 + nrm(k[6], (Dp, SSM_GROUPS, SSM_STATE), 0.01),
        "ssm_lam_im": math.pi * n + nrm(k[7], (Dp, SSM_GROUPS, SSM_STATE), 0.01),
        "ssm_log_dt": jax.random.uniform(k[8], (Dp, SSM_GROUPS), f32, math.log(1e-3), math.log(1e-1)),
        "ssm_b_re": nrm(k[9], (Dp, SSM_GROUPS, SSM_STATE, SSM_GROUP_CH), (2 * SSM_GROUP_CH) ** -0.5),
        "ssm_b_im": nrm(k[10], (Dp, SSM_GROUPS, SSM_STATE, SSM_GROUP_CH), (2 * SSM_GROUP_CH) ** -0.5),
        "ssm_c_re": nrm(k[11], (Dp, SSM_GROUPS, SSM_GROUP_CH, SSM_STATE), SSM_STATE ** -0.5),
        "ssm_c_im": nrm(k[12], (Dp, SSM_GROUPS, SSM_GROUP_CH, SSM_STATE), SSM_STATE ** -0.5),
        "ssm_d": nrm(k[13], (Dp, SSM_GROUPS, SSM_GROUP_CH), 1.0),
        "w_glu": nrm(k[14], (Dp, SSM_WIDTH, SSM_WIDTH), SSM_WIDTH ** -0.5),
        "b_glu": nrm(k[15], (Dp, SSM_WIDTH), 0.02),
        "cmp_pe_k": nrm(k[16], (Dp, CMP_BLOCK, HEAD_DIM), 0.1),
        "cmp_pe_v": nrm(k[17], (Dp, CMP_BLOCK, HEAD_DIM), 0.1),
        "cmp_wk1": nrm(k[18], (Dp, CMP_BLOCK * HEAD_DIM, CMP_HIDDEN), (CMP_BLOCK * HEAD_DIM) ** -0.5),
        "cmp_wk2": nrm(k[19], (Dp, CMP_HIDDEN, HEAD_DIM), CMP_HIDDEN ** -0.5),
        "cmp_wv1": nrm(k[20], (Dp, CMP_BLOCK * HEAD_DIM, CMP_HIDDEN), (CMP_BLOCK * HEAD_DIM) ** -0.5),
        "cmp_wv2": nrm(k[21], (Dp, CMP_HIDDEN, HEAD_DIM), CMP_HIDDEN ** -0.5),
        "w_up_pool": nrm(k[22], (Dp, POOL_WIDTH, D_MODEL), POOL_WIDTH ** -0.5),
        "w_up_ssm": nrm(k[23], (Dp, SSM_WIDTH, D_MODEL), SSM_WIDTH ** -0.5),
        "w_up_nsa": nrm(k[24], (Dp, NSA_WIDTH, D_MODEL), NSA_WIDTH ** -0.5),
        "w_out": nrm(k[25], (Dp, D_MODEL, D_MODEL), BETA * D_MODEL ** -0.5),
        "ln1_g": 1.0 + nrm(k[26], (Dp, D_MODEL), 0.02),
        "ln1_b": nrm(k[27], (Dp, D_MODEL), 0.02),
        "w_ff1": nrm(k[28], (Dp, D_MODEL, D_FF), D_MODEL ** -0.5),
        "w_ff2": nrm(k[29], (Dp, D_FF, D_MODEL), BETA * D_FF ** -0.5),
        "ln2_g": 1.0 + nrm(k[30], (Dp, D_MODEL), 0.02),
        "ln2_b": nrm(k[31], (Dp, D_MODEL), 0.02),
    }


def reference(x, ln_in_g, ln_in_b, w_in, w_pool, pool_scale, ssm_lam_re, ssm_lam_im, ssm_log_dt,
              ssm_b_re, ssm_b_im, ssm_c_re, ssm_c_im, ssm_d, w_glu, b_glu,
              cmp_pe_k, cmp_pe_v, cmp_wk1, cmp_wk2, cmp_wv1, cmp_wv2,
              w_up_pool, w_up_ssm, w_up_nsa, w_out, ln1_g, ln1_b, w_ff1, w_ff2, ln2_g, ln2_b):
    cos, sin = rope_tables(x.shape[1])
    x = layer_norm(x, ln_in_g, ln_in_b)
    for i in range(DEPTH):
        x = hybrid_layer(x, cos, sin, w_in[i], w_pool[i], pool_scale[i], ssm_lam_re[i], ssm_lam_im[i],
                         ssm_log_dt[i], ssm_b_re[i], ssm_b_im[i], ssm_c_re[i], ssm_c_im[i], ssm_d[i],
                         w_glu[i], b_glu[i], cmp_pe_k[i], cmp_pe_v[i], cmp_wk1[i], cmp_wk2[i],
                         cmp_wv1[i], cmp_wv2[i], w_up_pool[i], w_up_ssm[i], w_up_nsa[i], w_out[i],
                         ln1_g[i], ln1_b[i], w_ff1[i], w_ff2[i], ln2_g[i], ln2_b[i])
    return x
```

```python
import contextlib
import math
import numpy as np
import ml_dtypes
import concourse.bass as bass
import concourse.mybir as mybir
from concourse.bass_utils import run_bass_kernel_spmd

F32 = mybir.dt.float32
BF16 = mybir.dt.bfloat16
AF = mybir.ActivationFunctionType
ALU = mybir.AluOpType
AX = mybir.AxisListType

L = 4096
D = 1024
NT = 32
DEPTH = 2
ALPHA = (2 * DEPTH) ** 0.25
EPS = 1e-5
BIG = 30000.0
SB_BASE = 16512
SB_END = 229344


class Res:
    __slots__ = ("name", "w", "r")

    def __init__(self, name=""):
        self.name = name
        self.w = None
        self.r = []


class Prog:
    ENGS = ("pe", "act", "dve", "pool", "sp")
    NDMA = 8

    def __init__(self, nc):
        self.nc = nc
        self.ops = {e: [] for e in self.ENGS}
        self.cnt = {}
        self.seen = {e: {} for e in self.ENGS}
        self.dma_n = {e: 0 for e in self.ENGS}
        self.sem_keys = []
        for e in ("pe", "act", "dve", "pool"):
            self._mk(("c", e))
        for e in ("sp", "act", "pool"):
            for j in range(self.NDMA):
                self._mk(("d", e, j))
        self.nops = 0

    def _mk(self, key):
        self.cnt[key] = 0
        self.sem_keys.append(key)

    def _need(self, eng, waits, dep):
        if dep is None:
            return
        key, val = dep
        if self.seen[eng].get(key, 0) >= val:
            return
        waits[key] = max(waits.get(key, 0), val)

    def _deps(self, eng, reads, writes, self_ok=False):
        waits = {}
        for r in reads:
            self._need(eng, waits, r.w)
        for w in writes:
            self._need(eng, waits, w.w)
            for rd in w.r:
                self._need(eng, waits, rd)
        if self_ok:
            waits.pop(("c", eng), None)
        return waits

    def _commit(self, eng, waits, tok, reads, writes):
        for k, v in waits.items():
            self.seen[eng][k] = v
        for r in reads:
            r.r.append(tok)
        for w in writes:
            w.w = tok
            w.r = []
        self.nops += 1

    def op(self, eng, fn, reads=(), writes=(), self_ok=False):
        waits = self._deps(eng, reads, writes, self_ok)
        key = ("c", eng)
        self.cnt[key] += 1
        tok = (key, self.cnt[key])
        self.ops[eng].append((list(waits.items()), fn, key, 1))
        self._commit(eng, waits, tok, reads, writes)
        return tok

    def dma(self, q, fn, ndma, reads=(), writes=()):
        waits = self._deps(q, reads, writes)
        j = self.dma_n[q] % self.NDMA
        self.dma_n[q] += 1
        key = ("d", q, j)
        if self.cnt[key] > 0:
            self._need(q, waits, (key, self.cnt[key]))
        self.cnt[key] += 16 * ndma
        tok = (key, self.cnt[key])
        self.ops[q].append((list(waits.items()), fn, key, 16))
        self._commit(q, waits, tok, reads, writes)
        return tok

    def barrier(self):
        for e in self.ENGS:
            waits = {}
            for k in self.sem_keys:
                if self.cnt[k] > 0 and not (k[0] == "c" and k[1] == e):
                    self._need(e, waits, (k, self.cnt[k]))
            for k, v in waits.items():
                self.seen[e][k] = v
            if waits:
                self.ops[e].append((list(waits.items()), None, None, 0))

    def emit(self):
        nc = self.nc
        with contextlib.ExitStack() as st:
            sems = {}
            for k in self.sem_keys:
                sems[k] = st.enter_context(nc.semaphore("s_" + "_".join(str(x) for x in k)))
            block = st.enter_context(nc.Block())
            final = [(k, v) for k, v in self.cnt.items() if v > 0]

            def run(engname):
                def body(eng):
                    for waits, fn, key, inc in self.ops[engname]:
                        for k, v in waits:
                            eng.wait_ge(sems[k], v)
                        if fn is None:
                            continue
                        if inc == 16:
                            fn(eng, lambda ins: ins.then_inc(sems[key], 16))
                        else:
                            fn(eng).then_inc(sems[key], 1)
                    if engname == "sp":
                        for k, v in final:
                            eng.wait_ge(sems[k], v)
                return body

            block.tensor(run("pe"))
            block.scalar(run("act"))
            block.vector(run("dve"))
            block.gpsimd(run("pool"))
            block.sync(run("sp"))


PARAMS = [("ln_in_g", (D,)), ("ln_in_b", (D,)), ("w_in", (2, D, 5400)), ("w_pool", (2, 4, 128, 128)),
          ("pool_scale", (2, 512)), ("ssm_lam_re", (2, 32, 64)), ("ssm_lam_im", (2, 32, 64)),
          ("ssm_log_dt", (2, 32)), ("ssm_b_re", (2, 32, 64, 16)), ("ssm_b_im", (2, 32, 64, 16)),
          ("ssm_c_re", (2, 32, 16, 64)), ("ssm_c_im", (2, 32, 16, 64)), ("ssm_d", (2, 32, 16)),
          ("w_glu", (2, 512, 512)), ("b_glu", (2, 512)), ("cmp_pe_k", (2, 32, 64)), ("cmp_pe_v", (2, 32, 64)),
          ("cmp_wk1", (2, 2048, 128)), ("cmp_wk2", (2, 128, 64)), ("cmp_wv1", (2, 2048, 128)),
          ("cmp_wv2", (2, 128, 64)), ("w_up_pool", (2, 512, D)), ("w_up_ssm", (2, 512, D)),
          ("w_up_nsa", (2, 512, D)), ("w_out", (2, D, D)), ("ln1_g", (2, D)), ("ln1_b", (2, D)),
          ("w_ff1", (2, D, 4096)), ("w_ff2", (2, 4096, D)), ("ln2_g", (2, D)), ("ln2_b", (2, D))]


def host_consts():
    c = {}
    c["c_ident"] = np.eye(128, dtype=np.float32)
    pos = np.arange(L, dtype=np.float32)
    inv_freq = (np.float32(500000.0) ** (-np.arange(0, 16, 2, dtype=np.float32) / np.float32(16))).astype(np.float32)
    ang = (pos[:, None] * inv_freq[None, :]).astype(np.float32)
    c["c_rope"] = np.concatenate([np.cos(ang), np.sin(ang)], axis=1).astype(np.float32)
    eb = np.zeros((64, 4096), np.float32)
    for s in range(64):
        eb[s, s * 64:(s + 1) * 64] = 1.0
    c["c_ebig"] = eb.astype(ml_dtypes.bfloat16)
    cs = np.arange(256)[:, None] * 16
    ss = np.arange(64)[None, :] * 64
    ov = np.maximum(np.minimum(cs + 32, ss + 64) - np.maximum(cs, ss), 0) / 16.0
    ov[255:] = 0.0
    c["c_ov"] = ov.astype(np.float32)
    ic = np.zeros((4, 16), np.float32)
    for gi, w in enumerate((2, 4, 8, 16)):
        ic[gi] = 1.0 / np.minimum(np.arange(16) + 1, w)
    c["c_invcnt"] = ic.reshape(1, 64)
    return c


class Builder:
    def __init__(self, dbg=(), stop_after=None, layers=DEPTH, inject=(), skip=()):
        self.inject = set(inject)
        self.skip = set(skip)
        self.nc = nc = bass.Bass("TRN2", target_bir_lowering=False)
        self.P = Prog(nc)
        self.dbg = set(dbg)
        self.stop_after = stop_after
        self.layers = layers
        self.d = {}
        self.d["x"] = nc.dram_tensor("x", [L, D], F32, kind="ExternalInput").ap()
        for n, shp in PARAMS:
            self.d[n] = nc.dram_tensor(n, list(shp), F32, kind="ExternalInput").ap()
        for n, a in host_consts().items():
            dt = BF16 if a.dtype == ml_dtypes.bfloat16 else F32
            self.d[n] = nc.dram_tensor(n, list(a.shape), dt, kind="ExternalInput").ap()
        self.d["out"] = nc.dram_tensor("out", [L, D], F32, kind="ExternalOutput").ap()
        self.scr = {}
        self.sres = {}
        for n, shp, dt in [("XR", [L, D], F32), ("XT", [D, L], BF16), ("UPT", [512, L], F32), ("UST", [512, L], F32),
                           ("QT", [128, 4, L], BF16), ("KCT", [128, L], BF16), ("KST", [128, L], BF16),
                           ("KWT", [128, L], BF16), ("VCT", [128, L], BF16), ("VS", [L, 128], BF16),
                           ("VW", [L, 128], BF16), ("GN", [L, 24], F32), ("YPT", [512, L], BF16),
                           ("YST", [512, L], BF16), ("YNT", [512, L], BF16), ("XR1", [L, D], F32),
                           ("X1T", [D, L], BF16)]:
            kind = "ExternalOutput" if n in self.dbg else ("ExternalInput" if n in self.inject else "Internal")
            self.scr[n] = nc.dram_tensor("s_" + n, shp, dt, kind=kind).ap()
            self.sres[n] = Res(n)
        self.ps = [nc.alloc_psum_tensor("psb%d" % i, [128, 512], F32) for i in range(8)]
        self.psr = [Res("ps%d" % i) for i in range(8)]
        self.cur = SB_BASE
        self.uid = 0
        self.ident = self.sb([128, 128], F32)
        self.Rconst = Res("const")
        self.P.dma("sp", lambda e, inc: inc(e.dma_start(out=self.ident[:], in_=self.d["c_ident"][:, :])), 1,
                   writes=[self.Rconst])
        self.persist = self.cur

    def sb(self, shape, dt):
        nbytes = int(np.prod(shape[1:])) * (2 if dt == BF16 else 4)
        off = (self.cur + 63) // 64 * 64
        assert off + nbytes <= SB_END, ("SBUF overflow", off + nbytes - SB_END)
        self.cur = off + nbytes
        self.uid += 1
        return self.nc.alloc_sbuf_tensor_at("t%d" % self.uid, list(shape), dt, offset=off)

    def phase_end(self):
        self.P.barrier()
        self.cur = self.persist

    def load_cast(self, pieces, res, stage_cols=2048):
        P = self.P
        if not hasattr(self, "_lc"):
            self._lc = None
        stg = [self.sb([128, stage_cols], F32) for _ in range(3)]
        rs = [Res("stg%d" % i) for i in range(3)]
        for i, pc in enumerate(pieces):
            dst, src, shp = pc[0], pc[1], pc[2]
            p0 = pc[3] if len(pc) > 3 else 0
            j = i % 3
            n = int(np.prod(shp[1:]))
            assert n <= stage_cols
            p = shp[0]
            sview = stg[j][p0:p0 + p, 0:n]
            if len(shp) == 3:
                sview = sview.rearrange("p (a b) -> p a b", b=shp[2])
            P.dma("sp", (lambda sv, s: lambda e, inc: inc(e.dma_start(out=sv, in_=s)))(sview, src), 1, writes=[rs[j]])
            eng = "pool" if i % 2 == 0 else "dve"
            P.op(eng, (lambda dd, sv: lambda e: e.tensor_copy(out=dd, in_=sv))(dst, sview), reads=[rs[j]], writes=[res])

    def bcast_load(self, dst, src_row, res):
        self.P.dma("sp", lambda e, inc: inc(e.dma_start(out=dst, in_=src_row.partition_broadcast(128))), 1, writes=[res])

    def ln_tile(self, i, src, Rsrc, g_rep, b_rep, Rgb, dst_res_d, Rres, dst_T_d, RT, bufs):
        P = self.P
        st6, mv, rstd, xT, RxT, Rst = bufs
        P.op("dve", lambda e: e.bn_stats(out=st6[:, 0, :], in_=src[:, 0:512]), reads=[Rsrc], writes=[Rst])
        P.op("dve", lambda e: e.bn_stats(out=st6[:, 1, :], in_=src[:, 512:1024]), reads=[Rsrc, Rst], writes=[Rst])
        P.op("dve", lambda e: e.bn_aggr(out=mv[:], in_=st6[:]), reads=[Rst], writes=[Rst])
        P.op("dve", lambda e: e.tensor_scalar_add(out=rstd[:], in0=mv[:, 1:2], scalar1=EPS), reads=[Rst], writes=[Rst])
        P.op("act", lambda e: e.sqrt(out=rstd[:], in_=rstd[:]), reads=[Rst], writes=[Rst])
        P.op("dve", lambda e: e.reciprocal(out=rstd[:], in_=rstd[:]), reads=[Rst], writes=[Rst])
        P.op("dve", lambda e: e.tensor_scalar(out=src[:], in0=src[:], scalar1=mv[:, 0:1], scalar2=rstd[:, 0:1],
                                              op0=ALU.subtract, op1=ALU.mult), reads=[Rst, Rsrc], writes=[Rsrc])
        P.op("pool", lambda e: e.tensor_tensor(out=src[:], in0=src[:], in1=g_rep[:], op=ALU.mult), reads=[Rsrc, Rgb],
             writes=[Rsrc])
        P.op("dve", lambda e: e.tensor_tensor(out=src[:], in0=src[:], in1=b_rep[:], op=ALU.add), reads=[Rsrc, Rgb],
             writes=[Rsrc])
        P.dma("sp", lambda e, inc: inc(e.dma_start(out=dst_res_d[i * 128:(i + 1) * 128, :], in_=src[:])), 1,
              reads=[Rsrc], writes=[Rres])
        if dst_T_d is None:
            return
        for half in range(2):
            pb = 6 + half
            ps = self.ps[pb]

            def tr(e, half=half, ps=ps):
                ins = None
                for q in range(4):
                    dt = half * 4 + q
                    ins = e.transpose(out=ps[:, q * 128:(q + 1) * 128], in_=src[:, dt * 128:(dt + 1) * 128],
                                      identity=self.ident[:])
                return ins
            P.op("pe", tr, reads=[Rsrc, self.Rconst], writes=[self.psr[pb]])
            P.op("act", lambda e, half=half, ps=ps: e.copy(out=xT[:, half * 4:(half + 1) * 4, :],
                                                         in_=ps[:].rearrange("p (a b) -> p a b", b=128)),
                 reads=[self.psr[pb]], writes=[RxT])
        P.dma("sp", lambda e, inc: inc(e.dma_start(
            out=dst_T_d.rearrange("(a p) t -> p a t", p=128)[:, :, i * 128:(i + 1) * 128], in_=xT[:])), 1,
            reads=[RxT], writes=[RT])

    def ln_bufs(self):
        return (self.sb([128, 2, 6], F32), self.sb([128, 2], F32), self.sb([128, 1], F32),
                self.sb([128, 8, 128], BF16), Res("xT"), Res("lnst"))

    def phase_ln_in(self):
        P = self.P
        g_rep = self.sb([128, D], F32)
        b_rep = self.sb([128, D], F32)
        Rgb = Res("gb")
        self.bcast_load(g_rep[:], self.d["ln_in_g"].unsqueeze(0), Rgb)
        self.bcast_load(b_rep[:], self.d["ln_in_b"].unsqueeze(0), Rgb)
        xb = [self.sb([128, D], F32) for _ in range(2)]
        Rx = [Res("x0"), Res("x1")]
        bufs = [self.ln_bufs() for _ in range(2)]
        for i in range(NT):
            j = i % 2
            P.dma("sp", lambda e, inc, i=i, j=j: inc(e.dma_start(out=xb[j][:], in_=self.d["x"][i * 128:(i + 1) * 128, :])),
                  1, writes=[Rx[j]])
            self.ln_tile(i, xb[j], Rx[j], g_rep, b_rep, Rgb, self.scr["XR"], self.sres["XR"], self.scr["XT"],
                         self.sres["XT"], bufs[j])
        self.phase_end()

    def phase_proj(self, l):
        P = self.P
        w_in = self.d["w_in"][l].rearrange("(a p) c -> p a c", p=128)
        NC_ = 1024 + 1304
        WP = self.sb([128, 8, NC_], BF16)
        RW = Res("WP")
        plan = [(0, 0, 1024)]
        qorder = (0, 4, 1, 5, 2, 6, 3, 7)
        for j, h in enumerate(qorder):
            plan.append((1024 + j * 64, 1024 + h * 64, 64))
        kv0 = 1536
        base = 1024 + 512
        for j, s in enumerate((0, 2, 4, 1)):
            plan.append((base + j * 128, kv0 + s * 128, 128))
        base += 512
        for j, s in enumerate((3, 5)):
            plan.append((base + j * 128, kv0 + s * 128, 128))
        plan.append((base + 256, 2304, 24))
        pieces = []
        for dc, sc, n in plan:
            o = 0
            while o < n:
                m = min(256, n - o)
                pieces.append((WP[:, :, dc + o:dc + o + m], w_in[:, :, sc + o:sc + o + m], [128, 8, m]))
                o += m
        self.load_cast(pieces, RW)
        rope = self.sb([128, NT, 16], F32)
        Rrope = Res("rope")
        P.dma("sp", lambda e, inc: inc(e.dma_start(out=rope[:], in_=self.d["c_rope"].rearrange("(a p) c -> p a c", p=128))),
              1, writes=[Rrope])
        xt = [self.sb([128, 8, 512], BF16) for _ in range(2)]
        Rxt = [Res("xt0"), Res("xt1")]
        fst = [self.sb([128, 512], F32) for _ in range(2)]
        Rfst = [Res("f0"), Res("f1")]
        tm = [self.sb([128, 1304], F32) for _ in range(2)]
        Rtm = [Res("tm0"), Res("tm1")]
        t1 = self.sb([128, 14, 8], F32)
        t2 = self.sb([128, 14, 8], F32)
        t3 = self.sb([128, 14, 8], F32)
        Rt = Res("ropetmp")
        trs = [self.sb([128, 8, 128], BF16) for _ in range(2)]
        Rtrs = [Res("trs0"), Res("trs1")]
        vst = [self.sb([128, 256], BF16) for _ in range(2)]
        Rvst = [Res("vst0"), Res("vst1")]
        XTd = self.scr["XT"].rearrange("(a p) t -> p a t", p=128)
        S = self.scr
        SR = self.sres
        kcnt = 0
        for c in range(8):
            j = c % 2
            P.dma("sp", lambda e, inc, c=c, j=j: inc(e.dma_start(out=xt[j][:], in_=XTd[:, :, c * 512:(c + 1) * 512])), 1,
                  reads=[SR["XT"]], writes=[Rxt[j]])
            for ft in range(8):
                pb = ft % 2
                ps = self.ps[pb]

                def mm(e, ft=ft, ps=ps, j=j):
                    ins = None
                    for a in range(8):
                        ins = e.matmul(ps[:], lhsT=WP[:, a, ft * 128:(ft + 1) * 128], rhs=xt[j][:, a, :], start=(a == 0),
                                       stop=(a == 7))
                    return ins
                P.op("pe", mm, reads=[RW, Rxt[j]], writes=[self.psr[pb]])
                fj = ft % 2
                P.op("act", lambda e, ps=ps, fj=fj: e.copy(out=fst[fj][:], in_=ps[:]), reads=[self.psr[pb]],
                     writes=[Rfst[fj]])
                dst = (S["UPT"] if ft < 4 else S["UST"])
                dres = SR["UPT"] if ft < 4 else SR["UST"]
                r0 = (ft % 4) * 128
                P.dma("sp", lambda e, inc, dst=dst, r0=r0, c=c, fj=fj: inc(e.dma_start(
                    out=dst[r0:r0 + 128, c * 512:(c + 1) * 512], in_=fst[fj][:])), 1, reads=[Rfst[fj]], writes=[dres])
            for tt in range(4):
                i = c * 4 + tt
                tj = i % 2
                T = tm[tj]
                for ch, (c0, n) in enumerate(((0, 512), (512, 512), (1024, 280))):
                    pb = 2 + (kcnt % 3)
                    kcnt += 1
                    ps = self.ps[pb]

                    def mm2(e, ps=ps, j=j, tt=tt, c0=c0, n=n):
                        ins = None
                        for a in range(8):
                            ins = e.matmul(ps[:, 0:n], lhsT=xt[j][:, a, tt * 128:(tt + 1) * 128],
                                           rhs=WP[:, a, 1024 + c0:1024 + c0 + n], start=(a == 0), stop=(a == 7))
                        return ins
                    P.op("pe", mm2, reads=[RW, Rxt[j]], writes=[self.psr[pb]])
                    P.op("act", lambda e, ps=ps, T=T, c0=c0, n=n: e.copy(out=T[:, c0:c0 + n], in_=ps[:, 0:n]),
                         reads=[self.psr[pb]], writes=[Rtm[tj]])
                H3 = T[:, 0:896].rearrange("p (h d) -> p h d", d=64)
                x1 = H3[:, :, 0:8]
                x2 = H3[:, :, 8:16]
                cosb = rope[:, i, 0:8].unsqueeze(1).to_broadcast([128, 14, 8])
                sinb = rope[:, i, 8:16].unsqueeze(1).to_broadcast([128, 14, 8])
                P.op("dve", lambda e, x1=x1, sinb=sinb: e.tensor_tensor(out=t1[:], in0=x1, in1=sinb, op=ALU.mult),
                     reads=[Rtm[tj], Rrope], writes=[Rt])
                P.op("dve", lambda e, x2=x2, sinb=sinb: e.tensor_tensor(out=t2[:], in0=x2, in1=sinb, op=ALU.mult),
                     reads=[Rtm[tj], Rrope, Rt], writes=[Rt])
                P.op("dve", lambda e, x1=x1, cosb=cosb: e.tensor_tensor(out=x1, in0=x1, in1=cosb, op=ALU.mult),
                     reads=[Rtm[tj], Rrope, Rt], writes=[Rtm[tj]])
                P.op("dve", lambda e, x2=x2, cosb=cosb: e.tensor_tensor(out=t3[:], in0=x2, in1=cosb, op=ALU.mult),
                     reads=[Rtm[tj], Rrope, Rt], writes=[Rt])
                P.op("dve", lambda e, x1=x1: e.tensor_tensor(out=x1, in0=x1, in1=t2[:], op=ALU.subtract),
                     reads=[Rtm[tj], Rt], writes=[Rtm[tj]])
                P.op("dve", lambda e, x2=x2: e.tensor_tensor(out=x2, in0=t1[:], in1=t3[:], op=ALU.add),
                     reads=[Rtm[tj], Rt], writes=[Rtm[tj]])
                for half in range(2):
                    pb = 6 + half
                    ps = self.ps[pb]

                    def tr(e, half=half, ps=ps, T=T):
                        ins = None
                        for q in range(4):
                            a = half * 4 + q
                            ins = e.transpose(out=ps[:, q * 128:(q + 1) * 128], in_=T[:, a * 128:(a + 1) * 128],
                                              identity=self.ident[:])
                        return ins
                    P.op("pe", tr, reads=[Rtm[tj], self.Rconst], writes=[self.psr[pb]])
                    P.op("act", lambda e, half=half, ps=ps, tj=tj: e.copy(
                        out=trs[tj][:, half * 4:(half + 1) * 4, :], in_=ps[:].rearrange("p (a b) -> p a b", b=128)),
                        reads=[self.psr[pb]], writes=[Rtrs[tj]])
                t0 = i * 128
                P.dma("sp", lambda e, inc, tj=tj, t0=t0: inc(e.dma_start(out=S["QT"][:, :, t0:t0 + 128],
                                                                       in_=trs[tj][:, 0:4, :])), 1,
                      reads=[Rtrs[tj]], writes=[SR["QT"]])
                for a, nm in ((4, "KCT"), (5, "KST"), (6, "KWT"), (7, "VCT")):
                    P.dma("sp", lambda e, inc, tj=tj, t0=t0, a=a, nm=nm: inc(e.dma_start(
                        out=S[nm][:, t0:t0 + 128], in_=trs[tj][:, a, :])), 1, reads=[Rtrs[tj]], writes=[SR[nm]])
                P.op("pool", lambda e, tj=tj, T=T: e.tensor_copy(out=vst[tj][:], in_=T[:, 1024:1280]), reads=[Rtm[tj]],
                     writes=[Rvst[tj]])
                P.dma("sp", lambda e, inc, tj=tj, t0=t0: inc(e.dma_start(out=S["VS"][t0:t0 + 128, :], in_=vst[tj][:, 0:128])),
                      1, reads=[Rvst[tj]], writes=[SR["VS"]])
                P.dma("sp", lambda e, inc, tj=tj, t0=t0: inc(e.dma_start(out=S["VW"][t0:t0 + 128, :], in_=vst[tj][:, 128:256])),
                      1, reads=[Rvst[tj]], writes=[SR["VW"]])
                P.op("act", lambda e, T=T: e.activation(out=T[:, 1280:1304], in_=T[:, 1280:1304], func=AF.Sigmoid),
                     reads=[Rtm[tj]], writes=[Rtm[tj]])
                P.dma("sp", lambda e, inc, T=T, t0=t0: inc(e.dma_start(out=S["GN"][t0:t0 + 128, :], in_=T[:, 1280:1304])),
                      1, reads=[Rtm[tj]], writes=[SR["GN"]])
        self.phase_end()

    def phase_pool(self, l):
        P = self.P
        S, SR = self.scr, self.sres
        wp = self.sb([128, 4, 128], BF16)
        Rwp = Res("wpool")
        self.load_cast([(wp[:, g, :], self.d["w_pool"][l, g], [128, 128]) for g in range(4)], Rwp, stage_cols=128)
        psc = self.sb([128, 4], F32)
        Rpsc = Res("psc")
        P.dma("sp", lambda e, inc: inc(e.dma_start(out=psc[:], in_=self.d["pool_scale"][l].rearrange("(g p) -> p g", p=128),
                                                       allow_slow_non_contiguous=True)),
              1, writes=[Rpsc])
        icn = self.sb([128, 64], F32)
        P.dma("sp", lambda e, inc: inc(e.dma_start(out=icn[:], in_=self.d["c_invcnt"][0:1, :].partition_broadcast(128))),
              1, writes=[Rpsc])
        H = 16
        U = [self.sb([128, H + L], F32) for _ in range(2)]
        A = [self.sb([128, H + L], F32) for _ in range(2)]
        Bb = [self.sb([128, H + L], F32) for _ in range(2)]
        Z = [self.sb([128, L], BF16) for _ in range(2)]
        tmp16 = [self.sb([128, 16], F32) for _ in range(2)]
        yst = [self.sb([128, 512], BF16) for _ in range(2)]
        Ryst = [Res("y0"), Res("y1")]
        RU = [Res("U0"), Res("U1")]
        for j in range(2):
            eng = "dve" if j == 0 else "pool"
            for buf in (U[j], A[j], Bb[j]):
                P.op(eng, lambda e, buf=buf: e.memset(buf[:, 0:H], 0.0), writes=[RU[j]])
        for g in range(4):
            j = g % 2
            eng = "dve" if j == 0 else "pool"
            w = 2 << g
            P.dma("sp", lambda e, inc, g=g, j=j: inc(e.dma_start(out=U[j][:, H:], in_=S["UPT"][g * 128:(g + 1) * 128, :])), 1,
                  reads=[SR["UPT"]], writes=[RU[j]])
            src = U[j]
            dsts = [A[j], Bb[j]]
            for k in range(g + 1):
                sh = 1 << k
                dst = dsts[k % 2]
                P.op(eng, lambda e, dst=dst, src=src, sh=sh: e.tensor_tensor(out=dst[:, H:], in0=src[:, H:],
                                                                              in1=src[:, H - sh:H + L - sh], op=ALU.add),
                     reads=[RU[j]], writes=[RU[j]])
                src = dst
            P.op("dve", lambda e, src=src, j=j, w=w: e.scalar_tensor_tensor(out=Z[j][:], in0=src[:, H:], scalar=1.0 / w,
                                                                          in1=U[j][:, H:], op0=ALU.mult,
                                                                          op1=ALU.subtract), reads=[RU[j]], writes=[RU[j]])
            P.op(eng, lambda e, src=src, j=j, g=g: e.tensor_tensor(out=tmp16[j][:], in0=src[:, H:H + 16], in1=icn[:, g * 16:(g + 1) * 16],
                                                                   op=ALU.mult), reads=[RU[j], Rpsc], writes=[RU[j]])
            P.op(eng, lambda e, j=j: e.tensor_tensor(out=Z[j][:, 0:16], in0=tmp16[j][:], in1=U[j][:, H:H + 16],
                                                     op=ALU.subtract), reads=[RU[j]], writes=[RU[j]])
            for c in range(8):
                pb = c % 2
                ps = self.ps[pb]
                P.op("pe", lambda e, ps=ps, g=g, j=j, c=c: e.matmul(ps[:], lhsT=wp[:, g, :], rhs=Z[j][:, c * 512:(c + 1) * 512],
                                                                  start=True, stop=True), reads=[Rwp, RU[j]],
                     writes=[self.psr[pb]])
                yj = c % 2
                P.op("act", lambda e, ps=ps, yj=yj, g=g: e.activation(out=yst[yj][:], in_=ps[:], func=AF.Copy,
                                                                     scale=psc[:, g:g + 1]), reads=[self.psr[pb], Rpsc],
                     writes=[Ryst[yj]])
                P.dma("sp", lambda e, inc, g=g, c=c, yj=yj: inc(e.dma_start(
                    out=S["YPT"][g * 128:(g + 1) * 128, c * 512:(c + 1) * 512], in_=yst[yj][:])), 1, reads=[Ryst[yj]],
                    writes=[SR["YPT"]])
        self.phase_end()

    def cmul(self, eng, outr, outi, ar, ai, br, bi, t1, t2, Rin, Rout, Rt):
        P = self.P
        tt = lambda o, a, b, op: (lambda e: e.tensor_tensor(out=o, in0=a, in1=b, op=op))
        P.op(eng, tt(t1, ar, br, ALU.mult), reads=Rin, writes=[Rt])
        P.op(eng, tt(t2, ai, bi, ALU.mult), reads=Rin + [Rt], writes=[Rt])
        P.op(eng, tt(outr, t1, t2, ALU.subtract), reads=[Rt], writes=[Rout])
        P.op(eng, tt(t1, ar, bi, ALU.mult), reads=Rin + [Rt, Rout], writes=[Rt])
        P.op(eng, tt(t2, ai, br, ALU.mult), reads=Rin + [Rt], writes=[Rt])
        P.op(eng, tt(outi, t1, t2, ALU.add), reads=[Rt], writes=[Rout])

    def phase_s5(self, l):
        P = self.P
        S, SR = self.scr, self.sres
        T = 512
        sm = lambda: self.sb([128, 16], F32)
        lam_n = self.sb([16, 256], F32)
        Rp = Res("s5prep")
        P.dma("sp", lambda e, inc: inc(e.dma_start(out=lam_n[:, 0:128], in_=self.d["ssm_lam_re"][l].rearrange("(j g) p -> j (g p)", g=2))),
              1, writes=[Rp])
        P.dma("sp", lambda e, inc: inc(e.dma_start(out=lam_n[:, 128:256], in_=self.d["ssm_lam_im"][l].rearrange("(j g) p -> j (g p)", g=2))),
              1, writes=[Rp])
        lr, li, stp, rho, th, sn, cs, t1, t2, t3, kr, ki, nki, den = [sm() for _ in range(14)]
        P.op("pe", lambda e: (e.transpose(out=self.ps[7][:, 0:16], in_=lam_n[:, 0:128], identity=self.ident[0:16, 0:16]),
                              e.transpose(out=self.ps[7][:, 16:32], in_=lam_n[:, 128:256], identity=self.ident[0:16, 0:16]))[1],
             reads=[Rp, self.Rconst], writes=[self.psr[7]])
        P.op("dve", lambda e: e.tensor_copy(out=lr[:], in_=self.ps[7][:, 0:16]), reads=[self.psr[7]], writes=[Rp])
        P.op("dve", lambda e: e.tensor_copy(out=li[:], in_=self.ps[7][:, 16:32]), reads=[self.psr[7], Rp], writes=[Rp])
        ldt = self.d["ssm_log_dt"][l:l + 1, :].rearrange("o (j g) -> o j g", g=2)
        for gl in range(2):
            P.dma("sp", lambda e, inc, gl=gl: inc(e.dma_start(out=stp[gl * 64:(gl + 1) * 64, :],
                                                            in_=ldt[:, :, gl].partition_broadcast(64),
                                                            allow_slow_non_contiguous=True)), 1, reads=[Rp], writes=[Rp])
        P.op("act", lambda e: e.activation(out=stp[:], in_=stp[:], func=AF.Exp), reads=[Rp], writes=[Rp])
        V = lambda fn: P.op("dve", fn, reads=[Rp], writes=[Rp])
        V(lambda e: e.tensor_tensor(out=t1[:], in0=lr[:], in1=stp[:], op=ALU.mult))
        P.op("act", lambda e: e.activation(out=rho[:], in_=t1[:], func=AF.Exp), reads=[Rp], writes=[Rp])
        V(lambda e: e.tensor_tensor(out=th[:], in0=li[:], in1=stp[:], op=ALU.mult))
        TWO_PI = 2.0 * math.pi
        ni = self.sb([128, 16], mybir.dt.int32)
        nf = sm()

        def reduce_turns(dst, shift):
            V(lambda e: e.tensor_scalar(out=dst[:], in0=th[:], scalar1=1.0 / TWO_PI, scalar2=shift, op0=ALU.mult, op1=ALU.add))
            V(lambda e: e.tensor_copy(out=ni[:], in_=dst[:]))
            V(lambda e: e.tensor_copy(out=nf[:], in_=ni[:]))
            V(lambda e: e.tensor_tensor(out=dst[:], in0=dst[:], in1=nf[:], op=ALU.subtract))
            V(lambda e: e.tensor_single_scalar(out=nf[:], in_=dst[:], scalar=0.5, op=ALU.is_gt))
            V(lambda e: e.tensor_tensor(out=dst[:], in0=dst[:], in1=nf[:], op=ALU.subtract))
            V(lambda e: e.tensor_single_scalar(out=nf[:], in_=dst[:], scalar=-0.5, op=ALU.is_lt))
            V(lambda e: e.tensor_tensor(out=dst[:], in0=dst[:], in1=nf[:], op=ALU.add))
        reduce_turns(t1, 0.0)
        reduce_turns(t2, 0.25)
        P.op("act", lambda e: e.activation(out=sn[:], in_=t1[:], func=AF.Sin, scale=TWO_PI), reads=[Rp], writes=[Rp])
        P.op("act", lambda e: e.activation(out=cs[:], in_=t2[:], func=AF.Sin, scale=TWO_PI), reads=[Rp], writes=[Rp])
        V(lambda e: e.tensor_tensor(out=t1[:], in0=rho[:], in1=cs[:], op=ALU.mult))
        V(lambda e: e.tensor_scalar_add(out=t1[:], in0=t1[:], scalar1=-1.0))
        V(lambda e: e.tensor_tensor(out=t2[:], in0=rho[:], in1=sn[:], op=ALU.mult))
        V(lambda e: e.tensor_tensor(out=den[:], in0=lr[:], in1=lr[:], op=ALU.mult))
        V(lambda e: e.tensor_tensor(out=t3[:], in0=li[:], in1=li[:], op=ALU.mult))
        V(lambda e: e.tensor_tensor(out=den[:], in0=den[:], in1=t3[:], op=ALU.add))
        V(lambda e: e.reciprocal(out=den[:], in_=den[:]))
        V(lambda e: e.tensor_tensor(out=kr[:], in0=t1[:], in1=lr[:], op=ALU.mult))
        V(lambda e: e.tensor_tensor(out=t3[:], in0=t2[:], in1=li[:], op=ALU.mult))
        V(lambda e: e.tensor_tensor(out=kr[:], in0=kr[:], in1=t3[:], op=ALU.add))
        V(lambda e: e.tensor_tensor(out=kr[:], in0=kr[:], in1=den[:], op=ALU.mult))
        V(lambda e: e.tensor_tensor(out=ki[:], in0=t2[:], in1=lr[:], op=ALU.mult))
        V(lambda e: e.tensor_tensor(out=t3[:], in0=t1[:], in1=li[:], op=ALU.mult))
        V(lambda e: e.tensor_tensor(out=ki[:], in0=ki[:], in1=t3[:], op=ALU.subtract))
        V(lambda e: e.tensor_tensor(out=ki[:], in0=ki[:], in1=den[:], op=ALU.mult))
        V(lambda e: e.tensor_scalar_mul(out=nki[:], in0=ki[:], scalar1=-1.0))
        Er = self.sb([128, 16, T], F32)
        Ei = self.sb([128, 16, T], F32)
        ETr, ETi, emr, emi = sm(), sm(), sm(), sm()
        RE = Res("E")
        Rt = Res("ctmp")
        P.op("pool", lambda e: e.memset(Er[:, :, 0:1], 1.0), writes=[RE])
        P.op("pool", lambda e: e.memset(Ei[:, :, 0:1], 0.0), reads=[RE], writes=[RE])
        P.op("pool", lambda e: e.tensor_copy(out=Er[:, :, 1:2], in_=cs[:].unsqueeze(2)), reads=[Rp, RE], writes=[RE])
        P.op("pool", lambda e: e.tensor_copy(out=Ei[:, :, 1:2], in_=sn[:].unsqueeze(2)), reads=[Rp, RE], writes=[RE])
        LB = self.sb([128, 16, 2, 128], BF16)
        LC = self.sb([128, 16, 2, 128], BF16)
        RL = Res("LBC")
        ct = self.sb([128, 128], F32)
        dsk = self.sb([128, 4], F32)
        bgl = self.sb([128, 4], F32)
        Wg = self.sb([128, 4, 512], BF16)
        RWg = Res("wglu")
        wgl = self.d["w_glu"][l].rearrange("(a p) c -> p a c", p=128)
        self.load_cast([(Wg[:, :, 0:256], wgl[:, :, 0:256], [128, 4, 256]), (Wg[:, :, 256:512], wgl[:, :, 256:512], [128, 4, 256])],
                       RWg, stage_cols=1024)
        mark = self.cur
        big1 = self.sb([128, 16, 256], F32)
        big2 = self.sb([128, 16, 256], F32)
        m = 2
        while m < T:
            self.cmul("dve", emr[:], emi[:], Er[:, :, m - 1], Ei[:, :, m - 1], cs[:], sn[:], t1[:], t2[:], [RE, Rp], RE, Rt)
            bc = lambda a: a[:].unsqueeze(2).to_broadcast([128, 16, m])
            self.cmul("dve", Er[:, :, m:2 * m], Ei[:, :, m:2 * m], Er[:, :, 0:m], Ei[:, :, 0:m], bc(emr), bc(emi),
                      big1[:, :, 0:m], big2[:, :, 0:m], [RE], RE, Rt)
            m *= 2
        self.cmul("dve", ETr[:], ETi[:], Er[:, :, T - 1], Ei[:, :, T - 1], cs[:], sn[:], t1[:], t2[:], [RE, Rp], RE, Rt)
        Bp = [self.sb([128, 16, 128], F32) for _ in range(2)]
        Cp = [self.sb([128, 16, 128], F32) for _ in range(2)]
        Rpad = Res("pads")
        for t_ in Bp + Cp:
            P.op("pool", lambda e, t_=t_: e.memset(t_[:], 0.0), writes=[Rpad])
        for ri, (bn, cn) in enumerate((("ssm_b_re", "ssm_c_re"), ("ssm_b_im", "ssm_c_im"))):
            bsrc = self.d[bn][l].rearrange("(a q g) p c -> g q p a c", q=4, g=2)
            csrc = self.d[cn][l].rearrange("(a q g) c p -> g q c a p", q=4, g=2)
            for gl in range(2):
                for q in range(4):
                    P.dma("sp", lambda e, inc, ri=ri, gl=gl, q=q, bsrc=bsrc: inc(e.dma_start(
                        out=Bp[ri][gl * 64:(gl + 1) * 64, q:16:4, q * 32 + gl * 16:q * 32 + gl * 16 + 16], in_=bsrc[gl, q])), 1,
                        reads=[Rpad], writes=[Rpad])
                    P.dma("sp", lambda e, inc, ri=ri, gl=gl, q=q, csrc=csrc: inc(e.dma_start(
                        out=Cp[ri][q * 32 + gl * 16:q * 32 + gl * 16 + 16, q:16:4, gl * 64:(gl + 1) * 64], in_=csrc[gl, q])), 1,
                        reads=[Rpad], writes=[Rpad])
        for j in range(16):
            pb = 6 + j % 2
            ps = self.ps[pb]

            def tr(e, j=j, ps=ps):
                ins = None
                for q, src in enumerate((Bp[0], Bp[1], Cp[0], Cp[1])):
                    ins = e.transpose(out=ps[:, q * 128:(q + 1) * 128], in_=src[:, j, :], identity=self.ident[:])
                return ins
            P.op("pe", tr, reads=[Rpad, self.Rconst], writes=[self.psr[pb]])
            P.op("act", lambda e, j=j, ps=ps: e.copy(out=LB[:, j, :, :], in_=ps[:, 0:256].rearrange("p (a b) -> p a b", b=128)),
                 reads=[self.psr[pb]], writes=[RL])
            P.op("dve", lambda e, j=j, ps=ps: e.tensor_scalar(out=ct[:], in0=ps[:, 384:512], scalar1=ki[:, j:j + 1], scalar2=None,
                                                             op0=ALU.mult), reads=[self.psr[pb], Rp, RL], writes=[Rt])
            P.op("dve", lambda e, j=j, ps=ps: e.scalar_tensor_tensor(out=LC[:, j, 0, :], in0=ps[:, 256:384], scalar=kr[:, j:j + 1],
                                                                    in1=ct[:], op0=ALU.mult, op1=ALU.subtract),
                 reads=[self.psr[pb], Rp, Rt], writes=[RL])
            P.op("dve", lambda e, j=j, ps=ps: e.tensor_scalar(out=ct[:], in0=ps[:, 384:512], scalar1=kr[:, j:j + 1], scalar2=None,
                                                             op0=ALU.mult), reads=[self.psr[pb], Rp, RL], writes=[Rt])
            P.op("dve", lambda e, j=j, ps=ps: e.scalar_tensor_tensor(out=LC[:, j, 1, :], in0=ps[:, 256:384], scalar=nki[:, j:j + 1],
                                                                    in1=ct[:], op0=ALU.mult, op1=ALU.subtract),
                 reads=[self.psr[pb], Rp, Rt], writes=[RL])
        P.dma("sp", lambda e, inc: inc(e.dma_start(out=dsk[:], in_=self.d["ssm_d"][l].rearrange("g c -> (g c)").rearrange("(a p) -> p a", p=128),
                                                   allow_slow_non_contiguous=True)), 1, writes=[Rp])
        P.dma("sp", lambda e, inc: inc(e.dma_start(out=bgl[:], in_=self.d["b_glu"][l].rearrange("(a p) -> p a", p=128),
                                                   allow_slow_non_contiguous=True)), 1, writes=[Rp])
        self.P.barrier()
        self.cur = mark
        uf = [self.sb([128, 4, T], F32) for _ in range(2)]
        ub = [self.sb([128, 4, T], BF16) for _ in range(2)]
        Ruf = [Res("uf0"), Res("uf1")]
        Rub = [Res("ub0"), Res("ub1")]
        NB = 2
        Pr = [self.sb([128, T], F32) for _ in range(NB)]
        Pi = [self.sb([128, T], F32) for _ in range(NB)]
        m1 = [self.sb([128, T], F32) for _ in range(NB)]
        m2 = [self.sb([128, T], F32) for _ in range(NB)]
        m3 = [self.sb([128, T], F32) for _ in range(NB)]
        m4 = [self.sb([128, T], F32) for _ in range(NB)]
        gr = [self.sb([128, T], F32) for _ in range(NB)]
        gi = [self.sb([128, T], F32) for _ in range(NB)]
        Hr = [self.sb([128, T], BF16) for _ in range(NB)]
        Hi = [self.sb([128, T], BF16) for _ in range(NB)]
        RPs = [Res("Ps%d" % i) for i in range(NB)]
        Rm = [Res("m%d" % i) for i in range(NB)]
        Rg = [Res("g%d" % i) for i in range(NB)]
        RH = [Res("H%d" % i) for i in range(NB)]
        glr, gli, gir, gii = sm(), sm(), sm(), sm()
        Rgl = Res("glast")
        Rgi = Res("ginit")
        P.op("pool", lambda e: e.memset(gir[:], 0.0), writes=[Rgi])
        P.op("pool", lambda e: e.memset(gii[:], 0.0), reads=[Rgi], writes=[Rgi])
        ysb = self.sb([128, T], F32)
        Rysb = Res("ysb")
        zf = self.sb([128, 4, T], F32)
        zb = self.sb([128, 4, T], BF16)
        Rzf, Rzb = Res("zf"), Res("zb")
        sgl = self.sb([128, T], F32)
        Rsgl = Res("sgl")
        ost = [self.sb([128, T], BF16) for _ in range(2)]
        Rost = [Res("ost0"), Res("ost1")]
        USTd = S["UST"].rearrange("(a p) t -> p a t", p=128)
        n = 0
        for c in range(L // T):
            cj = c % 2
            P.dma("sp", lambda e, inc, c=c, cj=cj: inc(e.dma_start(out=uf[cj][:], in_=USTd[:, :, c * T:(c + 1) * T])), 1,
                  reads=[SR["UST"]], writes=[Ruf[cj]])
            P.op("pool", lambda e, cj=cj: e.tensor_copy(out=ub[cj][:], in_=uf[cj][:]), reads=[Ruf[cj]], writes=[Rub[cj]])
            for j in range(16):
                kt = j // 4
                b_ = n % NB
                n += 1
                pa, pbk = (0, 1) if b_ == 0 else (2, 3)
                P.op("pe", lambda e, j=j, kt=kt, cj=cj, pa=pa: e.matmul(self.ps[pa][:], lhsT=LB[:, j, 0, :], rhs=ub[cj][:, kt, :],
                                                                     start=True, stop=True), reads=[RL, Rub[cj]],
                     writes=[self.psr[pa]])
                P.op("pe", lambda e, j=j, kt=kt, cj=cj, pbk=pbk: e.matmul(self.ps[pbk][:], lhsT=LB[:, j, 1, :], rhs=ub[cj][:, kt, :],
                                                                       start=True, stop=True), reads=[RL, Rub[cj]],
                     writes=[self.psr[pbk]])
                P.op("act", lambda e, b_=b_, pa=pa: e.copy(out=Pr[b_][:], in_=self.ps[pa][:]), reads=[self.psr[pa]],
                     writes=[RPs[b_]])
                P.op("act", lambda e, b_=b_, pbk=pbk: e.copy(out=Pi[b_][:], in_=self.ps[pbk][:]), reads=[self.psr[pbk], RPs[b_]],
                     writes=[RPs[b_]])
                PL = lambda o, a, bb, rd, wr: P.op("pool", lambda e: e.tensor_tensor(out=o, in0=a, in1=bb, op=ALU.mult),
                                                   reads=rd, writes=wr)
                PL(m1[b_][:], Er[:, j, :], Pr[b_][:], [RE, RPs[b_]], [Rm[b_]])
                PL(m2[b_][:], Ei[:, j, :], Pi[b_][:], [RE, RPs[b_], Rm[b_]], [Rm[b_]])
                PL(m3[b_][:], Er[:, j, :], Pi[b_][:], [RE, RPs[b_], Rm[b_]], [Rm[b_]])
                PL(m4[b_][:], Ei[:, j, :], Pr[b_][:], [RE, RPs[b_], Rm[b_]], [Rm[b_]])
                P.op("dve", lambda e, b_=b_: e.tensor_tensor(out=m1[b_][:], in0=m1[b_][:], in1=m2[b_][:], op=ALU.add),
                     reads=[Rm[b_]], writes=[Rm[b_]])
                P.op("dve", lambda e, b_=b_: e.tensor_tensor(out=m3[b_][:], in0=m3[b_][:], in1=m4[b_][:], op=ALU.subtract),
                     reads=[Rm[b_]], writes=[Rm[b_]])
                P.op("dve", lambda e, b_=b_, j=j: e.tensor_tensor_scan(
                    out=gr[b_][:], data0=rho[:, j:j + 1].to_broadcast([128, T]), data1=m1[b_][:], initial=gir[:, j:j + 1],
                    op0=ALU.mult, op1=ALU.add), reads=[Rm[b_], Rp, Rgi], writes=[Rg[b_]])
                P.op("dve", lambda e, b_=b_, j=j: e.tensor_tensor_scan(
                    out=gi[b_][:], data0=rho[:, j:j + 1].to_broadcast([128, T]), data1=m3[b_][:], initial=gii[:, j:j + 1],
                    op0=ALU.mult, op1=ALU.add), reads=[Rm[b_], Rp, Rgi, Rg[b_]], writes=[Rg[b_]])
                P.op("pool", lambda e, b_=b_, j=j: e.tensor_copy(out=glr[:, j:j + 1], in_=gr[b_][:, T - 1:T]), reads=[Rg[b_]],
                     writes=[Rgl])
                P.op("pool", lambda e, b_=b_, j=j: e.tensor_copy(out=gli[:, j:j + 1], in_=gi[b_][:, T - 1:T]), reads=[Rg[b_], Rgl],
                     writes=[Rgl])
                PL(m1[b_][:], Er[:, j, :], gr[b_][:], [RE, Rg[b_], Rm[b_]], [Rm[b_]])
                PL(m2[b_][:], Ei[:, j, :], gi[b_][:], [RE, Rg[b_], Rm[b_]], [Rm[b_]])
                PL(m3[b_][:], Er[:, j, :], gi[b_][:], [RE, Rg[b_], Rm[b_]], [Rm[b_]])
                PL(m4[b_][:], Ei[:, j, :], gr[b_][:], [RE, Rg[b_], Rm[b_]], [Rm[b_]])
                P.op("dve", lambda e, b_=b_: e.tensor_tensor(out=Hr[b_][:], in0=m1[b_][:], in1=m2[b_][:], op=ALU.subtract),
                     reads=[Rm[b_]], writes=[RH[b_]])
                P.op("dve", lambda e, b_=b_: e.tensor_tensor(out=Hi[b_][:], in0=m3[b_][:], in1=m4[b_][:], op=ALU.add),
                     reads=[Rm[b_], RH[b_]], writes=[RH[b_]])
                py = 4 + kt % 2

                def mmy(e, j=j, b_=b_, py=py):
                    e.matmul(self.ps[py][:], lhsT=LC[:, j, 0, :], rhs=Hr[b_][:], start=(j % 4 == 0), stop=False)
                    return e.matmul(self.ps[py][:], lhsT=LC[:, j, 1, :], rhs=Hi[b_][:], start=False, stop=(j % 4 == 3))
                P.op("pe", mmy, reads=[RL, RH[b_]], writes=[self.psr[py]])
                if j % 4 == 3:
                    P.op("dve", lambda e, kt=kt, cj=cj, py=py: e.scalar_tensor_tensor(
                        out=ysb[:], in0=uf[cj][:, kt, :], scalar=dsk[:, kt:kt + 1], in1=self.ps[py][:], op0=ALU.mult, op1=ALU.add),
                        reads=[self.psr[py], Ruf[cj], Rp], writes=[Rysb])
                    P.op("act", lambda e, kt=kt: e.activation(out=zf[:, kt, :], in_=ysb[:], func=AF.Gelu_apprx_tanh),
                         reads=[Rysb], writes=[Rzf])
                    P.op("pool", lambda e, kt=kt: e.tensor_copy(out=zb[:, kt, :], in_=zf[:, kt, :]), reads=[Rzf], writes=[Rzb])
            self.cmul("dve", gir[:], gii[:], ETr[:], ETi[:], glr[:], gli[:], t1[:], t2[:], [RE, Rgl], Rgi, Rt)
            for ot in range(4):
                def mmg(e, ot=ot):
                    ins = None
                    for a in range(4):
                        ins = e.matmul(self.ps[6][:], lhsT=Wg[:, a, ot * 128:(ot + 1) * 128], rhs=zb[:, a, :], start=(a == 0),
                                       stop=(a == 3))
                    return ins
                P.op("pe", mmg, reads=[RWg, Rzb], writes=[self.psr[6]])
                P.op("act", lambda e, ot=ot: e.activation(out=sgl[:], in_=self.ps[6][:], func=AF.Sigmoid, bias=bgl[:, ot:ot + 1]),
                     reads=[self.psr[6], Rp], writes=[Rsgl])
                oj = ot % 2
                P.op("dve", lambda e, ot=ot, oj=oj: e.tensor_tensor(out=ost[oj][:], in0=zf[:, ot, :], in1=sgl[:], op=ALU.mult),
                     reads=[Rzf, Rsgl], writes=[Rost[oj]])
                P.dma("sp", lambda e, inc, ot=ot, oj=oj, c=c: inc(e.dma_start(
                    out=S["YST"][ot * 128:(ot + 1) * 128, c * T:(c + 1) * T], in_=ost[oj][:])), 1, reads=[Rost[oj]],
                    writes=[SR["YST"]])
        self.phase_end()

    def phase_nsa(self, l):
        P = self.P
        S, SR = self.scr, self.sres
        ps, psr = self.ps, self.psr
        QT = self.sb([128, 4, L], BF16)
        KT = {n: self.sb([128, L], BF16) for n in ("KCT", "KST", "KWT", "VCT")}
        RQ = Res("QTs")
        P.dma("sp", lambda e, inc: inc(e.dma_start(out=QT[:], in_=S["QT"][:, :, :])), 1, reads=[SR["QT"]], writes=[RQ])
        for n in KT:
            P.dma("sp", lambda e, inc, n=n: inc(e.dma_start(out=KT[n][:], in_=S[n][:, :])), 1, reads=[SR[n]], writes=[RQ])
        Va = {n: self.sb([128, NT, 2, 65], BF16) for n in ("VS", "VW")}
        for n in Va:
            P.op("pool", lambda e, n=n: e.memset(Va[n][:, :, :, 64:65], 1.0), writes=[RQ])
            for k in range(2):
                P.dma("sp", lambda e, inc, n=n, k=k: inc(e.dma_start(
                    out=Va[n][:, :, k, 0:64], in_=S[n].rearrange("(a p) c -> p a c", p=128)[:, :, k * 64:(k + 1) * 64])), 1,
                    reads=[SR[n], RQ], writes=[RQ])
        EB = self.sb([128, L], BF16)
        for half in range(2):
            P.dma("sp", lambda e, inc, half=half: inc(e.dma_start(out=EB[half * 64:(half + 1) * 64, :], in_=self.d["c_ebig"][:, :])),
                  1, writes=[RQ])
        G = self.sb([128, NT, 24], F32)
        P.dma("sp", lambda e, inc: inc(e.dma_start(out=G[:], in_=S["GN"].rearrange("(a p) c -> p a c", p=128))), 1,
              reads=[SR["GN"]], writes=[RQ])
        W1 = {t: self.sb([128, 32, 128], BF16) for t in "kv"}
        W2kd = self.sb([128, 128], BF16)
        W2v = self.sb([128, 64], BF16)
        RWc = Res("Wc")
        pieces = []
        for t, nm in (("k", "cmp_wk1"), ("v", "cmp_wv1")):
            src = self.d[nm][l].rearrange("(l d) h -> d l h", d=64)
            for half in range(2):
                for o in range(0, 32, 8):
                    pieces.append((W1[t][half * 64:(half + 1) * 64, o:o + 8, :], src[:, o:o + 8, :], [64, 8, 128], half * 64))
        pieces.append((W2kd[:, 0:64], self.d["cmp_wk2"][l], [128, 64]))
        pieces.append((W2kd[:, 64:128], self.d["cmp_wk2"][l], [128, 64]))
        pieces.append((W2v[:], self.d["cmp_wv2"][l], [128, 64]))
        self.load_cast(pieces, RWc, stage_cols=1024)
        pe_n = self.sb([32, 256], F32)
        Rpe = Res("pe")
        for ti, nm in enumerate(("cmp_pe_k", "cmp_pe_v")):
            for half in range(2):
                P.dma("sp", lambda e, inc, ti=ti, nm=nm, half=half: inc(e.dma_start(
                    out=pe_n[:, ti * 128 + half * 64:ti * 128 + half * 64 + 64], in_=self.d[nm][l])), 1, writes=[Rpe])
        peT = self.sb([128, 2, 32], BF16)
        P.op("pe", lambda e: (e.transpose(out=ps[7][:, 0:32], in_=pe_n[:, 0:128], identity=self.ident[0:32, 0:32]),
                              e.transpose(out=ps[7][:, 32:64], in_=pe_n[:, 128:256], identity=self.ident[0:32, 0:32]))[1],
             reads=[Rpe, self.Rconst], writes=[psr[7]])
        P.op("dve", lambda e: e.tensor_copy(out=peT[:], in_=ps[7][:, 0:64].rearrange("p (a b) -> p a b", b=32)), reads=[psr[7]],
             writes=[Rpe])
        cb = self.sb([128, 2], F32)
        for ti, t in enumerate("kv"):
            def mmb(e, ti=ti, t=t):
                ins = None
                for li_ in range(32):
                    ins = e.matmul(ps[7][:, 64 + ti:65 + ti], lhsT=W1[t][0:64, li_, :], rhs=peT[0:64, ti, li_:li_ + 1],
                                   start=(li_ == 0), stop=(li_ == 31))
                return ins
            P.op("pe", mmb, reads=[RWc, Rpe], writes=[psr[7]])
        P.op("dve", lambda e: e.tensor_copy(out=cb[:], in_=ps[7][:, 64:66]), reads=[psr[7]], writes=[Rpe])
        KcT = self.sb([128, 2, 256], BF16)
        rcmp = self.sb([128, 2, 2, 129], BF16)
        Rcmp = Res("cmpops")
        P.op("pool", lambda e: e.memset(KcT[:], 0.0), writes=[Rcmp])
        P.op("pool", lambda e: e.memset(rcmp[:], 0.0), reads=[Rcmp], writes=[Rcmp])
        P.op("pool", lambda e: e.memset(rcmp[:, :, :, 64:65], 1.0), reads=[Rcmp], writes=[Rcmp])
        ovf = self.sb([128, 2, 64], F32)
        P.dma("sp", lambda e, inc: inc(e.dma_start(out=ovf[:], in_=self.d["c_ov"].rearrange("(a p) s -> p a s", p=128))), 1,
              writes=[Rpe])
        for k in range(2):
            P.op("pool", lambda e, k=k: e.tensor_copy(out=rcmp[:, :, k, 65:129], in_=ovf[:]), reads=[Rpe, Rcmp], writes=[Rcmp])
        hid = [self.sb([128, 256], BF16) for _ in range(2)]
        Rhid = [Res("hid0"), Res("hid1")]
        for hj in range(2):
            P.op("pool", lambda e, hj=hj: e.memset(hid[hj][:], 0.0), writes=[Rhid[hj]])
        ci = 0
        for ti, (t, srcn) in enumerate((("k", "KCT"), ("v", "VCT"))):
            for k in range(2):
                hj = ci % 2
                pb = ci % 2
                ci += 1

                def mmh(e, t=t, srcn=srcn, k=k, pb=pb):
                    ins = None
                    for li_ in range(32):
                        ins = e.matmul(ps[pb][:, 0:255], lhsT=W1[t][k * 64:(k + 1) * 64, li_, :],
                                       rhs=KT[srcn][k * 64:(k + 1) * 64, li_:li_ + 4065:16], start=(li_ == 0), stop=(li_ == 31))
                    return ins
                P.op("pe", mmh, reads=[RWc, RQ], writes=[psr[pb]])
                P.op("act", lambda e, hj=hj, pb=pb, ti=ti: e.activation(out=hid[hj][:, 0:255], in_=ps[pb][:, 0:255],
                                                                      func=AF.Gelu_apprx_tanh, bias=cb[:, ti:ti + 1]),
                     reads=[psr[pb], Rpe], writes=[Rhid[hj]])
                if t == "k":
                    P.op("pe", lambda e, hj=hj: e.matmul(ps[2][:, 0:255], lhsT=W2kd[:], rhs=hid[hj][:, 0:255], start=True, stop=True),
                         reads=[RWc, Rhid[hj]], writes=[psr[2]])
                    P.op("dve", lambda e, k=k: e.tensor_copy(out=KcT[:, k, 0:255], in_=ps[2][:, 0:255]), reads=[psr[2], Rcmp],
                         writes=[Rcmp])
                else:
                    for nt, rows in ((0, 128), (1, 127)):
                        P.op("pe", lambda e, hj=hj, nt=nt, rows=rows: e.matmul(
                            ps[3][0:rows, nt * 64:(nt + 1) * 64], lhsT=hid[hj][:, nt * 128:nt * 128 + rows], rhs=W2v[:], start=True,
                            stop=True), reads=[RWc, Rhid[hj]], writes=[psr[3]])
                        P.op("dve", lambda e, k=k, nt=nt, rows=rows: e.tensor_copy(out=rcmp[0:rows, nt, k, 0:64],
                                                                                 in_=ps[3][0:rows, nt * 64:(nt + 1) * 64]),
                             reads=[psr[3], Rcmp], writes=[Rcmp])
        import os
        if os.environ.get("NSA_UNITS") == "0":
            self.phase_end()
            return
        PT = [self.sb([128, 512], BF16) for _ in range(3)]
        RPT = [Res("PT%d" % i) for i in range(3)]
        yt = [self.sb([128, 512], F32) for _ in range(2)]
        Ryt = [Res("yt0"), Res("yt1")]
        yT = [self.sb([128, 4, 128], BF16) for _ in range(2)]
        RyT = [Res("yT0"), Res("yT1")]
        rsc = self.sb([128, 4], F32)
        wgt = self.sb([128, 4], F32)
        acc = self.sb([128, 64], F32)
        sc = self.sb([128, 64], F32)
        sc2 = self.sb([128, 64], F32)
        m8 = self.sb([128, 16], F32)
        biasf = self.sb([128, 128], F32)
        biasT = self.sb([128, 128], BF16)
        Rpost = Res("post")
        Rbias = Res("biasf")
        RbT = Res("biasT")
        units = []
        deferred = []

        def flush_deferred():
            for f in deferred:
                f()
            deferred.clear()

        for qt in range(NT):
            for k in range(2):
                QTt = QT[k * 64:(k + 1) * 64, :, qt * 128:(qt + 1) * 128]
                yj = qt % 2
                yacc = yt[yj][:, k * 256:(k + 1) * 256].rearrange("p (h d) -> p h d", d=64)
                gsl = lambda b, qt=qt, k=k: G[:, qt, k * 12 + b:k * 12 + 12:3]

                def post_cmp(qt=qt, k=k, yacc=yacc, gsl=gsl, yj=yj):
                    Dv = lambda fn, rd, wr: P.op("dve", fn, reads=rd, writes=wr)
                    for bi, b_ in enumerate((3, 4)):
                        Dv(lambda e, bi=bi, b_=b_: e.tensor_scalar(out=rsc[:, bi * 2:bi * 2 + 2], in0=ps[b_][:, 64:64 + 129 + 1:129],
                                                                    scalar1=1e-30, scalar2=None, op0=ALU.max), [psr[b_], Rpost], [Rpost])
                    Dv(lambda e: e.reciprocal(out=rsc[:], in_=rsc[:]), [Rpost], [Rpost])
                    for h in range(4):
                        b_ = 3 + h // 2
                        o = (h % 2) * 129
                        if h == 0:
                            Dv(lambda e, b_=b_, o=o: e.tensor_scalar(out=acc[:], in0=ps[b_][:, o + 65:o + 129], scalar1=rsc[:, 0:1],
                                                                     scalar2=None, op0=ALU.mult), [psr[b_], Rpost], [Rpost])
                        else:
                            Dv(lambda e, b_=b_, o=o, h=h: e.scalar_tensor_tensor(out=acc[:], in0=ps[b_][:, o + 65:o + 129],
                                                                                 scalar=rsc[:, h:h + 1], in1=acc[:], op0=ALU.mult,
                                                                                 op1=ALU.add), [psr[b_], Rpost], [Rpost])
                    Dv(lambda e: e.tensor_tensor(out=wgt[:], in0=rsc[:], in1=gsl(0), op=ALU.mult), [Rpost, RQ], [Rpost])
                    for h in range(4):
                        b_ = 3 + h // 2
                        o = (h % 2) * 129
                        Dv(lambda e, b_=b_, o=o, h=h: e.tensor_scalar(out=yacc[:, h, :], in0=ps[b_][:, o:o + 64], scalar1=wgt[:, h:h + 1],
                                                                       scalar2=None, op0=ALU.mult), [psr[b_], Rpost, Ryt[yj]], [Ryt[yj]])
                    Dv(lambda e: e.tensor_copy(out=sc[:], in_=acc[:]), [Rpost], [Rpost])
                    for e_ in range(2):
                        cur = 2 * qt + e_
                        rows = slice(e_ * 64, (e_ + 1) * 64)
                        if cur + 1 < 64:
                            Dv(lambda e, rows=rows, cur=cur: e.memset(sc[rows, cur + 1:64], -1e6), [Rpost], [Rpost])
                        Dv(lambda e, rows=rows: e.memset(sc[rows, 0:1], 1e6), [Rpost], [Rpost])
                        lo = max(cur - 1, 0)
                        Dv(lambda e, rows=rows, lo=lo, cur=cur: e.memset(sc[rows, lo:cur + 1], 1e6), [Rpost], [Rpost])
                    Dv(lambda e: e.max(out=m8[:, 0:8], in_=sc[:]), [Rpost], [Rpost])
                    Dv(lambda e: e.match_replace(out=sc2[:], in_to_replace=m8[:, 0:8], in_values=sc[:], imm_value=-1e30), [Rpost],
                       [Rpost])
                    Dv(lambda e: e.max(out=m8[:, 8:16], in_=sc2[:]), [Rpost], [Rpost])
                    for hh in range(2):
                        Dv(lambda e, hh=hh: e.tensor_scalar(out=biasf[:, hh * 64:(hh + 1) * 64], in0=sc[:], scalar1=m8[:, 15:16],
                                                            scalar2=-BIG, op0=ALU.is_lt, op1=ALU.mult), [Rpost, Rbias], [Rbias])

                def pre_sel():
                    flush_deferred()
                    P.op("pe", lambda e: e.transpose(out=ps[7][:, 0:128], in_=biasf[:], identity=self.ident[:]),
                         reads=[Rbias, self.Rconst], writes=[psr[7]])
                    P.op("act", lambda e: e.copy(out=biasT[:], in_=ps[7][:, 0:128]), reads=[psr[7]], writes=[RbT])

                def post_branch(bank, b, yacc=yacc, gsl=gsl, yj=yj):
                    def f():
                        Dv = lambda fn, rd, wr: P.op("dve", fn, reads=rd, writes=wr)
                        Dv(lambda e: e.reciprocal(out=rsc[:], in_=ps[bank][:, 64:64 + 3 * 65 + 1:65]), [psr[bank], Rpost], [Rpost])
                        Dv(lambda e: e.tensor_tensor(out=wgt[:], in0=rsc[:], in1=gsl(b), op=ALU.mult), [Rpost, RQ], [Rpost])
                        for h in range(4):
                            Dv(lambda e, h=h: e.scalar_tensor_tensor(out=yacc[:, h, :], in0=ps[bank][:, h * 65:h * 65 + 64],
                                                                     scalar=wgt[:, h:h + 1], in1=yacc[:, h, :], op0=ALU.mult,
                                                                     op1=ALU.add), [psr[bank], Rpost, Ryt[yj]], [Ryt[yj]])
                    return f

                def post_qt(qt=qt, yj=yj):
                    def tr_store():
                        def tr(e):
                            ins = None
                            for a in range(4):
                                ins = e.transpose(out=ps[7][:, a * 128:(a + 1) * 128], in_=yt[yj][:, a * 128:(a + 1) * 128],
                                                  identity=self.ident[:])
                            return ins
                        P.op("pe", tr, reads=[Ryt[yj], self.Rconst], writes=[psr[7]])
                        P.op("act", lambda e: e.copy(out=yT[yj][:], in_=ps[7][:].rearrange("p (a b) -> p a b", b=128)),
                             reads=[psr[7]], writes=[RyT[yj]])
                        P.dma("sp", lambda e, inc: inc(e.dma_start(
                            out=S["YNT"].rearrange("(a p) t -> p a t", p=128)[:, :, qt * 128:(qt + 1) * 128], in_=yT[yj][:])), 1,
                            reads=[RyT[yj]], writes=[SR["YNT"]])
                    return lambda: deferred.append(tr_store)

                nts = [0] if qt < 16 else [0, 1]
                for nt in nts:
                    rows = 128 if nt == 0 else 127
                    full = (128 * nt + rows - 1) <= 8 * qt - 2
                    mask = None if full else dict(pattern=[[0, 4], [1, 128]], base=128 * qt - 2048 * nt - 31, cm=-16)
                    units.append(dict(kind="cmp", rows=rows, lhsT=KcT[k * 64:(k + 1) * 64, k, nt * 128:nt * 128 + rows], rhs=QTt,
                                      mask=mask, v=rcmp[0:rows, nt, k, :], first=(nt == 0), last=(nt == nts[-1]), pre=None,
                                      post=(post_cmp if nt == nts[-1] else None)))
                wk = list(range(max(0, qt - 4), qt + 1))
                for kt in wk:
                    mask = None
                    if kt == qt:
                        mask = dict(pattern=[[0, 4], [1, 128]], base=0, cm=-1)
                    elif kt == qt - 4:
                        mask = dict(pattern=[[0, 4], [-1, 128]], base=-1, cm=1)
                    units.append(dict(kind="win", rows=128, lhsT=KT["KWT"][k * 64:(k + 1) * 64, kt * 128:(kt + 1) * 128], rhs=QTt,
                                      mask=mask, v=Va["VW"][:, kt, k, :], first=(kt == wk[0]), last=(kt == qt), pre=None,
                                      post=(post_branch(6, 2) if kt == qt else None), bank=6))
                for kt in range(qt + 1):
                    mask = dict(pattern=[[0, 4], [1, 128]], base=0, cm=-1) if kt == qt else None
                    posts = None
                    if kt == qt:
                        pb_ = post_branch(5, 1)
                        if k == 1:
                            pq = post_qt()
                            posts = (lambda pb_=pb_, pq=pq: (pb_(), post_qt_call(pq)))
                        else:
                            posts = pb_
                    units.append(dict(kind="sel", rows=128, lhsT=KT["KST"][k * 64:(k + 1) * 64, kt * 128:(kt + 1) * 128], rhs=QTt,
                                      mask=mask, v=Va["VS"][:, kt, k, :], first=(kt == 0), last=(kt == qt),
                                      pre=(pre_sel if kt == 0 else None), post=posts, bank=5, k=k,
                                      eb=EB[k * 64:(k + 1) * 64, kt * 128:(kt + 1) * 128]))

        def post_qt_call(pq):
            pq()

        def emit_qk(i):
            u = units[i]
            sb_ = i % 3
            if u["pre"] is not None:
                u["pre"]()
            rows = u["rows"]
            out3 = ps[sb_][0:rows, :].rearrange("p (a b) -> p a b", b=128)
            if u["kind"] == "sel":
                def f(e, u=u, out3=out3):
                    e.matmul(out3, lhsT=u["lhsT"], rhs=u["rhs"], start=True, stop=False)
                    kk = u["k"]
                    return e.matmul(out3, lhsT=u["eb"], rhs=biasT[kk * 64:(kk + 1) * 64, :].unsqueeze(1).to_broadcast([64, 4, 128]),
                                    start=False, stop=True)
                P.op("pe", f, reads=[RQ, RbT, Rcmp], writes=[psr[sb_]])
            else:
                P.op("pe", lambda e, u=u, out3=out3: e.matmul(out3, lhsT=u["lhsT"], rhs=u["rhs"], start=True, stop=True),
                     reads=[RQ, Rcmp], writes=[psr[sb_]])

        def emit_rest(i):
            u = units[i]
            sb_ = i % 3
            rows = u["rows"]
            P.op("act", lambda e, sb_=sb_, rows=rows: e.activation(out=PT[sb_][0:rows, :], in_=ps[sb_][0:rows, :], func=AF.Exp,
                                                                  scale=0.125), reads=[psr[sb_]], writes=[RPT[sb_]])
            if u["mask"] is not None:
                mk = u["mask"]
                v3 = PT[sb_][0:rows, :].rearrange("p (a b) -> p a b", b=128)
                P.op("pool", lambda e, v3=v3, mk=mk: e.affine_select(out=v3, in_=v3, pattern=mk["pattern"], compare_op=ALU.is_ge,
                                                                    fill=0.0, base=mk["base"], channel_multiplier=mk["cm"]),
                     reads=[RPT[sb_]], writes=[RPT[sb_]])
            if u["kind"] == "cmp":
                def pv(e, u=u, sb_=sb_, rows=rows):
                    ins = None
                    for h in range(4):
                        b_ = 3 + h // 2
                        o = (h % 2) * 129
                        ins = e.matmul(ps[b_][:, o:o + 129], lhsT=PT[sb_][0:rows, h * 128:(h + 1) * 128], rhs=u["v"],
                                       start=(u["first"] and h % 2 == 0), stop=(u["last"] and h % 2 == 1))
                    return ins
                P.op("pe", pv, reads=[RPT[sb_], Rcmp], writes=[psr[3], psr[4]], self_ok=not u["first"])
            else:
                bank = u["bank"]

                def pv(e, u=u, sb_=sb_, bank=bank):
                    ins = None
                    for h in range(4):
                        ins = e.matmul(ps[bank][:, h * 65:(h + 1) * 65], lhsT=PT[sb_][:, h * 128:(h + 1) * 128], rhs=u["v"],
                                       start=(u["first"] and h == 0), stop=(u["last"] and h == 3))
                    return ins
                P.op("pe", pv, reads=[RPT[sb_], RQ], writes=[psr[bank]], self_ok=not u["first"])
            if u["post"] is not None:
                u["post"]()

        import os
        n = len(units)
        if os.environ.get("NSA_UNITS"):
            n = int(os.environ["NSA_UNITS"])
            deferred.clear()
        emit_qk(0)
        for i in range(n):
            if i + 1 < n:
                emit_qk(i + 1)
            emit_rest(i)
            if os.environ.get("NSA_UNITS") and i == n - 1:
                break
        flush_deferred()
        self.phase_end()

    def phase_merge(self, l):
        P = self.P
        S, SR = self.scr, self.sres
        w_in = self.d["w_in"][l].rearrange("(a p) c -> p a c", p=128)
        WU = self.sb([128, 12, D], BF16)
        WG = self.sb([128, 8, 3072], BF16)
        WO = self.sb([128, 8, D], BF16)
        RW = Res("WM")
        pieces = []
        for b, nm in enumerate(("w_up_pool", "w_up_ssm", "w_up_nsa")):
            wv = self.d[nm][l].rearrange("(a p) c -> p a c", p=128)
            for o in range(0, D, 512):
                pieces.append((WU[:, b * 4:(b + 1) * 4, o:o + 512], wv[:, :, o:o + 512], [128, 4, 512]))
        for o in range(0, 3072, 256):
            pieces.append((WG[:, :, o:o + 256], w_in[:, :, 2328 + o:2328 + o + 256], [128, 8, 256]))
        wo = self.d["w_out"][l].rearrange("(a p) c -> p a c", p=128)
        for o in range(0, D, 256):
            pieces.append((WO[:, :, o:o + 256], wo[:, :, o:o + 256], [128, 8, 256]))
        self.load_cast(pieces, RW)
        g_rep = self.sb([128, D], F32)
        b_rep = self.sb([128, D], F32)
        Rgb = Res("gb")
        self.bcast_load(g_rep[:], self.d["ln1_g"][l:l + 1, :], Rgb)
        self.bcast_load(b_rep[:], self.d["ln1_b"][l:l + 1, :], Rgb)
        xt = [self.sb([128, 8, 512], BF16) for _ in range(2)]
        Rxt = [Res("xt0"), Res("xt1")]
        yb = [self.sb([128, 12, 512], BF16) for _ in range(2)]
        Ryb = [Res("yb0"), Res("yb1")]
        mg = self.sb([128, 8, 512], BF16)
        Rmg = Res("mg")
        sg = [self.sb([128, 512], F32) for _ in range(2)]
        Rsg = [Res("sg0"), Res("sg1")]
        acc = self.sb([128, 512], F32)
        tmp = self.sb([128, 512], F32)
        Racc = Res("acc")
        Rtmp = Res("tmp")
        xr = [self.sb([128, D], F32) for _ in range(2)]
        Rxr = [Res("xr0"), Res("xr1")]
        bufs = [self.ln_bufs() for _ in range(2)]
        XTd = S["XT"].rearrange("(a p) t -> p a t", p=128)
        k = 0
        for c in range(8):
            j = c % 2
            P.dma("sp", lambda e, inc, c=c, j=j: inc(e.dma_start(out=xt[j][:], in_=XTd[:, :, c * 512:(c + 1) * 512])), 1,
                  reads=[SR["XT"]], writes=[Rxt[j]])
            for b, nm in enumerate(("YPT", "YST", "YNT")):
                P.dma("sp", lambda e, inc, c=c, j=j, b=b, nm=nm: inc(e.dma_start(
                    out=yb[j][:, b * 4:(b + 1) * 4, :],
                    in_=S[nm].rearrange("(a p) t -> p a t", p=128)[:, :, c * 512:(c + 1) * 512])), 1,
                    reads=[SR[nm]], writes=[Ryb[j]])
            for m in range(8):
                for b in range(3):
                    pu = k % 2
                    pg = 2 + k % 2
                    sj = k % 2
                    k += 1

                    def mmu(e, pu=pu, b=b, m=m, j=j):
                        ins = None
                        for a in range(4):
                            ins = e.matmul(self.ps[pu][:], lhsT=WU[:, b * 4 + a, m * 128:(m + 1) * 128], rhs=yb[j][:, b * 4 + a, :],
                                           start=(a == 0), stop=(a == 3))
                        return ins

                    def mmg(e, pg=pg, b=b, m=m, j=j):
                        ins = None
                        for a in range(8):
                            ins = e.matmul(self.ps[pg][:], lhsT=WG[:, a, b * 1024 + m * 128:b * 1024 + (m + 1) * 128],
                                           rhs=xt[j][:, a, :], start=(a == 0), stop=(a == 7))
                        return ins
                    P.op("pe", mmg, reads=[RW, Rxt[j]], writes=[self.psr[pg]])
                    P.op("pe", mmu, reads=[RW, Ryb[j]], writes=[self.psr[pu]])
                    P.op("act", lambda e, pg=pg, sj=sj: e.activation(out=sg[sj][:], in_=self.ps[pg][:], func=AF.Sigmoid),
                         reads=[self.psr[pg]], writes=[Rsg[sj]])
                    if b == 0:
                        P.op("dve", lambda e, pu=pu, sj=sj: e.tensor_tensor(out=acc[:], in0=self.ps[pu][:], in1=sg[sj][:],
                                                                            op=ALU.mult), reads=[self.psr[pu], Rsg[sj]],
                             writes=[Racc])
                    else:
                        P.op("dve", lambda e, pu=pu, sj=sj: e.tensor_tensor(out=tmp[:], in0=self.ps[pu][:], in1=sg[sj][:],
                                                                            op=ALU.mult), reads=[self.psr[pu], Rsg[sj]],
                             writes=[Rtmp])
                        if b == 1:
                            P.op("pool", lambda e: e.tensor_tensor(out=acc[:], in0=acc[:], in1=tmp[:], op=ALU.add),
                                 reads=[Rtmp, Racc], writes=[Racc])
                        else:
                            P.op("pool", lambda e, m=m: e.tensor_tensor(out=mg[:, m, :], in0=acc[:], in1=tmp[:], op=ALU.add),
                                 reads=[Rtmp, Racc], writes=[Rmg])
            for tt in range(4):
                i = c * 4 + tt
                xj = i % 2
                P.dma("sp", lambda e, inc, i=i, xj=xj: inc(e.dma_start(out=xr[xj][:], in_=S["XR"][i * 128:(i + 1) * 128, :])),
                      1, reads=[SR["XR"]], writes=[Rxr[xj]])
                for half in range(2):
                    pb = 4 + half

                    def mmo(e, pb=pb, tt=tt, half=half):
                        ins = None
                        for a in range(8):
                            ins = e.matmul(self.ps[pb][:], lhsT=mg[:, a, tt * 128:(tt + 1) * 128],
                                           rhs=WO[:, a, half * 512:(half + 1) * 512], start=(a == 0), stop=(a == 7))
                        return ins
                    P.op("pe", mmo, reads=[RW, Rmg], writes=[self.psr[pb]])
                    P.op("dve", lambda e, pb=pb, xj=xj, half=half: e.scalar_tensor_tensor(
                        out=xr[xj][:, half * 512:(half + 1) * 512], in0=xr[xj][:, half * 512:(half + 1) * 512], scalar=ALPHA,
                        in1=self.ps[pb][:], op0=ALU.mult, op1=ALU.add), reads=[self.psr[pb], Rxr[xj]], writes=[Rxr[xj]])
                self.ln_tile(i, xr[xj], Rxr[xj], g_rep, b_rep, Rgb, S["XR1"], SR["XR1"], S["X1T"], SR["X1T"], bufs[xj])
        self.phase_end()

    def phase_ffn(self, l, last):
        P = self.P
        S, SR = self.scr, self.sres
        W1 = self.sb([128, 8, 4096], BF16)
        W2 = self.sb([128, 32, D], BF16)
        RW = Res("WF")
        w1 = self.d["w_ff1"][l].rearrange("(a p) c -> p a c", p=128)
        w2 = self.d["w_ff2"][l].rearrange("(a p) c -> p a c", p=128)
        pieces = []
        for o in range(0, 4096, 128):
            pieces.append((W1[:, :, o:o + 128], w1[:, :, o:o + 128], [128, 8, 128]))
        for a in range(32):
            pieces.append((W2[:, a, :], w2[:, a, :], [128, D]))
        self.load_cast(pieces, RW, stage_cols=1024)
        g_rep = self.sb([128, D], F32)
        b_rep = self.sb([128, D], F32)
        Rgb = Res("gb")
        self.bcast_load(g_rep[:], self.d["ln2_g"][l:l + 1, :], Rgb)
        self.bcast_load(b_rep[:], self.d["ln2_b"][l:l + 1, :], Rgb)
        xt = [self.sb([128, 8, 256], BF16) for _ in range(2)]
        Rxt = [Res("xt0"), Res("xt1")]
        hT = self.sb([128, 32, 256], BF16)
        RhT = Res("hT")
        rl = [self.sb([128, 512], F32) for _ in range(2)]
        Rrl = [Res("rl0"), Res("rl1")]
        xr = [self.sb([128, D], F32) for _ in range(2)]
        Rxr = [Res("xr0"), Res("xr1")]
        bufs = [self.ln_bufs() for _ in range(2)]
        X1Td = S["X1T"].rearrange("(a p) t -> p a t", p=128)
        dst_res = self.d["out"] if last else S["XR"]
        Rdst = Res("outres") if last else SR["XR"]
        dst_T = None if last else S["XT"]
        k = 0
        for c in range(16):
            j = c % 2
            P.dma("sp", lambda e, inc, c=c, j=j: inc(e.dma_start(out=xt[j][:], in_=X1Td[:, :, c * 256:(c + 1) * 256])), 1,
                  reads=[SR["X1T"]], writes=[Rxt[j]])
            for f2 in range(16):
                pb = k % 4
                rj = k % 2
                k += 1

                def mm1(e, pb=pb, f2=f2, j=j):
                    ins = None
                    for q in range(2):
                        f = f2 * 2 + q
                        for a in range(8):
                            ins = e.matmul(self.ps[pb][:, q * 256:(q + 1) * 256], lhsT=W1[:, a, f * 128:(f + 1) * 128],
                                           rhs=xt[j][:, a, :], start=(a == 0), stop=(a == 7))
                    return ins
                P.op("pe", mm1, reads=[RW, Rxt[j]], writes=[self.psr[pb]])
                P.op("act", lambda e, pb=pb, rj=rj: e.activation(out=rl[rj][:], in_=self.ps[pb][:], func=AF.Relu),
                     reads=[self.psr[pb]], writes=[Rrl[rj]])
                P.op("pool", lambda e, rj=rj, f2=f2: e.tensor_tensor(
                    out=hT[:, f2 * 2:f2 * 2 + 2, :], in0=rl[rj][:].rearrange("p (a b) -> p a b", b=256),
                    in1=rl[rj][:].rearrange("p (a b) -> p a b", b=256), op=ALU.mult), reads=[Rrl[rj]], writes=[RhT])
            for tt in range(2):
                i = c * 2 + tt
                xj = i % 2
                P.dma("sp", lambda e, inc, i=i, xj=xj: inc(e.dma_start(out=xr[xj][:], in_=S["XR1"][i * 128:(i + 1) * 128, :])),
                      1, reads=[SR["XR1"]], writes=[Rxr[xj]])
                for half in range(2):
                    pb = 4 + half

                    def mm2(e, pb=pb, tt=tt, half=half):
                        ins = None
                        for f in range(32):
                            ins = e.matmul(self.ps[pb][:], lhsT=hT[:, f, tt * 128:(tt + 1) * 128],
                                           rhs=W2[:, f, half * 512:(half + 1) * 512], start=(f == 0), stop=(f == 31))
                        return ins
                    P.op("pe", mm2, reads=[RW, RhT], writes=[self.psr[pb]])
                    P.op("dve", lambda e, pb=pb, xj=xj, half=half: e.scalar_tensor_tensor(
                        out=xr[xj][:, half * 512:(half + 1) * 512], in0=xr[xj][:, half * 512:(half + 1) * 512], scalar=ALPHA,
                        in1=self.ps[pb][:], op0=ALU.mult, op1=ALU.add), reads=[self.psr[pb], Rxr[xj]], writes=[Rxr[xj]])
                self.ln_tile(i, xr[xj], Rxr[xj], g_rep, b_rep, Rgb, dst_res, Rdst, dst_T, SR["XT"], bufs[xj])
        self.phase_end()

    def build(self):
        self.phase_ln_in()
        for l in range(self.layers):
            stop = self.stop_after
            skip = self.skip
            if "proj" not in skip:
                self.phase_proj(l)
            if stop == "proj":
                break
            if "pool" not in skip:
                self.phase_pool(l)
            if stop == "pool":
                break
            if "s5" not in skip:
                self.phase_s5(l)
            if stop == "s5":
                break
            if "nsa" not in skip:
                self.phase_nsa(l)
            if stop == "nsa":
                break
            self.phase_merge(l)
            if stop == "merge":
                break
            self.phase_ffn(l, last=(l == self.layers - 1))
        self.P.emit()
        return self.nc


def make_inputs(inputs, ncores=8):
    consts = host_consts()
    shared = {n: np.ascontiguousarray(np.asarray(inputs[n], dtype=np.float32)) for n, _ in PARAMS}
    shared.update(consts)
    x = np.asarray(inputs["x"], dtype=np.float32)
    maps = []
    for c in range(ncores):
        m = dict(shared)
        m["x"] = np.ascontiguousarray(x[c])
        maps.append(m)
    return maps


def kernel(**inputs):
    b = Builder()
    nc = b.build()
    maps = make_inputs(inputs)
    res = run_bass_kernel_spmd(nc, maps, core_ids=list(range(8)))
    return np.stack([np.asarray(r["out"], dtype=np.float32) for r in res.results], axis=0)
```

```python
import contextlib
import math
import numpy as np
import ml_dtypes
import concourse.bass as bass
import concourse.mybir as mybir
from concourse.bass_utils import run_bass_kernel_spmd

F32 = mybir.dt.float32
BF16 = mybir.dt.bfloat16
AF = mybir.ActivationFunctionType
ALU = mybir.AluOpType
AX = mybir.AxisListType

L = 4096
D = 1024
NT = 32
DEPTH = 2
ALPHA = (2 * DEPTH) ** 0.25
EPS = 1e-5
BIG = 30000.0
SB_BASE = 16512
SB_END = 229344


class Res:
    __slots__ = ("name", "w", "r")

    def __init__(self, name=""):
        self.name = name
        self.w = None
        self.r = []


class Prog:
    ENGS = ("pe", "act", "dve", "pool", "sp")
    NDMA = 8

    def __init__(self, nc):
        self.nc = nc
        self.ops = {e: [] for e in self.ENGS}
        self.cnt = {}
        self.seen = {e: {} for e in self.ENGS}
        self.dma_n = {e: 0 for e in self.ENGS}
        self.sem_keys = []
        for e in ("pe", "act", "dve", "pool"):
            self._mk(("c", e))
        for e in ("sp", "act", "pool"):
            for j in range(self.NDMA):
                self._mk(("d", e, j))
        self.nops = 0

    def _mk(self, key):
        self.cnt[key] = 0
        self.sem_keys.append(key)

    def _need(self, eng, waits, dep):
        if dep is None:
            return
        key, val = dep
        if self.seen[eng].get(key, 0) >= val:
            return
        waits[key] = max(waits.get(key, 0), val)

    def _deps(self, eng, reads, writes, self_ok=False):
        waits = {}
        for r in reads:
            self._need(eng, waits, r.w)
        for w in writes:
            self._need(eng, waits, w.w)
            for rd in w.r:
                self._need(eng, waits, rd)
        if self_ok:
            waits.pop(("c", eng), None)
        return waits

    def _commit(self, eng, waits, tok, reads, writes):
        for k, v in waits.items():
            self.seen[eng][k] = v
        for r in reads:
            r.r.append(tok)
        for w in writes:
            w.w = tok
            w.r = []
        self.nops += 1

    def op(self, eng, fn, reads=(), writes=(), self_ok=False):
        waits = self._deps(eng, reads, writes, self_ok)
        key = ("c", eng)
        self.cnt[key] += 1
        tok = (key, self.cnt[key])
        self.ops[eng].append((list(waits.items()), fn, key, 1))
        self._commit(eng, waits, tok, reads, writes)
        return tok

    def dma(self, q, fn, ndma, reads=(), writes=()):
        waits = self._deps(q, reads, writes)
        j = self.dma_n[q] % self.NDMA
        self.dma_n[q] += 1
        key = ("d", q, j)
        if self.cnt[key] > 0:
            self._need(q, waits, (key, self.cnt[key]))
        self.cnt[key] += 16 * ndma
        tok = (key, self.cnt[key])
        self.ops[q].append((list(waits.items()), fn, key, 16))
        self._commit(q, waits, tok, reads, writes)
        return tok

    def barrier(self):
        for e in self.ENGS:
            waits = {}
            for k in self.sem_keys:
                if self.cnt[k] > 0 and not (k[0] == "c" and k[1] == e):
                    self._need(e, waits, (k, self.cnt[k]))
            for k, v in waits.items():
                self.seen[e][k] = v
            if waits:
                self.ops[e].append((list(waits.items()), None, None, 0))

    def emit(self):
        nc = self.nc
        with contextlib.ExitStack() as st:
            sems = {}
            for k in self.sem_keys:
                sems[k] = st.enter_context(nc.semaphore("s_" + "_".join(str(x) for x in k)))
            block = st.enter_context(nc.Block())
            final = [(k, v) for k, v in self.cnt.items() if v > 0]

            def run(engname):
                def body(eng):
                    for waits, fn, key, inc in self.ops[engname]:
                        for k, v in waits:
                            eng.wait_ge(sems[k], v)
                        if fn is None:
                            continue
                        if inc == 16:
                            fn(eng, lambda ins: ins.then_inc(sems[key], 16))
                        else:
                            fn(eng).then_inc(sems[key], 1)
                    if engname == "sp":
                        for k, v in final:
                            eng.wait_ge(sems[k], v)
                return body

            block.tensor(run("pe"))
            block.scalar(run("act"))
            block.vector(run("dve"))
            block.gpsimd(run("pool"))
            block.sync(run("sp"))


PARAMS = [("ln_in_g", (D,)), ("ln_in_b", (D,)), ("w_in", (2, D, 5400)), ("w_pool", (2, 4, 128, 128)),
          ("pool_scale", (2, 512)), ("ssm_lam_re", (2, 32, 64)), ("ssm_lam_im", (2, 32, 64)),
          ("ssm_log_dt", (2, 32)), ("ssm_b_re", (2, 32, 64, 16)), ("ssm_b_im", (2, 32, 64, 16)),
          ("ssm_c_re", (2, 32, 16, 64)), ("ssm_c_im", (2, 32, 16, 64)), ("ssm_d", (2, 32, 16)),
          ("w_glu", (2, 512, 512)), ("b_glu", (2, 512)), ("cmp_pe_k", (2, 32, 64)), ("cmp_pe_v", (2, 32, 64)),
          ("cmp_wk1", (2, 2048, 128)), ("cmp_wk2", (2, 128, 64)), ("cmp_wv1", (2, 2048, 128)),
          ("cmp_wv2", (2, 128, 64)), ("w_up_pool", (2, 512, D)), ("w_up_ssm", (2, 512, D)),
          ("w_up_nsa", (2, 512, D)), ("w_out", (2, D, D)), ("ln1_g", (2, D)), ("ln1_b", (2, D)),
          ("w_ff1", (2, D, 4096)), ("w_ff2", (2, 4096, D)), ("ln2_g", (2, D)), ("ln2_b", (2, D))]


def host_consts():
    c = {}
    c["c_ident"] = np.eye(128, dtype=np.float32)
    pos = np.arange(L, dtype=np.float32)
    inv_freq = (np.float32(500000.0) ** (-np.arange(0, 16, 2, dtype=np.float32) / np.float32(16))).astype(np.float32)
    ang = (pos[:, None] * inv_freq[None, :]).astype(np.float32)
    c["c_rope"] = np.concatenate([np.cos(ang), np.sin(ang)], axis=1).astype(np.float32)
    eb = np.zeros((64, 4096), np.float32)
    for s in range(64):
        eb[s, s * 64:(s + 1) * 64] = 1.0
    c["c_ebig"] = eb.astype(ml_dtypes.bfloat16)
    cs = np.arange(256)[:, None] * 16
    ss = np.arange(64)[None, :] * 64
    ov = np.maximum(np.minimum(cs + 32, ss + 64) - np.maximum(cs, ss), 0) / 16.0
    ov[255:] = 0.0
    c["c_ov"] = ov.astype(np.float32)
    ic = np.zeros((4, 16), np.float32)
    for gi, w in enumerate((2, 4, 8, 16)):
        ic[gi] = 1.0 / np.minimum(np.arange(16) + 1, w)
    c["c_invcnt"] = ic.reshape(1, 64)
    return c


class Builder:
    def __init__(self, dbg=(), stop_after=None, layers=DEPTH, inject=(), skip=()):
        self.inject = set(inject)
        self.skip = set(skip)
        self.nc = nc = bass.Bass("TRN2", target_bir_lowering=False)
        self.P = Prog(nc)
        self.dbg = set(dbg)
        self.stop_after = stop_after
        self.layers = layers
        self.d = {}
        self.d["x"] = nc.dram_tensor("x", [L, D], F32, kind="ExternalInput").ap()
        for n, shp in PARAMS:
            self.d[n] = nc.dram_tensor(n, list(shp), F32, kind="ExternalInput").ap()
        for n, a in host_consts().items():
            dt = BF16 if a.dtype == ml_dtypes.bfloat16 else F32
            self.d[n] = nc.dram_tensor(n, list(a.shape), dt, kind="ExternalInput").ap()
        self.d["out"] = nc.dram_tensor("out", [L, D], F32, kind="ExternalOutput").ap()
        self.scr = {}
        self.sres = {}
        for n, shp, dt in [("XR", [L, D], F32), ("XT", [D, L], BF16), ("UPT", [512, L], F32), ("UST", [512, L], F32),
                           ("QT", [128, 4, L], BF16), ("KCT", [128, L], BF16), ("KST", [128, L], BF16),
                           ("KWT", [128, L], BF16), ("VCT", [128, L], BF16), ("VS", [L, 128], BF16),
                           ("VW", [L, 128], BF16), ("GN", [L, 24], F32), ("YPT", [512, L], BF16),
                           ("YST", [512, L], BF16), ("YNT", [512, L], BF16), ("XR1", [L, D], F32),
                           ("X1T", [D, L], BF16)]:
            kind = "ExternalOutput" if n in self.dbg else ("ExternalInput" if n in self.inject else "Internal")
            self.scr[n] = nc.dram_tensor("s_" + n, shp, dt, kind=kind).ap()
            self.sres[n] = Res(n)
        self.ps = [nc.alloc_psum_tensor("psb%d" % i, [128, 512], F32) for i in range(8)]
        self.psr = [Res("ps%d" % i) for i in range(8)]
        self.cur = SB_BASE
        self.uid = 0
        self.ident = self.sb([128, 128], F32)
        self.Rconst = Res("const")
        self.P.dma("sp", lambda e, inc: inc(e.dma_start(out=self.ident[:], in_=self.d["c_ident"][:, :])), 1,
                   writes=[self.Rconst])
        self.persist = self.cur

    def sb(self, shape, dt):
        nbytes = int(np.prod(shape[1:])) * (2 if dt == BF16 else 4)
        off = (self.cur + 63) // 64 * 64
        assert off + nbytes <= SB_END, ("SBUF overflow", off + nbytes - SB_END)
        self.cur = off + nbytes
        self.uid += 1
        return self.nc.alloc_sbuf_tensor_at("t%d" % self.uid, list(shape), dt, offset=off)

    def phase_end(self):
        self.P.barrier()
        self.cur = self.persist

    def load_cast(self, pieces, res, stage_cols=2048):
        P = self.P
        if not hasattr(self, "_lc"):
            self._lc = None
        stg = [self.sb([128, stage_cols], F32) for _ in range(3)]
        rs = [Res("stg%d" % i) for i in range(3)]
        for i, pc in enumerate(pieces):
            dst, src, shp = pc[0], pc[1], pc[2]
            p0 = pc[3] if len(pc) > 3 else 0
            j = i % 3
            n = int(np.prod(shp[1:]))
            assert n <= stage_cols
            p = shp[0]
            sview = stg[j][p0:p0 + p, 0:n]
            if len(shp) == 3:
                sview = sview.rearrange("p (a b) -> p a b", b=shp[2])
            P.dma("sp", (lambda sv, s: lambda e, inc: inc(e.dma_start(out=sv, in_=s)))(sview, src), 1, writes=[rs[j]])
            eng = "pool" if i % 2 == 0 else "dve"
            P.op(eng, (lambda dd, sv: lambda e: e.tensor_copy(out=dd, in_=sv))(dst, sview), reads=[rs[j]], writes=[res])

    def bcast_load(self, dst, src_row, res):
        self.P.dma("sp", lambda e, inc: inc(e.dma_start(out=dst, in_=src_row.partition_broadcast(128))), 1, writes=[res])

    def ln_tile(self, i, src, Rsrc, g_rep, b_rep, Rgb, dst_res_d, Rres, dst_T_d, RT, bufs):
        P = self.P
        st6, mv, rstd, xT, RxT, Rst = bufs
        P.op("dve", lambda e: e.bn_stats(out=st6[:, 0, :], in_=src[:, 0:512]), reads=[Rsrc], writes=[Rst])
        P.op("dve", lambda e: e.bn_stats(out=st6[:, 1, :], in_=src[:, 512:1024]), reads=[Rsrc, Rst], writes=[Rst])
        P.op("dve", lambda e: e.bn_aggr(out=mv[:], in_=st6[:]), reads=[Rst], writes=[Rst])
        P.op("dve", lambda e: e.tensor_scalar_add(out=rstd[:], in0=mv[:, 1:2], scalar1=EPS), reads=[Rst], writes=[Rst])
        P.op("act", lambda e: e.sqrt(out=rstd[:], in_=rstd[:]), reads=[Rst], writes=[Rst])
        P.op("dve", lambda e: e.reciprocal(out=rstd[:], in_=rstd[:]), reads=[Rst], writes=[Rst])
        P.op("dve", lambda e: e.tensor_scalar(out=src[:], in0=src[:], scalar1=mv[:, 0:1], scalar2=rstd[:, 0:1],
                                              op0=ALU.subtract, op1=ALU.mult), reads=[Rst, Rsrc], writes=[Rsrc])
        P.op("pool", lambda e: e.tensor_tensor(out=src[:], in0=src[:], in1=g_rep[:], op=ALU.mult), reads=[Rsrc, Rgb],
             writes=[Rsrc])
        P.op("dve", lambda e: e.tensor_tensor(out=src[:], in0=src[:], in1=b_rep[:], op=ALU.add), reads=[Rsrc, Rgb],
             writes=[Rsrc])
        P.dma("sp", lambda e, inc: inc(e.dma_start(out=dst_res_d[i * 128:(i + 1) * 128, :], in_=src[:])), 1,
              reads=[Rsrc], writes=[Rres])
        if dst_T_d is None:
            return
        for half in range(2):
            pb = 6 + half
            ps = self.ps[pb]

            def tr(e, half=half, ps=ps):
                ins = None
                for q in range(4):
                    dt = half * 4 + q
                    ins = e.transpose(out=ps[:, q * 128:(q + 1) * 128], in_=src[:, dt * 128:(dt + 1) * 128],
                                      identity=self.ident[:])
                return ins
            P.op("pe", tr, reads=[Rsrc, self.Rconst], writes=[self.psr[pb]])
            P.op("act", lambda e, half=half, ps=ps: e.copy(out=xT[:, half * 4:(half + 1) * 4, :],
                                                         in_=ps[:].rearrange("p (a b) -> p a b", b=128)),
                 reads=[self.psr[pb]], writes=[RxT])
        P.dma("sp", lambda e, inc: inc(e.dma_start(
            out=dst_T_d.rearrange("(a p) t -> p a t", p=128)[:, :, i * 128:(i + 1) * 128], in_=xT[:])), 1,
            reads=[RxT], writes=[RT])

    def ln_bufs(self):
        return (self.sb([128, 2, 6], F32), self.sb([128, 2], F32), self.sb([128, 1], F32),
                self.sb([128, 8, 128], BF16), Res("xT"), Res("lnst"))

    def phase_ln_in(self):
        P = self.P
        g_rep = self.sb([128, D], F32)
        b_rep = self.sb([128, D], F32)
        Rgb = Res("gb")
        self.bcast_load(g_rep[:], self.d["ln_in_g"].unsqueeze(0), Rgb)
        self.bcast_load(b_rep[:], self.d["ln_in_b"].unsqueeze(0), Rgb)
        xb = [self.sb([128, D], F32) for _ in range(2)]
        Rx = [Res("x0"), Res("x1")]
        bufs = [self.ln_bufs() for _ in range(2)]
        for i in range(NT):
            j = i % 2
            P.dma("sp", lambda e, inc, i=i, j=j: inc(e.dma_start(out=xb[j][:], in_=self.d["x"][i * 128:(i + 1) * 128, :])),
                  1, writes=[Rx[j]])
            self.ln_tile(i, xb[j], Rx[j], g_rep, b_rep, Rgb, self.scr["XR"], self.sres["XR"], self.scr["XT"],
                         self.sres["XT"], bufs[j])
        self.phase_end()

    def phase_proj(self, l):
        P = self.P
        w_in = self.d["w_in"][l].rearrange("(a p) c -> p a c", p=128)
        NC_ = 1024 + 1304
        WP = self.sb([128, 8, NC_], BF16)
        RW = Res("WP")
        plan = [(0, 0, 1024)]
        qorder = (0, 4, 1, 5, 2, 6, 3, 7)
        for j, h in enumerate(qorder):
            plan.append((1024 + j * 64, 1024 + h * 64, 64))
        kv0 = 1536
        base = 1024 + 512
        for j, s in enumerate((0, 2, 4, 1)):
            plan.append((base + j * 128, kv0 + s * 128, 128))
        base += 512
        for j, s in enumerate((3, 5)):
            plan.append((base + j * 128, kv0 + s * 128, 128))
        plan.append((base + 256, 2304, 24))
        pieces = []
        for dc, sc, n in plan:
            o = 0
            while o < n:
                m = min(256, n - o)
                pieces.append((WP[:, :, dc + o:dc + o + m], w_in[:, :, sc + o:sc + o + m], [128, 8, m]))
                o += m
        self.load_cast(pieces, RW)
        rope = self.sb([128, NT, 16], F32)
        Rrope = Res("rope")
        P.dma("sp", lambda e, inc: inc(e.dma_start(out=rope[:], in_=self.d["c_rope"].rearrange("(a p) c -> p a c", p=128))),
              1, writes=[Rrope])
        xt = [self.sb([128, 8, 512], BF16) for _ in range(2)]
        Rxt = [Res("xt0"), Res("xt1")]
        fst = [self.sb([128, 512], F32) for _ in range(2)]
        Rfst = [Res("f0"), Res("f1")]
        tm = [self.sb([128, 1304], F32) for _ in range(2)]
        Rtm = [Res("tm0"), Res("tm1")]
        t1 = self.sb([128, 14, 8], F32)
        t2 = self.sb([128, 14, 8], F32)
        t3 = self.sb([128, 14, 8], F32)
        Rt = Res("ropetmp")
        trs = [self.sb([128, 8, 128], BF16) for _ in range(2)]
        Rtrs = [Res("trs0"), Res("trs1")]
        vst = [self.sb([128, 256], BF16) for _ in range(2)]
        Rvst = [Res("vst0"), Res("vst1")]
        XTd = self.scr["XT"].rearrange("(a p) t -> p a t", p=128)
        S = self.scr
        SR = self.sres
        kcnt = 0
        for c in range(8):
            j = c % 2
            P.dma("sp", lambda e, inc, c=c, j=j: inc(e.dma_start(out=xt[j][:], in_=XTd[:, :, c * 512:(c + 1) * 512])), 1,
                  reads=[SR["XT"]], writes=[Rxt[j]])
            for ft in range(8):
                pb = ft % 2
                ps = self.ps[pb]

                def mm(e, ft=ft, ps=ps, j=j):
                    ins = None
                    for a in range(8):
                        ins = e.matmul(ps[:], lhsT=WP[:, a, ft * 128:(ft + 1) * 128], rhs=xt[j][:, a, :], start=(a == 0),
                                       stop=(a == 7))
                    return ins
                P.op("pe", mm, reads=[RW, Rxt[j]], writes=[self.psr[pb]])
                fj = ft % 2
                P.op("act", lambda e, ps=ps, fj=fj: e.copy(out=fst[fj][:], in_=ps[:]), reads=[self.psr[pb]],
                     writes=[Rfst[fj]])
                dst = (S["UPT"] if ft < 4 else S["UST"])
                dres = SR["UPT"] if ft < 4 else SR["UST"]
                r0 = (ft % 4) * 128
                P.dma("sp", lambda e, inc, dst=dst, r0=r0, c=c, fj=fj: inc(e.dma_start(
                    out=dst[r0:r0 + 128, c * 512:(c + 1) * 512], in_=fst[fj][:])), 1, reads=[Rfst[fj]], writes=[dres])
            for tt in range(4):
                i = c * 4 + tt
                tj = i % 2
                T = tm[tj]
                for ch, (c0, n) in enumerate(((0, 512), (512, 512), (1024, 280))):
                    pb = 2 + (kcnt % 3)
                    kcnt += 1
                    ps = self.ps[pb]

                    def mm2(e, ps=ps, j=j, tt=tt, c0=c0, n=n):
                        ins = None
                        for a in range(8):
                            ins = e.matmul(ps[:, 0:n], lhsT=xt[j][:, a, tt * 128:(tt + 1) * 128],
                                           rhs=WP[:, a, 1024 + c0:1024 + c0 + n], start=(a == 0), stop=(a == 7))
                        return ins
                    P.op("pe", mm2, reads=[RW, Rxt[j]], writes=[self.psr[pb]])
                    P.op("act", lambda e, ps=ps, T=T, c0=c0, n=n: e.copy(out=T[:, c0:c0 + n], in_=ps[:, 0:n]),
                         reads=[self.psr[pb]], writes=[Rtm[tj]])
                H3 = T[:, 0:896].rearrange("p (h d) -> p h d", d=64)
                x1 = H3[:, :, 0:8]
                x2 = H3[:, :, 8:16]
                cosb = rope[:, i, 0:8].unsqueeze(1).to_broadcast([128, 14, 8])
                sinb = rope[:, i, 8:16].unsqueeze(1).to_broadcast([128, 14, 8])
                P.op("dve", lambda e, x1=x1, sinb=sinb: e.tensor_tensor(out=t1[:], in0=x1, in1=sinb, op=ALU.mult),
                     reads=[Rtm[tj], Rrope], writes=[Rt])
                P.op("dve", lambda e, x2=x2, sinb=sinb: e.tensor_tensor(out=t2[:], in0=x2, in1=sinb, op=ALU.mult),
                     reads=[Rtm[tj], Rrope, Rt], writes=[Rt])
                P.op("dve", lambda e, x1=x1, cosb=cosb: e.tensor_tensor(out=x1, in0=x1, in1=cosb, op=ALU.mult),
                     reads=[Rtm[tj], Rrope, Rt], writes=[Rtm[tj]])
                P.op("dve", lambda e, x2=x2, cosb=cosb: e.tensor_tensor(out=t3[:], in0=x2, in1=cosb, op=ALU.mult),
                     reads=[Rtm[tj], Rrope, Rt], writes=[Rt])
                P.op("dve", lambda e, x1=x1: e.tensor_tensor(out=x1, in0=x1, in1=t2[:], op=ALU.subtract),
                     reads=[Rtm[tj], Rt], writes=[Rtm[tj]])
                P.op("dve", lambda e, x2=x2: e.tensor_tensor(out=x2, in0=t1[:], in1=t3[:], op=ALU.add),
                     reads=[Rtm[tj], Rt], writes=[Rtm[tj]])
                for half in range(2):
                    pb = 6 + half
                    ps = self.ps[pb]

                    def tr(e, half=half, ps=ps, T=T):
                        ins = None
                        for q in range(4):
                            a = half * 4 + q
                            ins = e.transpose(out=ps[:, q * 128:(q + 1) * 128], in_=T[:, a * 128:(a + 1) * 128],
                                              identity=self.ident[:])
                        return ins
                    P.op("pe", tr, reads=[Rtm[tj], self.Rconst], writes=[self.psr[pb]])
                    P.op("act", lambda e, half=half, ps=ps, tj=tj: e.copy(
                        out=trs[tj][:, half * 4:(half + 1) * 4, :], in_=ps[:].rearrange("p (a b) -> p a b", b=128)),
                        reads=[self.psr[pb]], writes=[Rtrs[tj]])
                t0 = i * 128
                P.dma("sp", lambda e, inc, tj=tj, t0=t0: inc(e.dma_start(out=S["QT"][:, :, t0:t0 + 128],
                                                                       in_=trs[tj][:, 0:4, :])), 1,
                      reads=[Rtrs[tj]], writes=[SR["QT"]])
                for a, nm in ((4, "KCT"), (5, "KST"), (6, "KWT"), (7, "VCT")):
                    P.dma("sp", lambda e, inc, tj=tj, t0=t0, a=a, nm=nm: inc(e.dma_start(
                        out=S[nm][:, t0:t0 + 128], in_=trs[tj][:, a, :])), 1, reads=[Rtrs[tj]], writes=[SR[nm]])
                P.op("pool", lambda e, tj=tj, T=T: e.tensor_copy(out=vst[tj][:], in_=T[:, 1024:1280]), reads=[Rtm[tj]],
                     writes=[Rvst[tj]])
                P.dma("sp", lambda e, inc, tj=tj, t0=t0: inc(e.dma_start(out=S["VS"][t0:t0 + 128, :], in_=vst[tj][:, 0:128])),
                      1, reads=[Rvst[tj]], writes=[SR["VS"]])
                P.dma("sp", lambda e, inc, tj=tj, t0=t0: inc(e.dma_start(out=S["VW"][t0:t0 + 128, :], in_=vst[tj][:, 128:256])),
                      1, reads=[Rvst[tj]], writes=[SR["VW"]])
                P.op("act", lambda e, T=T: e.activation(out=T[:, 1280:1304], in_=T[:, 1280:1304], func=AF.Sigmoid),
                     reads=[Rtm[tj]], writes=[Rtm[tj]])
                P.dma("sp", lambda e, inc, T=T, t0=t0: inc(e.dma_start(out=S["GN"][t0:t0 + 128, :], in_=T[:, 1280:1304])),
                      1, reads=[Rtm[tj]], writes=[SR["GN"]])
        self.phase_end()

    def phase_pool(self, l):
        P = self.P
        S, SR = self.scr, self.sres
        wp = self.sb([128, 4, 128], BF16)
        Rwp = Res("wpool")
        self.load_cast([(wp[:, g, :], self.d["w_pool"][l, g], [128, 128]) for g in range(4)], Rwp, stage_cols=128)
        psc = self.sb([128, 4], F32)
        Rpsc = Res("psc")
        P.dma("sp", lambda e, inc: inc(e.dma_start(out=psc[:], in_=self.d["pool_scale"][l].rearrange("(g p) -> p g", p=128),
                                                       allow_slow_non_contiguous=True)),
              1, writes=[Rpsc])
        icn = self.sb([128, 64], F32)
        P.dma("sp", lambda e, inc: inc(e.dma_start(out=icn[:], in_=self.d["c_invcnt"][0:1, :].partition_broadcast(128))),
              1, writes=[Rpsc])
        H = 16
        U = [self.sb([128, H + L], F32) for _ in range(2)]
        A = [self.sb([128, H + L], F32) for _ in range(2)]
        Bb = [self.sb([128, H + L], F32) for _ in range(2)]
        Z = [self.sb([128, L], BF16) for _ in range(2)]
        tmp16 = [self.sb([128, 16], F32) for _ in range(2)]
        yst = [self.sb([128, 512], BF16) for _ in range(2)]
        Ryst = [Res("y0"), Res("y1")]
        RU = [Res("U0"), Res("U1")]
        for j in range(2):
            eng = "dve" if j == 0 else "pool"
            for buf in (U[j], A[j], Bb[j]):
                P.op(eng, lambda e, buf=buf: e.memset(buf[:, 0:H], 0.0), writes=[RU[j]])
        for g in range(4):
            j = g % 2
            eng = "dve" if j == 0 else "pool"
            w = 2 << g
            P.dma("sp", lambda e, inc, g=g, j=j: inc(e.dma_start(out=U[j][:, H:], in_=S["UPT"][g * 128:(g + 1) * 128, :])), 1,
                  reads=[SR["UPT"]], writes=[RU[j]])
            src = U[j]
            dsts = [A[j], Bb[j]]
            for k in range(g + 1):
                sh = 1 << k
                dst = dsts[k % 2]
                P.op(eng, lambda e, dst=dst, src=src, sh=sh: e.tensor_tensor(out=dst[:, H:], in0=src[:, H:],
                                                                              in1=src[:, H - sh:H + L - sh], op=ALU.add),
                     reads=[RU[j]], writes=[RU[j]])
                src = dst
            P.op("dve", lambda e, src=src, j=j, w=w: e.scalar_tensor_tensor(out=Z[j][:], in0=src[:, H:], scalar=1.0 / w,
                                                                          in1=U[j][:, H:], op0=ALU.mult,
                                                                          op1=ALU.subtract), reads=[RU[j]], writes=[RU[j]])
            P.op(eng, lambda e, src=src, j=j, g=g: e.tensor_tensor(out=tmp16[j][:], in0=src[:, H:H + 16], in1=icn[:, g * 16:(g + 1) * 16],
                                                                   op=ALU.mult), reads=[RU[j], Rpsc], writes=[RU[j]])
            P.op(eng, lambda e, j=j: e.tensor_tensor(out=Z[j][:, 0:16], in0=tmp16[j][:], in1=U[j][:, H:H + 16],
                                                     op=ALU.subtract), reads=[RU[j]], writes=[RU[j]])
            for c in range(8):
                pb = c % 2
                ps = self.ps[pb]
                P.op("pe", lambda e, ps=ps, g=g, j=j, c=c: e.matmul(ps[:], lhsT=wp[:, g, :], rhs=Z[j][:, c * 512:(c + 1) * 512],
                                                                  start=True, stop=True), reads=[Rwp, RU[j]],
                     writes=[self.psr[pb]])
                yj = c % 2
                P.op("act", lambda e, ps=ps, yj=yj, g=g: e.activation(out=yst[yj][:], in_=ps[:], func=AF.Copy,
                                                                     scale=psc[:, g:g + 1]), reads=[self.psr[pb], Rpsc],
                     writes=[Ryst[yj]])
                P.dma("sp", lambda e, inc, g=g, c=c, yj=yj: inc(e.dma_start(
                    out=S["YPT"][g * 128:(g + 1) * 128, c * 512:(c + 1) * 512], in_=yst[yj][:])), 1, reads=[Ryst[yj]],
                    writes=[SR["YPT"]])
        self.phase_end()

    def cmul(self, eng, outr, outi, ar, ai, br, bi, t1, t2, Rin, Rout, Rt):
        P = self.P
        tt = lambda o, a, b, op: (lambda e: e.tensor_tensor(out=o, in0=a, in1=b, op=op))
        P.op(eng, tt(t1, ar, br, ALU.mult), reads=Rin, writes=[Rt])
        P.op(eng, tt(t2, ai, bi, ALU.mult), reads=Rin + [Rt], writes=[Rt])
        P.op(eng, tt(outr, t1, t2, ALU.subtract), reads=[Rt], writes=[Rout])
        P.op(eng, tt(t1, ar, bi, ALU.mult), reads=Rin + [Rt, Rout], writes=[Rt])
        P.op(eng, tt(t2, ai, br, ALU.mult), reads=Rin + [Rt], writes=[Rt])
        P.op(eng, tt(outi, t1, t2, ALU.add), reads=[Rt], writes=[Rout])

    def phase_s5(self, l):
        P = self.P
        S, SR = self.scr, self.sres
        T = 512
        sm = lambda: self.sb([128, 16], F32)
        lam_n = self.sb([16, 256], F32)
        Rp = Res("s5prep")
        P.dma("sp", lambda e, inc: inc(e.dma_start(out=lam_n[:, 0:128], in_=self.d["ssm_lam_re"][l].rearrange("(j g) p -> j (g p)", g=2))),
              1, writes=[Rp])
        P.dma("sp", lambda e, inc: inc(e.dma_start(out=lam_n[:, 128:256], in_=self.d["ssm_lam_im"][l].rearrange("(j g) p -> j (g p)", g=2))),
              1, writes=[Rp])
        lr, li, stp, rho, th, sn, cs, t1, t2, t3, kr, ki, nki, den = [sm() for _ in range(14)]
        P.op("pe", lambda e: (e.transpose(out=self.ps[7][:, 0:16], in_=lam_n[:, 0:128], identity=self.ident[0:16, 0:16]),
                              e.transpose(out=self.ps[7][:, 16:32], in_=lam_n[:, 128:256], identity=self.ident[0:16, 0:16]))[1],
             reads=[Rp, self.Rconst], writes=[self.psr[7]])
        P.op("dve", lambda e: e.tensor_copy(out=lr[:], in_=self.ps[7][:, 0:16]), reads=[self.psr[7]], writes=[Rp])
        P.op("dve", lambda e: e.tensor_copy(out=li[:], in_=self.ps[7][:, 16:32]), reads=[self.psr[7], Rp], writes=[Rp])
        ldt = self.d["ssm_log_dt"][l:l + 1, :].rearrange("o (j g) -> o j g", g=2)
        for gl in range(2):
            P.dma("sp", lambda e, inc, gl=gl: inc(e.dma_start(out=stp[gl * 64:(gl + 1) * 64, :],
                                                            in_=ldt[:, :, gl].partition_broadcast(64),
                                                            allow_slow_non_contiguous=True)), 1, reads=[Rp], writes=[Rp])
        P.op("act", lambda e: e.activation(out=stp[:], in_=stp[:], func=AF.Exp), reads=[Rp], writes=[Rp])
        V = lambda fn: P.op("dve", fn, reads=[Rp], writes=[Rp])
        V(lambda e: e.tensor_tensor(out=t1[:], in0=lr[:], in1=stp[:], op=ALU.mult))
        P.op("act", lambda e: e.activation(out=rho[:], in_=t1[:], func=AF.Exp), reads=[Rp], writes=[Rp])
        V(lambda e: e.tensor_tensor(out=th[:], in0=li[:], in1=stp[:], op=ALU.mult))
        TWO_PI = 2.0 * math.pi
        ni = self.sb([128, 16], mybir.dt.int32)
        nf = sm()

        def reduce_turns(dst, shift):
            V(lambda e: e.tensor_scalar(out=dst[:], in0=th[:], scalar1=1.0 / TWO_PI, scalar2=shift, op0=ALU.mult, op1=ALU.add))
            V(lambda e: e.tensor_copy(out=ni[:], in_=dst[:]))
            V(lambda e: e.tensor_copy(out=nf[:], in_=ni[:]))
            V(lambda e: e.tensor_tensor(out=dst[:], in0=dst[:], in1=nf[:], op=ALU.subtract))
            V(lambda e: e.tensor_single_scalar(out=nf[:], in_=dst[:], scalar=0.5, op=ALU.is_gt))
            V(lambda e: e.tensor_tensor(out=dst[:], in0=dst[:], in1=nf[:], op=ALU.subtract))
            V(lambda e: e.tensor_single_scalar(out=nf[:], in_=dst[:], scalar=-0.5, op=ALU.is_lt))
            V(lambda e: e.tensor_tensor(out=dst[:], in0=dst[:], in1=nf[:], op=ALU.add))
        reduce_turns(t1, 0.0)
        reduce_turns(t2, 0.25)
        P.op("act", lambda e: e.activation(out=sn[:], in_=t1[:], func=AF.Sin, scale=TWO_PI), reads=[Rp], writes=[Rp])
        P.op("act", lambda e: e.activation(out=cs[:], in_=t2[:], func=AF.Sin, scale=TWO_PI), reads=[Rp], writes=[Rp])
        V(lambda e: e.tensor_tensor(out=t1[:], in0=rho[:], in1=cs[:], op=ALU.mult))
        V(lambda e: e.tensor_scalar_add(out=t1[:], in0=t1[:], scalar1=-1.0))
        V(lambda e: e.tensor_tensor(out=t2[:], in0=rho[:], in1=sn[:], op=ALU.mult))
        V(lambda e: e.tensor_tensor(out=den[:], in0=lr[:], in1=lr[:], op=ALU.mult))
        V(lambda e: e.tensor_tensor(out=t3[:], in0=li[:], in1=li[:], op=ALU.mult))
        V(lambda e: e.tensor_tensor(out=den[:], in0=den[:], in1=t3[:], op=ALU.add))
        V(lambda e: e.reciprocal(out=den[:], in_=den[:]))
        V(lambda e: e.tensor_tensor(out=kr[:], in0=t1[:], in1=lr[:], op=ALU.mult))
        V(lambda e: e.tensor_tensor(out=t3[:], in0=t2[:], in1=li[:], op=ALU.mult))
        V(lambda e: e.tensor_tensor(out=kr[:], in0=kr[:], in1=t3[:], op=ALU.add))
        V(lambda e: e.tensor_tensor(out=kr[:], in0=kr[:], in1=den[:], op=ALU.mult))
        V(lambda e: e.tensor_tensor(out=ki[:], in0=t2[:], in1=lr[:], op=ALU.mult))
        V(lambda e: e.tensor_tensor(out=t3[:], in0=t1[:], in1=li[:], op=ALU.mult))
        V(lambda e: e.tensor_tensor(out=ki[:], in0=ki[:], in1=t3[:], op=ALU.subtract))
        V(lambda e: e.tensor_tensor(out=ki[:], in0=ki[:], in1=den[:], op=ALU.mult))
        V(lambda e: e.tensor_scalar_mul(out=nki[:], in0=ki[:], scalar1=-1.0))
        Er = self.sb([128, 16, T], F32)
        Ei = self.sb([128, 16, T], F32)
        ETr, ETi, emr, emi = sm(), sm(), sm(), sm()
        RE = Res("E")
        Rt = Res("ctmp")
        P.op("pool", lambda e: e.memset(Er[:, :, 0:1], 1.0), writes=[RE])
        P.op("pool", lambda e: e.memset(Ei[:, :, 0:1], 0.0), reads=[RE], writes=[RE])
        P.op("pool", lambda e: e.tensor_copy(out=Er[:, :, 1:2], in_=cs[:].unsqueeze(2)), reads=[Rp, RE], writes=[RE])
        P.op("pool", lambda e: e.tensor_copy(out=Ei[:, :, 1:2], in_=sn[:].unsqueeze(2)), reads=[Rp, RE], writes=[RE])
        LB = self.sb([128, 16, 2, 128], BF16)
        LC = self.sb([128, 16, 2, 128], BF16)
        RL = Res("LBC")
        ct = self.sb([128, 128], F32)
        dsk = self.sb([128, 4], F32)
        bgl = self.sb([128, 4], F32)
        Wg = self.sb([128, 4, 512], BF16)
        RWg = Res("wglu")
        wgl = self.d["w_glu"][l].rearrange("(a p) c -> p a c", p=128)
        self.load_cast([(Wg[:, :, 0:256], wgl[:, :, 0:256], [128, 4, 256]), (Wg[:, :, 256:512], wgl[:, :, 256:512], [128, 4, 256])],
                       RWg, stage_cols=1024)
        mark = self.cur
        big1 = self.sb([128, 16, 256], F32)
        big2 = self.sb([128, 16, 256], F32)
        m = 2
        while m < T:
            self.cmul("dve", emr[:], emi[:], Er[:, :, m - 1], Ei[:, :, m - 1], cs[:], sn[:], t1[:], t2[:], [RE, Rp], RE, Rt)
            bc = lambda a: a[:].unsqueeze(2).to_broadcast([128, 16, m])
            self.cmul("dve", Er[:, :, m:2 * m], Ei[:, :, m:2 * m], Er[:, :, 0:m], Ei[:, :, 0:m], bc(emr), bc(emi),
                      big1[:, :, 0:m], big2[:, :, 0:m], [RE], RE, Rt)
            m *= 2
        self.cmul("dve", ETr[:], ETi[:], Er[:, :, T - 1], Ei[:, :, T - 1], cs[:], sn[:], t1[:], t2[:], [RE, Rp], RE, Rt)
        Bp = [self.sb([128, 16, 128], F32) for _ in range(2)]
        Cp = [self.sb([128, 16, 128], F32) for _ in range(2)]
        Rpad = Res("pads")
        for t_ in Bp + Cp:
            P.op("pool", lambda e, t_=t_: e.memset(t_[:], 0.0), writes=[Rpad])
        for ri, (bn, cn) in enumerate((("ssm_b_re", "ssm_c_re"), ("ssm_b_im", "ssm_c_im"))):
            bsrc = self.d[bn][l].rearrange("(a q g) p c -> g q p a c", q=4, g=2)
            csrc = self.d[cn][l].rearrange("(a q g) c p -> g q c a p", q=4, g=2)
            for gl in range(2):
                for q in range(4):
                    P.dma("sp", lambda e, inc, ri=ri, gl=gl, q=q, bsrc=bsrc: inc(e.dma_start(
                        out=Bp[ri][gl * 64:(gl + 1) * 64, q:16:4, q * 32 + gl * 16:q * 32 + gl * 16 + 16], in_=bsrc[gl, q])), 1,
                        reads=[Rpad], writes=[Rpad])
                    P.dma("sp", lambda e, inc, ri=ri, gl=gl, q=q, csrc=csrc: inc(e.dma_start(
                        out=Cp[ri][q * 32 + gl * 16:q * 32 + gl * 16 + 16, q:16:4, gl * 64:(gl + 1) * 64], in_=csrc[gl, q])), 1,
                        reads=[Rpad], writes=[Rpad])
        for j in range(16):
            pb = 6 + j % 2
            ps = self.ps[pb]

            def tr(e, j=j, ps=ps):
                ins = None
                for q, src in enumerate((Bp[0], Bp[1], Cp[0], Cp[1])):
                    ins = e.transpose(out=ps[:, q * 128:(q + 1) * 128], in_=src[:, j, :], identity=self.ident[:])
                return ins
            P.op("pe", tr, reads=[Rpad, self.Rconst], writes=[self.psr[pb]])
            P.op("act", lambda e, j=j, ps=ps: e.copy(out=LB[:, j, :, :], in_=ps[:, 0:256].rearrange("p (a b) -> p a b", b=128)),
                 reads=[self.psr[pb]], writes=[RL])
            P.op("dve", lambda e, j=j, ps=ps: e.tensor_scalar(out=ct[:], in0=ps[:, 384:512], scalar1=ki[:, j:j + 1], scalar2=None,
                                                             op0=ALU.mult), reads=[self.psr[pb], Rp, RL], writes=[Rt])
            P.op("dve", lambda e, j=j, ps=ps: e.scalar_tensor_tensor(out=LC[:, j, 0, :], in0=ps[:, 256:384], scalar=kr[:, j:j + 1],
                                                                    in1=ct[:], op0=ALU.mult, op1=ALU.subtract),
                 reads=[self.psr[pb], Rp, Rt], writes=[RL])
            P.op("dve", lambda e, j=j, ps=ps: e.tensor_scalar(out=ct[:], in0=ps[:, 384:512], scalar1=kr[:, j:j + 1], scalar2=None,
                                                             op0=ALU.mult), reads=[self.psr[pb], Rp, RL], writes=[Rt])
            P.op("dve", lambda e, j=j, ps=ps: e.scalar_tensor_tensor(out=LC[:, j, 1, :], in0=ps[:, 256:384], scalar=nki[:, j:j + 1],
                                                                    in1=ct[:], op0=ALU.mult, op1=ALU.subtract),
                 reads=[self.psr[pb], Rp, Rt], writes=[RL])
        P.dma("sp", lambda e, inc: inc(e.dma_start(out=dsk[:], in_=self.d["ssm_d"][l].rearrange("g c -> (g c)").rearrange("(a p) -> p a", p=128),
                                                   allow_slow_non_contiguous=True)), 1, writes=[Rp])
        P.dma("sp", lambda e, inc: inc(e.dma_start(out=bgl[:], in_=self.d["b_glu"][l].rearrange("(a p) -> p a", p=128),
                                                   allow_slow_non_contiguous=True)), 1, writes=[Rp])
        self.P.barrier()
        self.cur = mark
        uf = [self.sb([128, 4, T], F32) for _ in range(2)]
        ub = [self.sb([128, 4, T], BF16) for _ in range(2)]
        Ruf = [Res("uf0"), Res("uf1")]
        Rub = [Res("ub0"), Res("ub1")]
        NB = 3
        m1 = [self.sb([128, T], F32) for _ in range(NB)]
        m2 = [self.sb([128, T], F32) for _ in range(NB)]
        m3 = [self.sb([128, T], F32) for _ in range(NB)]
        m4 = [self.sb([128, T], F32) for _ in range(NB)]
        gr = [self.sb([128, T], F32) for _ in range(NB)]
        gi = [self.sb([128, T], F32) for _ in range(NB)]
        Hr = [self.sb([128, T], BF16) for _ in range(NB)]
        Hi = [self.sb([128, T], BF16) for _ in range(NB)]
        Rm = [Res("m%d" % i) for i in range(NB)]
        Rh = [Res("h%d" % i) for i in range(NB)]
        Rg = [Res("g%d" % i) for i in range(NB)]
        RH = [Res("H%d" % i) for i in range(NB)]
        glr, gli, gir, gii = sm(), sm(), sm(), sm()
        Rgl = Res("glast")
        Rgi = Res("ginit")
        P.op("pool", lambda e: e.memset(gir[:], 0.0), writes=[Rgi])
        P.op("pool", lambda e: e.memset(gii[:], 0.0), reads=[Rgi], writes=[Rgi])
        ysb = self.sb([128, T], F32)
        Rysb = Res("ysb")
        zf = self.sb([128, 4, T], F32)
        zb = self.sb([128, 4, T], BF16)
        Rzf, Rzb = Res("zf"), Res("zb")
        sgl = self.sb([128, T], F32)
        Rsgl = Res("sgl")
        ost = [self.sb([128, T], BF16) for _ in range(2)]
        Rost = [Res("ost0"), Res("ost1")]
        USTd = S["UST"].rearrange("(a p) t -> p a t", p=128)
        hm1, hm2, hm3, hm4 = m1, m2, m3, m4
        state = {"n": 0}

        def load_chunk(c):
            cj = c % 2
            P.dma("sp", lambda e, inc, c=c, cj=cj: inc(e.dma_start(out=uf[cj][:], in_=USTd[:, :, c * T:(c + 1) * T])), 1,
                  reads=[SR["UST"]], writes=[Ruf[cj]])
            P.op("pool", lambda e, cj=cj: e.tensor_copy(out=ub[cj][:], in_=uf[cj][:]), reads=[Ruf[cj]], writes=[Rub[cj]])

        def stageA(c, j):
            cj = c % 2
            kt = j // 4
            n = state["n"]
            state["n"] += 1
            b_ = n % NB
            pa, pbk = ((0, 1), (2, 3))[n % 2]
            P.op("pe", lambda e, j=j, kt=kt, cj=cj, pa=pa: e.matmul(self.ps[pa][:], lhsT=LB[:, j, 0, :], rhs=ub[cj][:, kt, :],
                                                                 start=True, stop=True), reads=[RL, Rub[cj]], writes=[self.psr[pa]])
            P.op("pe", lambda e, j=j, kt=kt, cj=cj, pbk=pbk: e.matmul(self.ps[pbk][:], lhsT=LB[:, j, 1, :], rhs=ub[cj][:, kt, :],
                                                                   start=True, stop=True), reads=[RL, Rub[cj]],
                 writes=[self.psr[pbk]])
            DV = lambda o, a, bb, op, rd, wr: P.op("dve", lambda e: e.tensor_tensor(out=o, in0=a, in1=bb, op=op), reads=rd, writes=wr)
            Pr_, Pi_ = self.ps[pa][:], self.ps[pbk][:]
            DV(m1[b_][:], Pr_, Er[:, j, :], ALU.mult, [RE, self.psr[pa]], [Rm[b_]])
            DV(m2[b_][:], Pi_, Ei[:, j, :], ALU.mult, [RE, self.psr[pbk], Rm[b_]], [Rm[b_]])
            DV(m3[b_][:], Pi_, Er[:, j, :], ALU.mult, [RE, self.psr[pbk], Rm[b_]], [Rm[b_]])
            DV(m4[b_][:], Pr_, Ei[:, j, :], ALU.mult, [RE, self.psr[pa], Rm[b_]], [Rm[b_]])
            DV(m1[b_][:], m1[b_][:], m2[b_][:], ALU.add, [Rm[b_]], [Rm[b_]])
            DV(m3[b_][:], m3[b_][:], m4[b_][:], ALU.subtract, [Rm[b_]], [Rm[b_]])
            P.op("dve", lambda e, b_=b_, j=j: e.tensor_tensor_scan(
                out=gr[b_][:], data0=rho[:, j:j + 1].to_broadcast([128, T]), data1=m1[b_][:], initial=gir[:, j:j + 1],
                op0=ALU.mult, op1=ALU.add), reads=[Rm[b_], Rp, Rgi], writes=[Rg[b_]])
            P.op("dve", lambda e, b_=b_, j=j: e.tensor_tensor_scan(
                out=gi[b_][:], data0=rho[:, j:j + 1].to_broadcast([128, T]), data1=m3[b_][:], initial=gii[:, j:j + 1],
                op0=ALU.mult, op1=ALU.add), reads=[Rm[b_], Rp, Rgi, Rg[b_]], writes=[Rg[b_]])
            return b_

        def stageB(c, j, b_):
            cj = c % 2
            kt = j // 4
            P.op("pool", lambda e, b_=b_, j=j: e.tensor_copy(out=glr[:, j:j + 1], in_=gr[b_][:, T - 1:T]), reads=[Rg[b_]], writes=[Rgl])
            P.op("pool", lambda e, b_=b_, j=j: e.tensor_copy(out=gli[:, j:j + 1], in_=gi[b_][:, T - 1:T]), reads=[Rg[b_], Rgl],
                 writes=[Rgl])
            PL = lambda o, a, bb, rd, wr: P.op("pool", lambda e: e.tensor_tensor(out=o, in0=a, in1=bb, op=ALU.mult), reads=rd, writes=wr)
            PL(hm2[b_][:], Ei[:, j, :], gi[b_][:], [RE, Rg[b_], Rm[b_]], [Rm[b_]])
            PL(hm4[b_][:], Ei[:, j, :], gr[b_][:], [RE, Rg[b_], Rm[b_]], [Rm[b_]])
            PL(hm1[b_][:], Er[:, j, :], gr[b_][:], [RE, Rg[b_], Rm[b_]], [Rm[b_]])
            PL(hm3[b_][:], Er[:, j, :], gi[b_][:], [RE, Rg[b_], Rm[b_]], [Rm[b_]])
            P.op("dve", lambda e, b_=b_: e.tensor_tensor(out=Hr[b_][:], in0=hm1[b_][:], in1=hm2[b_][:], op=ALU.subtract),
                 reads=[Rm[b_]], writes=[RH[b_]])
            P.op("dve", lambda e, b_=b_: e.tensor_tensor(out=Hi[b_][:], in0=hm3[b_][:], in1=hm4[b_][:], op=ALU.add),
                 reads=[Rm[b_], RH[b_]], writes=[RH[b_]])
            py = 4 + kt % 2

            def mmy(e, j=j, b_=b_, py=py):
                e.matmul(self.ps[py][:], lhsT=LC[:, j, 0, :], rhs=Hr[b_][:], start=(j % 4 == 0), stop=False)
                return e.matmul(self.ps[py][:], lhsT=LC[:, j, 1, :], rhs=Hi[b_][:], start=False, stop=(j % 4 == 3))
            P.op("pe", mmy, reads=[RL, RH[b_]], writes=[self.psr[py]], self_ok=(j % 4 != 0))
            if j % 4 == 3:
                P.op("dve", lambda e, kt=kt, cj=cj, py=py: e.scalar_tensor_tensor(
                    out=ysb[:], in0=uf[cj][:, kt, :], scalar=dsk[:, kt:kt + 1], in1=self.ps[py][:], op0=ALU.mult, op1=ALU.add),
                    reads=[self.psr[py], Ruf[cj], Rp], writes=[Rysb])
                P.op("act", lambda e, kt=kt: e.activation(out=zf[:, kt, :], in_=ysb[:], func=AF.Gelu_apprx_tanh),
                     reads=[Rysb], writes=[Rzf])
                P.op("act", lambda e, kt=kt: e.copy(out=zb[:, kt, :], in_=zf[:, kt, :]), reads=[Rzf], writes=[Rzb])

        load_chunk(0)
        for c in range(L // T):
            if c + 1 < L // T:
                load_chunk(c + 1)
            prev = None
            for j in range(16):
                b_ = stageA(c, j)
                if prev is not None:
                    stageB(c, prev[0], prev[1])
                prev = (j, b_)
            stageB(c, prev[0], prev[1])
            self.cmul("dve", gir[:], gii[:], ETr[:], ETi[:], glr[:], gli[:], t1[:], t2[:], [RE, Rgl], Rgi, Rt)
            for ot in range(4):
                def mmg(e, ot=ot):
                    ins = None
                    for a in range(4):
                        ins = e.matmul(self.ps[6][:], lhsT=Wg[:, a, ot * 128:(ot + 1) * 128], rhs=zb[:, a, :], start=(a == 0),
                                       stop=(a == 3))
                    return ins
                P.op("pe", mmg, reads=[RWg, Rzb], writes=[self.psr[6]])
                P.op("act", lambda e, ot=ot: e.activation(out=sgl[:], in_=self.ps[6][:], func=AF.Sigmoid, bias=bgl[:, ot:ot + 1]),
                     reads=[self.psr[6], Rp], writes=[Rsgl])
                oj = ot % 2
                P.op("dve", lambda e, ot=ot, oj=oj: e.tensor_tensor(out=ost[oj][:], in0=zf[:, ot, :], in1=sgl[:], op=ALU.mult),
                     reads=[Rzf, Rsgl], writes=[Rost[oj]])
                P.dma("sp", lambda e, inc, ot=ot, oj=oj, c=c: inc(e.dma_start(
                    out=S["YST"][ot * 128:(ot + 1) * 128, c * T:(c + 1) * T], in_=ost[oj][:])), 1, reads=[Rost[oj]],
                    writes=[SR["YST"]])
        self.phase_end()

    def phase_nsa(self, l):
        P = self.P
        S, SR = self.scr, self.sres
        ps, psr = self.ps, self.psr
        QT = self.sb([128, 4, L], BF16)
        KT = {n: self.sb([128, L], BF16) for n in ("KCT", "VCT")}
        RQ = Res("QTs")
        P.dma("sp", lambda e, inc: inc(e.dma_start(out=QT[:], in_=S["QT"][:, :, :])), 1, reads=[SR["QT"]], writes=[RQ])
        for n in KT:
            P.dma("sp", lambda e, inc, n=n: inc(e.dma_start(out=KT[n][:], in_=S[n][:, :])), 1, reads=[SR[n]], writes=[RQ])
        KWz = [self.sb([128, L], BF16) for _ in range(2)]
        KE = [self.sb([128, L], BF16) for _ in range(2)]
        for k in range(2):
            o = 1 - k
            P.op("pool", lambda e, k=k, o=o: e.memset(KWz[k][o * 64:(o + 1) * 64, :], 0.0), writes=[RQ])
            P.dma("sp", lambda e, inc, k=k: inc(e.dma_start(out=KWz[k][k * 64:(k + 1) * 64, :], in_=S["KWT"][k * 64:(k + 1) * 64, :])),
                  1, reads=[SR["KWT"], RQ], writes=[RQ])
            P.dma("sp", lambda e, inc, k=k: inc(e.dma_start(out=KE[k][k * 64:(k + 1) * 64, :], in_=S["KST"][k * 64:(k + 1) * 64, :])),
                  1, reads=[SR["KST"], RQ], writes=[RQ])
            P.dma("sp", lambda e, inc, k=k, o=o: inc(e.dma_start(out=KE[k][o * 64:(o + 1) * 64, :], in_=self.d["c_ebig"][:, :])),
                  1, reads=[RQ], writes=[RQ])
        Va = {n: self.sb([128, NT, 2, 65], BF16) for n in ("VS", "VW")}
        for n in Va:
            P.op("pool", lambda e, n=n: e.memset(Va[n][:, :, :, 64:65], 1.0), writes=[RQ])
            for k in range(2):
                P.dma("sp", lambda e, inc, n=n, k=k: inc(e.dma_start(
                    out=Va[n][:, :, k, 0:64], in_=S[n].rearrange("(a p) c -> p a c", p=128)[:, :, k * 64:(k + 1) * 64])), 1,
                    reads=[SR[n], RQ], writes=[RQ])
        G = self.sb([128, NT, 24], F32)
        P.dma("sp", lambda e, inc: inc(e.dma_start(out=G[:], in_=S["GN"].rearrange("(a p) c -> p a c", p=128))), 1,
              reads=[SR["GN"]], writes=[RQ])
        W1 = {t: self.sb([128, 32, 128], BF16) for t in "kv"}
        W2kd = self.sb([128, 128], BF16)
        W2v = self.sb([128, 64], BF16)
        RWc = Res("Wc")
        pieces = []
        for t, nm in (("k", "cmp_wk1"), ("v", "cmp_wv1")):
            src = self.d[nm][l].rearrange("(l d) h -> d l h", d=64)
            for half in range(2):
                for o in range(0, 32, 8):
                    pieces.append((W1[t][half * 64:(half + 1) * 64, o:o + 8, :], src[:, o:o + 8, :], [64, 8, 128], half * 64))
        pieces.append((W2kd[:, 0:64], self.d["cmp_wk2"][l], [128, 64]))
        pieces.append((W2kd[:, 64:128], self.d["cmp_wk2"][l], [128, 64]))
        pieces.append((W2v[:], self.d["cmp_wv2"][l], [128, 64]))
        self.load_cast(pieces, RWc, stage_cols=1024)
        pe_n = self.sb([32, 256], F32)
        Rpe = Res("pe")
        for ti, nm in enumerate(("cmp_pe_k", "cmp_pe_v")):
            for half in range(2):
                P.dma("sp", lambda e, inc, ti=ti, nm=nm, half=half: inc(e.dma_start(
                    out=pe_n[:, ti * 128 + half * 64:ti * 128 + half * 64 + 64], in_=self.d[nm][l])), 1, writes=[Rpe])
        peT = self.sb([128, 2, 32], BF16)
        P.op("pe", lambda e: (e.transpose(out=ps[7][:, 0:32], in_=pe_n[:, 0:128], identity=self.ident[0:32, 0:32]),
                              e.transpose(out=ps[7][:, 32:64], in_=pe_n[:, 128:256], identity=self.ident[0:32, 0:32]))[1],
             reads=[Rpe, self.Rconst], writes=[psr[7]])
        P.op("dve", lambda e: e.tensor_copy(out=peT[:], in_=ps[7][:, 0:64].rearrange("p (a b) -> p a b", b=32)), reads=[psr[7]],
             writes=[Rpe])
        cb = self.sb([128, 2], F32)
        for ti, t in enumerate("kv"):
            def mmb(e, ti=ti, t=t):
                ins = None
                for li_ in range(32):
                    ins = e.matmul(ps[7][:, 64 + ti:65 + ti], lhsT=W1[t][0:64, li_, :], rhs=peT[0:64, ti, li_:li_ + 1],
                                   start=(li_ == 0), stop=(li_ == 31))
                return ins
            P.op("pe", mmb, reads=[RWc, Rpe], writes=[psr[7]])
        P.op("dve", lambda e: e.tensor_copy(out=cb[:], in_=ps[7][:, 64:66]), reads=[psr[7]], writes=[Rpe])
        KcT = self.sb([128, 2, 256], BF16)
        rcmp = self.sb([128, 2, 2, 129], BF16)
        Rcmp = Res("cmpops")
        P.op("pool", lambda e: e.memset(KcT[:], 0.0), writes=[Rcmp])
        P.op("pool", lambda e: e.memset(rcmp[:], 0.0), reads=[Rcmp], writes=[Rcmp])
        P.op("pool", lambda e: e.memset(rcmp[:, :, :, 64:65], 1.0), reads=[Rcmp], writes=[Rcmp])
        ovf = self.sb([128, 2, 64], F32)
        P.dma("sp", lambda e, inc: inc(e.dma_start(out=ovf[:], in_=self.d["c_ov"].rearrange("(a p) s -> p a s", p=128))), 1,
              writes=[Rpe])
        for k in range(2):
            P.op("pool", lambda e, k=k: e.tensor_copy(out=rcmp[:, :, k, 65:129], in_=ovf[:]), reads=[Rpe, Rcmp], writes=[Rcmp])
        hid = [self.sb([128, 256], BF16) for _ in range(2)]
        Rhid = [Res("hid0"), Res("hid1")]
        for hj in range(2):
            P.op("pool", lambda e, hj=hj: e.memset(hid[hj][:], 0.0), writes=[Rhid[hj]])
        ci = 0
        for ti, (t, srcn) in enumerate((("k", "KCT"), ("v", "VCT"))):
            for k in range(2):
                hj = ci % 2
                pb = ci % 2
                ci += 1

                def mmh(e, t=t, srcn=srcn, k=k, pb=pb):
                    ins = None
                    for li_ in range(32):
                        ins = e.matmul(ps[pb][:, 0:255], lhsT=W1[t][k * 64:(k + 1) * 64, li_, :],
                                       rhs=KT[srcn][k * 64:(k + 1) * 64, li_:li_ + 4065:16], start=(li_ == 0), stop=(li_ == 31))
                    return ins
                P.op("pe", mmh, reads=[RWc, RQ], writes=[psr[pb]])
                P.op("act", lambda e, hj=hj, pb=pb, ti=ti: e.activation(out=hid[hj][:, 0:255], in_=ps[pb][:, 0:255],
                                                                      func=AF.Gelu_apprx_tanh, bias=cb[:, ti:ti + 1]),
                     reads=[psr[pb], Rpe], writes=[Rhid[hj]])
                if t == "k":
                    P.op("pe", lambda e, hj=hj: e.matmul(ps[2][:, 0:255], lhsT=W2kd[:], rhs=hid[hj][:, 0:255], start=True, stop=True),
                         reads=[RWc, Rhid[hj]], writes=[psr[2]])
                    P.op("dve", lambda e, k=k: e.tensor_copy(out=KcT[k * 64:(k + 1) * 64, k, 0:255],
                                                             in_=ps[2][k * 64:(k + 1) * 64, 0:255]), reads=[psr[2], Rcmp],
                         writes=[Rcmp])
                else:
                    for nt, rows in ((0, 128), (1, 127)):
                        P.op("pe", lambda e, hj=hj, nt=nt, rows=rows: e.matmul(
                            ps[3][0:rows, nt * 64:(nt + 1) * 64], lhsT=hid[hj][:, nt * 128:nt * 128 + rows], rhs=W2v[:], start=True,
                            stop=True), reads=[RWc, Rhid[hj]], writes=[psr[3]])
                        P.op("dve", lambda e, k=k, nt=nt, rows=rows: e.tensor_copy(out=rcmp[0:rows, nt, k, 0:64],
                                                                                 in_=ps[3][0:rows, nt * 64:(nt + 1) * 64]),
                             reads=[psr[3], Rcmp], writes=[Rcmp])
        import os
        if os.environ.get("NSA_UNITS") == "0":
            self.phase_end()
            return
        PT = [self.sb([128, 512], BF16) for _ in range(3)]
        RPT = [Res("PT%d" % i) for i in range(3)]
        yt = [self.sb([128, 512], F32) for _ in range(2)]
        Ryt = [Res("yt0"), Res("yt1")]
        yT = [self.sb([128, 4, 128], BF16) for _ in range(2)]
        RyT = [Res("yT0"), Res("yT1")]
        rsc = self.sb([128, 4], F32)
        wgt = self.sb([128, 4], F32)
        acc = self.sb([128, 64], F32)
        sc = self.sb([128, 64], F32)
        sc2 = self.sb([128, 64], F32)
        m8 = self.sb([128, 16], F32)
        biasf = self.sb([128, 128], F32)
        Rk = [self.sb([128, 4, 128], BF16) for _ in range(2)]
        Rpost = Res("post")
        Rbias = Res("biasf")
        RbT = [Res("R0"), Res("R1")]
        units = []
        deferred = []

        def flush_deferred():
            for f in deferred:
                f()
            deferred.clear()

        for qt in range(NT):
            for k in range(2):
                QTt = QT[:, :, qt * 128:(qt + 1) * 128]
                yj = qt % 2
                yacc = yt[yj][:, k * 256:(k + 1) * 256].rearrange("p (h d) -> p h d", d=64)
                gsl = lambda b, qt=qt, k=k: G[:, qt, k * 12 + b:k * 12 + 12:3]

                def post_cmp(qt=qt, k=k, yacc=yacc, gsl=gsl, yj=yj):
                    Dv = lambda fn, rd, wr: P.op("dve", fn, reads=rd, writes=wr)
                    for bi, b_ in enumerate((3, 4)):
                        Dv(lambda e, bi=bi, b_=b_: e.tensor_scalar(out=rsc[:, bi * 2:bi * 2 + 2], in0=ps[b_][:, 64:64 + 129 + 1:129],
                                                                    scalar1=1e-30, scalar2=None, op0=ALU.max), [psr[b_], Rpost], [Rpost])
                    Dv(lambda e: e.reciprocal(out=rsc[:], in_=rsc[:]), [Rpost], [Rpost])
                    for h in range(4):
                        b_ = 3 + h // 2
                        o = (h % 2) * 129
                        if h == 0:
                            Dv(lambda e, b_=b_, o=o: e.tensor_scalar(out=acc[:], in0=ps[b_][:, o + 65:o + 129], scalar1=rsc[:, 0:1],
                                                                     scalar2=None, op0=ALU.mult), [psr[b_], Rpost], [Rpost])
                        else:
                            Dv(lambda e, b_=b_, o=o, h=h: e.scalar_tensor_tensor(out=acc[:], in0=ps[b_][:, o + 65:o + 129],
                                                                                 scalar=rsc[:, h:h + 1], in1=acc[:], op0=ALU.mult,
                                                                                 op1=ALU.add), [psr[b_], Rpost], [Rpost])
                    Dv(lambda e: e.tensor_tensor(out=wgt[:], in0=rsc[:], in1=gsl(0), op=ALU.mult), [Rpost, RQ], [Rpost])
                    for h in range(4):
                        b_ = 3 + h // 2
                        o = (h % 2) * 129
                        Dv(lambda e, b_=b_, o=o, h=h: e.tensor_scalar(out=yacc[:, h, :], in0=ps[b_][:, o:o + 64], scalar1=wgt[:, h:h + 1],
                                                                       scalar2=None, op0=ALU.mult), [psr[b_], Rpost, Ryt[yj]], [Ryt[yj]])
                    Dv(lambda e: e.tensor_copy(out=sc[:], in_=acc[:]), [Rpost], [Rpost])
                    for e_ in range(2):
                        cur = 2 * qt + e_
                        rows = slice(e_ * 64, (e_ + 1) * 64)
                        if cur + 1 < 64:
                            Dv(lambda e, rows=rows, cur=cur: e.memset(sc[rows, cur + 1:64], -1e6), [Rpost], [Rpost])
                        Dv(lambda e, rows=rows: e.memset(sc[rows, 0:1], 1e6), [Rpost], [Rpost])
                        lo = max(cur - 1, 0)
                        Dv(lambda e, rows=rows, lo=lo, cur=cur: e.memset(sc[rows, lo:cur + 1], 1e6), [Rpost], [Rpost])
                    Dv(lambda e: e.max(out=m8[:, 0:8], in_=sc[:]), [Rpost], [Rpost])
                    Dv(lambda e: e.match_replace(out=sc2[:], in_to_replace=m8[:, 0:8], in_values=sc[:], imm_value=-1e30), [Rpost],
                       [Rpost])
                    Dv(lambda e: e.max(out=m8[:, 8:16], in_=sc2[:]), [Rpost], [Rpost])
                    for hh in range(2):
                        Dv(lambda e, hh=hh: e.tensor_scalar(out=biasf[:, hh * 64:(hh + 1) * 64], in0=sc[:], scalar1=m8[:, 15:16],
                                                            scalar2=-BIG, op0=ALU.is_lt, op1=ALU.mult), [Rpost, Rbias], [Rbias])

                def pre_sel(k=k, qt=qt):
                    flush_deferred()
                    o = 1 - k
                    P.op("pe", lambda e: e.transpose(out=ps[7][:, 0:128], in_=biasf[:], identity=self.ident[:]),
                         reads=[Rbias, self.Rconst], writes=[psr[7]])
                    P.op("pool", lambda e: e.tensor_copy(out=Rk[k][k * 64:(k + 1) * 64, :, :],
                                                         in_=QT[k * 64:(k + 1) * 64, :, qt * 128:(qt + 1) * 128]),
                         reads=[RQ], writes=[RbT[k]])
                    P.op("act", lambda e: e.copy(out=Rk[k][o * 64:(o + 1) * 64, :, :],
                                                 in_=ps[7][o * 64:(o + 1) * 64, 0:128].unsqueeze(1).to_broadcast([64, 4, 128])),
                         reads=[psr[7]], writes=[RbT[k]])

                def post_branch(bank, b, yacc=yacc, gsl=gsl, yj=yj):
                    def f():
                        Dv = lambda fn, rd, wr: P.op("dve", fn, reads=rd, writes=wr)
                        Dv(lambda e: e.reciprocal(out=rsc[:], in_=ps[bank][:, 64:64 + 3 * 65 + 1:65]), [psr[bank], Rpost], [Rpost])
                        Dv(lambda e: e.tensor_tensor(out=wgt[:], in0=rsc[:], in1=gsl(b), op=ALU.mult), [Rpost, RQ], [Rpost])
                        for h in range(4):
                            Dv(lambda e, h=h: e.scalar_tensor_tensor(out=yacc[:, h, :], in0=ps[bank][:, h * 65:h * 65 + 64],
                                                                     scalar=wgt[:, h:h + 1], in1=yacc[:, h, :], op0=ALU.mult,
                                                                     op1=ALU.add), [psr[bank], Rpost, Ryt[yj]], [Ryt[yj]])
                    return f

                def post_qt(qt=qt, yj=yj):
                    def tr_store():
                        def tr(e):
                            ins = None
                            for a in range(4):
                                ins = e.transpose(out=ps[7][:, a * 128:(a + 1) * 128], in_=yt[yj][:, a * 128:(a + 1) * 128],
                                                  identity=self.ident[:])
                            return ins
                        P.op("pe", tr, reads=[Ryt[yj], self.Rconst], writes=[psr[7]])
                        P.op("act", lambda e: e.copy(out=yT[yj][:], in_=ps[7][:].rearrange("p (a b) -> p a b", b=128)),
                             reads=[psr[7]], writes=[RyT[yj]])
                        P.dma("sp", lambda e, inc: inc(e.dma_start(
                            out=S["YNT"].rearrange("(a p) t -> p a t", p=128)[:, :, qt * 128:(qt + 1) * 128], in_=yT[yj][:])), 1,
                            reads=[RyT[yj]], writes=[SR["YNT"]])
                    return lambda: deferred.append(tr_store)

                nts = [0] if qt < 16 else [0, 1]
                for nt in nts:
                    rows = 128 if nt == 0 else 127
                    full = (128 * nt + rows - 1) <= 8 * qt - 2
                    mask = None if full else dict(pattern=[[0, 4], [1, 128]], base=128 * qt - 2048 * nt - 31, cm=-16)
                    units.append(dict(kind="cmp", rows=rows, lhsT=KcT[:, k, nt * 128:nt * 128 + rows], rhs=QTt,
                                      mask=mask, v=rcmp[0:rows, nt, k, :], first=(nt == 0), last=(nt == nts[-1]), pre=None,
                                      post=(post_cmp if nt == nts[-1] else None)))
                wk = list(range(max(0, qt - 4), qt + 1))
                for kt in wk:
                    mask = None
                    if kt == qt:
                        mask = dict(pattern=[[0, 4], [1, 128]], base=0, cm=-1)
                    elif kt == qt - 4:
                        mask = dict(pattern=[[0, 4], [-1, 128]], base=-1, cm=1)
                    units.append(dict(kind="win", rows=128, lhsT=KWz[k][:, kt * 128:(kt + 1) * 128], rhs=QTt,
                                      mask=mask, v=Va["VW"][:, kt, k, :], first=(kt == wk[0]), last=(kt == qt), pre=None,
                                      post=(post_branch(6, 2) if kt == qt else None), bank=6))
                for kt in range(qt + 1):
                    mask = dict(pattern=[[0, 4], [1, 128]], base=0, cm=-1) if kt == qt else None
                    posts = None
                    if kt == qt:
                        pb_ = post_branch(5, 1)
                        if k == 1:
                            pq = post_qt()
                            posts = (lambda pb_=pb_, pq=pq: (pb_(), post_qt_call(pq)))
                        else:
                            posts = pb_
                    units.append(dict(kind="sel", rows=128, lhsT=KE[k][:, kt * 128:(kt + 1) * 128], rhs=Rk[k][:],
                                      mask=mask, v=Va["VS"][:, kt, k, :], first=(kt == 0), last=(kt == qt),
                                      pre=(pre_sel if kt == 0 else None), post=posts, bank=5, k=k))

        def post_qt_call(pq):
            pq()

        def emit_qk(i):
            u = units[i]
            sb_ = i % 3
            if u["pre"] is not None:
                u["pre"]()
            rows = u["rows"]
            out3 = ps[sb_][0:rows, :].rearrange("p (a b) -> p a b", b=128)
            if u["kind"] == "sel":
                P.op("pe", lambda e, u=u, out3=out3: e.matmul(out3, lhsT=u["lhsT"], rhs=u["rhs"], start=True, stop=True),
                     reads=[RQ, RbT[u["k"]]], writes=[psr[sb_]])
            else:
                P.op("pe", lambda e, u=u, out3=out3: e.matmul(out3, lhsT=u["lhsT"], rhs=u["rhs"], start=True, stop=True),
                     reads=[RQ, Rcmp], writes=[psr[sb_]])

        def emit_rest(i):
            u = units[i]
            sb_ = i % 3
            rows = u["rows"]
            P.op("act", lambda e, sb_=sb_, rows=rows: e.activation(out=PT[sb_][0:rows, :], in_=ps[sb_][0:rows, :], func=AF.Exp,
                                                                  scale=0.125), reads=[psr[sb_]], writes=[RPT[sb_]])
            if u["mask"] is not None:
                mk = u["mask"]
                v3 = PT[sb_][0:rows, :].rearrange("p (a b) -> p a b", b=128)
                P.op("pool", lambda e, v3=v3, mk=mk: e.affine_select(out=v3, in_=v3, pattern=mk["pattern"], compare_op=ALU.is_ge,
                                                                    fill=0.0, base=mk["base"], channel_multiplier=mk["cm"]),
                     reads=[RPT[sb_]], writes=[RPT[sb_]])
            if u["kind"] == "cmp":
                def pv(e, u=u, sb_=sb_, rows=rows):
                    ins = None
                    for h in range(4):
                        b_ = 3 + h // 2
                        o = (h % 2) * 129
                        ins = e.matmul(ps[b_][:, o:o + 129], lhsT=PT[sb_][0:rows, h * 128:(h + 1) * 128], rhs=u["v"],
                                       start=(u["first"] and h % 2 == 0), stop=(u["last"] and h % 2 == 1))
                    return ins
                P.op("pe", pv, reads=[RPT[sb_], Rcmp], writes=[psr[3], psr[4]], self_ok=not u["first"])
            else:
                bank = u["bank"]

                def pv(e, u=u, sb_=sb_, bank=bank):
                    ins = None
                    for h in range(4):
                        ins = e.matmul(ps[bank][:, h * 65:(h + 1) * 65], lhsT=PT[sb_][:, h * 128:(h + 1) * 128], rhs=u["v"],
                                       start=(u["first"] and h == 0), stop=(u["last"] and h == 3))
                    return ins
                P.op("pe", pv, reads=[RPT[sb_], RQ], writes=[psr[bank]], self_ok=not u["first"])
            if u["post"] is not None:
                u["post"]()

        import os
        n = len(units)
        if os.environ.get("NSA_UNITS"):
            n = int(os.environ["NSA_UNITS"])
            deferred.clear()
        emit_qk(0)
        for i in range(n):
            if i + 1 < n:
                emit_qk(i + 1)
            emit_rest(i)
            if os.environ.get("NSA_UNITS") and i == n - 1:
                break
        flush_deferred()
        self.phase_end()

    def phase_merge(self, l):
        P = self.P
        S, SR = self.scr, self.sres
        w_in = self.d["w_in"][l].rearrange("(a p) c -> p a c", p=128)
        WU = self.sb([128, 12, D], BF16)
        WG = self.sb([128, 8, 3072], BF16)
        WO = self.sb([128, 8, D], BF16)
        RW = Res("WM")
        pieces = []
        for b, nm in enumerate(("w_up_pool", "w_up_ssm", "w_up_nsa")):
            wv = self.d[nm][l].rearrange("(a p) c -> p a c", p=128)
            for o in range(0, D, 512):
                pieces.append((WU[:, b * 4:(b + 1) * 4, o:o + 512], wv[:, :, o:o + 512], [128, 4, 512]))
        for o in range(0, 3072, 256):
            pieces.append((WG[:, :, o:o + 256], w_in[:, :, 2328 + o:2328 + o + 256], [128, 8, 256]))
        wo = self.d["w_out"][l].rearrange("(a p) c -> p a c", p=128)
        for o in range(0, D, 256):
            pieces.append((WO[:, :, o:o + 256], wo[:, :, o:o + 256], [128, 8, 256]))
        self.load_cast(pieces, RW)
        g_rep = self.sb([128, D], F32)
        b_rep = self.sb([128, D], F32)
        Rgb = Res("gb")
        self.bcast_load(g_rep[:], self.d["ln1_g"][l:l + 1, :], Rgb)
        self.bcast_load(b_rep[:], self.d["ln1_b"][l:l + 1, :], Rgb)
        xt = [self.sb([128, 8, 512], BF16) for _ in range(2)]
        Rxt = [Res("xt0"), Res("xt1")]
        yb = [self.sb([128, 12, 512], BF16) for _ in range(2)]
        Ryb = [Res("yb0"), Res("yb1")]
        mg = self.sb([128, 8, 512], BF16)
        Rmg = Res("mg")
        sg = [self.sb([128, 512], F32) for _ in range(2)]
        Rsg = [Res("sg0"), Res("sg1")]
        acc = self.sb([128, 512], F32)
        tmp = self.sb([128, 512], F32)
        Racc = Res("acc")
        Rtmp = Res("tmp")
        xr = [self.sb([128, D], F32) for _ in range(2)]
        Rxr = [Res("xr0"), Res("xr1")]
        bufs = [self.ln_bufs() for _ in range(2)]
        XTd = S["XT"].rearrange("(a p) t -> p a t", p=128)
        k = 0
        for c in range(8):
            j = c % 2
            P.dma("sp", lambda e, inc, c=c, j=j: inc(e.dma_start(out=xt[j][:], in_=XTd[:, :, c * 512:(c + 1) * 512])), 1,
                  reads=[SR["XT"]], writes=[Rxt[j]])
            for b, nm in enumerate(("YPT", "YST", "YNT")):
                P.dma("sp", lambda e, inc, c=c, j=j, b=b, nm=nm: inc(e.dma_start(
                    out=yb[j][:, b * 4:(b + 1) * 4, :],
                    in_=S[nm].rearrange("(a p) t -> p a t", p=128)[:, :, c * 512:(c + 1) * 512])), 1,
                    reads=[SR[nm]], writes=[Ryb[j]])
            for m in range(8):
                for b in range(3):
                    pu = k % 2
                    pg = 2 + k % 2
                    sj = k % 2
                    k += 1

                    def mmu(e, pu=pu, b=b, m=m, j=j):
                        ins = None
                        for a in range(4):
                            ins = e.matmul(self.ps[pu][:], lhsT=WU[:, b * 4 + a, m * 128:(m + 1) * 128], rhs=yb[j][:, b * 4 + a, :],
                                           start=(a == 0), stop=(a == 3))
                        return ins

                    def mmg(e, pg=pg, b=b, m=m, j=j):
                        ins = None
                        for a in range(8):
                            ins = e.matmul(self.ps[pg][:], lhsT=WG[:, a, b * 1024 + m * 128:b * 1024 + (m + 1) * 128],
                                           rhs=xt[j][:, a, :], start=(a == 0), stop=(a == 7))
                        return ins
                    P.op("pe", mmg, reads=[RW, Rxt[j]], writes=[self.psr[pg]])
                    P.op("pe", mmu, reads=[RW, Ryb[j]], writes=[self.psr[pu]])
                    P.op("act", lambda e, pg=pg, sj=sj: e.activation(out=sg[sj][:], in_=self.ps[pg][:], func=AF.Sigmoid),
                         reads=[self.psr[pg]], writes=[Rsg[sj]])
                    if b == 0:
                        P.op("dve", lambda e, pu=pu, sj=sj: e.tensor_tensor(out=acc[:], in0=self.ps[pu][:], in1=sg[sj][:],
                                                                            op=ALU.mult), reads=[self.psr[pu], Rsg[sj]],
                             writes=[Racc])
                    else:
                        P.op("dve", lambda e, pu=pu, sj=sj: e.tensor_tensor(out=tmp[:], in0=self.ps[pu][:], in1=sg[sj][:],
                                                                            op=ALU.mult), reads=[self.psr[pu], Rsg[sj]],
                             writes=[Rtmp])
                        if b == 1:
                            P.op("pool", lambda e: e.tensor_tensor(out=acc[:], in0=acc[:], in1=tmp[:], op=ALU.add),
                                 reads=[Rtmp, Racc], writes=[Racc])
                        else:
                            P.op("pool", lambda e, m=m: e.tensor_tensor(out=mg[:, m, :], in0=acc[:], in1=tmp[:], op=ALU.add),
                                 reads=[Rtmp, Racc], writes=[Rmg])
            for tt in range(4):
                i = c * 4 + tt
                xj = i % 2
                P.dma("sp", lambda e, inc, i=i, xj=xj: inc(e.dma_start(out=xr[xj][:], in_=S["XR"][i * 128:(i + 1) * 128, :])),
                      1, reads=[SR["XR"]], writes=[Rxr[xj]])
                for half in range(2):
                    pb = 4 + half

                    def mmo(e, pb=pb, tt=tt, half=half):
                        ins = None
                        for a in range(8):
                            ins = e.matmul(self.ps[pb][:], lhsT=mg[:, a, tt * 128:(tt + 1) * 128],
                                           rhs=WO[:, a, half * 512:(half + 1) * 512], start=(a == 0), stop=(a == 7))
                        return ins
                    P.op("pe", mmo, reads=[RW, Rmg], writes=[self.psr[pb]])
                    P.op("dve", lambda e, pb=pb, xj=xj, half=half: e.scalar_tensor_tensor(
                        out=xr[xj][:, half * 512:(half + 1) * 512], in0=xr[xj][:, half * 512:(half + 1) * 512], scalar=ALPHA,
                        in1=self.ps[pb][:], op0=ALU.mult, op1=ALU.add), reads=[self.psr[pb], Rxr[xj]], writes=[Rxr[xj]])
                self.ln_tile(i, xr[xj], Rxr[xj], g_rep, b_rep, Rgb, S["XR1"], SR["XR1"], S["X1T"], SR["X1T"], bufs[xj])
        self.phase_end()

    def phase_ffn(self, l, last):
        P = self.P
        S, SR = self.scr, self.sres
        W1 = self.sb([128, 8, 4096], BF16)
        W2 = self.sb([128, 32, D], BF16)
        RW = Res("WF")
        w1 = self.d["w_ff1"][l].rearrange("(a p) c -> p a c", p=128)
        w2 = self.d["w_ff2"][l].rearrange("(a p) c -> p a c", p=128)
        pieces = []
        for o in range(0, 4096, 128):
            pieces.append((W1[:, :, o:o + 128], w1[:, :, o:o + 128], [128, 8, 128]))
        for a in range(32):
            pieces.append((W2[:, a, :], w2[:, a, :], [128, D]))
        self.load_cast(pieces, RW, stage_cols=1024)
        g_rep = self.sb([128, D], F32)
        b_rep = self.sb([128, D], F32)
        Rgb = Res("gb")
        self.bcast_load(g_rep[:], self.d["ln2_g"][l:l + 1, :], Rgb)
        self.bcast_load(b_rep[:], self.d["ln2_b"][l:l + 1, :], Rgb)
        xt = [self.sb([128, 8, 256], BF16) for _ in range(2)]
        Rxt = [Res("xt0"), Res("xt1")]
        hT = self.sb([128, 32, 256], BF16)
        RhT = Res("hT")
        rl = [self.sb([128, 512], F32) for _ in range(2)]
        Rrl = [Res("rl0"), Res("rl1")]
        xr = [self.sb([128, D], F32) for _ in range(2)]
        Rxr = [Res("xr0"), Res("xr1")]
        bufs = [self.ln_bufs() for _ in range(2)]
        X1Td = S["X1T"].rearrange("(a p) t -> p a t", p=128)
        dst_res = self.d["out"] if last else S["XR"]
        Rdst = Res("outres") if last else SR["XR"]
        dst_T = None if last else S["XT"]
        k = 0
        for c in range(16):
            j = c % 2
            P.dma("sp", lambda e, inc, c=c, j=j: inc(e.dma_start(out=xt[j][:], in_=X1Td[:, :, c * 256:(c + 1) * 256])), 1,
                  reads=[SR["X1T"]], writes=[Rxt[j]])
            for f2 in range(16):
                pb = k % 4
                rj = k % 2
                k += 1

                def mm1(e, pb=pb, f2=f2, j=j):
                    ins = None
                    for q in range(2):
                        f = f2 * 2 + q
                        for a in range(8):
                            ins = e.matmul(self.ps[pb][:, q * 256:(q + 1) * 256], lhsT=W1[:, a, f * 128:(f + 1) * 128],
                                           rhs=xt[j][:, a, :], start=(a == 0), stop=(a == 7))
                    return ins
                P.op("pe", mm1, reads=[RW, Rxt[j]], writes=[self.psr[pb]])
                P.op("act", lambda e, pb=pb, rj=rj: e.activation(out=rl[rj][:], in_=self.ps[pb][:], func=AF.Relu),
                     reads=[self.psr[pb]], writes=[Rrl[rj]])
                P.op("pool", lambda e, rj=rj, f2=f2: e.tensor_tensor(
                    out=hT[:, f2 * 2:f2 * 2 + 2, :], in0=rl[rj][:].rearrange("p (a b) -> p a b", b=256),
                    in1=rl[rj][:].rearrange("p (a b) -> p a b", b=256), op=ALU.mult), reads=[Rrl[rj]], writes=[RhT])
            for tt in range(2):
                i = c * 2 + tt
                xj = i % 2
                P.dma("sp", lambda e, inc, i=i, xj=xj: inc(e.dma_start(out=xr[xj][:], in_=S["XR1"][i * 128:(i + 1) * 128, :])),
                      1, reads=[SR["XR1"]], writes=[Rxr[xj]])
                for half in range(2):
                    pb = 4 + half

                    def mm2(e, pb=pb, tt=tt, half=half):
                        ins = None
                        for f in range(32):
                            ins = e.matmul(self.ps[pb][:], lhsT=hT[:, f, tt * 128:(tt + 1) * 128],
                                           rhs=W2[:, f, half * 512:(half + 1) * 512], start=(f == 0), stop=(f == 31))
                        return ins
                    P.op("pe", mm2, reads=[RW, RhT], writes=[self.psr[pb]])
                    P.op("dve", lambda e, pb=pb, xj=xj, half=half: e.scalar_tensor_tensor(
                        out=xr[xj][:, half * 512:(half + 1) * 512], in0=xr[xj][:, half * 512:(half + 1) * 512], scalar=ALPHA,
                        in1=self.ps[pb][:], op0=ALU.mult, op1=ALU.add), reads=[self.psr[pb], Rxr[xj]], writes=[Rxr[xj]])
                self.ln_tile(i, xr[xj], Rxr[xj], g_rep, b_rep, Rgb, dst_res, Rdst, dst_T, SR["XT"], bufs[xj])
        self.phase_end()

    def build(self):
        self.phase_ln_in()
        for l in range(self.layers):
            stop = self.stop_after
            skip = self.skip
            if "proj" not in skip:
                self.phase_proj(l)
            if stop == "proj":
                break
            if "pool" not in skip:
                self.phase_pool(l)
            if stop == "pool":
                break
            if "s5" not in skip:
                self.phase_s5(l)
            if stop == "s5":
                break
            if "nsa" not in skip:
                self.phase_nsa(l)
            if stop == "nsa":
                break
            self.phase_merge(l)
            if stop == "merge":
                break
            self.phase_ffn(l, last=(l == self.layers - 1))
        self.P.emit()
        return self.nc


def make_inputs(inputs, ncores=8):
    consts = host_consts()
    shared = {n: np.ascontiguousarray(np.asarray(inputs[n], dtype=np.float32)) for n, _ in PARAMS}
    shared.update(consts)
    x = np.asarray(inputs["x"], dtype=np.float32)
    maps = []
    for c in range(ncores):
        m = dict(shared)
        m["x"] = np.ascontiguousarray(x[c])
        maps.append(m)
    return maps


def kernel(**inputs):
    b = Builder()
    nc = b.build()
    maps = make_inputs(inputs)
    res = run_bass_kernel_spmd(nc, maps, core_ids=list(range(8)))
    return np.stack([np.asarray(r["out"], dtype=np.float32) for r in res.results], axis=0)
```

```python
import contextlib
import math
import numpy as np
import ml_dtypes
import concourse.bass as bass
import concourse.mybir as mybir
from concourse.bass_utils import run_bass_kernel_spmd

F32 = mybir.dt.float32
BF16 = mybir.dt.bfloat16
AF = mybir.ActivationFunctionType
ALU = mybir.AluOpType
AX = mybir.AxisListType

L = 4096
D = 1024
NT = 32
DEPTH = 2
ALPHA = (2 * DEPTH) ** 0.25
EPS = 1e-5
BIG = 30000.0
SB_BASE = 16512
SB_END = 229344


class Res:
    __slots__ = ("name", "w", "r")

    def __init__(self, name=""):
        self.name = name
        self.w = None
        self.r = []


class Prog:
    ENGS = ("pe", "act", "dve", "pool", "sp")
    NDMA = 8

    def __init__(self, nc):
        self.nc = nc
        self.ops = {e: [] for e in self.ENGS}
        self.cnt = {}
        self.seen = {e: {} for e in self.ENGS}
        self.dma_n = {e: 0 for e in self.ENGS}
        self.sem_keys = []
        for e in ("pe", "act", "dve", "pool"):
            self._mk(("c", e))
        for e in ("sp", "act", "pool"):
            for j in range(self.NDMA):
                self._mk(("d", e, j))
        self.nops = 0

    def _mk(self, key):
        self.cnt[key] = 0
        self.sem_keys.append(key)

    def _need(self, eng, waits, dep):
        if dep is None:
            return
        key, val = dep
        if self.seen[eng].get(key, 0) >= val:
            return
        waits[key] = max(waits.get(key, 0), val)

    def _deps(self, eng, reads, writes, self_ok=False):
        waits = {}
        for r in reads:
            self._need(eng, waits, r.w)
        for w in writes:
            self._need(eng, waits, w.w)
            for rd in w.r:
                self._need(eng, waits, rd)
        if self_ok:
            waits.pop(("c", eng), None)
        return waits

    def _commit(self, eng, waits, tok, reads, writes):
        for k, v in waits.items():
            self.seen[eng][k] = v
        for r in reads:
            r.r.append(tok)
        for w in writes:
            w.w = tok
            w.r = []
        self.nops += 1

    def op(self, eng, fn, reads=(), writes=(), self_ok=False):
        waits = self._deps(eng, reads, writes, self_ok)
        key = ("c", eng)
        self.cnt[key] += 1
        tok = (key, self.cnt[key])
        self.ops[eng].append((list(waits.items()), fn, key, 1))
        self._commit(eng, waits, tok, reads, writes)
        return tok

    def dma(self, q, fn, ndma, reads=(), writes=()):
        waits = self._deps(q, reads, writes)
        j = self.dma_n[q] % self.NDMA
        self.dma_n[q] += 1
        key = ("d", q, j)
        if self.cnt[key] > 0:
            self._need(q, waits, (key, self.cnt[key]))
        self.cnt[key] += 16 * ndma
        tok = (key, self.cnt[key])
        self.ops[q].append((list(waits.items()), fn, key, 16))
        self._commit(q, waits, tok, reads, writes)
        return tok

    def barrier(self):
        for e in self.ENGS:
            waits = {}
            for k in self.sem_keys:
                if self.cnt[k] > 0 and not (k[0] == "c" and k[1] == e):
                    self._need(e, waits, (k, self.cnt[k]))
            for k, v in waits.items():
                self.seen[e][k] = v
            if waits:
                self.ops[e].append((list(waits.items()), None, None, 0))

    def emit(self):
        nc = self.nc
        with contextlib.ExitStack() as st:
            sems = {}
            for k in self.sem_keys:
                sems[k] = st.enter_context(nc.semaphore("s_" + "_".join(str(x) for x in k)))
            block = st.enter_context(nc.Block())
            final = [(k, v) for k, v in self.cnt.items() if v > 0]

            def run(engname):
                def body(eng):
                    for waits, fn, key, inc in self.ops[engname]:
                        for k, v in waits:
                            eng.wait_ge(sems[k], v)
                        if fn is None:
                            continue
                        if inc == 16:
                            fn(eng, lambda ins: ins.then_inc(sems[key], 16))
                        else:
                            fn(eng).then_inc(sems[key], 1)
                    if engname == "sp":
                        for k, v in final:
                            eng.wait_ge(sems[k], v)
                return body

            block.tensor(run("pe"))
            block.scalar(run("act"))
            block.vector(run("dve"))
            block.gpsimd(run("pool"))
            block.sync(run("sp"))


PARAMS = [("ln_in_g", (D,)), ("ln_in_b", (D,)), ("w_in", (2, D, 5400)), ("w_pool", (2, 4, 128, 128)),
          ("pool_scale", (2, 512)), ("ssm_lam_re", (2, 32, 64)), ("ssm_lam_im", (2, 32, 64)),
          ("ssm_log_dt", (2, 32)), ("ssm_b_re", (2, 32, 64, 16)), ("ssm_b_im", (2, 32, 64, 16)),
          ("ssm_c_re", (2, 32, 16, 64)), ("ssm_c_im", (2, 32, 16, 64)), ("ssm_d", (2, 32, 16)),
          ("w_glu", (2, 512, 512)), ("b_glu", (2, 512)), ("cmp_pe_k", (2, 32, 64)), ("cmp_pe_v", (2, 32, 64)),
          ("cmp_wk1", (2, 2048, 128)), ("cmp_wk2", (2, 128, 64)), ("cmp_wv1", (2, 2048, 128)),
          ("cmp_wv2", (2, 128, 64)), ("w_up_pool", (2, 512, D)), ("w_up_ssm", (2, 512, D)),
          ("w_up_nsa", (2, 512, D)), ("w_out", (2, D, D)), ("ln1_g", (2, D)), ("ln1_b", (2, D)),
          ("w_ff1", (2, D, 4096)), ("w_ff2", (2, 4096, D)), ("ln2_g", (2, D)), ("ln2_b", (2, D))]


def host_consts():
    c = {}
    c["c_ident"] = np.eye(128, dtype=np.float32)
    pos = np.arange(L, dtype=np.float32)
    inv_freq = (np.float32(500000.0) ** (-np.arange(0, 16, 2, dtype=np.float32) / np.float32(16))).astype(np.float32)
    ang = (pos[:, None] * inv_freq[None, :]).astype(np.float32)
    c["c_rope"] = np.concatenate([np.cos(ang), np.sin(ang)], axis=1).astype(np.float32)
    eb = np.zeros((64, 4096), np.float32)
    for s in range(64):
        eb[s, s * 64:(s + 1) * 64] = 1.0
    c["c_ebig"] = eb.astype(ml_dtypes.bfloat16)
    cs = np.arange(256)[:, None] * 16
    ss = np.arange(64)[None, :] * 64
    ov = np.maximum(np.minimum(cs + 32, ss + 64) - np.maximum(cs, ss), 0) / 16.0
    ov[255:] = 0.0
    c["c_ov"] = ov.astype(np.float32)
    ic = np.zeros((4, 16), np.float32)
    for gi, w in enumerate((2, 4, 8, 16)):
        ic[gi] = 1.0 / np.minimum(np.arange(16) + 1, w)
    c["c_invcnt"] = ic.reshape(1, 64)
    return c


class Builder:
    def __init__(self, dbg=(), stop_after=None, layers=DEPTH, inject=(), skip=()):
        self.inject = set(inject)
        self.skip = set(skip)
        self.nc = nc = bass.Bass("TRN2", target_bir_lowering=False)
        self.P = Prog(nc)
        self.dbg = set(dbg)
        self.stop_after = stop_after
        self.layers = layers
        self.d = {}
        self.d["x"] = nc.dram_tensor("x", [L, D], F32, kind="ExternalInput").ap()
        for n, shp in PARAMS:
            self.d[n] = nc.dram_tensor(n, list(shp), F32, kind="ExternalInput").ap()
        for n, a in host_consts().items():
            dt = BF16 if a.dtype == ml_dtypes.bfloat16 else F32
            self.d[n] = nc.dram_tensor(n, list(a.shape), dt, kind="ExternalInput").ap()
        self.d["out"] = nc.dram_tensor("out", [L, D], F32, kind="ExternalOutput").ap()
        self.scr = {}
        self.sres = {}
        for n, shp, dt in [("XR", [L, D], F32), ("XT", [D, L], BF16), ("UPT", [512, L], F32), ("UST", [512, L], F32),
                           ("QT", [128, 4, L], BF16), ("KCT", [128, L], BF16), ("KST", [128, L], BF16),
                           ("KWT", [128, L], BF16), ("VCT", [128, L], BF16), ("VS", [L, 128], BF16),
                           ("VW", [L, 128], BF16), ("GN", [L, 24], F32), ("YPT", [512, L], BF16),
                           ("YST", [512, L], BF16), ("YNT", [512, L], BF16), ("XR1", [L, D], F32),
                           ("X1T", [D, L], BF16)]:
            kind = "ExternalOutput" if n in self.dbg else ("ExternalInput" if n in self.inject else "Internal")
            self.scr[n] = nc.dram_tensor("s_" + n, shp, dt, kind=kind).ap()
            self.sres[n] = Res(n)
        self.ps = [nc.alloc_psum_tensor("psb%d" % i, [128, 512], F32) for i in range(8)]
        self.psr = [Res("ps%d" % i) for i in range(8)]
        self.cur = SB_BASE
        self.uid = 0
        self.ident = self.sb([128, 128], F32)
        self.Rconst = Res("const")
        self.P.dma("sp", lambda e, inc: inc(e.dma_start(out=self.ident[:], in_=self.d["c_ident"][:, :])), 1,
                   writes=[self.Rconst])
        self.persist = self.cur

    def sb(self, shape, dt):
        nbytes = int(np.prod(shape[1:])) * (2 if dt == BF16 else 4)
        off = (self.cur + 63) // 64 * 64
        assert off + nbytes <= SB_END, ("SBUF overflow", off + nbytes - SB_END)
        self.cur = off + nbytes
        self.uid += 1
        return self.nc.alloc_sbuf_tensor_at("t%d" % self.uid, list(shape), dt, offset=off)

    def phase_end(self):
        self.P.barrier()
        self.cur = self.persist

    def load_cast(self, pieces, res, stage_cols=2048):
        P = self.P
        if not hasattr(self, "_lc"):
            self._lc = None
        stg = [self.sb([128, stage_cols], F32) for _ in range(3)]
        rs = [Res("stg%d" % i) for i in range(3)]
        for i, pc in enumerate(pieces):
            dst, src, shp = pc[0], pc[1], pc[2]
            p0 = pc[3] if len(pc) > 3 else 0
            j = i % 3
            n = int(np.prod(shp[1:]))
            assert n <= stage_cols
            p = shp[0]
            sview = stg[j][p0:p0 + p, 0:n]
            if len(shp) == 3:
                sview = sview.rearrange("p (a b) -> p a b", b=shp[2])
            P.dma("sp", (lambda sv, s: lambda e, inc: inc(e.dma_start(out=sv, in_=s)))(sview, src), 1, writes=[rs[j]])
            eng = "pool" if i % 2 == 0 else "dve"
            P.op(eng, (lambda dd, sv: lambda e: e.tensor_copy(out=dd, in_=sv))(dst, sview), reads=[rs[j]], writes=[res])

    def bcast_load(self, dst, src_row, res):
        self.P.dma("sp", lambda e, inc: inc(e.dma_start(out=dst, in_=src_row.partition_broadcast(128))), 1, writes=[res])

    def ln_tile(self, i, src, Rsrc, g_rep, b_rep, Rgb, dst_res_d, Rres, dst_T_d, RT, bufs):
        P = self.P
        st6, mv, rstd, xT, RxT, Rst = bufs
        P.op("dve", lambda e: e.bn_stats(out=st6[:, 0, :], in_=src[:, 0:512]), reads=[Rsrc], writes=[Rst])
        P.op("dve", lambda e: e.bn_stats(out=st6[:, 1, :], in_=src[:, 512:1024]), reads=[Rsrc, Rst], writes=[Rst])
        P.op("dve", lambda e: e.bn_aggr(out=mv[:], in_=st6[:]), reads=[Rst], writes=[Rst])
        P.op("dve", lambda e: e.tensor_scalar_add(out=rstd[:], in0=mv[:, 1:2], scalar1=EPS), reads=[Rst], writes=[Rst])
        P.op("act", lambda e: e.sqrt(out=rstd[:], in_=rstd[:]), reads=[Rst], writes=[Rst])
        P.op("dve", lambda e: e.reciprocal(out=rstd[:], in_=rstd[:]), reads=[Rst], writes=[Rst])
        P.op("dve", lambda e: e.tensor_scalar(out=src[:], in0=src[:], scalar1=mv[:, 0:1], scalar2=rstd[:, 0:1],
                                              op0=ALU.subtract, op1=ALU.mult), reads=[Rst, Rsrc], writes=[Rsrc])
        P.op("pool", lambda e: e.tensor_tensor(out=src[:], in0=src[:], in1=g_rep[:], op=ALU.mult), reads=[Rsrc, Rgb],
             writes=[Rsrc])
        P.op("dve", lambda e: e.tensor_tensor(out=src[:], in0=src[:], in1=b_rep[:], op=ALU.add), reads=[Rsrc, Rgb],
             writes=[Rsrc])
        P.dma("sp", lambda e, inc: inc(e.dma_start(out=dst_res_d[i * 128:(i + 1) * 128, :], in_=src[:])), 1,
              reads=[Rsrc], writes=[Rres])
        if dst_T_d is None:
            return
        for half in range(2):
            pb = 6 + half
            ps = self.ps[pb]

            def tr(e, half=half, ps=ps):
                ins = None
                for q in range(4):
                    dt = half * 4 + q
                    ins = e.transpose(out=ps[:, q * 128:(q + 1) * 128], in_=src[:, dt * 128:(dt + 1) * 128],
                                      identity=self.ident[:])
                return ins
            P.op("pe", tr, reads=[Rsrc, self.Rconst], writes=[self.psr[pb]])
            P.op("act", lambda e, half=half, ps=ps: e.copy(out=xT[:, half * 4:(half + 1) * 4, :],
                                                         in_=ps[:].rearrange("p (a b) -> p a b", b=128)),
                 reads=[self.psr[pb]], writes=[RxT])
        P.dma("sp", lambda e, inc: inc(e.dma_start(
            out=dst_T_d.rearrange("(a p) t -> p a t", p=128)[:, :, i * 128:(i + 1) * 128], in_=xT[:])), 1,
            reads=[RxT], writes=[RT])

    def ln_bufs(self):
        return (self.sb([128, 2, 6], F32), self.sb([128, 2], F32), self.sb([128, 1], F32),
                self.sb([128, 8, 128], BF16), Res("xT"), Res("lnst"))

    def phase_ln_in(self):
        P = self.P
        g_rep = self.sb([128, D], F32)
        b_rep = self.sb([128, D], F32)
        Rgb = Res("gb")
        self.bcast_load(g_rep[:], self.d["ln_in_g"].unsqueeze(0), Rgb)
        self.bcast_load(b_rep[:], self.d["ln_in_b"].unsqueeze(0), Rgb)
        xb = [self.sb([128, D], F32) for _ in range(2)]
        Rx = [Res("x0"), Res("x1")]
        bufs = [self.ln_bufs() for _ in range(2)]
        for i in range(NT):
            j = i % 2
            P.dma("sp", lambda e, inc, i=i, j=j: inc(e.dma_start(out=xb[j][:], in_=self.d["x"][i * 128:(i + 1) * 128, :])),
                  1, writes=[Rx[j]])
            self.ln_tile(i, xb[j], Rx[j], g_rep, b_rep, Rgb, self.scr["XR"], self.sres["XR"], self.scr["XT"],
                         self.sres["XT"], bufs[j])
        self.phase_end()

    def phase_proj(self, l):
        P = self.P
        w_in = self.d["w_in"][l].rearrange("(a p) c -> p a c", p=128)
        NC_ = 1024 + 1304
        WP = self.sb([128, 8, NC_], BF16)
        RW = Res("WP")
        plan = [(0, 0, 1024)]
        qorder = (0, 4, 1, 5, 2, 6, 3, 7)
        for j, h in enumerate(qorder):
            plan.append((1024 + j * 64, 1024 + h * 64, 64))
        kv0 = 1536
        base = 1024 + 512
        for j, s in enumerate((0, 2, 4, 1)):
            plan.append((base + j * 128, kv0 + s * 128, 128))
        base += 512
        for j, s in enumerate((3, 5)):
            plan.append((base + j * 128, kv0 + s * 128, 128))
        plan.append((base + 256, 2304, 24))
        pieces = []
        for dc, sc, n in plan:
            o = 0
            while o < n:
                m = min(256, n - o)
                pieces.append((WP[:, :, dc + o:dc + o + m], w_in[:, :, sc + o:sc + o + m], [128, 8, m]))
                o += m
        self.load_cast(pieces, RW)
        rope = self.sb([128, NT, 16], F32)
        Rrope = Res("rope")
        P.dma("sp", lambda e, inc: inc(e.dma_start(out=rope[:], in_=self.d["c_rope"].rearrange("(a p) c -> p a c", p=128))),
              1, writes=[Rrope])
        xt = [self.sb([128, 8, 512], BF16) for _ in range(2)]
        Rxt = [Res("xt0"), Res("xt1")]
        fst = [self.sb([128, 512], F32) for _ in range(2)]
        Rfst = [Res("f0"), Res("f1")]
        tm = [self.sb([128, 1304], F32) for _ in range(2)]
        Rtm = [Res("tm0"), Res("tm1")]
        t1 = self.sb([128, 14, 8], F32)
        t2 = self.sb([128, 14, 8], F32)
        t3 = self.sb([128, 14, 8], F32)
        Rt = Res("ropetmp")
        trs = [self.sb([128, 8, 128], BF16) for _ in range(2)]
        Rtrs = [Res("trs0"), Res("trs1")]
        vst = [self.sb([128, 256], BF16) for _ in range(2)]
        Rvst = [Res("vst0"), Res("vst1")]
        XTd = self.scr["XT"].rearrange("(a p) t -> p a t", p=128)
        S = self.scr
        SR = self.sres
        kcnt = 0
        for c in range(8):
            j = c % 2
            P.dma("sp", lambda e, inc, c=c, j=j: inc(e.dma_start(out=xt[j][:], in_=XTd[:, :, c * 512:(c + 1) * 512])), 1,
                  reads=[SR["XT"]], writes=[Rxt[j]])
            for ft in range(8):
                pb = ft % 2
                ps = self.ps[pb]

                def mm(e, ft=ft, ps=ps, j=j):
                    ins = None
                    for a in range(8):
                        ins = e.matmul(ps[:], lhsT=WP[:, a, ft * 128:(ft + 1) * 128], rhs=xt[j][:, a, :], start=(a == 0),
                                       stop=(a == 7))
                    return ins
                P.op("pe", mm, reads=[RW, Rxt[j]], writes=[self.psr[pb]])
                fj = ft % 2
                P.op("act", lambda e, ps=ps, fj=fj: e.copy(out=fst[fj][:], in_=ps[:]), reads=[self.psr[pb]],
                     writes=[Rfst[fj]])
                dst = (S["UPT"] if ft < 4 else S["UST"])
                dres = SR["UPT"] if ft < 4 else SR["UST"]
                r0 = (ft % 4) * 128
                P.dma("sp", lambda e, inc, dst=dst, r0=r0, c=c, fj=fj: inc(e.dma_start(
                    out=dst[r0:r0 + 128, c * 512:(c + 1) * 512], in_=fst[fj][:])), 1, reads=[Rfst[fj]], writes=[dres])
            for tt in range(4):
                i = c * 4 + tt
                tj = i % 2
                T = tm[tj]
                for ch, (c0, n) in enumerate(((0, 512), (512, 512), (1024, 280))):
                    pb = 2 + (kcnt % 3)
                    kcnt += 1
                    ps = self.ps[pb]

                    def mm2(e, ps=ps, j=j, tt=tt, c0=c0, n=n):
                        ins = None
                        for a in range(8):
                            ins = e.matmul(ps[:, 0:n], lhsT=xt[j][:, a, tt * 128:(tt + 1) * 128],
                                           rhs=WP[:, a, 1024 + c0:1024 + c0 + n], start=(a == 0), stop=(a == 7))
                        return ins
                    P.op("pe", mm2, reads=[RW, Rxt[j]], writes=[self.psr[pb]])
                    P.op("act", lambda e, ps=ps, T=T, c0=c0, n=n: e.copy(out=T[:, c0:c0 + n], in_=ps[:, 0:n]),
                         reads=[self.psr[pb]], writes=[Rtm[tj]])
                H3 = T[:, 0:896].rearrange("p (h d) -> p h d", d=64)
                x1 = H3[:, :, 0:8]
                x2 = H3[:, :, 8:16]
                cosb = rope[:, i, 0:8].unsqueeze(1).to_broadcast([128, 14, 8])
                sinb = rope[:, i, 8:16].unsqueeze(1).to_broadcast([128, 14, 8])
                P.op("dve", lambda e, x1=x1, sinb=sinb: e.tensor_tensor(out=t1[:], in0=x1, in1=sinb, op=ALU.mult),
                     reads=[Rtm[tj], Rrope], writes=[Rt])
                P.op("dve", lambda e, x2=x2, sinb=sinb: e.tensor_tensor(out=t2[:], in0=x2, in1=sinb, op=ALU.mult),
                     reads=[Rtm[tj], Rrope, Rt], writes=[Rt])
                P.op("dve", lambda e, x1=x1, cosb=cosb: e.tensor_tensor(out=x1, in0=x1, in1=cosb, op=ALU.mult),
                     reads=[Rtm[tj], Rrope, Rt], writes=[Rtm[tj]])
                P.op("dve", lambda e, x2=x2, cosb=cosb: e.tensor_tensor(out=t3[:], in0=x2, in1=cosb, op=ALU.mult),
                     reads=[Rtm[tj], Rrope, Rt], writes=[Rt])
                P.op("dve", lambda e, x1=x1: e.tensor_tensor(out=x1, in0=x1, in1=t2[:], op=ALU.subtract),
                     reads=[Rtm[tj], Rt], writes=[Rtm[tj]])
                P.op("dve", lambda e, x2=x2: e.tensor_tensor(out=x2, in0=t1[:], in1=t3[:], op=ALU.add),
                     reads=[Rtm[tj], Rt], writes=[Rtm[tj]])
                for half in range(2):
                    pb = 6 + half
                    ps = self.ps[pb]

                    def tr(e, half=half, ps=ps, T=T):
                        ins = None
                        for q in range(4):
                            a = half * 4 + q
                            ins = e.transpose(out=ps[:, q * 128:(q + 1) * 128], in_=T[:, a * 128:(a + 1) * 128],
                                              identity=self.ident[:])
                        return ins
                    P.op("pe", tr, reads=[Rtm[tj], self.Rconst], writes=[self.psr[pb]])
                    P.op("act", lambda e, half=half, ps=ps, tj=tj: e.copy(
                        out=trs[tj][:, half * 4:(half + 1) * 4, :], in_=ps[:].rearrange("p (a b) -> p a b", b=128)),
                        reads=[self.psr[pb]], writes=[Rtrs[tj]])
                t0 = i * 128
                P.dma("sp", lambda e, inc, tj=tj, t0=t0: inc(e.dma_start(out=S["QT"][:, :, t0:t0 + 128],
                                                                       in_=trs[tj][:, 0:4, :])), 1,
                      reads=[Rtrs[tj]], writes=[SR["QT"]])
                for a, nm in ((4, "KCT"), (5, "KST"), (6, "KWT"), (7, "VCT")):
                    P.dma("sp", lambda e, inc, tj=tj, t0=t0, a=a, nm=nm: inc(e.dma_start(
                        out=S[nm][:, t0:t0 + 128], in_=trs[tj][:, a, :])), 1, reads=[Rtrs[tj]], writes=[SR[nm]])
                P.op("pool", lambda e, tj=tj, T=T: e.tensor_copy(out=vst[tj][:], in_=T[:, 1024:1280]), reads=[Rtm[tj]],
                     writes=[Rvst[tj]])
                P.dma("sp", lambda e, inc, tj=tj, t0=t0: inc(e.dma_start(out=S["VS"][t0:t0 + 128, :], in_=vst[tj][:, 0:128])),
                      1, reads=[Rvst[tj]], writes=[SR["VS"]])
                P.dma("sp", lambda e, inc, tj=tj, t0=t0: inc(e.dma_start(out=S["VW"][t0:t0 + 128, :], in_=vst[tj][:, 128:256])),
                      1, reads=[Rvst[tj]], writes=[SR["VW"]])
                P.op("act", lambda e, T=T: e.activation(out=T[:, 1280:1304], in_=T[:, 1280:1304], func=AF.Sigmoid),
                     reads=[Rtm[tj]], writes=[Rtm[tj]])
                P.dma("sp", lambda e, inc, T=T, t0=t0: inc(e.dma_start(out=S["GN"][t0:t0 + 128, :], in_=T[:, 1280:1304])),
                      1, reads=[Rtm[tj]], writes=[SR["GN"]])
        self.phase_end()

    def phase_pool(self, l):
        P = self.P
        S, SR = self.scr, self.sres
        wp = self.sb([128, 4, 128], BF16)
        Rwp = Res("wpool")
        self.load_cast([(wp[:, g, :], self.d["w_pool"][l, g], [128, 128]) for g in range(4)], Rwp, stage_cols=128)
        psc = self.sb([128, 4], F32)
        Rpsc = Res("psc")
        P.dma("sp", lambda e, inc: inc(e.dma_start(out=psc[:], in_=self.d["pool_scale"][l].rearrange("(g p) -> p g", p=128),
                                                       allow_slow_non_contiguous=True)),
              1, writes=[Rpsc])
        icn = self.sb([128, 64], F32)
        P.dma("sp", lambda e, inc: inc(e.dma_start(out=icn[:], in_=self.d["c_invcnt"][0:1, :].partition_broadcast(128))),
              1, writes=[Rpsc])
        H = 16
        U = [self.sb([128, H + L], F32) for _ in range(2)]
        A = [self.sb([128, H + L], F32) for _ in range(2)]
        Bb = [self.sb([128, H + L], F32) for _ in range(2)]
        Z = [self.sb([128, L], BF16) for _ in range(2)]
        tmp16 = [self.sb([128, 16], F32) for _ in range(2)]
        yst = [self.sb([128, 512], BF16) for _ in range(2)]
        Ryst = [Res("y0"), Res("y1")]
        RU = [Res("U0"), Res("U1")]
        for j in range(2):
            eng = "dve" if j == 0 else "pool"
            for buf in (U[j], A[j], Bb[j]):
                P.op(eng, lambda e, buf=buf: e.memset(buf[:, 0:H], 0.0), writes=[RU[j]])
        for g in range(4):
            j = g % 2
            eng = "dve" if j == 0 else "pool"
            w = 2 << g
            P.dma("sp", lambda e, inc, g=g, j=j: inc(e.dma_start(out=U[j][:, H:], in_=S["UPT"][g * 128:(g + 1) * 128, :])), 1,
                  reads=[SR["UPT"]], writes=[RU[j]])
            src = U[j]
            dsts = [A[j], Bb[j]]
            for k in range(g + 1):
                sh = 1 << k
                dst = dsts[k % 2]
                P.op(eng, lambda e, dst=dst, src=src, sh=sh: e.tensor_tensor(out=dst[:, H:], in0=src[:, H:],
                                                                              in1=src[:, H - sh:H + L - sh], op=ALU.add),
                     reads=[RU[j]], writes=[RU[j]])
                src = dst
            P.op("dve", lambda e, src=src, j=j, w=w: e.scalar_tensor_tensor(out=Z[j][:], in0=src[:, H:], scalar=1.0 / w,
                                                                          in1=U[j][:, H:], op0=ALU.mult,
                                                                          op1=ALU.subtract), reads=[RU[j]], writes=[RU[j]])
            P.op(eng, lambda e, src=src, j=j, g=g: e.tensor_tensor(out=tmp16[j][:], in0=src[:, H:H + 16], in1=icn[:, g * 16:(g + 1) * 16],
                                                                   op=ALU.mult), reads=[RU[j], Rpsc], writes=[RU[j]])
            P.op(eng, lambda e, j=j: e.tensor_tensor(out=Z[j][:, 0:16], in0=tmp16[j][:], in1=U[j][:, H:H + 16],
                                                     op=ALU.subtract), reads=[RU[j]], writes=[RU[j]])
            for c in range(8):
                pb = c % 2
                ps = self.ps[pb]
                P.op("pe", lambda e, ps=ps, g=g, j=j, c=c: e.matmul(ps[:], lhsT=wp[:, g, :], rhs=Z[j][:, c * 512:(c + 1) * 512],
                                                                  start=True, stop=True), reads=[Rwp, RU[j]],
                     writes=[self.psr[pb]])
                yj = c % 2
                P.op("act", lambda e, ps=ps, yj=yj, g=g: e.activation(out=yst[yj][:], in_=ps[:], func=AF.Copy,
                                                                     scale=psc[:, g:g + 1]), reads=[self.psr[pb], Rpsc],
                     writes=[Ryst[yj]])
                P.dma("sp", lambda e, inc, g=g, c=c, yj=yj: inc(e.dma_start(
                    out=S["YPT"][g * 128:(g + 1) * 128, c * 512:(c + 1) * 512], in_=yst[yj][:])), 1, reads=[Ryst[yj]],
                    writes=[SR["YPT"]])
        self.phase_end()

    def cmul(self, eng, outr, outi, ar, ai, br, bi, t1, t2, Rin, Rout, Rt):
        P = self.P
        tt = lambda o, a, b, op: (lambda e: e.tensor_tensor(out=o, in0=a, in1=b, op=op))
        P.op(eng, tt(t1, ar, br, ALU.mult), reads=Rin, writes=[Rt])
        P.op(eng, tt(t2, ai, bi, ALU.mult), reads=Rin + [Rt], writes=[Rt])
        P.op(eng, tt(outr, t1, t2, ALU.subtract), reads=[Rt], writes=[Rout])
        P.op(eng, tt(t1, ar, bi, ALU.mult), reads=Rin + [Rt, Rout], writes=[Rt])
        P.op(eng, tt(t2, ai, br, ALU.mult), reads=Rin + [Rt], writes=[Rt])
        P.op(eng, tt(outi, t1, t2, ALU.add), reads=[Rt], writes=[Rout])

    def phase_s5(self, l):
        P = self.P
        S, SR = self.scr, self.sres
        T = 512
        sm = lambda: self.sb([128, 16], F32)
        lam_n = self.sb([16, 256], F32)
        Rp = Res("s5prep")
        P.dma("sp", lambda e, inc: inc(e.dma_start(out=lam_n[:, 0:128], in_=self.d["ssm_lam_re"][l].rearrange("(j g) p -> j (g p)", g=2))),
              1, writes=[Rp])
        P.dma("sp", lambda e, inc: inc(e.dma_start(out=lam_n[:, 128:256], in_=self.d["ssm_lam_im"][l].rearrange("(j g) p -> j (g p)", g=2))),
              1, writes=[Rp])
        lr, li, stp, rho, th, sn, cs, t1, t2, t3, kr, ki, nki, den = [sm() for _ in range(14)]
        P.op("pe", lambda e: (e.transpose(out=self.ps[7][:, 0:16], in_=lam_n[:, 0:128], identity=self.ident[0:16, 0:16]),
                              e.transpose(out=self.ps[7][:, 16:32], in_=lam_n[:, 128:256], identity=self.ident[0:16, 0:16]))[1],
             reads=[Rp, self.Rconst], writes=[self.psr[7]])
        P.op("dve", lambda e: e.tensor_copy(out=lr[:], in_=self.ps[7][:, 0:16]), reads=[self.psr[7]], writes=[Rp])
        P.op("dve", lambda e: e.tensor_copy(out=li[:], in_=self.ps[7][:, 16:32]), reads=[self.psr[7], Rp], writes=[Rp])
        ldt = self.d["ssm_log_dt"][l:l + 1, :].rearrange("o (j g) -> o j g", g=2)
        for gl in range(2):
            P.dma("sp", lambda e, inc, gl=gl: inc(e.dma_start(out=stp[gl * 64:(gl + 1) * 64, :],
                                                            in_=ldt[:, :, gl].partition_broadcast(64),
                                                            allow_slow_non_contiguous=True)), 1, reads=[Rp], writes=[Rp])
        P.op("act", lambda e: e.activation(out=stp[:], in_=stp[:], func=AF.Exp), reads=[Rp], writes=[Rp])
        V = lambda fn: P.op("dve", fn, reads=[Rp], writes=[Rp])
        V(lambda e: e.tensor_tensor(out=t1[:], in0=lr[:], in1=stp[:], op=ALU.mult))
        P.op("act", lambda e: e.activation(out=rho[:], in_=t1[:], func=AF.Exp), reads=[Rp], writes=[Rp])
        V(lambda e: e.tensor_tensor(out=th[:], in0=li[:], in1=stp[:], op=ALU.mult))
        TWO_PI = 2.0 * math.pi
        ni = self.sb([128, 16], mybir.dt.int32)
        nf = sm()

        def reduce_turns(dst, shift):
            V(lambda e: e.tensor_scalar(out=dst[:], in0=th[:], scalar1=1.0 / TWO_PI, scalar2=shift, op0=ALU.mult, op1=ALU.add))
            V(lambda e: e.tensor_copy(out=ni[:], in_=dst[:]))
            V(lambda e: e.tensor_copy(out=nf[:], in_=ni[:]))
            V(lambda e: e.tensor_tensor(out=dst[:], in0=dst[:], in1=nf[:], op=ALU.subtract))
            V(lambda e: e.tensor_single_scalar(out=nf[:], in_=dst[:], scalar=0.5, op=ALU.is_gt))
            V(lambda e: e.tensor_tensor(out=dst[:], in0=dst[:], in1=nf[:], op=ALU.subtract))
            V(lambda e: e.tensor_single_scalar(out=nf[:], in_=dst[:], scalar=-0.5, op=ALU.is_lt))
            V(lambda e: e.tensor_tensor(out=dst[:], in0=dst[:], in1=nf[:], op=ALU.add))
        reduce_turns(t1, 0.0)
        reduce_turns(t2, 0.25)
        P.op("act", lambda e: e.activation(out=sn[:], in_=t1[:], func=AF.Sin, scale=TWO_PI), reads=[Rp], writes=[Rp])
        P.op("act", lambda e: e.activation(out=cs[:], in_=t2[:], func=AF.Sin, scale=TWO_PI), reads=[Rp], writes=[Rp])
        V(lambda e: e.tensor_tensor(out=t1[:], in0=rho[:], in1=cs[:], op=ALU.mult))
        V(lambda e: e.tensor_scalar_add(out=t1[:], in0=t1[:], scalar1=-1.0))
        V(lambda e: e.tensor_tensor(out=t2[:], in0=rho[:], in1=sn[:], op=ALU.mult))
        V(lambda e: e.tensor_tensor(out=den[:], in0=lr[:], in1=lr[:], op=ALU.mult))
        V(lambda e: e.tensor_tensor(out=t3[:], in0=li[:], in1=li[:], op=ALU.mult))
        V(lambda e: e.tensor_tensor(out=den[:], in0=den[:], in1=t3[:], op=ALU.add))
        V(lambda e: e.reciprocal(out=den[:], in_=den[:]))
        V(lambda e: e.tensor_tensor(out=kr[:], in0=t1[:], in1=lr[:], op=ALU.mult))
        V(lambda e: e.tensor_tensor(out=t3[:], in0=t2[:], in1=li[:], op=ALU.mult))
        V(lambda e: e.tensor_tensor(out=kr[:], in0=kr[:], in1=t3[:], op=ALU.add))
        V(lambda e: e.tensor_tensor(out=kr[:], in0=kr[:], in1=den[:], op=ALU.mult))
        V(lambda e: e.tensor_tensor(out=ki[:], in0=t2[:], in1=lr[:], op=ALU.mult))
        V(lambda e: e.tensor_tensor(out=t3[:], in0=t1[:], in1=li[:], op=ALU.mult))
        V(lambda e: e.tensor_tensor(out=ki[:], in0=ki[:], in1=t3[:], op=ALU.subtract))
        V(lambda e: e.tensor_tensor(out=ki[:], in0=ki[:], in1=den[:], op=ALU.mult))
        V(lambda e: e.tensor_scalar_mul(out=nki[:], in0=ki[:], scalar1=-1.0))
        Er = self.sb([128, 16, T], F32)
        Ei = self.sb([128, 16, T], F32)
        ETr, ETi, emr, emi = sm(), sm(), sm(), sm()
        RE = Res("E")
        Rt = Res("ctmp")
        P.op("pool", lambda e: e.memset(Er[:, :, 0:1], 1.0), writes=[RE])
        P.op("pool", lambda e: e.memset(Ei[:, :, 0:1], 0.0), reads=[RE], writes=[RE])
        P.op("pool", lambda e: e.tensor_copy(out=Er[:, :, 1:2], in_=cs[:].unsqueeze(2)), reads=[Rp, RE], writes=[RE])
        P.op("pool", lambda e: e.tensor_copy(out=Ei[:, :, 1:2], in_=sn[:].unsqueeze(2)), reads=[Rp, RE], writes=[RE])
        LB = self.sb([128, 16, 2, 128], BF16)
        LC = self.sb([128, 16, 2, 128], BF16)
        RL = Res("LBC")
        ct = self.sb([128, 128], F32)
        dsk = self.sb([128, 4], F32)
        bgl = self.sb([128, 4], F32)
        Wg = self.sb([128, 4, 512], BF16)
        RWg = Res("wglu")
        wgl = self.d["w_glu"][l].rearrange("(a p) c -> p a c", p=128)
        self.load_cast([(Wg[:, :, 0:256], wgl[:, :, 0:256], [128, 4, 256]), (Wg[:, :, 256:512], wgl[:, :, 256:512], [128, 4, 256])],
                       RWg, stage_cols=1024)
        mark = self.cur
        big1 = self.sb([128, 16, 256], F32)
        big2 = self.sb([128, 16, 256], F32)
        m = 2
        while m < T:
            self.cmul("dve", emr[:], emi[:], Er[:, :, m - 1], Ei[:, :, m - 1], cs[:], sn[:], t1[:], t2[:], [RE, Rp], RE, Rt)
            bc = lambda a: a[:].unsqueeze(2).to_broadcast([128, 16, m])
            self.cmul("dve", Er[:, :, m:2 * m], Ei[:, :, m:2 * m], Er[:, :, 0:m], Ei[:, :, 0:m], bc(emr), bc(emi),
                      big1[:, :, 0:m], big2[:, :, 0:m], [RE], RE, Rt)
            m *= 2
        self.cmul("dve", ETr[:], ETi[:], Er[:, :, T - 1], Ei[:, :, T - 1], cs[:], sn[:], t1[:], t2[:], [RE, Rp], RE, Rt)
        Bp = [self.sb([128, 16, 128], F32) for _ in range(2)]
        Cp = [self.sb([128, 16, 128], F32) for _ in range(2)]
        Rpad = Res("pads")
        for t_ in Bp + Cp:
            P.op("pool", lambda e, t_=t_: e.memset(t_[:], 0.0), writes=[Rpad])
        for ri, (bn, cn) in enumerate((("ssm_b_re", "ssm_c_re"), ("ssm_b_im", "ssm_c_im"))):
            bsrc = self.d[bn][l].rearrange("(a q g) p c -> g q p a c", q=4, g=2)
            csrc = self.d[cn][l].rearrange("(a q g) c p -> g q c a p", q=4, g=2)
            for gl in range(2):
                for q in range(4):
                    P.dma("sp", lambda e, inc, ri=ri, gl=gl, q=q, bsrc=bsrc: inc(e.dma_start(
                        out=Bp[ri][gl * 64:(gl + 1) * 64, q:16:4, q * 32 + gl * 16:q * 32 + gl * 16 + 16], in_=bsrc[gl, q])), 1,
                        reads=[Rpad], writes=[Rpad])
                    P.dma("sp", lambda e, inc, ri=ri, gl=gl, q=q, csrc=csrc: inc(e.dma_start(
                        out=Cp[ri][q * 32 + gl * 16:q * 32 + gl * 16 + 16, q:16:4, gl * 64:(gl + 1) * 64], in_=csrc[gl, q])), 1,
                        reads=[Rpad], writes=[Rpad])
        for j in range(16):
            pb = 6 + j % 2
            ps = self.ps[pb]

            def tr(e, j=j, ps=ps):
                ins = None
                for q, src in enumerate((Bp[0], Bp[1], Cp[0], Cp[1])):
                    ins = e.transpose(out=ps[:, q * 128:(q + 1) * 128], in_=src[:, j, :], identity=self.ident[:])
                return ins
            P.op("pe", tr, reads=[Rpad, self.Rconst], writes=[self.psr[pb]])
            P.op("act", lambda e, j=j, ps=ps: e.copy(out=LB[:, j, :, :], in_=ps[:, 0:256].rearrange("p (a b) -> p a b", b=128)),
                 reads=[self.psr[pb]], writes=[RL])
            P.op("dve", lambda e, j=j, ps=ps: e.tensor_scalar(out=ct[:], in0=ps[:, 384:512], scalar1=ki[:, j:j + 1], scalar2=None,
                                                             op0=ALU.mult), reads=[self.psr[pb], Rp, RL], writes=[Rt])
            P.op("dve", lambda e, j=j, ps=ps: e.scalar_tensor_tensor(out=LC[:, j, 0, :], in0=ps[:, 256:384], scalar=kr[:, j:j + 1],
                                                                    in1=ct[:], op0=ALU.mult, op1=ALU.subtract),
                 reads=[self.psr[pb], Rp, Rt], writes=[RL])
            P.op("dve", lambda e, j=j, ps=ps: e.tensor_scalar(out=ct[:], in0=ps[:, 384:512], scalar1=kr[:, j:j + 1], scalar2=None,
                                                             op0=ALU.mult), reads=[self.psr[pb], Rp, RL], writes=[Rt])
            P.op("dve", lambda e, j=j, ps=ps: e.scalar_tensor_tensor(out=LC[:, j, 1, :], in0=ps[:, 256:384], scalar=nki[:, j:j + 1],
                                                                    in1=ct[:], op0=ALU.mult, op1=ALU.subtract),
                 reads=[self.psr[pb], Rp, Rt], writes=[RL])
        P.dma("sp", lambda e, inc: inc(e.dma_start(out=dsk[:], in_=self.d["ssm_d"][l].rearrange("g c -> (g c)").rearrange("(a p) -> p a", p=128),
                                                   allow_slow_non_contiguous=True)), 1, writes=[Rp])
        P.dma("sp", lambda e, inc: inc(e.dma_start(out=bgl[:], in_=self.d["b_glu"][l].rearrange("(a p) -> p a", p=128),
                                                   allow_slow_non_contiguous=True)), 1, writes=[Rp])
        self.P.barrier()
        self.cur = mark
        uf = [self.sb([128, 4, T], F32) for _ in range(2)]
        ub = [self.sb([128, 4, T], BF16) for _ in range(2)]
        Ruf = [Res("uf0"), Res("uf1")]
        Rub = [Res("ub0"), Res("ub1")]
        NB = 3
        m1 = [self.sb([128, T], F32) for _ in range(NB)]
        m2 = [self.sb([128, T], F32) for _ in range(NB)]
        m3 = [self.sb([128, T], F32) for _ in range(NB)]
        m4 = [self.sb([128, T], F32) for _ in range(NB)]
        gr = [self.sb([128, T], F32) for _ in range(NB)]
        gi = [self.sb([128, T], F32) for _ in range(NB)]
        Hr = [self.sb([128, T], BF16) for _ in range(NB)]
        Hi = [self.sb([128, T], BF16) for _ in range(NB)]
        Rm4 = [[Res("m%d_%d" % (i, q)) for q in range(4)] for i in range(NB)]
        Rgs = [[Res("gr%d" % i), Res("gi%d" % i)] for i in range(NB)]
        RHi = [Res("Hi%d" % i) for i in range(NB)]
        Rgl2 = Res("glast2")
        Rg = [Res("g%d" % i) for i in range(NB)]
        RH = [Res("H%d" % i) for i in range(NB)]
        glr, gli, gir, gii = sm(), sm(), sm(), sm()
        Rgl = Res("glast")
        Rgi = Res("ginit")
        P.op("pool", lambda e: e.memset(gir[:], 0.0), writes=[Rgi])
        P.op("pool", lambda e: e.memset(gii[:], 0.0), reads=[Rgi], writes=[Rgi])
        ysb = self.sb([128, T], F32)
        Rysb = Res("ysb")
        zf = self.sb([128, 4, T], F32)
        zb = self.sb([128, 4, T], BF16)
        Rzf, Rzb = Res("zf"), Res("zb")
        sgl = self.sb([128, T], F32)
        Rsgl = Res("sgl")
        ost = [self.sb([128, T], BF16) for _ in range(2)]
        Rost = [Res("ost0"), Res("ost1")]
        USTd = S["UST"].rearrange("(a p) t -> p a t", p=128)
        hm1, hm2, hm3, hm4 = m1, m2, m3, m4
        state = {"n": 0}

        def load_chunk(c):
            cj = c % 2
            P.dma("sp", lambda e, inc, c=c, cj=cj: inc(e.dma_start(out=uf[cj][:], in_=USTd[:, :, c * T:(c + 1) * T])), 1,
                  reads=[SR["UST"]], writes=[Ruf[cj]])
            P.op("pool", lambda e, cj=cj: e.tensor_copy(out=ub[cj][:], in_=uf[cj][:]), reads=[Ruf[cj]], writes=[Rub[cj]])

        def stageA(c, j):
            cj = c % 2
            kt = j // 4
            n = state["n"]
            state["n"] += 1
            b_ = n % NB
            pa, pbk = ((0, 1), (2, 3))[n % 2]
            P.op("pe", lambda e, j=j, kt=kt, cj=cj, pa=pa: e.matmul(self.ps[pa][:], lhsT=LB[:, j, 0, :], rhs=ub[cj][:, kt, :],
                                                                 start=True, stop=True), reads=[RL, Rub[cj]], writes=[self.psr[pa]])
            P.op("pe", lambda e, j=j, kt=kt, cj=cj, pbk=pbk: e.matmul(self.ps[pbk][:], lhsT=LB[:, j, 1, :], rhs=ub[cj][:, kt, :],
                                                                   start=True, stop=True), reads=[RL, Rub[cj]],
                 writes=[self.psr[pbk]])
            DV = lambda o, a, bb, op, rd, wr: P.op("dve", lambda e: e.tensor_tensor(out=o, in0=a, in1=bb, op=op), reads=rd, writes=wr)
            Pr_, Pi_ = self.ps[pa][:], self.ps[pbk][:]
            R1, R2, R3, R4 = Rm4[b_]
            DV(m3[b_][:], Pi_, Er[:, j, :], ALU.mult, [RE, self.psr[pbk]], [R3])
            DV(m4[b_][:], Pr_, Ei[:, j, :], ALU.mult, [RE, self.psr[pa]], [R4])
            DV(m1[b_][:], Pr_, Er[:, j, :], ALU.mult, [RE, self.psr[pa]], [R1])
            DV(m2[b_][:], Pi_, Ei[:, j, :], ALU.mult, [RE, self.psr[pbk]], [R2])
            DV(m3[b_][:], m3[b_][:], m4[b_][:], ALU.subtract, [R4], [R3])
            DV(m1[b_][:], m1[b_][:], m2[b_][:], ALU.add, [R2], [R1])
            P.op("dve", lambda e, b_=b_, j=j: e.tensor_tensor_scan(
                out=gi[b_][:], data0=rho[:, j:j + 1].to_broadcast([128, T]), data1=m3[b_][:], initial=gii[:, j:j + 1],
                op0=ALU.mult, op1=ALU.add), reads=[R3, Rp, Rgi], writes=[Rgs[b_][1]])
            P.op("dve", lambda e, b_=b_, j=j: e.tensor_tensor_scan(
                out=gr[b_][:], data0=rho[:, j:j + 1].to_broadcast([128, T]), data1=m1[b_][:], initial=gir[:, j:j + 1],
                op0=ALU.mult, op1=ALU.add), reads=[R1, Rp, Rgi], writes=[Rgs[b_][0]])
            return b_

        def stageB(c, j, b_):
            cj = c % 2
            kt = j // 4
            R1, R2, R3, R4 = Rm4[b_]
            Rgr, Rgi_ = Rgs[b_]
            P.op("pool", lambda e, b_=b_, j=j: e.tensor_copy(out=glr[:, j:j + 1], in_=gr[b_][:, T - 1:T]), reads=[Rgr], writes=[Rgl])
            P.op("pool", lambda e, b_=b_, j=j: e.tensor_copy(out=gli[:, j:j + 1], in_=gi[b_][:, T - 1:T]), reads=[Rgi_], writes=[Rgl2])
            PL = lambda o, a, bb, rd, wr: P.op("pool", lambda e: e.tensor_tensor(out=o, in0=a, in1=bb, op=ALU.mult), reads=rd, writes=wr)
            PL(hm2[b_][:], Ei[:, j, :], gi[b_][:], [RE, Rgi_], [R2])
            PL(hm3[b_][:], Er[:, j, :], gi[b_][:], [RE, Rgi_], [R3])
            PL(hm1[b_][:], Er[:, j, :], gr[b_][:], [RE, Rgr], [R1])
            PL(hm4[b_][:], Ei[:, j, :], gr[b_][:], [RE, Rgr], [R4])
            P.op("dve", lambda e, b_=b_: e.tensor_tensor(out=Hr[b_][:], in0=hm1[b_][:], in1=hm2[b_][:], op=ALU.subtract),
                 reads=[R1, R2], writes=[RH[b_]])
            P.op("dve", lambda e, b_=b_: e.tensor_tensor(out=Hi[b_][:], in0=hm3[b_][:], in1=hm4[b_][:], op=ALU.add),
                 reads=[R3, R4], writes=[RHi[b_]])
            py = 4 + kt % 2

            def mmy(e, j=j, b_=b_, py=py):
                e.matmul(self.ps[py][:], lhsT=LC[:, j, 0, :], rhs=Hr[b_][:], start=(j % 4 == 0), stop=False)
                return e.matmul(self.ps[py][:], lhsT=LC[:, j, 1, :], rhs=Hi[b_][:], start=False, stop=(j % 4 == 3))
            P.op("pe", mmy, reads=[RL, RH[b_], RHi[b_]], writes=[self.psr[py]], self_ok=(j % 4 != 0))
            if j % 4 == 3:
                P.op("dve", lambda e, kt=kt, cj=cj, py=py: e.scalar_tensor_tensor(
                    out=ysb[:], in0=uf[cj][:, kt, :], scalar=dsk[:, kt:kt + 1], in1=self.ps[py][:], op0=ALU.mult, op1=ALU.add),
                    reads=[self.psr[py], Ruf[cj], Rp], writes=[Rysb])
                P.op("act", lambda e, kt=kt: e.activation(out=zf[:, kt, :], in_=ysb[:], func=AF.Gelu_apprx_tanh),
                     reads=[Rysb], writes=[Rzf])
                P.op("act", lambda e, kt=kt: e.copy(out=zb[:, kt, :], in_=zf[:, kt, :]), reads=[Rzf], writes=[Rzb])

        load_chunk(0)
        for c in range(L // T):
            if c + 1 < L // T:
                load_chunk(c + 1)
            prev = None
            for j in range(16):
                b_ = stageA(c, j)
                if prev is not None:
                    stageB(c, prev[0], prev[1])
                prev = (j, b_)
            stageB(c, prev[0], prev[1])
            self.cmul("dve", gir[:], gii[:], ETr[:], ETi[:], glr[:], gli[:], t1[:], t2[:], [RE, Rgl, Rgl2], Rgi, Rt)
            for ot in range(4):
                def mmg(e, ot=ot):
                    ins = None
                    for a in range(4):
                        ins = e.matmul(self.ps[6][:], lhsT=Wg[:, a, ot * 128:(ot + 1) * 128], rhs=zb[:, a, :], start=(a == 0),
                                       stop=(a == 3))
                    return ins
                P.op("pe", mmg, reads=[RWg, Rzb], writes=[self.psr[6]])
                P.op("act", lambda e, ot=ot: e.activation(out=sgl[:], in_=self.ps[6][:], func=AF.Sigmoid, bias=bgl[:, ot:ot + 1]),
                     reads=[self.psr[6], Rp], writes=[Rsgl])
                oj = ot % 2
                P.op("dve", lambda e, ot=ot, oj=oj: e.tensor_tensor(out=ost[oj][:], in0=zf[:, ot, :], in1=sgl[:], op=ALU.mult),
                     reads=[Rzf, Rsgl], writes=[Rost[oj]])
                P.dma("sp", lambda e, inc, ot=ot, oj=oj, c=c: inc(e.dma_start(
                    out=S["YST"][ot * 128:(ot + 1) * 128, c * T:(c + 1) * T], in_=ost[oj][:])), 1, reads=[Rost[oj]],
                    writes=[SR["YST"]])
        self.phase_end()

    def phase_nsa(self, l):
        P = self.P
        S, SR = self.scr, self.sres
        ps, psr = self.ps, self.psr
        QT = self.sb([128, 4, L], BF16)
        KT = {n: self.sb([128, L], BF16) for n in ("KCT", "VCT")}
        RQ = Res("QTs")
        P.dma("sp", lambda e, inc: inc(e.dma_start(out=QT[:], in_=S["QT"][:, :, :])), 1, reads=[SR["QT"]], writes=[RQ])
        for n in KT:
            P.dma("sp", lambda e, inc, n=n: inc(e.dma_start(out=KT[n][:], in_=S[n][:, :])), 1, reads=[SR[n]], writes=[RQ])
        KWz = [self.sb([128, L], BF16) for _ in range(2)]
        KE = [self.sb([128, L], BF16) for _ in range(2)]
        for k in range(2):
            o = 1 - k
            P.op("pool", lambda e, k=k, o=o: e.memset(KWz[k][o * 64:(o + 1) * 64, :], 0.0), writes=[RQ])
            P.dma("sp", lambda e, inc, k=k: inc(e.dma_start(out=KWz[k][k * 64:(k + 1) * 64, :], in_=S["KWT"][k * 64:(k + 1) * 64, :])),
                  1, reads=[SR["KWT"], RQ], writes=[RQ])
            P.dma("sp", lambda e, inc, k=k: inc(e.dma_start(out=KE[k][k * 64:(k + 1) * 64, :], in_=S["KST"][k * 64:(k + 1) * 64, :])),
                  1, reads=[SR["KST"], RQ], writes=[RQ])
            P.dma("sp", lambda e, inc, k=k, o=o: inc(e.dma_start(out=KE[k][o * 64:(o + 1) * 64, :], in_=self.d["c_ebig"][:, :])),
                  1, reads=[RQ], writes=[RQ])
        Va = {n: self.sb([128, NT, 2, 65], BF16) for n in ("VS", "VW")}
        for n in Va:
            P.op("pool", lambda e, n=n: e.memset(Va[n][:, :, :, 64:65], 1.0), writes=[RQ])
            for k in range(2):
                P.dma("sp", lambda e, inc, n=n, k=k: inc(e.dma_start(
                    out=Va[n][:, :, k, 0:64], in_=S[n].rearrange("(a p) c -> p a c", p=128)[:, :, k * 64:(k + 1) * 64])), 1,
                    reads=[SR[n], RQ], writes=[RQ])
        G = self.sb([128, NT, 24], F32)
        P.dma("sp", lambda e, inc: inc(e.dma_start(out=G[:], in_=S["GN"].rearrange("(a p) c -> p a c", p=128))), 1,
              reads=[SR["GN"]], writes=[RQ])
        W1 = {t: self.sb([128, 32, 128], BF16) for t in "kv"}
        W2kd = self.sb([128, 128], BF16)
        W2v = self.sb([128, 64], BF16)
        RWc = Res("Wc")
        pieces = []
        for t, nm in (("k", "cmp_wk1"), ("v", "cmp_wv1")):
            src = self.d[nm][l].rearrange("(l d) h -> d l h", d=64)
            for half in range(2):
                for o in range(0, 32, 8):
                    pieces.append((W1[t][half * 64:(half + 1) * 64, o:o + 8, :], src[:, o:o + 8, :], [64, 8, 128], half * 64))
        pieces.append((W2kd[:, 0:64], self.d["cmp_wk2"][l], [128, 64]))
        pieces.append((W2kd[:, 64:128], self.d["cmp_wk2"][l], [128, 64]))
        pieces.append((W2v[:], self.d["cmp_wv2"][l], [128, 64]))
        self.load_cast(pieces, RWc, stage_cols=1024)
        pe_n = self.sb([32, 256], F32)
        Rpe = Res("pe")
        for ti, nm in enumerate(("cmp_pe_k", "cmp_pe_v")):
            for half in range(2):
                P.dma("sp", lambda e, inc, ti=ti, nm=nm, half=half: inc(e.dma_start(
                    out=pe_n[:, ti * 128 + half * 64:ti * 128 + half * 64 + 64], in_=self.d[nm][l])), 1, writes=[Rpe])
        peT = self.sb([128, 2, 32], BF16)
        P.op("pe", lambda e: (e.transpose(out=ps[7][:, 0:32], in_=pe_n[:, 0:128], identity=self.ident[0:32, 0:32]),
                              e.transpose(out=ps[7][:, 32:64], in_=pe_n[:, 128:256], identity=self.ident[0:32, 0:32]))[1],
             reads=[Rpe, self.Rconst], writes=[psr[7]])
        P.op("dve", lambda e: e.tensor_copy(out=peT[:], in_=ps[7][:, 0:64].rearrange("p (a b) -> p a b", b=32)), reads=[psr[7]],
             writes=[Rpe])
        cb = self.sb([128, 2], F32)
        for ti, t in enumerate("kv"):
            def mmb(e, ti=ti, t=t):
                ins = None
                for li_ in range(32):
                    ins = e.matmul(ps[7][:, 64 + ti:65 + ti], lhsT=W1[t][0:64, li_, :], rhs=peT[0:64, ti, li_:li_ + 1],
                                   start=(li_ == 0), stop=(li_ == 31))
                return ins
            P.op("pe", mmb, reads=[RWc, Rpe], writes=[psr[7]])
        P.op("dve", lambda e: e.tensor_copy(out=cb[:], in_=ps[7][:, 64:66]), reads=[psr[7]], writes=[Rpe])
        KcT = self.sb([128, 2, 256], BF16)
        rcmp = self.sb([128, 2, 2, 129], BF16)
        Rcmp = Res("cmpops")
        P.op("pool", lambda e: e.memset(KcT[:], 0.0), writes=[Rcmp])
        P.op("pool", lambda e: e.memset(rcmp[:], 0.0), reads=[Rcmp], writes=[Rcmp])
        P.op("pool", lambda e: e.memset(rcmp[:, :, :, 64:65], 1.0), reads=[Rcmp], writes=[Rcmp])
        ovf = self.sb([128, 2, 64], F32)
        P.dma("sp", lambda e, inc: inc(e.dma_start(out=ovf[:], in_=self.d["c_ov"].rearrange("(a p) s -> p a s", p=128))), 1,
              writes=[Rpe])
        for k in range(2):
            P.op("pool", lambda e, k=k: e.tensor_copy(out=rcmp[:, :, k, 65:129], in_=ovf[:]), reads=[Rpe, Rcmp], writes=[Rcmp])
        hid = [self.sb([128, 256], BF16) for _ in range(2)]
        Rhid = [Res("hid0"), Res("hid1")]
        for hj in range(2):
            P.op("pool", lambda e, hj=hj: e.memset(hid[hj][:], 0.0), writes=[Rhid[hj]])
        ci = 0
        for ti, (t, srcn) in enumerate((("k", "KCT"), ("v", "VCT"))):
            for k in range(2):
                hj = ci % 2
                pb = ci % 2
                ci += 1

                def mmh(e, t=t, srcn=srcn, k=k, pb=pb):
                    ins = None
                    for li_ in range(32):
                        ins = e.matmul(ps[pb][:, 0:255], lhsT=W1[t][k * 64:(k + 1) * 64, li_, :],
                                       rhs=KT[srcn][k * 64:(k + 1) * 64, li_:li_ + 4065:16], start=(li_ == 0), stop=(li_ == 31))
                    return ins
                P.op("pe", mmh, reads=[RWc, RQ], writes=[psr[pb]])
                P.op("act", lambda e, hj=hj, pb=pb, ti=ti: e.activation(out=hid[hj][:, 0:255], in_=ps[pb][:, 0:255],
                                                                      func=AF.Gelu_apprx_tanh, bias=cb[:, ti:ti + 1]),
                     reads=[psr[pb], Rpe], writes=[Rhid[hj]])
                if t == "k":
                    P.op("pe", lambda e, hj=hj: e.matmul(ps[2][:, 0:255], lhsT=W2kd[:], rhs=hid[hj][:, 0:255], start=True, stop=True),
                         reads=[RWc, Rhid[hj]], writes=[psr[2]])
                    P.op("dve", lambda e, k=k: e.tensor_copy(out=KcT[k * 64:(k + 1) * 64, k, 0:255],
                                                             in_=ps[2][k * 64:(k + 1) * 64, 0:255]), reads=[psr[2], Rcmp],
                         writes=[Rcmp])
                else:
                    for nt, rows in ((0, 128), (1, 127)):
                        P.op("pe", lambda e, hj=hj, nt=nt, rows=rows: e.matmul(
                            ps[3][0:rows, nt * 64:(nt + 1) * 64], lhsT=hid[hj][:, nt * 128:nt * 128 + rows], rhs=W2v[:], start=True,
                            stop=True), reads=[RWc, Rhid[hj]], writes=[psr[3]])
                        P.op("dve", lambda e, k=k, nt=nt, rows=rows: e.tensor_copy(out=rcmp[0:rows, nt, k, 0:64],
                                                                                 in_=ps[3][0:rows, nt * 64:(nt + 1) * 64]),
                             reads=[psr[3], Rcmp], writes=[Rcmp])
        import os
        if os.environ.get("NSA_UNITS") == "0":
            self.phase_end()
            return
        PT = [self.sb([128, 512], BF16) for _ in range(3)]
        RPT = [Res("PT%d" % i) for i in range(3)]
        yt = [self.sb([128, 512], F32) for _ in range(2)]
        Ryt = [Res("yt0"), Res("yt1")]
        yT = [self.sb([128, 4, 128], BF16) for _ in range(2)]
        RyT = [Res("yT0"), Res("yT1")]
        rsc = self.sb([128, 4], F32)
        wgt = self.sb([128, 4], F32)
        acc = self.sb([128, 64], F32)
        sc = self.sb([128, 64], F32)
        sc2 = self.sb([128, 64], F32)
        m8 = self.sb([128, 16], F32)
        biasf = self.sb([128, 128], F32)
        Rk = [self.sb([128, 4, 128], BF16) for _ in range(2)]
        osb = [self.sb([128, 512], F32) for _ in range(2)]
        Rosb = [Res("osb0"), Res("osb1")]
        Rpost = Res("post")
        Rbias = Res("biasf")
        RbT = [Res("R0"), Res("R1")]
        units = []
        deferred = []

        def flush_deferred():
            for f in deferred:
                f()
            deferred.clear()

        for qt in range(NT):
            for k in range(2):
                QTt = QT[:, :, qt * 128:(qt + 1) * 128]
                yj = qt % 2
                yacc = yt[yj][:, k * 256:(k + 1) * 256].rearrange("p (h d) -> p h d", d=64)
                gsl = lambda b, qt=qt, k=k: G[:, qt, k * 12 + b:k * 12 + 12:3]

                def post_cmp(qt=qt, k=k, yacc=yacc, gsl=gsl, yj=yj):
                    Dv = lambda fn, rd, wr: P.op("dve", fn, reads=rd, writes=wr)
                    for bi, b_ in enumerate((3, 4)):
                        Dv(lambda e, bi=bi, b_=b_: e.tensor_scalar(out=rsc[:, bi * 2:bi * 2 + 2], in0=ps[b_][:, 64:64 + 129 + 1:129],
                                                                    scalar1=1e-30, scalar2=None, op0=ALU.max), [psr[b_], Rpost], [Rpost])
                    Dv(lambda e: e.reciprocal(out=rsc[:], in_=rsc[:]), [Rpost], [Rpost])
                    for h in range(4):
                        b_ = 3 + h // 2
                        o = (h % 2) * 129
                        if h == 0:
                            Dv(lambda e, b_=b_, o=o: e.tensor_scalar(out=acc[:], in0=ps[b_][:, o + 65:o + 129], scalar1=rsc[:, 0:1],
                                                                     scalar2=None, op0=ALU.mult), [psr[b_], Rpost], [Rpost])
                        else:
                            Dv(lambda e, b_=b_, o=o, h=h: e.scalar_tensor_tensor(out=acc[:], in0=ps[b_][:, o + 65:o + 129],
                                                                                 scalar=rsc[:, h:h + 1], in1=acc[:], op0=ALU.mult,
                                                                                 op1=ALU.add), [psr[b_], Rpost], [Rpost])
                    Dv(lambda e: e.tensor_tensor(out=wgt[:], in0=rsc[:], in1=gsl(0), op=ALU.mult), [Rpost, RQ], [Rpost])
                    for h in range(4):
                        b_ = 3 + h // 2
                        o = (h % 2) * 129
                        Dv(lambda e, b_=b_, o=o, h=h: e.tensor_scalar(out=yacc[:, h, :], in0=ps[b_][:, o:o + 64], scalar1=wgt[:, h:h + 1],
                                                                       scalar2=None, op0=ALU.mult), [psr[b_], Rpost, Ryt[yj]], [Ryt[yj]])
                    Dv(lambda e: e.tensor_copy(out=sc[:], in_=acc[:]), [Rpost], [Rpost])
                    for e_ in range(2):
                        cur = 2 * qt + e_
                        rows = slice(e_ * 64, (e_ + 1) * 64)
                        if cur + 1 < 64:
                            Dv(lambda e, rows=rows, cur=cur: e.memset(sc[rows, cur + 1:64], -1e6), [Rpost], [Rpost])
                        Dv(lambda e, rows=rows: e.memset(sc[rows, 0:1], 1e6), [Rpost], [Rpost])
                        lo = max(cur - 1, 0)
                        Dv(lambda e, rows=rows, lo=lo, cur=cur: e.memset(sc[rows, lo:cur + 1], 1e6), [Rpost], [Rpost])
                    Dv(lambda e: e.max(out=m8[:, 0:8], in_=sc[:]), [Rpost], [Rpost])
                    Dv(lambda e: e.match_replace(out=sc2[:], in_to_replace=m8[:, 0:8], in_values=sc[:], imm_value=-1e30), [Rpost],
                       [Rpost])
                    Dv(lambda e: e.max(out=m8[:, 8:16], in_=sc2[:]), [Rpost], [Rpost])
                    for hh in range(2):
                        Dv(lambda e, hh=hh: e.tensor_scalar(out=biasf[:, hh * 64:(hh + 1) * 64], in0=sc[:], scalar1=m8[:, 15:16],
                                                            scalar2=-BIG, op0=ALU.is_lt, op1=ALU.mult), [Rpost, Rbias], [Rbias])

                def pre_sel(k=k, qt=qt):
                    flush_deferred()
                    o = 1 - k
                    P.op("pe", lambda e: e.transpose(out=ps[7][:, 0:128], in_=biasf[:], identity=self.ident[:]),
                         reads=[Rbias, self.Rconst], writes=[psr[7]])
                    P.op("pool", lambda e: e.tensor_copy(out=Rk[k][k * 64:(k + 1) * 64, :, :],
                                                         in_=QT[k * 64:(k + 1) * 64, :, qt * 128:(qt + 1) * 128]),
                         reads=[RQ], writes=[RbT[k]])
                    P.op("act", lambda e: e.copy(out=Rk[k][o * 64:(o + 1) * 64, :, :],
                                                 in_=ps[7][o * 64:(o + 1) * 64, 0:128].unsqueeze(1).to_broadcast([64, 4, 128])),
                         reads=[psr[7]], writes=[RbT[k]])

                def post_branch(bank, b, yacc=yacc, gsl=gsl, yj=yj):
                    def f():
                        Dv = lambda fn, rd, wr: P.op("dve", fn, reads=rd, writes=wr)
                        ob = osb[bank - 5]
                        P.op("act", lambda e: e.copy(out=ob[0:65, :], in_=ps[bank][0:65, :]), reads=[psr[bank]], writes=[Rosb[bank - 5]])

                        def trb(e):
                            ins = None
                            for h in range(4):
                                ins = e.transpose(out=ps[bank][:, h * 65:(h + 1) * 65], in_=ob[0:65, h * 128:(h + 1) * 128],
                                                  identity=self.ident[0:65, 0:65])
                            return ins
                        P.op("pe", trb, reads=[Rosb[bank - 5], self.Rconst], writes=[psr[bank]])
                        Dv(lambda e: e.reciprocal(out=rsc[:], in_=ps[bank][:, 64:64 + 3 * 65 + 1:65]), [psr[bank], Rpost], [Rpost])
                        Dv(lambda e: e.tensor_tensor(out=wgt[:], in0=rsc[:], in1=gsl(b), op=ALU.mult), [Rpost, RQ], [Rpost])
                        for h in range(4):
                            Dv(lambda e, h=h: e.scalar_tensor_tensor(out=yacc[:, h, :], in0=ps[bank][:, h * 65:h * 65 + 64],
                                                                     scalar=wgt[:, h:h + 1], in1=yacc[:, h, :], op0=ALU.mult,
                                                                     op1=ALU.add), [psr[bank], Rpost, Ryt[yj]], [Ryt[yj]])
                    return f

                def post_qt(qt=qt, yj=yj):
                    def tr_store():
                        def tr(e):
                            ins = None
                            for a in range(4):
                                ins = e.transpose(out=ps[7][:, a * 128:(a + 1) * 128], in_=yt[yj][:, a * 128:(a + 1) * 128],
                                                  identity=self.ident[:])
                            return ins
                        P.op("pe", tr, reads=[Ryt[yj], self.Rconst], writes=[psr[7]])
                        P.op("act", lambda e: e.copy(out=yT[yj][:], in_=ps[7][:].rearrange("p (a b) -> p a b", b=128)),
                             reads=[psr[7]], writes=[RyT[yj]])
                        P.dma("sp", lambda e, inc: inc(e.dma_start(
                            out=S["YNT"].rearrange("(a p) t -> p a t", p=128)[:, :, qt * 128:(qt + 1) * 128], in_=yT[yj][:])), 1,
                            reads=[RyT[yj]], writes=[SR["YNT"]])
                    return lambda: deferred.append(tr_store)

                nts = [0] if qt < 16 else [0, 1]
                for nt in nts:
                    rows = 128 if nt == 0 else 127
                    full = (128 * nt + rows - 1) <= 8 * qt - 2
                    mask = None if full else dict(pattern=[[0, 4], [1, 128]], base=128 * qt - 2048 * nt - 31, cm=-16)
                    units.append(dict(kind="cmp", rows=rows, lhsT=KcT[:, k, nt * 128:nt * 128 + rows], rhs=QTt,
                                      mask=mask, v=rcmp[0:rows, nt, k, :], first=(nt == 0), last=(nt == nts[-1]), pre=None,
                                      post=(post_cmp if nt == nts[-1] else None)))
                wk = list(range(max(0, qt - 4), qt + 1))
                for kt in wk:
                    mask = None
                    if kt == qt:
                        mask = dict(pattern=[[0, 4], [1, 128]], base=0, cm=-1)
                    elif kt == qt - 4:
                        mask = dict(pattern=[[0, 4], [-1, 128]], base=-1, cm=1)
                    units.append(dict(kind="win", rows=128, lhsT=KWz[k][:, kt * 128:(kt + 1) * 128], rhs=QTt,
                                      mask=mask, v=Va["VW"][:, kt, k, :], first=(kt == wk[0]), last=(kt == qt), pre=None,
                                      post=(post_branch(6, 2) if kt == qt else None), bank=6))
                for kt in range(qt + 1):
                    mask = dict(pattern=[[0, 4], [1, 128]], base=0, cm=-1) if kt == qt else None
                    posts = None
                    if kt == qt:
                        pb_ = post_branch(5, 1)
                        if k == 1:
                            pq = post_qt()
                            posts = (lambda pb_=pb_, pq=pq: (pb_(), post_qt_call(pq)))
                        else:
                            posts = pb_
                    units.append(dict(kind="sel", rows=128, lhsT=KE[k][:, kt * 128:(kt + 1) * 128], rhs=Rk[k][:],
                                      mask=mask, v=Va["VS"][:, kt, k, :], first=(kt == 0), last=(kt == qt),
                                      pre=(pre_sel if kt == 0 else None), post=posts, bank=5, k=k))

        def post_qt_call(pq):
            pq()

        def emit_qk(i):
            u = units[i]
            sb_ = i % 3
            if u["pre"] is not None:
                u["pre"]()
            rows = u["rows"]
            out3 = ps[sb_][0:rows, :].rearrange("p (a b) -> p a b", b=128)
            if u["kind"] == "sel":
                P.op("pe", lambda e, u=u, out3=out3: e.matmul(out3, lhsT=u["lhsT"], rhs=u["rhs"], start=True, stop=True),
                     reads=[RQ, RbT[u["k"]]], writes=[psr[sb_]])
            else:
                P.op("pe", lambda e, u=u, out3=out3: e.matmul(out3, lhsT=u["lhsT"], rhs=u["rhs"], start=True, stop=True),
                     reads=[RQ, Rcmp], writes=[psr[sb_]])

        def emit_rest(i):
            u = units[i]
            sb_ = i % 3
            rows = u["rows"]
            P.op("act", lambda e, sb_=sb_, rows=rows: e.activation(out=PT[sb_][0:rows, :], in_=ps[sb_][0:rows, :], func=AF.Exp,
                                                                  scale=0.125), reads=[psr[sb_]], writes=[RPT[sb_]])
            if u["mask"] is not None:
                mk = u["mask"]
                v3 = PT[sb_][0:rows, :].rearrange("p (a b) -> p a b", b=128)
                P.op("pool", lambda e, v3=v3, mk=mk: e.affine_select(out=v3, in_=v3, pattern=mk["pattern"], compare_op=ALU.is_ge,
                                                                    fill=0.0, base=mk["base"], channel_multiplier=mk["cm"]),
                     reads=[RPT[sb_]], writes=[RPT[sb_]])
            if u["kind"] == "cmp":
                def pv(e, u=u, sb_=sb_, rows=rows):
                    ins = None
                    for h in range(4):
                        b_ = 3 + h // 2
                        o = (h % 2) * 129
                        ins = e.matmul(ps[b_][:, o:o + 129], lhsT=PT[sb_][0:rows, h * 128:(h + 1) * 128], rhs=u["v"],
                                       start=(u["first"] and h % 2 == 0), stop=(u["last"] and h % 2 == 1))
                    return ins
                P.op("pe", pv, reads=[RPT[sb_], Rcmp], writes=[psr[3], psr[4]], self_ok=not u["first"])
            else:
                bank = u["bank"]

                P.op("pe", lambda e, u=u, sb_=sb_, bank=bank: e.matmul(ps[bank][0:65, :], lhsT=u["v"], rhs=PT[sb_][:, :],
                                                                     start=u["first"], stop=u["last"]),
                     reads=[RPT[sb_], RQ], writes=[psr[bank]], self_ok=not u["first"])
            if u["post"] is not None:
                u["post"]()

        import os
        n = len(units)
        if os.environ.get("NSA_UNITS"):
            n = int(os.environ["NSA_UNITS"])
            deferred.clear()
        emit_qk(0)
        for i in range(n):
            if i + 1 < n:
                emit_qk(i + 1)
            emit_rest(i)
            if os.environ.get("NSA_UNITS") and i == n - 1:
                break
        flush_deferred()
        self.phase_end()

    def phase_merge(self, l):
        P = self.P
        S, SR = self.scr, self.sres
        w_in = self.d["w_in"][l].rearrange("(a p) c -> p a c", p=128)
        WU = self.sb([128, 12, D], BF16)
        WG = self.sb([128, 8, 3072], BF16)
        WO = self.sb([128, 8, D], BF16)
        RW = Res("WM")
        pieces = []
        for b, nm in enumerate(("w_up_pool", "w_up_ssm", "w_up_nsa")):
            wv = self.d[nm][l].rearrange("(a p) c -> p a c", p=128)
            for o in range(0, D, 512):
                pieces.append((WU[:, b * 4:(b + 1) * 4, o:o + 512], wv[:, :, o:o + 512], [128, 4, 512]))
        for o in range(0, 3072, 256):
            pieces.append((WG[:, :, o:o + 256], w_in[:, :, 2328 + o:2328 + o + 256], [128, 8, 256]))
        wo = self.d["w_out"][l].rearrange("(a p) c -> p a c", p=128)
        for o in range(0, D, 256):
            pieces.append((WO[:, :, o:o + 256], wo[:, :, o:o + 256], [128, 8, 256]))
        self.load_cast(pieces, RW)
        g_rep = self.sb([128, D], F32)
        b_rep = self.sb([128, D], F32)
        Rgb = Res("gb")
        self.bcast_load(g_rep[:], self.d["ln1_g"][l:l + 1, :], Rgb)
        self.bcast_load(b_rep[:], self.d["ln1_b"][l:l + 1, :], Rgb)
        xt = [self.sb([128, 8, 512], BF16) for _ in range(2)]
        Rxt = [Res("xt0"), Res("xt1")]
        yb = [self.sb([128, 12, 512], BF16) for _ in range(2)]
        Ryb = [Res("yb0"), Res("yb1")]
        mg = self.sb([128, 8, 512], BF16)
        Rmg = Res("mg")
        sg = [self.sb([128, 512], F32) for _ in range(2)]
        Rsg = [Res("sg0"), Res("sg1")]
        acc = self.sb([128, 512], F32)
        tmp = self.sb([128, 512], F32)
        Racc = Res("acc")
        Rtmp = Res("tmp")
        xr = [self.sb([128, D], F32) for _ in range(2)]
        Rxr = [Res("xr0"), Res("xr1")]
        bufs = [self.ln_bufs() for _ in range(2)]
        XTd = S["XT"].rearrange("(a p) t -> p a t", p=128)
        k = 0
        for c in range(8):
            j = c % 2
            P.dma("sp", lambda e, inc, c=c, j=j: inc(e.dma_start(out=xt[j][:], in_=XTd[:, :, c * 512:(c + 1) * 512])), 1,
                  reads=[SR["XT"]], writes=[Rxt[j]])
            for b, nm in enumerate(("YPT", "YST", "YNT")):
                P.dma("sp", lambda e, inc, c=c, j=j, b=b, nm=nm: inc(e.dma_start(
                    out=yb[j][:, b * 4:(b + 1) * 4, :],
                    in_=S[nm].rearrange("(a p) t -> p a t", p=128)[:, :, c * 512:(c + 1) * 512])), 1,
                    reads=[SR[nm]], writes=[Ryb[j]])
            for m in range(8):
                for b in range(3):
                    pu = k % 2
                    pg = 2 + k % 2
                    sj = k % 2
                    k += 1

                    def mmu(e, pu=pu, b=b, m=m, j=j):
                        ins = None
                        for a in range(4):
                            ins = e.matmul(self.ps[pu][:], lhsT=WU[:, b * 4 + a, m * 128:(m + 1) * 128], rhs=yb[j][:, b * 4 + a, :],
                                           start=(a == 0), stop=(a == 3))
                        return ins

                    def mmg(e, pg=pg, b=b, m=m, j=j):
                        ins = None
                        for a in range(8):
                            ins = e.matmul(self.ps[pg][:], lhsT=WG[:, a, b * 1024 + m * 128:b * 1024 + (m + 1) * 128],
                                           rhs=xt[j][:, a, :], start=(a == 0), stop=(a == 7))
                        return ins
                    P.op("pe", mmg, reads=[RW, Rxt[j]], writes=[self.psr[pg]])
                    P.op("pe", mmu, reads=[RW, Ryb[j]], writes=[self.psr[pu]])
                    P.op("act", lambda e, pg=pg, sj=sj: e.activation(out=sg[sj][:], in_=self.ps[pg][:], func=AF.Sigmoid),
                         reads=[self.psr[pg]], writes=[Rsg[sj]])
                    if b == 0:
                        P.op("dve", lambda e, pu=pu, sj=sj: e.tensor_tensor(out=acc[:], in0=self.ps[pu][:], in1=sg[sj][:],
                                                                            op=ALU.mult), reads=[self.psr[pu], Rsg[sj]],
                             writes=[Racc])
                    else:
                        P.op("dve", lambda e, pu=pu, sj=sj: e.tensor_tensor(out=tmp[:], in0=self.ps[pu][:], in1=sg[sj][:],
                                                                            op=ALU.mult), reads=[self.psr[pu], Rsg[sj]],
                             writes=[Rtmp])
                        if b == 1:
                            P.op("pool", lambda e: e.tensor_tensor(out=acc[:], in0=acc[:], in1=tmp[:], op=ALU.add),
                                 reads=[Rtmp, Racc], writes=[Racc])
                        else:
                            P.op("pool", lambda e, m=m: e.tensor_tensor(out=mg[:, m, :], in0=acc[:], in1=tmp[:], op=ALU.add),
                                 reads=[Rtmp, Racc], writes=[Rmg])
            for tt in range(4):
                i = c * 4 + tt
                xj = i % 2
                P.dma("sp", lambda e, inc, i=i, xj=xj: inc(e.dma_start(out=xr[xj][:], in_=S["XR"][i * 128:(i + 1) * 128, :])),
                      1, reads=[SR["XR"]], writes=[Rxr[xj]])
                for half in range(2):
                    pb = 4 + half

                    def mmo(e, pb=pb, tt=tt, half=half):
                        ins = None
                        for a in range(8):
                            ins = e.matmul(self.ps[pb][:], lhsT=mg[:, a, tt * 128:(tt + 1) * 128],
                                           rhs=WO[:, a, half * 512:(half + 1) * 512], start=(a == 0), stop=(a == 7))
                        return ins
                    P.op("pe", mmo, reads=[RW, Rmg], writes=[self.psr[pb]])
                    P.op("dve", lambda e, pb=pb, xj=xj, half=half: e.scalar_tensor_tensor(
                        out=xr[xj][:, half * 512:(half + 1) * 512], in0=xr[xj][:, half * 512:(half + 1) * 512], scalar=ALPHA,
                        in1=self.ps[pb][:], op0=ALU.mult, op1=ALU.add), reads=[self.psr[pb], Rxr[xj]], writes=[Rxr[xj]])
                self.ln_tile(i, xr[xj], Rxr[xj], g_rep, b_rep, Rgb, S["XR1"], SR["XR1"], S["X1T"], SR["X1T"], bufs[xj])
        self.phase_end()

    def phase_ffn(self, l, last):
        P = self.P
        S, SR = self.scr, self.sres
        W1 = self.sb([128, 8, 4096], BF16)
        W2 = self.sb([128, 32, D], BF16)
        RW = Res("WF")
        w1 = self.d["w_ff1"][l].rearrange("(a p) c -> p a c", p=128)
        w2 = self.d["w_ff2"][l].rearrange("(a p) c -> p a c", p=128)
        pieces = []
        for o in range(0, 4096, 128):
            pieces.append((W1[:, :, o:o + 128], w1[:, :, o:o + 128], [128, 8, 128]))
        for a in range(32):
            pieces.append((W2[:, a, :], w2[:, a, :], [128, D]))
        self.load_cast(pieces, RW, stage_cols=1024)
        g_rep = self.sb([128, D], F32)
        b_rep = self.sb([128, D], F32)
        Rgb = Res("gb")
        self.bcast_load(g_rep[:], self.d["ln2_g"][l:l + 1, :], Rgb)
        self.bcast_load(b_rep[:], self.d["ln2_b"][l:l + 1, :], Rgb)
        xt = [self.sb([128, 8, 512], BF16) for _ in range(1)]
        Rxt = [Res("xt0")]
        hT = self.sb([128, 32, 512], BF16)
        RhT = Res("hT")
        rl = [self.sb([128, 512], F32) for _ in range(2)]
        Rrl = [Res("rl0"), Res("rl1")]
        xr = [self.sb([128, D], F32) for _ in range(2)]
        Rxr = [Res("xr0"), Res("xr1")]
        bufs = [self.ln_bufs() for _ in range(2)]
        X1Td = S["X1T"].rearrange("(a p) t -> p a t", p=128)
        dst_res = self.d["out"] if last else S["XR"]
        Rdst = Res("outres") if last else SR["XR"]
        dst_T = None if last else S["XT"]
        k = 0
        for c in range(8):
            j = 0
            P.dma("sp", lambda e, inc, c=c, j=j: inc(e.dma_start(out=xt[j][:], in_=X1Td[:, :, c * 512:(c + 1) * 512])), 1,
                  reads=[SR["X1T"]], writes=[Rxt[j]])
            for f in range(32):
                pb = k % 4
                rj = k % 2
                k += 1

                def mm1(e, pb=pb, f=f, j=j):
                    ins = None
                    for a in range(8):
                        ins = e.matmul(self.ps[pb][:], lhsT=W1[:, a, f * 128:(f + 1) * 128], rhs=xt[j][:, a, :], start=(a == 0),
                                       stop=(a == 7))
                    return ins
                P.op("pe", mm1, reads=[RW, Rxt[j]], writes=[self.psr[pb]])
                P.op("act", lambda e, pb=pb, rj=rj: e.activation(out=rl[rj][:], in_=self.ps[pb][:], func=AF.Relu),
                     reads=[self.psr[pb]], writes=[Rrl[rj]])
                P.op("pool", lambda e, rj=rj, f=f: e.tensor_tensor(out=hT[:, f, :], in0=rl[rj][:], in1=rl[rj][:], op=ALU.mult),
                     reads=[Rrl[rj]], writes=[RhT])
            for tt in range(4):
                i = c * 4 + tt
                xj = i % 2
                P.dma("sp", lambda e, inc, i=i, xj=xj: inc(e.dma_start(out=xr[xj][:], in_=S["XR1"][i * 128:(i + 1) * 128, :])),
                      1, reads=[SR["XR1"]], writes=[Rxr[xj]])
                for half in range(2):
                    pb = 4 + half

                    def mm2(e, pb=pb, tt=tt, half=half):
                        ins = None
                        for f in range(32):
                            ins = e.matmul(self.ps[pb][:], lhsT=hT[:, f, tt * 128:(tt + 1) * 128],
                                           rhs=W2[:, f, half * 512:(half + 1) * 512], start=(f == 0), stop=(f == 31))
                        return ins
                    P.op("pe", mm2, reads=[RW, RhT], writes=[self.psr[pb]])
                    P.op("dve", lambda e, pb=pb, xj=xj, half=half: e.scalar_tensor_tensor(
                        out=xr[xj][:, half * 512:(half + 1) * 512], in0=xr[xj][:, half * 512:(half + 1) * 512], scalar=ALPHA,
                        in1=self.ps[pb][:], op0=ALU.mult, op1=ALU.add), reads=[self.psr[pb], Rxr[xj]], writes=[Rxr[xj]])
                self.ln_tile(i, xr[xj], Rxr[xj], g_rep, b_rep, Rgb, dst_res, Rdst, dst_T, SR["XT"], bufs[xj])
        self.phase_end()

    def build(self):
        self.phase_ln_in()
        for l in range(self.layers):
            stop = self.stop_after
            skip = self.skip
            if "proj" not in skip:
                self.phase_proj(l)
            if stop == "proj":
                break
            if "pool" not in skip:
                self.phase_pool(l)
            if stop == "pool":
                break
            if "s5" not in skip:
                self.phase_s5(l)
            if stop == "s5":
                break
            if "nsa" not in skip:
                self.phase_nsa(l)
            if stop == "nsa":
                break
            self.phase_merge(l)
            if stop == "merge":
                break
            self.phase_ffn(l, last=(l == self.layers - 1))
        self.P.emit()
        return self.nc


def make_inputs(inputs, ncores=8):
    consts = host_consts()
    shared = {n: np.ascontiguousarray(np.asarray(inputs[n], dtype=np.float32)) for n, _ in PARAMS}
    shared.update(consts)
    x = np.asarray(inputs["x"], dtype=np.float32)
    maps = []
    for c in range(ncores):
        m = dict(shared)
        m["x"] = np.ascontiguousarray(x[c])
        maps.append(m)
    return maps


def kernel(**inputs):
    b = Builder()
    nc = b.build()
    maps = make_inputs(inputs)
    res = run_bass_kernel_spmd(nc, maps, core_ids=list(range(8)))
    return np.stack([np.asarray(r["out"], dtype=np.float32) for r in res.results], axis=0)
```

```python
import contextlib
import math
import numpy as np
import ml_dtypes
import concourse.bass as bass
import concourse.mybir as mybir
from concourse.bass_utils import run_bass_kernel_spmd

F32 = mybir.dt.float32
BF16 = mybir.dt.bfloat16
AF = mybir.ActivationFunctionType
ALU = mybir.AluOpType
AX = mybir.AxisListType

L = 4096
D = 1024
NT = 32
DEPTH = 2
ALPHA = (2 * DEPTH) ** 0.25
EPS = 1e-5
BIG = 30000.0
SB_BASE = 16512
SB_END = 229344


class Res:
    __slots__ = ("name", "w", "r")

    def __init__(self, name=""):
        self.name = name
        self.w = None
        self.r = []


class Prog:
    ENGS = ("pe", "act", "dve", "pool", "sp")
    NDMA = 8

    def __init__(self, nc):
        self.nc = nc
        self.ops = {e: [] for e in self.ENGS}
        self.cnt = {}
        self.seen = {e: {} for e in self.ENGS}
        self.dma_n = {e: 0 for e in self.ENGS}
        self.sem_keys = []
        for e in ("pe", "act", "dve", "pool"):
            self._mk(("c", e))
        for e in ("sp", "act", "pool"):
            for j in range(self.NDMA):
                self._mk(("d", e, j))
        self.nops = 0

    def _mk(self, key):
        self.cnt[key] = 0
        self.sem_keys.append(key)

    def _need(self, eng, waits, dep):
        if dep is None:
            return
        key, val = dep
        if self.seen[eng].get(key, 0) >= val:
            return
        waits[key] = max(waits.get(key, 0), val)

    def _deps(self, eng, reads, writes, self_ok=False):
        waits = {}
        for r in reads:
            self._need(eng, waits, r.w)
        for w in writes:
            self._need(eng, waits, w.w)
            for rd in w.r:
                self._need(eng, waits, rd)
        if self_ok:
            waits.pop(("c", eng), None)
        return waits

    def _commit(self, eng, waits, tok, reads, writes):
        for k, v in waits.items():
            self.seen[eng][k] = v
        for r in reads:
            r.r.append(tok)
        for w in writes:
            w.w = tok
            w.r = []
        self.nops += 1

    def op(self, eng, fn, reads=(), writes=(), self_ok=False):
        waits = self._deps(eng, reads, writes, self_ok)
        key = ("c", eng)
        self.cnt[key] += 1
        tok = (key, self.cnt[key])
        self.ops[eng].append((list(waits.items()), fn, key, 1))
        self._commit(eng, waits, tok, reads, writes)
        return tok

    def dma(self, q, fn, ndma, reads=(), writes=()):
        waits = self._deps(q, reads, writes)
        j = self.dma_n[q] % self.NDMA
        self.dma_n[q] += 1
        key = ("d", q, j)
        if self.cnt[key] > 0:
            self._need(q, waits, (key, self.cnt[key]))
        self.cnt[key] += 16 * ndma
        tok = (key, self.cnt[key])
        self.ops[q].append((list(waits.items()), fn, key, 16))
        self._commit(q, waits, tok, reads, writes)
        return tok

    def barrier(self):
        for e in self.ENGS:
            waits = {}
            for k in self.sem_keys:
                if self.cnt[k] > 0 and not (k[0] == "c" and k[1] == e):
                    self._need(e, waits, (k, self.cnt[k]))
            for k, v in waits.items():
                self.seen[e][k] = v
            if waits:
                self.ops[e].append((list(waits.items()), None, None, 0))

    def emit(self):
        nc = self.nc
        with contextlib.ExitStack() as st:
            sems = {}
            for k in self.sem_keys:
                sems[k] = st.enter_context(nc.semaphore("s_" + "_".join(str(x) for x in k)))
            block = st.enter_context(nc.Block())
            final = [(k, v) for k, v in self.cnt.items() if v > 0]

            def run(engname):
                def body(eng):
                    for waits, fn, key, inc in self.ops[engname]:
                        for k, v in waits:
                            eng.wait_ge(sems[k], v)
                        if fn is None:
                            continue
                        if inc == 16:
                            fn(eng, lambda ins: ins.then_inc(sems[key], 16))
                        else:
                            fn(eng).then_inc(sems[key], 1)
                    if engname == "sp":
                        for k, v in final:
                            eng.wait_ge(sems[k], v)
                return body

            block.tensor(run("pe"))
            block.scalar(run("act"))
            block.vector(run("dve"))
            block.gpsimd(run("pool"))
            block.sync(run("sp"))


PARAMS = [("ln_in_g", (D,)), ("ln_in_b", (D,)), ("w_in", (2, D, 5400)), ("w_pool", (2, 4, 128, 128)),
          ("pool_scale", (2, 512)), ("ssm_lam_re", (2, 32, 64)), ("ssm_lam_im", (2, 32, 64)),
          ("ssm_log_dt", (2, 32)), ("ssm_b_re", (2, 32, 64, 16)), ("ssm_b_im", (2, 32, 64, 16)),
          ("ssm_c_re", (2, 32, 16, 64)), ("ssm_c_im", (2, 32, 16, 64)), ("ssm_d", (2, 32, 16)),
          ("w_glu", (2, 512, 512)), ("b_glu", (2, 512)), ("cmp_pe_k", (2, 32, 64)), ("cmp_pe_v", (2, 32, 64)),
          ("cmp_wk1", (2, 2048, 128)), ("cmp_wk2", (2, 128, 64)), ("cmp_wv1", (2, 2048, 128)),
          ("cmp_wv2", (2, 128, 64)), ("w_up_pool", (2, 512, D)), ("w_up_ssm", (2, 512, D)),
          ("w_up_nsa", (2, 512, D)), ("w_out", (2, D, D)), ("ln1_g", (2, D)), ("ln1_b", (2, D)),
          ("w_ff1", (2, D, 4096)), ("w_ff2", (2, 4096, D)), ("ln2_g", (2, D)), ("ln2_b", (2, D))]


def host_consts():
    c = {}
    c["c_ident"] = np.eye(128, dtype=np.float32)
    pos = np.arange(L, dtype=np.float32)
    inv_freq = (np.float32(500000.0) ** (-np.arange(0, 16, 2, dtype=np.float32) / np.float32(16))).astype(np.float32)
    ang = (pos[:, None] * inv_freq[None, :]).astype(np.float32)
    c["c_rope"] = np.concatenate([np.cos(ang), np.sin(ang)], axis=1).astype(np.float32)
    eb = np.zeros((64, 4096), np.float32)
    for s in range(64):
        eb[s, s * 64:(s + 1) * 64] = 1.0
    c["c_ebig"] = eb.astype(ml_dtypes.bfloat16)
    cs = np.arange(256)[:, None] * 16
    ss = np.arange(64)[None, :] * 64
    ov = np.maximum(np.minimum(cs + 32, ss + 64) - np.maximum(cs, ss), 0) / 16.0
    ov[255:] = 0.0
    c["c_ov"] = ov.astype(np.float32)
    ic = np.zeros((4, 16), np.float32)
    for gi, w in enumerate((2, 4, 8, 16)):
        ic[gi] = 1.0 / np.minimum(np.arange(16) + 1, w)
    c["c_invcnt"] = ic.reshape(1, 64)
    return c


class Builder:
    def __init__(self, dbg=(), stop_after=None, layers=DEPTH, inject=(), skip=()):
        self.inject = set(inject)
        self.skip = set(skip)
        self.nc = nc = bass.Bass("TRN2", target_bir_lowering=False)
        self.P = Prog(nc)
        self.dbg = set(dbg)
        self.stop_after = stop_after
        self.layers = layers
        self.d = {}
        self.d["x"] = nc.dram_tensor("x", [L, D], F32, kind="ExternalInput").ap()
        for n, shp in PARAMS:
            self.d[n] = nc.dram_tensor(n, list(shp), F32, kind="ExternalInput").ap()
        for n, a in host_consts().items():
            dt = BF16 if a.dtype == ml_dtypes.bfloat16 else F32
            self.d[n] = nc.dram_tensor(n, list(a.shape), dt, kind="ExternalInput").ap()
        self.d["out"] = nc.dram_tensor("out", [L, D], F32, kind="ExternalOutput").ap()
        self.scr = {}
        self.sres = {}
        for n, shp, dt in [("XR", [L, D], F32), ("XT", [D, L], BF16), ("UPT", [512, L], F32), ("UST", [512, L], F32),
                           ("QT", [128, 4, L], BF16), ("KCT", [128, L], BF16), ("KST", [128, L], BF16),
                           ("KWT", [128, L], BF16), ("VCT", [128, L], BF16), ("VS", [L, 128], BF16),
                           ("VW", [L, 128], BF16), ("GN", [L, 24], F32), ("YPT", [512, L], BF16),
                           ("YST", [512, L], BF16), ("YNT", [512, L], BF16), ("XR1", [L, D], F32),
                           ("X1T", [D, L], BF16)]:
            kind = "ExternalOutput" if n in self.dbg else ("ExternalInput" if n in self.inject else "Internal")
            self.scr[n] = nc.dram_tensor("s_" + n, shp, dt, kind=kind).ap()
            self.sres[n] = Res(n)
        self.ps = [nc.alloc_psum_tensor("psb%d" % i, [128, 512], F32) for i in range(8)]
        self.psr = [Res("ps%d" % i) for i in range(8)]
        self.cur = SB_BASE
        self.uid = 0
        self.ident = self.sb([128, 128], F32)
        self.Rconst = Res("const")
        self.P.dma("sp", lambda e, inc: inc(e.dma_start(out=self.ident[:], in_=self.d["c_ident"][:, :])), 1,
                   writes=[self.Rconst])
        self.persist = self.cur

    def sb(self, shape, dt):
        nbytes = int(np.prod(shape[1:])) * (2 if dt == BF16 else 4)
        off = (self.cur + 63) // 64 * 64
        assert off + nbytes <= SB_END, ("SBUF overflow", off + nbytes - SB_END)
        self.cur = off + nbytes
        self.uid += 1
        return self.nc.alloc_sbuf_tensor_at("t%d" % self.uid, list(shape), dt, offset=off)

    def phase_end(self):
        self.P.barrier()
        self.cur = self.persist

    def load_cast(self, pieces, res, stage_cols=2048):
        P = self.P
        if not hasattr(self, "_lc"):
            self._lc = None
        stg = [self.sb([128, stage_cols], F32) for _ in range(3)]
        rs = [Res("stg%d" % i) for i in range(3)]
        for i, pc in enumerate(pieces):
            dst, src, shp = pc[0], pc[1], pc[2]
            p0 = pc[3] if len(pc) > 3 else 0
            j = i % 3
            n = int(np.prod(shp[1:]))
            assert n <= stage_cols
            p = shp[0]
            sview = stg[j][p0:p0 + p, 0:n]
            if len(shp) == 3:
                sview = sview.rearrange("p (a b) -> p a b", b=shp[2])
            P.dma("sp", (lambda sv, s: lambda e, inc: inc(e.dma_start(out=sv, in_=s)))(sview, src), 1, writes=[rs[j]])
            eng = "pool" if i % 2 == 0 else "dve"
            P.op(eng, (lambda dd, sv: lambda e: e.tensor_copy(out=dd, in_=sv))(dst, sview), reads=[rs[j]], writes=[res])

    def bcast_load(self, dst, src_row, res):
        self.P.dma("sp", lambda e, inc: inc(e.dma_start(out=dst, in_=src_row.partition_broadcast(128))), 1, writes=[res])

    def ln_tile(self, i, src, Rsrc, g_rep, b_rep, Rgb, dst_res_d, Rres, dst_T_d, RT, bufs):
        P = self.P
        st6, mv, rstd, xT, RxT, Rst = bufs
        P.op("dve", lambda e: e.bn_stats(out=st6[:, 0, :], in_=src[:, 0:512]), reads=[Rsrc], writes=[Rst])
        P.op("dve", lambda e: e.bn_stats(out=st6[:, 1, :], in_=src[:, 512:1024]), reads=[Rsrc, Rst], writes=[Rst])
        P.op("dve", lambda e: e.bn_aggr(out=mv[:], in_=st6[:]), reads=[Rst], writes=[Rst])
        P.op("dve", lambda e: e.tensor_scalar_add(out=rstd[:], in0=mv[:, 1:2], scalar1=EPS), reads=[Rst], writes=[Rst])
        P.op("act", lambda e: e.sqrt(out=rstd[:], in_=rstd[:]), reads=[Rst], writes=[Rst])
        P.op("dve", lambda e: e.reciprocal(out=rstd[:], in_=rstd[:]), reads=[Rst], writes=[Rst])
        P.op("dve", lambda e: e.tensor_scalar(out=src[:], in0=src[:], scalar1=mv[:, 0:1], scalar2=rstd[:, 0:1],
                                              op0=ALU.subtract, op1=ALU.mult), reads=[Rst, Rsrc], writes=[Rsrc])
        P.op("pool", lambda e: e.tensor_tensor(out=src[:], in0=src[:], in1=g_rep[:], op=ALU.mult), reads=[Rsrc, Rgb],
             writes=[Rsrc])
        P.op("dve", lambda e: e.tensor_tensor(out=src[:], in0=src[:], in1=b_rep[:], op=ALU.add), reads=[Rsrc, Rgb],
             writes=[Rsrc])
        P.dma("sp", lambda e, inc: inc(e.dma_start(out=dst_res_d[i * 128:(i + 1) * 128, :], in_=src[:])), 1,
              reads=[Rsrc], writes=[Rres])
        if dst_T_d is None:
            return
        for half in range(2):
            pb = 6 + half
            ps = self.ps[pb]

            def tr(e, half=half, ps=ps):
                ins = None
                for q in range(4):
                    dt = half * 4 + q
                    ins = e.transpose(out=ps[:, q * 128:(q + 1) * 128], in_=src[:, dt * 128:(dt + 1) * 128],
                                      identity=self.ident[:])
                return ins
            P.op("pe", tr, reads=[Rsrc, self.Rconst], writes=[self.psr[pb]])
            P.op("act", lambda e, half=half, ps=ps: e.copy(out=xT[:, half * 4:(half + 1) * 4, :],
                                                         in_=ps[:].rearrange("p (a b) -> p a b", b=128)),
                 reads=[self.psr[pb]], writes=[RxT])
        P.dma("sp", lambda e, inc: inc(e.dma_start(
            out=dst_T_d.rearrange("(a p) t -> p a t", p=128)[:, :, i * 128:(i + 1) * 128], in_=xT[:])), 1,
            reads=[RxT], writes=[RT])

    def ln_bufs(self):
        return (self.sb([128, 2, 6], F32), self.sb([128, 2], F32), self.sb([128, 1], F32),
                self.sb([128, 8, 128], BF16), Res("xT"), Res("lnst"))

    def phase_ln_in(self):
        P = self.P
        g_rep = self.sb([128, D], F32)
        b_rep = self.sb([128, D], F32)
        Rgb = Res("gb")
        self.bcast_load(g_rep[:], self.d["ln_in_g"].unsqueeze(0), Rgb)
        self.bcast_load(b_rep[:], self.d["ln_in_b"].unsqueeze(0), Rgb)
        xb = [self.sb([128, D], F32) for _ in range(2)]
        Rx = [Res("x0"), Res("x1")]
        bufs = [self.ln_bufs() for _ in range(2)]
        for i in range(NT):
            j = i % 2
            P.dma("sp", lambda e, inc, i=i, j=j: inc(e.dma_start(out=xb[j][:], in_=self.d["x"][i * 128:(i + 1) * 128, :])),
                  1, writes=[Rx[j]])
            self.ln_tile(i, xb[j], Rx[j], g_rep, b_rep, Rgb, self.scr["XR"], self.sres["XR"], self.scr["XT"],
                         self.sres["XT"], bufs[j])
        self.phase_end()

    def phase_proj(self, l):
        P = self.P
        w_in = self.d["w_in"][l].rearrange("(a p) c -> p a c", p=128)
        NC_ = 1024 + 1304
        WP = self.sb([128, 8, NC_], BF16)
        RW = Res("WP")
        plan = [(0, 0, 1024)]
        qorder = (0, 4, 1, 5, 2, 6, 3, 7)
        for j, h in enumerate(qorder):
            plan.append((1024 + j * 64, 1024 + h * 64, 64))
        kv0 = 1536
        base = 1024 + 512
        for j, s in enumerate((0, 2, 4, 1)):
            plan.append((base + j * 128, kv0 + s * 128, 128))
        base += 512
        for j, s in enumerate((3, 5)):
            plan.append((base + j * 128, kv0 + s * 128, 128))
        plan.append((base + 256, 2304, 24))
        pieces = []
        for dc, sc, n in plan:
            o = 0
            while o < n:
                m = min(256, n - o)
                pieces.append((WP[:, :, dc + o:dc + o + m], w_in[:, :, sc + o:sc + o + m], [128, 8, m]))
                o += m
        self.load_cast(pieces, RW)
        rope = self.sb([128, NT, 16], F32)
        Rrope = Res("rope")
        P.dma("sp", lambda e, inc: inc(e.dma_start(out=rope[:], in_=self.d["c_rope"].rearrange("(a p) c -> p a c", p=128))),
              1, writes=[Rrope])
        xt = [self.sb([128, 8, 512], BF16) for _ in range(2)]
        Rxt = [Res("xt0"), Res("xt1")]
        fst = [self.sb([128, 512], F32) for _ in range(2)]
        Rfst = [Res("f0"), Res("f1")]
        tm = [self.sb([128, 1304], F32) for _ in range(2)]
        Rtm = [Res("tm0"), Res("tm1")]
        t1 = self.sb([128, 14, 8], F32)
        t2 = self.sb([128, 14, 8], F32)
        t3 = self.sb([128, 14, 8], F32)
        Rt = Res("ropetmp")
        trs = [self.sb([128, 8, 128], BF16) for _ in range(2)]
        Rtrs = [Res("trs0"), Res("trs1")]
        vst = [self.sb([128, 256], BF16) for _ in range(2)]
        Rvst = [Res("vst0"), Res("vst1")]
        XTd = self.scr["XT"].rearrange("(a p) t -> p a t", p=128)
        S = self.scr
        SR = self.sres
        kcnt = 0
        for c in range(8):
            j = c % 2
            P.dma("sp", lambda e, inc, c=c, j=j: inc(e.dma_start(out=xt[j][:], in_=XTd[:, :, c * 512:(c + 1) * 512])), 1,
                  reads=[SR["XT"]], writes=[Rxt[j]])
            for ft in range(8):
                pb = ft % 2
                ps = self.ps[pb]

                def mm(e, ft=ft, ps=ps, j=j):
                    ins = None
                    for a in range(8):
                        ins = e.matmul(ps[:], lhsT=WP[:, a, ft * 128:(ft + 1) * 128], rhs=xt[j][:, a, :], start=(a == 0),
                                       stop=(a == 7))
                    return ins
                P.op("pe", mm, reads=[RW, Rxt[j]], writes=[self.psr[pb]])
                fj = ft % 2
                P.op("act", lambda e, ps=ps, fj=fj: e.copy(out=fst[fj][:], in_=ps[:]), reads=[self.psr[pb]],
                     writes=[Rfst[fj]])
                dst = (S["UPT"] if ft < 4 else S["UST"])
                dres = SR["UPT"] if ft < 4 else SR["UST"]
                r0 = (ft % 4) * 128
                P.dma("sp", lambda e, inc, dst=dst, r0=r0, c=c, fj=fj: inc(e.dma_start(
                    out=dst[r0:r0 + 128, c * 512:(c + 1) * 512], in_=fst[fj][:])), 1, reads=[Rfst[fj]], writes=[dres])
            for tt in range(4):
                i = c * 4 + tt
                tj = i % 2
                T = tm[tj]
                for ch, (c0, n) in enumerate(((0, 512), (512, 512), (1024, 280))):
                    pb = 2 + (kcnt % 3)
                    kcnt += 1
                    ps = self.ps[pb]

                    def mm2(e, ps=ps, j=j, tt=tt, c0=c0, n=n):
                        ins = None
                        for a in range(8):
                            ins = e.matmul(ps[:, 0:n], lhsT=xt[j][:, a, tt * 128:(tt + 1) * 128],
                                           rhs=WP[:, a, 1024 + c0:1024 + c0 + n], start=(a == 0), stop=(a == 7))
                        return ins
                    P.op("pe", mm2, reads=[RW, Rxt[j]], writes=[self.psr[pb]])
                    P.op("act", lambda e, ps=ps, T=T, c0=c0, n=n: e.copy(out=T[:, c0:c0 + n], in_=ps[:, 0:n]),
                         reads=[self.psr[pb]], writes=[Rtm[tj]])
                H3 = T[:, 0:896].rearrange("p (h d) -> p h d", d=64)
                x1 = H3[:, :, 0:8]
                x2 = H3[:, :, 8:16]
                cosb = rope[:, i, 0:8].unsqueeze(1).to_broadcast([128, 14, 8])
                sinb = rope[:, i, 8:16].unsqueeze(1).to_broadcast([128, 14, 8])
                P.op("dve", lambda e, x1=x1, sinb=sinb: e.tensor_tensor(out=t1[:], in0=x1, in1=sinb, op=ALU.mult),
                     reads=[Rtm[tj], Rrope], writes=[Rt])
                P.op("dve", lambda e, x2=x2, sinb=sinb: e.tensor_tensor(out=t2[:], in0=x2, in1=sinb, op=ALU.mult),
                     reads=[Rtm[tj], Rrope, Rt], writes=[Rt])
                P.op("dve", lambda e, x1=x1, cosb=cosb: e.tensor_tensor(out=x1, in0=x1, in1=cosb, op=ALU.mult),
                     reads=[Rtm[tj], Rrope, Rt], writes=[Rtm[tj]])
                P.op("dve", lambda e, x2=x2, cosb=cosb: e.tensor_tensor(out=t3[:], in0=x2, in1=cosb, op=ALU.mult),
                     reads=[Rtm[tj], Rrope, Rt], writes=[Rt])
                P.op("dve", lambda e, x1=x1: e.tensor_tensor(out=x1, in0=x1, in1=t2[:], op=ALU.subtract),
                     reads=[Rtm[tj], Rt], writes=[Rtm[tj]])
                P.op("dve", lambda e, x2=x2: e.tensor_tensor(out=x2, in0=t1[:], in1=t3[:], op=ALU.add),
                     reads=[Rtm[tj], Rt], writes=[Rtm[tj]])
                for half in range(2):
                    pb = 6 + half
                    ps = self.ps[pb]

                    def tr(e, half=half, ps=ps, T=T):
                        ins = None
                        for q in range(4):
                            a = half * 4 + q
                            ins = e.transpose(out=ps[:, q * 128:(q + 1) * 128], in_=T[:, a * 128:(a + 1) * 128],
                                              identity=self.ident[:])
                        return ins
                    P.op("pe", tr, reads=[Rtm[tj], self.Rconst], writes=[self.psr[pb]])
                    P.op("act", lambda e, half=half, ps=ps, tj=tj: e.copy(
                        out=trs[tj][:, half * 4:(half + 1) * 4, :], in_=ps[:].rearrange("p (a b) -> p a b", b=128)),
                        reads=[self.psr[pb]], writes=[Rtrs[tj]])
                t0 = i * 128
                P.dma("sp", lambda e, inc, tj=tj, t0=t0: inc(e.dma_start(out=S["QT"][:, :, t0:t0 + 128],
                                                                       in_=trs[tj][:, 0:4, :])), 1,
                      reads=[Rtrs[tj]], writes=[SR["QT"]])
                for a, nm in ((4, "KCT"), (5, "KST"), (6, "KWT"), (7, "VCT")):
                    P.dma("sp", lambda e, inc, tj=tj, t0=t0, a=a, nm=nm: inc(e.dma_start(
                        out=S[nm][:, t0:t0 + 128], in_=trs[tj][:, a, :])), 1, reads=[Rtrs[tj]], writes=[SR[nm]])
                P.op("pool", lambda e, tj=tj, T=T: e.tensor_copy(out=vst[tj][:], in_=T[:, 1024:1280]), reads=[Rtm[tj]],
                     writes=[Rvst[tj]])
                P.dma("sp", lambda e, inc, tj=tj, t0=t0: inc(e.dma_start(out=S["VS"][t0:t0 + 128, :], in_=vst[tj][:, 0:128])),
                      1, reads=[Rvst[tj]], writes=[SR["VS"]])
                P.dma("sp", lambda e, inc, tj=tj, t0=t0: inc(e.dma_start(out=S["VW"][t0:t0 + 128, :], in_=vst[tj][:, 128:256])),
                      1, reads=[Rvst[tj]], writes=[SR["VW"]])
                P.op("act", lambda e, T=T: e.activation(out=T[:, 1280:1304], in_=T[:, 1280:1304], func=AF.Sigmoid),
                     reads=[Rtm[tj]], writes=[Rtm[tj]])
                P.dma("sp", lambda e, inc, T=T, t0=t0: inc(e.dma_start(out=S["GN"][t0:t0 + 128, :], in_=T[:, 1280:1304])),
                      1, reads=[Rtm[tj]], writes=[SR["GN"]])
        self.phase_end()

    def phase_pool(self, l):
        P = self.P
        S, SR = self.scr, self.sres
        wp = self.sb([128, 4, 128], BF16)
        Rwp = Res("wpool")
        self.load_cast([(wp[:, g, :], self.d["w_pool"][l, g], [128, 128]) for g in range(4)], Rwp, stage_cols=128)
        psc = self.sb([128, 4], F32)
        Rpsc = Res("psc")
        P.dma("sp", lambda e, inc: inc(e.dma_start(out=psc[:], in_=self.d["pool_scale"][l].rearrange("(g p) -> p g", p=128),
                                                       allow_slow_non_contiguous=True)),
              1, writes=[Rpsc])
        icn = self.sb([128, 64], F32)
        P.dma("sp", lambda e, inc: inc(e.dma_start(out=icn[:], in_=self.d["c_invcnt"][0:1, :].partition_broadcast(128))),
              1, writes=[Rpsc])
        H = 16
        U = [self.sb([128, H + L], F32) for _ in range(2)]
        A = [self.sb([128, H + L], F32) for _ in range(2)]
        Bb = [self.sb([128, H + L], F32) for _ in range(2)]
        Z = [self.sb([128, L], BF16) for _ in range(2)]
        tmp16 = [self.sb([128, 16], F32) for _ in range(2)]
        yst = [self.sb([128, 512], BF16) for _ in range(2)]
        Ryst = [Res("y0"), Res("y1")]
        RU = [Res("U0"), Res("U1")]
        for j in range(2):
            eng = "dve" if j == 0 else "pool"
            for buf in (U[j], A[j], Bb[j]):
                P.op(eng, lambda e, buf=buf: e.memset(buf[:, 0:H], 0.0), writes=[RU[j]])
        for g in range(4):
            j = g % 2
            eng = "dve" if j == 0 else "pool"
            w = 2 << g
            P.dma("sp", lambda e, inc, g=g, j=j: inc(e.dma_start(out=U[j][:, H:], in_=S["UPT"][g * 128:(g + 1) * 128, :])), 1,
                  reads=[SR["UPT"]], writes=[RU[j]])
            src = U[j]
            dsts = [A[j], Bb[j]]
            for k in range(g + 1):
                sh = 1 << k
                dst = dsts[k % 2]
                P.op(eng, lambda e, dst=dst, src=src, sh=sh: e.tensor_tensor(out=dst[:, H:], in0=src[:, H:],
                                                                              in1=src[:, H - sh:H + L - sh], op=ALU.add),
                     reads=[RU[j]], writes=[RU[j]])
                src = dst
            P.op("dve", lambda e, src=src, j=j, w=w: e.scalar_tensor_tensor(out=Z[j][:], in0=src[:, H:], scalar=1.0 / w,
                                                                          in1=U[j][:, H:], op0=ALU.mult,
                                                                          op1=ALU.subtract), reads=[RU[j]], writes=[RU[j]])
            P.op(eng, lambda e, src=src, j=j, g=g: e.tensor_tensor(out=tmp16[j][:], in0=src[:, H:H + 16], in1=icn[:, g * 16:(g + 1) * 16],
                                                                   op=ALU.mult), reads=[RU[j], Rpsc], writes=[RU[j]])
            P.op(eng, lambda e, j=j: e.tensor_tensor(out=Z[j][:, 0:16], in0=tmp16[j][:], in1=U[j][:, H:H + 16],
                                                     op=ALU.subtract), reads=[RU[j]], writes=[RU[j]])
            for c in range(8):
                pb = c % 2
                ps = self.ps[pb]
                P.op("pe", lambda e, ps=ps, g=g, j=j, c=c: e.matmul(ps[:], lhsT=wp[:, g, :], rhs=Z[j][:, c * 512:(c + 1) * 512],
                                                                  start=True, stop=True), reads=[Rwp, RU[j]],
                     writes=[self.psr[pb]])
                yj = c % 2
                P.op("act", lambda e, ps=ps, yj=yj, g=g: e.activation(out=yst[yj][:], in_=ps[:], func=AF.Copy,
                                                                     scale=psc[:, g:g + 1]), reads=[self.psr[pb], Rpsc],
                     writes=[Ryst[yj]])
                P.dma("sp", lambda e, inc, g=g, c=c, yj=yj: inc(e.dma_start(
                    out=S["YPT"][g * 128:(g + 1) * 128, c * 512:(c + 1) * 512], in_=yst[yj][:])), 1, reads=[Ryst[yj]],
                    writes=[SR["YPT"]])
        self.phase_end()

    def cmul(self, eng, outr, outi, ar, ai, br, bi, t1, t2, Rin, Rout, Rt):
        P = self.P
        tt = lambda o, a, b, op: (lambda e: e.tensor_tensor(out=o, in0=a, in1=b, op=op))
        P.op(eng, tt(t1, ar, br, ALU.mult), reads=Rin, writes=[Rt])
        P.op(eng, tt(t2, ai, bi, ALU.mult), reads=Rin + [Rt], writes=[Rt])
        P.op(eng, tt(outr, t1, t2, ALU.subtract), reads=[Rt], writes=[Rout])
        P.op(eng, tt(t1, ar, bi, ALU.mult), reads=Rin + [Rt, Rout], writes=[Rt])
        P.op(eng, tt(t2, ai, br, ALU.mult), reads=Rin + [Rt], writes=[Rt])
        P.op(eng, tt(outi, t1, t2, ALU.add), reads=[Rt], writes=[Rout])

    def phase_s5(self, l):
        P = self.P
        S, SR = self.scr, self.sres
        T = 512
        sm = lambda: self.sb([128, 16], F32)
        lam_n = self.sb([16, 256], F32)
        Rp = Res("s5prep")
        P.dma("sp", lambda e, inc: inc(e.dma_start(out=lam_n[:, 0:128], in_=self.d["ssm_lam_re"][l].rearrange("(j g) p -> j (g p)", g=2))),
              1, writes=[Rp])
        P.dma("sp", lambda e, inc: inc(e.dma_start(out=lam_n[:, 128:256], in_=self.d["ssm_lam_im"][l].rearrange("(j g) p -> j (g p)", g=2))),
              1, writes=[Rp])
        lr, li, stp, rho, th, sn, cs, t1, t2, t3, kr, ki, nki, den = [sm() for _ in range(14)]
        P.op("pe", lambda e: (e.transpose(out=self.ps[7][:, 0:16], in_=lam_n[:, 0:128], identity=self.ident[0:16, 0:16]),
                              e.transpose(out=self.ps[7][:, 16:32], in_=lam_n[:, 128:256], identity=self.ident[0:16, 0:16]))[1],
             reads=[Rp, self.Rconst], writes=[self.psr[7]])
        P.op("dve", lambda e: e.tensor_copy(out=lr[:], in_=self.ps[7][:, 0:16]), reads=[self.psr[7]], writes=[Rp])
        P.op("dve", lambda e: e.tensor_copy(out=li[:], in_=self.ps[7][:, 16:32]), reads=[self.psr[7], Rp], writes=[Rp])
        ldt = self.d["ssm_log_dt"][l:l + 1, :].rearrange("o (j g) -> o j g", g=2)
        for gl in range(2):
            P.dma("sp", lambda e, inc, gl=gl: inc(e.dma_start(out=stp[gl * 64:(gl + 1) * 64, :],
                                                            in_=ldt[:, :, gl].partition_broadcast(64),
                                                            allow_slow_non_contiguous=True)), 1, reads=[Rp], writes=[Rp])
        P.op("act", lambda e: e.activation(out=stp[:], in_=stp[:], func=AF.Exp), reads=[Rp], writes=[Rp])
        V = lambda fn: P.op("dve", fn, reads=[Rp], writes=[Rp])
        V(lambda e: e.tensor_tensor(out=t1[:], in0=lr[:], in1=stp[:], op=ALU.mult))
        P.op("act", lambda e: e.activation(out=rho[:], in_=t1[:], func=AF.Exp), reads=[Rp], writes=[Rp])
        V(lambda e: e.tensor_tensor(out=th[:], in0=li[:], in1=stp[:], op=ALU.mult))
        TWO_PI = 2.0 * math.pi
        ni = self.sb([128, 16], mybir.dt.int32)
        nf = sm()

        def reduce_turns(dst, shift):
            V(lambda e: e.tensor_scalar(out=dst[:], in0=th[:], scalar1=1.0 / TWO_PI, scalar2=shift, op0=ALU.mult, op1=ALU.add))
            V(lambda e: e.tensor_copy(out=ni[:], in_=dst[:]))
            V(lambda e: e.tensor_copy(out=nf[:], in_=ni[:]))
            V(lambda e: e.tensor_tensor(out=dst[:], in0=dst[:], in1=nf[:], op=ALU.subtract))
            V(lambda e: e.tensor_single_scalar(out=nf[:], in_=dst[:], scalar=0.5, op=ALU.is_gt))
            V(lambda e: e.tensor_tensor(out=dst[:], in0=dst[:], in1=nf[:], op=ALU.subtract))
            V(lambda e: e.tensor_single_scalar(out=nf[:], in_=dst[:], scalar=-0.5, op=ALU.is_lt))
            V(lambda e: e.tensor_tensor(out=dst[:], in0=dst[:], in1=nf[:], op=ALU.add))
        reduce_turns(t1, 0.0)
        reduce_turns(t2, 0.25)
        P.op("act", lambda e: e.activation(out=sn[:], in_=t1[:], func=AF.Sin, scale=TWO_PI), reads=[Rp], writes=[Rp])
        P.op("act", lambda e: e.activation(out=cs[:], in_=t2[:], func=AF.Sin, scale=TWO_PI), reads=[Rp], writes=[Rp])
        V(lambda e: e.tensor_tensor(out=t1[:], in0=rho[:], in1=cs[:], op=ALU.mult))
        V(lambda e: e.tensor_scalar_add(out=t1[:], in0=t1[:], scalar1=-1.0))
        V(lambda e: e.tensor_tensor(out=t2[:], in0=rho[:], in1=sn[:], op=ALU.mult))
        V(lambda e: e.tensor_tensor(out=den[:], in0=lr[:], in1=lr[:], op=ALU.mult))
        V(lambda e: e.tensor_tensor(out=t3[:], in0=li[:], in1=li[:], op=ALU.mult))
        V(lambda e: e.tensor_tensor(out=den[:], in0=den[:], in1=t3[:], op=ALU.add))
        V(lambda e: e.reciprocal(out=den[:], in_=den[:]))
        V(lambda e: e.tensor_tensor(out=kr[:], in0=t1[:], in1=lr[:], op=ALU.mult))
        V(lambda e: e.tensor_tensor(out=t3[:], in0=t2[:], in1=li[:], op=ALU.mult))
        V(lambda e: e.tensor_tensor(out=kr[:], in0=kr[:], in1=t3[:], op=ALU.add))
        V(lambda e: e.tensor_tensor(out=kr[:], in0=kr[:], in1=den[:], op=ALU.mult))
        V(lambda e: e.tensor_tensor(out=ki[:], in0=t2[:], in1=lr[:], op=ALU.mult))
        V(lambda e: e.tensor_tensor(out=t3[:], in0=t1[:], in1=li[:], op=ALU.mult))
        V(lambda e: e.tensor_tensor(out=ki[:], in0=ki[:], in1=t3[:], op=ALU.subtract))
        V(lambda e: e.tensor_tensor(out=ki[:], in0=ki[:], in1=den[:], op=ALU.mult))
        V(lambda e: e.tensor_scalar_mul(out=nki[:], in0=ki[:], scalar1=-1.0))
        Er = self.sb([128, 16, T], F32)
        Ei = self.sb([128, 16, T], F32)
        ETr, ETi, emr, emi = sm(), sm(), sm(), sm()
        RE = Res("E")
        Rt = Res("ctmp")
        P.op("pool", lambda e: e.memset(Er[:, :, 0:1], 1.0), writes=[RE])
        P.op("pool", lambda e: e.memset(Ei[:, :, 0:1], 0.0), reads=[RE], writes=[RE])
        P.op("pool", lambda e: e.tensor_copy(out=Er[:, :, 1:2], in_=cs[:].unsqueeze(2)), reads=[Rp, RE], writes=[RE])
        P.op("pool", lambda e: e.tensor_copy(out=Ei[:, :, 1:2], in_=sn[:].unsqueeze(2)), reads=[Rp, RE], writes=[RE])
        LB = self.sb([128, 16, 2, 128], BF16)
        LC = self.sb([128, 16, 2, 128], BF16)
        RL = Res("LBC")
        ct = self.sb([128, 128], F32)
        dsk = self.sb([128, 4], F32)
        bgl = self.sb([128, 4], F32)
        Wg = self.sb([128, 4, 512], BF16)
        RWg = Res("wglu")
        wgl = self.d["w_glu"][l].rearrange("(a p) c -> p a c", p=128)
        self.load_cast([(Wg[:, :, 0:256], wgl[:, :, 0:256], [128, 4, 256]), (Wg[:, :, 256:512], wgl[:, :, 256:512], [128, 4, 256])],
                       RWg, stage_cols=1024)
        mark = self.cur
        big1 = self.sb([128, 16, 256], F32)
        big2 = self.sb([128, 16, 256], F32)
        m = 2
        while m < T:
            self.cmul("dve", emr[:], emi[:], Er[:, :, m - 1], Ei[:, :, m - 1], cs[:], sn[:], t1[:], t2[:], [RE, Rp], RE, Rt)
            bc = lambda a: a[:].unsqueeze(2).to_broadcast([128, 16, m])
            self.cmul("dve", Er[:, :, m:2 * m], Ei[:, :, m:2 * m], Er[:, :, 0:m], Ei[:, :, 0:m], bc(emr), bc(emi),
                      big1[:, :, 0:m], big2[:, :, 0:m], [RE], RE, Rt)
            m *= 2
        self.cmul("dve", ETr[:], ETi[:], Er[:, :, T - 1], Ei[:, :, T - 1], cs[:], sn[:], t1[:], t2[:], [RE, Rp], RE, Rt)
        Bp = [self.sb([128, 16, 128], F32) for _ in range(2)]
        Cp = [self.sb([128, 16, 128], F32) for _ in range(2)]
        Rpad = Res("pads")
        for t_ in Bp + Cp:
            P.op("pool", lambda e, t_=t_: e.memset(t_[:], 0.0), writes=[Rpad])
        for ri, (bn, cn) in enumerate((("ssm_b_re", "ssm_c_re"), ("ssm_b_im", "ssm_c_im"))):
            bsrc = self.d[bn][l].rearrange("(a q g) p c -> g q p a c", q=4, g=2)
            csrc = self.d[cn][l].rearrange("(a q g) c p -> g q c a p", q=4, g=2)
            for gl in range(2):
                for q in range(4):
                    P.dma("sp", lambda e, inc, ri=ri, gl=gl, q=q, bsrc=bsrc: inc(e.dma_start(
                        out=Bp[ri][gl * 64:(gl + 1) * 64, q:16:4, q * 32 + gl * 16:q * 32 + gl * 16 + 16], in_=bsrc[gl, q])), 1,
                        reads=[Rpad], writes=[Rpad])
                    P.dma("sp", lambda e, inc, ri=ri, gl=gl, q=q, csrc=csrc: inc(e.dma_start(
                        out=Cp[ri][q * 32 + gl * 16:q * 32 + gl * 16 + 16, q:16:4, gl * 64:(gl + 1) * 64], in_=csrc[gl, q])), 1,
                        reads=[Rpad], writes=[Rpad])
        for j in range(16):
            pb = 6 + j % 2
            ps = self.ps[pb]

            def tr(e, j=j, ps=ps):
                ins = None
                for q, src in enumerate((Bp[0], Bp[1], Cp[0], Cp[1])):
                    ins = e.transpose(out=ps[:, q * 128:(q + 1) * 128], in_=src[:, j, :], identity=self.ident[:])
                return ins
            P.op("pe", tr, reads=[Rpad, self.Rconst], writes=[self.psr[pb]])
            P.op("act", lambda e, j=j, ps=ps: e.copy(out=LB[:, j, :, :], in_=ps[:, 0:256].rearrange("p (a b) -> p a b", b=128)),
                 reads=[self.psr[pb]], writes=[RL])
            P.op("dve", lambda e, j=j, ps=ps: e.tensor_scalar(out=ct[:], in0=ps[:, 384:512], scalar1=ki[:, j:j + 1], scalar2=None,
                                                             op0=ALU.mult), reads=[self.psr[pb], Rp, RL], writes=[Rt])
            P.op("dve", lambda e, j=j, ps=ps: e.scalar_tensor_tensor(out=LC[:, j, 0, :], in0=ps[:, 256:384], scalar=kr[:, j:j + 1],
                                                                    in1=ct[:], op0=ALU.mult, op1=ALU.subtract),
                 reads=[self.psr[pb], Rp, Rt], writes=[RL])
            P.op("dve", lambda e, j=j, ps=ps: e.tensor_scalar(out=ct[:], in0=ps[:, 384:512], scalar1=kr[:, j:j + 1], scalar2=None,
                                                             op0=ALU.mult), reads=[self.psr[pb], Rp, RL], writes=[Rt])
            P.op("dve", lambda e, j=j, ps=ps: e.scalar_tensor_tensor(out=LC[:, j, 1, :], in0=ps[:, 256:384], scalar=nki[:, j:j + 1],
                                                                    in1=ct[:], op0=ALU.mult, op1=ALU.subtract),
                 reads=[self.psr[pb], Rp, Rt], writes=[RL])
        P.dma("sp", lambda e, inc: inc(e.dma_start(out=dsk[:], in_=self.d["ssm_d"][l].rearrange("g c -> (g c)").rearrange("(a p) -> p a", p=128),
                                                   allow_slow_non_contiguous=True)), 1, writes=[Rp])
        P.dma("sp", lambda e, inc: inc(e.dma_start(out=bgl[:], in_=self.d["b_glu"][l].rearrange("(a p) -> p a", p=128),
                                                   allow_slow_non_contiguous=True)), 1, writes=[Rp])
        self.P.barrier()
        self.cur = mark
        uf = [self.sb([128, 4, T], F32) for _ in range(2)]
        ub = [self.sb([128, 4, T], BF16) for _ in range(2)]
        Ruf = [Res("uf0"), Res("uf1")]
        Rub = [Res("ub0"), Res("ub1")]
        NB = 3
        m1 = [self.sb([128, T], F32) for _ in range(NB)]
        m2 = [self.sb([128, T], F32) for _ in range(NB)]
        m3 = [self.sb([128, T], F32) for _ in range(NB)]
        m4 = [self.sb([128, T], F32) for _ in range(NB)]
        gr = [self.sb([128, T], F32) for _ in range(NB)]
        gi = [self.sb([128, T], F32) for _ in range(NB)]
        Hr = [self.sb([128, T], BF16) for _ in range(NB)]
        Hi = [self.sb([128, T], BF16) for _ in range(NB)]
        Rm4 = [[Res("m%d_%d" % (i, q)) for q in range(4)] for i in range(NB)]
        Rgs = [[Res("gr%d" % i), Res("gi%d" % i)] for i in range(NB)]
        RHi = [Res("Hi%d" % i) for i in range(NB)]
        Rgl2 = Res("glast2")
        Rg = [Res("g%d" % i) for i in range(NB)]
        RH = [Res("H%d" % i) for i in range(NB)]
        glr, gli, gir, gii = sm(), sm(), sm(), sm()
        Rgl = Res("glast")
        Rgi = Res("ginit")
        P.op("pool", lambda e: e.memset(gir[:], 0.0), writes=[Rgi])
        P.op("pool", lambda e: e.memset(gii[:], 0.0), reads=[Rgi], writes=[Rgi])
        ysb = self.sb([128, T], F32)
        Rysb = Res("ysb")
        zf = self.sb([128, 4, T], F32)
        zb = self.sb([128, 4, T], BF16)
        Rzf, Rzb = Res("zf"), Res("zb")
        sgl = self.sb([128, T], F32)
        Rsgl = Res("sgl")
        ost = [self.sb([128, T], BF16) for _ in range(2)]
        Rost = [Res("ost0"), Res("ost1")]
        USTd = S["UST"].rearrange("(a p) t -> p a t", p=128)
        hm1, hm2, hm3, hm4 = m1, m2, m3, m4
        state = {"n": 0}

        def load_chunk(c):
            cj = c % 2
            P.dma("sp", lambda e, inc, c=c, cj=cj: inc(e.dma_start(out=uf[cj][:], in_=USTd[:, :, c * T:(c + 1) * T])), 1,
                  reads=[SR["UST"]], writes=[Ruf[cj]])
            P.op("pool", lambda e, cj=cj: e.tensor_copy(out=ub[cj][:], in_=uf[cj][:]), reads=[Ruf[cj]], writes=[Rub[cj]])

        def stageA(c, j):
            cj = c % 2
            kt = j // 4
            n = state["n"]
            state["n"] += 1
            b_ = n % NB
            pa, pbk = ((0, 1), (2, 3))[n % 2]
            P.op("pe", lambda e, j=j, kt=kt, cj=cj, pa=pa: e.matmul(self.ps[pa][:], lhsT=LB[:, j, 0, :], rhs=ub[cj][:, kt, :],
                                                                 start=True, stop=True), reads=[RL, Rub[cj]], writes=[self.psr[pa]])
            P.op("pe", lambda e, j=j, kt=kt, cj=cj, pbk=pbk: e.matmul(self.ps[pbk][:], lhsT=LB[:, j, 1, :], rhs=ub[cj][:, kt, :],
                                                                   start=True, stop=True), reads=[RL, Rub[cj]],
                 writes=[self.psr[pbk]])
            DV = lambda o, a, bb, op, rd, wr: P.op("dve", lambda e: e.tensor_tensor(out=o, in0=a, in1=bb, op=op), reads=rd, writes=wr)
            Pr_, Pi_ = self.ps[pa][:], self.ps[pbk][:]
            R1, R2, R3, R4 = Rm4[b_]
            DV(m3[b_][:], Pi_, Er[:, j, :], ALU.mult, [RE, self.psr[pbk]], [R3])
            DV(m4[b_][:], Pr_, Ei[:, j, :], ALU.mult, [RE, self.psr[pa]], [R4])
            DV(m1[b_][:], Pr_, Er[:, j, :], ALU.mult, [RE, self.psr[pa]], [R1])
            DV(m2[b_][:], Pi_, Ei[:, j, :], ALU.mult, [RE, self.psr[pbk]], [R2])
            DV(m3[b_][:], m3[b_][:], m4[b_][:], ALU.subtract, [R4], [R3])
            DV(m1[b_][:], m1[b_][:], m2[b_][:], ALU.add, [R2], [R1])
            P.op("dve", lambda e, b_=b_, j=j: e.tensor_tensor_scan(
                out=gi[b_][:], data0=rho[:, j:j + 1].to_broadcast([128, T]), data1=m3[b_][:], initial=gii[:, j:j + 1],
                op0=ALU.mult, op1=ALU.add), reads=[R3, Rp, Rgi], writes=[Rgs[b_][1]])
            P.op("dve", lambda e, b_=b_, j=j: e.tensor_tensor_scan(
                out=gr[b_][:], data0=rho[:, j:j + 1].to_broadcast([128, T]), data1=m1[b_][:], initial=gir[:, j:j + 1],
                op0=ALU.mult, op1=ALU.add), reads=[R1, Rp, Rgi], writes=[Rgs[b_][0]])
            return b_

        def stageB(c, j, b_):
            cj = c % 2
            kt = j // 4
            R1, R2, R3, R4 = Rm4[b_]
            Rgr, Rgi_ = Rgs[b_]
            P.op("pool", lambda e, b_=b_, j=j: e.tensor_copy(out=glr[:, j:j + 1], in_=gr[b_][:, T - 1:T]), reads=[Rgr], writes=[Rgl])
            P.op("pool", lambda e, b_=b_, j=j: e.tensor_copy(out=gli[:, j:j + 1], in_=gi[b_][:, T - 1:T]), reads=[Rgi_], writes=[Rgl2])
            PL = lambda o, a, bb, rd, wr: P.op("pool", lambda e: e.tensor_tensor(out=o, in0=a, in1=bb, op=ALU.mult), reads=rd, writes=wr)
            PL(hm2[b_][:], Ei[:, j, :], gi[b_][:], [RE, Rgi_], [R2])
            PL(hm3[b_][:], Er[:, j, :], gi[b_][:], [RE, Rgi_], [R3])
            PL(hm1[b_][:], Er[:, j, :], gr[b_][:], [RE, Rgr], [R1])
            PL(hm4[b_][:], Ei[:, j, :], gr[b_][:], [RE, Rgr], [R4])
            P.op("dve", lambda e, b_=b_: e.tensor_tensor(out=Hr[b_][:], in0=hm1[b_][:], in1=hm2[b_][:], op=ALU.subtract),
                 reads=[R1, R2], writes=[RH[b_]])
            P.op("dve", lambda e, b_=b_: e.tensor_tensor(out=Hi[b_][:], in0=hm3[b_][:], in1=hm4[b_][:], op=ALU.add),
                 reads=[R3, R4], writes=[RHi[b_]])
            py = 4 + kt % 2

            def mmy(e, j=j, b_=b_, py=py):
                e.matmul(self.ps[py][:], lhsT=LC[:, j, 0, :], rhs=Hr[b_][:], start=(j % 4 == 0), stop=False)
                return e.matmul(self.ps[py][:], lhsT=LC[:, j, 1, :], rhs=Hi[b_][:], start=False, stop=(j % 4 == 3))
            P.op("pe", mmy, reads=[RL, RH[b_], RHi[b_]], writes=[self.psr[py]], self_ok=(j % 4 != 0))
            if j % 4 == 3:
                P.op("dve", lambda e, kt=kt, cj=cj, py=py: e.scalar_tensor_tensor(
                    out=ysb[:], in0=uf[cj][:, kt, :], scalar=dsk[:, kt:kt + 1], in1=self.ps[py][:], op0=ALU.mult, op1=ALU.add),
                    reads=[self.psr[py], Ruf[cj], Rp], writes=[Rysb])
                P.op("act", lambda e, kt=kt: e.activation(out=zf[:, kt, :], in_=ysb[:], func=AF.Gelu_apprx_tanh),
                     reads=[Rysb], writes=[Rzf])
                P.op("act", lambda e, kt=kt: e.copy(out=zb[:, kt, :], in_=zf[:, kt, :]), reads=[Rzf], writes=[Rzb])

        load_chunk(0)
        for c in range(L // T):
            if c + 1 < L // T:
                load_chunk(c + 1)
            prev = None
            for j in range(16):
                b_ = stageA(c, j)
                if prev is not None:
                    stageB(c, prev[0], prev[1])
                prev = (j, b_)
            stageB(c, prev[0], prev[1])
            self.cmul("dve", gir[:], gii[:], ETr[:], ETi[:], glr[:], gli[:], t1[:], t2[:], [RE, Rgl, Rgl2], Rgi, Rt)
            for ot in range(4):
                def mmg(e, ot=ot):
                    ins = None
                    for a in range(4):
                        ins = e.matmul(self.ps[6][:], lhsT=Wg[:, a, ot * 128:(ot + 1) * 128], rhs=zb[:, a, :], start=(a == 0),
                                       stop=(a == 3))
                    return ins
                P.op("pe", mmg, reads=[RWg, Rzb], writes=[self.psr[6]])
                P.op("act", lambda e, ot=ot: e.activation(out=sgl[:], in_=self.ps[6][:], func=AF.Sigmoid, bias=bgl[:, ot:ot + 1]),
                     reads=[self.psr[6], Rp], writes=[Rsgl])
                oj = ot % 2
                P.op("dve", lambda e, ot=ot, oj=oj: e.tensor_tensor(out=ost[oj][:], in0=zf[:, ot, :], in1=sgl[:], op=ALU.mult),
                     reads=[Rzf, Rsgl], writes=[Rost[oj]])
                P.dma("sp", lambda e, inc, ot=ot, oj=oj, c=c: inc(e.dma_start(
                    out=S["YST"][ot * 128:(ot + 1) * 128, c * T:(c + 1) * T], in_=ost[oj][:])), 1, reads=[Rost[oj]],
                    writes=[SR["YST"]])
        self.phase_end()

    def phase_nsa(self, l):
        P = self.P
        S, SR = self.scr, self.sres
        ps, psr = self.ps, self.psr
        QT = self.sb([128, 4, L], BF16)
        KT = {n: self.sb([128, L], BF16) for n in ("KCT", "VCT")}
        RQ = Res("QTs")
        P.dma("sp", lambda e, inc: inc(e.dma_start(out=QT[:], in_=S["QT"][:, :, :])), 1, reads=[SR["QT"]], writes=[RQ])
        for n in KT:
            P.dma("sp", lambda e, inc, n=n: inc(e.dma_start(out=KT[n][:], in_=S[n][:, :])), 1, reads=[SR[n]], writes=[RQ])
        KWz = [self.sb([128, L], BF16) for _ in range(2)]
        KE = [self.sb([128, L], BF16) for _ in range(2)]
        for k in range(2):
            o = 1 - k
            P.op("pool", lambda e, k=k, o=o: e.memset(KWz[k][o * 64:(o + 1) * 64, :], 0.0), writes=[RQ])
            P.dma("sp", lambda e, inc, k=k: inc(e.dma_start(out=KWz[k][k * 64:(k + 1) * 64, :], in_=S["KWT"][k * 64:(k + 1) * 64, :])),
                  1, reads=[SR["KWT"], RQ], writes=[RQ])
            P.dma("sp", lambda e, inc, k=k: inc(e.dma_start(out=KE[k][k * 64:(k + 1) * 64, :], in_=S["KST"][k * 64:(k + 1) * 64, :])),
                  1, reads=[SR["KST"], RQ], writes=[RQ])
            P.dma("sp", lambda e, inc, k=k, o=o: inc(e.dma_start(out=KE[k][o * 64:(o + 1) * 64, :], in_=self.d["c_ebig"][:, :])),
                  1, reads=[RQ], writes=[RQ])
        Va = {n: self.sb([128, NT, 2, 65], BF16) for n in ("VS", "VW")}
        for n in Va:
            P.op("pool", lambda e, n=n: e.memset(Va[n][:, :, :, 64:65], 1.0), writes=[RQ])
            for k in range(2):
                P.dma("sp", lambda e, inc, n=n, k=k: inc(e.dma_start(
                    out=Va[n][:, :, k, 0:64], in_=S[n].rearrange("(a p) c -> p a c", p=128)[:, :, k * 64:(k + 1) * 64])), 1,
                    reads=[SR[n], RQ], writes=[RQ])
        G = self.sb([128, NT, 24], F32)
        P.dma("sp", lambda e, inc: inc(e.dma_start(out=G[:], in_=S["GN"].rearrange("(a p) c -> p a c", p=128))), 1,
              reads=[SR["GN"]], writes=[RQ])
        W1 = {t: self.sb([128, 32, 128], BF16) for t in "kv"}
        W2kd = self.sb([128, 128], BF16)
        W2v = self.sb([128, 64], BF16)
        RWc = Res("Wc")
        pieces = []
        for t, nm in (("k", "cmp_wk1"), ("v", "cmp_wv1")):
            src = self.d[nm][l].rearrange("(l d) h -> d l h", d=64)
            for half in range(2):
                for o in range(0, 32, 8):
                    pieces.append((W1[t][half * 64:(half + 1) * 64, o:o + 8, :], src[:, o:o + 8, :], [64, 8, 128], half * 64))
        pieces.append((W2kd[:, 0:64], self.d["cmp_wk2"][l], [128, 64]))
        pieces.append((W2kd[:, 64:128], self.d["cmp_wk2"][l], [128, 64]))
        pieces.append((W2v[:], self.d["cmp_wv2"][l], [128, 64]))
        self.load_cast(pieces, RWc, stage_cols=1024)
        pe_n = self.sb([32, 256], F32)
        Rpe = Res("pe")
        for ti, nm in enumerate(("cmp_pe_k", "cmp_pe_v")):
            for half in range(2):
                P.dma("sp", lambda e, inc, ti=ti, nm=nm, half=half: inc(e.dma_start(
                    out=pe_n[:, ti * 128 + half * 64:ti * 128 + half * 64 + 64], in_=self.d[nm][l])), 1, writes=[Rpe])
        peT = self.sb([128, 2, 32], BF16)
        P.op("pe", lambda e: (e.transpose(out=ps[7][:, 0:32], in_=pe_n[:, 0:128], identity=self.ident[0:32, 0:32]),
                              e.transpose(out=ps[7][:, 32:64], in_=pe_n[:, 128:256], identity=self.ident[0:32, 0:32]))[1],
             reads=[Rpe, self.Rconst], writes=[psr[7]])
        P.op("dve", lambda e: e.tensor_copy(out=peT[:], in_=ps[7][:, 0:64].rearrange("p (a b) -> p a b", b=32)), reads=[psr[7]],
             writes=[Rpe])
        cb = self.sb([128, 2], F32)
        for ti, t in enumerate("kv"):
            def mmb(e, ti=ti, t=t):
                ins = None
                for li_ in range(32):
                    ins = e.matmul(ps[7][:, 64 + ti:65 + ti], lhsT=W1[t][0:64, li_, :], rhs=peT[0:64, ti, li_:li_ + 1],
                                   start=(li_ == 0), stop=(li_ == 31))
                return ins
            P.op("pe", mmb, reads=[RWc, Rpe], writes=[psr[7]])
        P.op("dve", lambda e: e.tensor_copy(out=cb[:], in_=ps[7][:, 64:66]), reads=[psr[7]], writes=[Rpe])
        KcT = self.sb([128, 2, 256], BF16)
        rcmp = self.sb([128, 2, 2, 129], BF16)
        Rcmp = Res("cmpops")
        P.op("pool", lambda e: e.memset(KcT[:], 0.0), writes=[Rcmp])
        P.op("pool", lambda e: e.memset(rcmp[:], 0.0), reads=[Rcmp], writes=[Rcmp])
        P.op("pool", lambda e: e.memset(rcmp[:, :, :, 64:65], 1.0), reads=[Rcmp], writes=[Rcmp])
        ovf = self.sb([128, 2, 64], F32)
        P.dma("sp", lambda e, inc: inc(e.dma_start(out=ovf[:], in_=self.d["c_ov"].rearrange("(a p) s -> p a s", p=128))), 1,
              writes=[Rpe])
        for k in range(2):
            P.op("pool", lambda e, k=k: e.tensor_copy(out=rcmp[:, :, k, 65:129], in_=ovf[:]), reads=[Rpe, Rcmp], writes=[Rcmp])
        hid = [self.sb([128, 256], BF16) for _ in range(2)]
        Rhid = [Res("hid0"), Res("hid1")]
        for hj in range(2):
            P.op("pool", lambda e, hj=hj: e.memset(hid[hj][:], 0.0), writes=[Rhid[hj]])
        ci = 0
        for ti, (t, srcn) in enumerate((("k", "KCT"), ("v", "VCT"))):
            for k in range(2):
                hj = ci % 2
                pb = ci % 2
                ci += 1

                def mmh(e, t=t, srcn=srcn, k=k, pb=pb):
                    ins = None
                    for li_ in range(32):
                        ins = e.matmul(ps[pb][:, 0:255], lhsT=W1[t][k * 64:(k + 1) * 64, li_, :],
                                       rhs=KT[srcn][k * 64:(k + 1) * 64, li_:li_ + 4065:16], start=(li_ == 0), stop=(li_ == 31))
                    return ins
                P.op("pe", mmh, reads=[RWc, RQ], writes=[psr[pb]])
                P.op("act", lambda e, hj=hj, pb=pb, ti=ti: e.activation(out=hid[hj][:, 0:255], in_=ps[pb][:, 0:255],
                                                                      func=AF.Gelu_apprx_tanh, bias=cb[:, ti:ti + 1]),
                     reads=[psr[pb], Rpe], writes=[Rhid[hj]])
                if t == "k":
                    P.op("pe", lambda e, hj=hj: e.matmul(ps[2][:, 0:255], lhsT=W2kd[:], rhs=hid[hj][:, 0:255], start=True, stop=True),
                         reads=[RWc, Rhid[hj]], writes=[psr[2]])
                    P.op("dve", lambda e, k=k: e.tensor_copy(out=KcT[k * 64:(k + 1) * 64, k, 0:255],
                                                             in_=ps[2][k * 64:(k + 1) * 64, 0:255]), reads=[psr[2], Rcmp],
                         writes=[Rcmp])
                else:
                    for nt, rows in ((0, 128), (1, 127)):
                        P.op("pe", lambda e, hj=hj, nt=nt, rows=rows: e.matmul(
                            ps[3][0:rows, nt * 64:(nt + 1) * 64], lhsT=hid[hj][:, nt * 128:nt * 128 + rows], rhs=W2v[:], start=True,
                            stop=True), reads=[RWc, Rhid[hj]], writes=[psr[3]])
                        P.op("dve", lambda e, k=k, nt=nt, rows=rows: e.tensor_copy(out=rcmp[0:rows, nt, k, 0:64],
                                                                                 in_=ps[3][0:rows, nt * 64:(nt + 1) * 64]),
                             reads=[psr[3], Rcmp], writes=[Rcmp])
        import os
        if os.environ.get("NSA_UNITS") == "0":
            self.phase_end()
            return
        PT = [self.sb([128, 512], BF16) for _ in range(3)]
        RPT = [Res("PT%d" % i) for i in range(3)]
        yt = [self.sb([128, 512], F32) for _ in range(2)]
        Ryt = [Res("yt0"), Res("yt1")]
        yT = [self.sb([128, 4, 128], BF16) for _ in range(2)]
        RyT = [Res("yT0"), Res("yT1")]
        rsc = self.sb([128, 4], F32)
        wgt = self.sb([128, 4], F32)
        acc = self.sb([128, 64], F32)
        sc = self.sb([128, 64], F32)
        sc2 = self.sb([128, 64], F32)
        m8 = self.sb([128, 16], F32)
        biasf = self.sb([128, 128], F32)
        Rk = [self.sb([128, 4, 128], BF16) for _ in range(2)]
        Rpost = Res("post")
        Rbias = Res("biasf")
        RbT = [Res("R0"), Res("R1")]
        units = []
        deferred = []

        def flush_deferred():
            for f in deferred:
                f()
            deferred.clear()

        for qt in range(NT):
            for k in range(2):
                QTt = QT[:, :, qt * 128:(qt + 1) * 128]
                yj = qt % 2
                yacc = yt[yj][:, k * 256:(k + 1) * 256].rearrange("p (h d) -> p h d", d=64)
                gsl = lambda b, qt=qt, k=k: G[:, qt, k * 12 + b:k * 12 + 12:3]

                def post_cmp(qt=qt, k=k, yacc=yacc, gsl=gsl, yj=yj):
                    Dv = lambda fn, rd, wr: P.op("dve", fn, reads=rd, writes=wr)
                    for bi, b_ in enumerate((3, 4)):
                        Dv(lambda e, bi=bi, b_=b_: e.tensor_scalar(out=rsc[:, bi * 2:bi * 2 + 2], in0=ps[b_][:, 64:64 + 129 + 1:129],
                                                                    scalar1=1e-30, scalar2=None, op0=ALU.max), [psr[b_], Rpost], [Rpost])
                    Dv(lambda e: e.reciprocal(out=rsc[:], in_=rsc[:]), [Rpost], [Rpost])
                    for h in range(4):
                        b_ = 3 + h // 2
                        o = (h % 2) * 129
                        if h == 0:
                            Dv(lambda e, b_=b_, o=o: e.tensor_scalar(out=acc[:], in0=ps[b_][:, o + 65:o + 129], scalar1=rsc[:, 0:1],
                                                                     scalar2=None, op0=ALU.mult), [psr[b_], Rpost], [Rpost])
                        else:
                            Dv(lambda e, b_=b_, o=o, h=h: e.scalar_tensor_tensor(out=acc[:], in0=ps[b_][:, o + 65:o + 129],
                                                                                 scalar=rsc[:, h:h + 1], in1=acc[:], op0=ALU.mult,
                                                                                 op1=ALU.add), [psr[b_], Rpost], [Rpost])
                    Dv(lambda e: e.tensor_tensor(out=wgt[:], in0=rsc[:], in1=gsl(0), op=ALU.mult), [Rpost, RQ], [Rpost])
                    for h in range(4):
                        b_ = 3 + h // 2
                        o = (h % 2) * 129
                        Dv(lambda e, b_=b_, o=o, h=h: e.tensor_scalar(out=yacc[:, h, :], in0=ps[b_][:, o:o + 64], scalar1=wgt[:, h:h + 1],
                                                                       scalar2=None, op0=ALU.mult), [psr[b_], Rpost, Ryt[yj]], [Ryt[yj]])
                    Dv(lambda e: e.tensor_copy(out=sc[:], in_=acc[:]), [Rpost], [Rpost])
                    for e_ in range(2):
                        cur = 2 * qt + e_
                        rows = slice(e_ * 64, (e_ + 1) * 64)
                        if cur + 1 < 64:
                            Dv(lambda e, rows=rows, cur=cur: e.memset(sc[rows, cur + 1:64], -1e6), [Rpost], [Rpost])
                        Dv(lambda e, rows=rows: e.memset(sc[rows, 0:1], 1e6), [Rpost], [Rpost])
                        lo = max(cur - 1, 0)
                        Dv(lambda e, rows=rows, lo=lo, cur=cur: e.memset(sc[rows, lo:cur + 1], 1e6), [Rpost], [Rpost])
                    Dv(lambda e: e.max(out=m8[:, 0:8], in_=sc[:]), [Rpost], [Rpost])
                    Dv(lambda e: e.match_replace(out=sc2[:], in_to_replace=m8[:, 0:8], in_values=sc[:], imm_value=-1e30), [Rpost],
                       [Rpost])
                    Dv(lambda e: e.max(out=m8[:, 8:16], in_=sc2[:]), [Rpost], [Rpost])
                    for hh in range(2):
                        Dv(lambda e, hh=hh: e.tensor_scalar(out=biasf[:, hh * 64:(hh + 1) * 64], in0=sc[:], scalar1=m8[:, 15:16],
                                                            scalar2=-BIG, op0=ALU.is_lt, op1=ALU.mult), [Rpost, Rbias], [Rbias])

                def pre_sel(k=k, qt=qt):
                    flush_deferred()
                    o = 1 - k
                    P.op("pe", lambda e: e.transpose(out=ps[7][:, 0:128], in_=biasf[:], identity=self.ident[:]),
                         reads=[Rbias, self.Rconst], writes=[psr[7]])
                    P.op("pool", lambda e: e.tensor_copy(out=Rk[k][k * 64:(k + 1) * 64, :, :],
                                                         in_=QT[k * 64:(k + 1) * 64, :, qt * 128:(qt + 1) * 128]),
                         reads=[RQ], writes=[RbT[k]])
                    P.op("act", lambda e: e.copy(out=Rk[k][o * 64:(o + 1) * 64, :, :],
                                                 in_=ps[7][o * 64:(o + 1) * 64, 0:128].unsqueeze(1).to_broadcast([64, 4, 128])),
                         reads=[psr[7]], writes=[RbT[k]])

                def post_branch(bank, b, yacc=yacc, gsl=gsl, yj=yj):
                    def f():
                        Dv = lambda fn, rd, wr: P.op("dve", fn, reads=rd, writes=wr)
                        Dv(lambda e: e.reciprocal(out=rsc[:], in_=ps[bank][:, 64:64 + 3 * 65 + 1:65]), [psr[bank], Rpost], [Rpost])
                        Dv(lambda e: e.tensor_tensor(out=wgt[:], in0=rsc[:], in1=gsl(b), op=ALU.mult), [Rpost, RQ], [Rpost])
                        for h in range(4):
                            Dv(lambda e, h=h: e.scalar_tensor_tensor(out=yacc[:, h, :], in0=ps[bank][:, h * 65:h * 65 + 64],
                                                                     scalar=wgt[:, h:h + 1], in1=yacc[:, h, :], op0=ALU.mult,
                                                                     op1=ALU.add), [psr[bank], Rpost, Ryt[yj]], [Ryt[yj]])
                    return f

                def post_qt(qt=qt, yj=yj):
                    def tr_store():
                        def tr(e):
                            ins = None
                            for a in range(4):
                                ins = e.transpose(out=ps[7][:, a * 128:(a + 1) * 128], in_=yt[yj][:, a * 128:(a + 1) * 128],
                                                  identity=self.ident[:])
                            return ins
                        P.op("pe", tr, reads=[Ryt[yj], self.Rconst], writes=[psr[7]])
                        P.op("act", lambda e: e.copy(out=yT[yj][:], in_=ps[7][:].rearrange("p (a b) -> p a b", b=128)),
                             reads=[psr[7]], writes=[RyT[yj]])
                        P.dma("sp", lambda e, inc: inc(e.dma_start(
                            out=S["YNT"].rearrange("(a p) t -> p a t", p=128)[:, :, qt * 128:(qt + 1) * 128], in_=yT[yj][:])), 1,
                            reads=[RyT[yj]], writes=[SR["YNT"]])
                    return lambda: deferred.append(tr_store)

                nts = [0] if qt < 16 else [0, 1]
                for nt in nts:
                    rows = 128 if nt == 0 else 127
                    full = (128 * nt + rows - 1) <= 8 * qt - 2
                    mask = None if full else dict(pattern=[[0, 4], [1, 128]], base=128 * qt - 2048 * nt - 31, cm=-16)
                    units.append(dict(kind="cmp", rows=rows, lhsT=KcT[:, k, nt * 128:nt * 128 + rows], rhs=QTt,
                                      mask=mask, v=rcmp[0:rows, nt, k, :], first=(nt == 0), last=(nt == nts[-1]), pre=None,
                                      post=(post_cmp if nt == nts[-1] else None)))
                cmp_last_idx = len(units) - 1
                wk = list(range(max(0, qt - 4), qt + 1))
                for kt in wk:
                    mask = None
                    if kt == qt:
                        mask = dict(pattern=[[0, 4], [1, 128]], base=0, cm=-1)
                    elif kt == qt - 4:
                        mask = dict(pattern=[[0, 4], [-1, 128]], base=-1, cm=1)
                    units.append(dict(kind="win", rows=128, lhsT=KWz[k][:, kt * 128:(kt + 1) * 128], rhs=QTt,
                                      mask=mask, v=Va["VW"][:, kt, k, :], first=(kt == wk[0]), last=(kt == qt), pre=None,
                                      post=(post_branch(6, 2) if kt == qt else None), bank=6))
                for kt in range(qt + 1):
                    mask = dict(pattern=[[0, 4], [1, 128]], base=0, cm=-1) if kt == qt else None
                    posts = None
                    if kt == qt:
                        pb_ = post_branch(5, 1)
                        if k == 1:
                            pq = post_qt()
                            posts = (lambda pb_=pb_, pq=pq: (pb_(), post_qt_call(pq)))
                        else:
                            posts = pb_
                    units.append(dict(kind="sel", rows=128, lhsT=KE[k][:, kt * 128:(kt + 1) * 128], rhs=Rk[k][:],
                                      mask=mask, v=Va["VS"][:, kt, k, :], first=(kt == 0), last=(kt == qt),
                                      pre=(pre_sel if kt == 0 else None), post=posts, bank=5, k=k,
                                      after=(cmp_last_idx if kt == 0 else None)))

        def post_qt_call(pq):
            pq()

        def emit_qk(i):
            u = units[i]
            sb_ = i % 3
            if u["pre"] is not None:
                u["pre"]()
            rows = u["rows"]
            out3 = ps[sb_][0:rows, :].rearrange("p (a b) -> p a b", b=128)
            if u["kind"] == "sel":
                P.op("pe", lambda e, u=u, out3=out3: e.matmul(out3, lhsT=u["lhsT"], rhs=u["rhs"], start=True, stop=True),
                     reads=[RQ, RbT[u["k"]]], writes=[psr[sb_]])
            else:
                P.op("pe", lambda e, u=u, out3=out3: e.matmul(out3, lhsT=u["lhsT"], rhs=u["rhs"], start=True, stop=True),
                     reads=[RQ, Rcmp], writes=[psr[sb_]])

        def emit_rest(i):
            u = units[i]
            sb_ = i % 3
            rows = u["rows"]
            P.op("act", lambda e, sb_=sb_, rows=rows: e.activation(out=PT[sb_][0:rows, :], in_=ps[sb_][0:rows, :], func=AF.Exp,
                                                                  scale=0.125), reads=[psr[sb_]], writes=[RPT[sb_]])
            if u["mask"] is not None:
                mk = u["mask"]
                v3 = PT[sb_][0:rows, :].rearrange("p (a b) -> p a b", b=128)
                P.op("pool", lambda e, v3=v3, mk=mk: e.affine_select(out=v3, in_=v3, pattern=mk["pattern"], compare_op=ALU.is_ge,
                                                                    fill=0.0, base=mk["base"], channel_multiplier=mk["cm"]),
                     reads=[RPT[sb_]], writes=[RPT[sb_]])
            if u["kind"] == "cmp":
                def pv(e, u=u, sb_=sb_, rows=rows):
                    ins = None
                    for h in range(4):
                        b_ = 3 + h // 2
                        o = (h % 2) * 129
                        ins = e.matmul(ps[b_][:, o:o + 129], lhsT=PT[sb_][0:rows, h * 128:(h + 1) * 128], rhs=u["v"],
                                       start=(u["first"] and h % 2 == 0), stop=(u["last"] and h % 2 == 1))
                    return ins
                P.op("pe", pv, reads=[RPT[sb_], Rcmp], writes=[psr[3], psr[4]], self_ok=not u["first"])
            else:
                bank = u["bank"]

                def pv(e, u=u, sb_=sb_, bank=bank):
                    ins = None
                    for h in range(4):
                        ins = e.matmul(ps[bank][:, h * 65:(h + 1) * 65], lhsT=PT[sb_][:, h * 128:(h + 1) * 128], rhs=u["v"],
                                       start=(u["first"] and h == 0), stop=(u["last"] and h == 3))
                    return ins
                P.op("pe", pv, reads=[RPT[sb_], RQ], writes=[psr[bank]], self_ok=not u["first"])
            if u["post"] is not None:
                u["post"]()

        import os
        n = len(units)
        if os.environ.get("NSA_UNITS"):
            n = int(os.environ["NSA_UNITS"])
            deferred.clear()
        st_ = {"next": 1}

        def try_emit(limit, done_rest):
            while st_["next"] < n and st_["next"] <= limit:
                dep = units[st_["next"]].get("after")
                if dep is not None and dep > done_rest:
                    break
                emit_qk(st_["next"])
                st_["next"] += 1

        emit_qk(0)
        try_emit(2, -1)
        for i in range(n):
            assert st_["next"] > i
            emit_rest(i)
            try_emit(i + 3, i)
            if os.environ.get("NSA_UNITS") and i == n - 1:
                break
        flush_deferred()
        self.phase_end()

    def phase_merge(self, l):
        P = self.P
        S, SR = self.scr, self.sres
        w_in = self.d["w_in"][l].rearrange("(a p) c -> p a c", p=128)
        WU = self.sb([128, 12, D], BF16)
        WG = self.sb([128, 8, 3072], BF16)
        WO = self.sb([128, 8, D], BF16)
        RW = Res("WM")
        pieces = []
        for b, nm in enumerate(("w_up_pool", "w_up_ssm", "w_up_nsa")):
            wv = self.d[nm][l].rearrange("(a p) c -> p a c", p=128)
            for o in range(0, D, 512):
                pieces.append((WU[:, b * 4:(b + 1) * 4, o:o + 512], wv[:, :, o:o + 512], [128, 4, 512]))
        for o in range(0, 3072, 256):
            pieces.append((WG[:, :, o:o + 256], w_in[:, :, 2328 + o:2328 + o + 256], [128, 8, 256]))
        wo = self.d["w_out"][l].rearrange("(a p) c -> p a c", p=128)
        for o in range(0, D, 256):
            pieces.append((WO[:, :, o:o + 256], wo[:, :, o:o + 256], [128, 8, 256]))
        self.load_cast(pieces, RW)
        g_rep = self.sb([128, D], F32)
        b_rep = self.sb([128, D], F32)
        Rgb = Res("gb")
        self.bcast_load(g_rep[:], self.d["ln1_g"][l:l + 1, :], Rgb)
        self.bcast_load(b_rep[:], self.d["ln1_b"][l:l + 1, :], Rgb)
        xt = [self.sb([128, 8, 512], BF16) for _ in range(2)]
        Rxt = [Res("xt0"), Res("xt1")]
        yb = [self.sb([128, 12, 512], BF16) for _ in range(2)]
        Ryb = [Res("yb0"), Res("yb1")]
        mg = self.sb([128, 8, 512], BF16)
        Rmg = Res("mg")
        sg = [self.sb([128, 512], F32) for _ in range(2)]
        Rsg = [Res("sg0"), Res("sg1")]
        acc = self.sb([128, 512], F32)
        tmp = self.sb([128, 512], F32)
        Racc = Res("acc")
        Rtmp = Res("tmp")
        xr = [self.sb([128, D], F32) for _ in range(2)]
        Rxr = [Res("xr0"), Res("xr1")]
        bufs = [self.ln_bufs() for _ in range(2)]
        XTd = S["XT"].rearrange("(a p) t -> p a t", p=128)
        k = 0
        for c in range(8):
            j = c % 2
            P.dma("sp", lambda e, inc, c=c, j=j: inc(e.dma_start(out=xt[j][:], in_=XTd[:, :, c * 512:(c + 1) * 512])), 1,
                  reads=[SR["XT"]], writes=[Rxt[j]])
            for b, nm in enumerate(("YPT", "YST", "YNT")):
                P.dma("sp", lambda e, inc, c=c, j=j, b=b, nm=nm: inc(e.dma_start(
                    out=yb[j][:, b * 4:(b + 1) * 4, :],
                    in_=S[nm].rearrange("(a p) t -> p a t", p=128)[:, :, c * 512:(c + 1) * 512])), 1,
                    reads=[SR[nm]], writes=[Ryb[j]])
            for m in range(8):
                for b in range(3):
                    pu = k % 2
                    pg = 2 + k % 2
                    sj = k % 2
                    k += 1

                    def mmu(e, pu=pu, b=b, m=m, j=j):
                        ins = None
                        for a in range(4):
                            ins = e.matmul(self.ps[pu][:], lhsT=WU[:, b * 4 + a, m * 128:(m + 1) * 128], rhs=yb[j][:, b * 4 + a, :],
                                           start=(a == 0), stop=(a == 3))
                        return ins

                    def mmg(e, pg=pg, b=b, m=m, j=j):
                        ins = None
                        for a in range(8):
                            ins = e.matmul(self.ps[pg][:], lhsT=WG[:, a, b * 1024 + m * 128:b * 1024 + (m + 1) * 128],
                                           rhs=xt[j][:, a, :], start=(a == 0), stop=(a == 7))
                        return ins
                    P.op("pe", mmg, reads=[RW, Rxt[j]], writes=[self.psr[pg]])
                    P.op("pe", mmu, reads=[RW, Ryb[j]], writes=[self.psr[pu]])
                    P.op("act", lambda e, pg=pg, sj=sj: e.activation(out=sg[sj][:], in_=self.ps[pg][:], func=AF.Sigmoid),
                         reads=[self.psr[pg]], writes=[Rsg[sj]])
                    if b == 0:
                        P.op("dve", lambda e, pu=pu, sj=sj: e.tensor_tensor(out=acc[:], in0=self.ps[pu][:], in1=sg[sj][:],
                                                                            op=ALU.mult), reads=[self.psr[pu], Rsg[sj]],
                             writes=[Racc])
                    else:
                        P.op("dve", lambda e, pu=pu, sj=sj: e.tensor_tensor(out=tmp[:], in0=self.ps[pu][:], in1=sg[sj][:],
                                                                            op=ALU.mult), reads=[self.psr[pu], Rsg[sj]],
                             writes=[Rtmp])
                        if b == 1:
                            P.op("pool", lambda e: e.tensor_tensor(out=acc[:], in0=acc[:], in1=tmp[:], op=ALU.add),
                                 reads=[Rtmp, Racc], writes=[Racc])
                        else:
                            P.op("pool", lambda e, m=m: e.tensor_tensor(out=mg[:, m, :], in0=acc[:], in1=tmp[:], op=ALU.add),
                                 reads=[Rtmp, Racc], writes=[Rmg])
            for tt in range(4):
                i = c * 4 + tt
                xj = i % 2
                P.dma("sp", lambda e, inc, i=i, xj=xj: inc(e.dma_start(out=xr[xj][:], in_=S["XR"][i * 128:(i + 1) * 128, :])),
                      1, reads=[SR["XR"]], writes=[Rxr[xj]])
                for half in range(2):
                    pb = 4 + half

                    def mmo(e, pb=pb, tt=tt, half=half):
                        ins = None
                        for a in range(8):
                            ins = e.matmul(self.ps[pb][:], lhsT=mg[:, a, tt * 128:(tt + 1) * 128],
                                           rhs=WO[:, a, half * 512:(half + 1) * 512], start=(a == 0), stop=(a == 7))
                        return ins
                    P.op("pe", mmo, reads=[RW, Rmg], writes=[self.psr[pb]])
                    P.op("dve", lambda e, pb=pb, xj=xj, half=half: e.scalar_tensor_tensor(
                        out=xr[xj][:, half * 512:(half + 1) * 512], in0=xr[xj][:, half * 512:(half + 1) * 512], scalar=ALPHA,
                        in1=self.ps[pb][:], op0=ALU.mult, op1=ALU.add), reads=[self.psr[pb], Rxr[xj]], writes=[Rxr[xj]])
                self.ln_tile(i, xr[xj], Rxr[xj], g_rep, b_rep, Rgb, S["XR1"], SR["XR1"], S["X1T"], SR["X1T"], bufs[xj])
        self.phase_end()

    def phase_ffn(self, l, last):
        P = self.P
        S, SR = self.scr, self.sres
        W1 = self.sb([128, 8, 4096], BF16)
        W2 = self.sb([128, 32, D], BF16)
        RW = Res("WF")
        w1 = self.d["w_ff1"][l].rearrange("(a p) c -> p a c", p=128)
        w2 = self.d["w_ff2"][l].rearrange("(a p) c -> p a c", p=128)
        pieces = []
        for o in range(0, 4096, 128):
            pieces.append((W1[:, :, o:o + 128], w1[:, :, o:o + 128], [128, 8, 128]))
        for a in range(32):
            pieces.append((W2[:, a, :], w2[:, a, :], [128, D]))
        self.load_cast(pieces, RW, stage_cols=1024)
        g_rep = self.sb([128, D], F32)
        b_rep = self.sb([128, D], F32)
        Rgb = Res("gb")
        self.bcast_load(g_rep[:], self.d["ln2_g"][l:l + 1, :], Rgb)
        self.bcast_load(b_rep[:], self.d["ln2_b"][l:l + 1, :], Rgb)
        xt = [self.sb([128, 8, 512], BF16) for _ in range(1)]
        Rxt = [Res("xt0")]
        hT = self.sb([128, 32, 512], BF16)
        RhT = Res("hT")
        rl = [self.sb([128, 512], F32) for _ in range(2)]
        Rrl = [Res("rl0"), Res("rl1")]
        xr = [self.sb([128, D], F32) for _ in range(2)]
        Rxr = [Res("xr0"), Res("xr1")]
        bufs = [self.ln_bufs() for _ in range(2)]
        X1Td = S["X1T"].rearrange("(a p) t -> p a t", p=128)
        dst_res = self.d["out"] if last else S["XR"]
        Rdst = Res("outres") if last else SR["XR"]
        dst_T = None if last else S["XT"]
        k = 0
        for c in range(8):
            j = 0
            P.dma("sp", lambda e, inc, c=c, j=j: inc(e.dma_start(out=xt[j][:], in_=X1Td[:, :, c * 512:(c + 1) * 512])), 1,
                  reads=[SR["X1T"]], writes=[Rxt[j]])
            for f in range(32):
                pb = k % 4
                rj = k % 2
                k += 1

                def mm1(e, pb=pb, f=f, j=j):
                    ins = None
                    for a in range(8):
                        ins = e.matmul(self.ps[pb][:], lhsT=W1[:, a, f * 128:(f + 1) * 128], rhs=xt[j][:, a, :], start=(a == 0),
                                       stop=(a == 7))
                    return ins
                P.op("pe", mm1, reads=[RW, Rxt[j]], writes=[self.psr[pb]])
                P.op("act", lambda e, pb=pb, rj=rj: e.activation(out=rl[rj][:], in_=self.ps[pb][:], func=AF.Relu),
                     reads=[self.psr[pb]], writes=[Rrl[rj]])
                P.op("pool", lambda e, rj=rj, f=f: e.tensor_tensor(out=hT[:, f, :], in0=rl[rj][:], in1=rl[rj][:], op=ALU.mult),
                     reads=[Rrl[rj]], writes=[RhT])
            for tt in range(4):
                i = c * 4 + tt
                xj = i % 2
                P.dma("sp", lambda e, inc, i=i, xj=xj: inc(e.dma_start(out=xr[xj][:], in_=S["XR1"][i * 128:(i + 1) * 128, :])),
                      1, reads=[SR["XR1"]], writes=[Rxr[xj]])
                for half in range(2):
                    pb = 4 + half

                    def mm2(e, pb=pb, tt=tt, half=half):
                        ins = None
                        for f in range(32):
                            ins = e.matmul(self.ps[pb][:], lhsT=hT[:, f, tt * 128:(tt + 1) * 128],
                                           rhs=W2[:, f, half * 512:(half + 1) * 512], start=(f == 0), stop=(f == 31))
                        return ins
                    P.op("pe", mm2, reads=[RW, RhT], writes=[self.psr[pb]])
                    P.op("dve", lambda e, pb=pb, xj=xj, half=half: e.scalar_tensor_tensor(
                        out=xr[xj][:, half * 512:(half + 1) * 512], in0=xr[xj][:, half * 512:(half + 1) * 512], scalar=ALPHA,
                        in1=self.ps[pb][:], op0=ALU.mult, op1=ALU.add), reads=[self.psr[pb], Rxr[xj]], writes=[Rxr[xj]])
                self.ln_tile(i, xr[xj], Rxr[xj], g_rep, b_rep, Rgb, dst_res, Rdst, dst_T, SR["XT"], bufs[xj])
        self.phase_end()

    def build(self):
        self.phase_ln_in()
        for l in range(self.layers):
            stop = self.stop_after
            skip = self.skip
            if "proj" not in skip:
                self.phase_proj(l)
            if stop == "proj":
                break
            if "pool" not in skip:
                self.phase_pool(l)
            if stop == "pool":
                break
            if "s5" not in skip:
                self.phase_s5(l)
            if stop == "s5":
                break
            if "nsa" not in skip:
                self.phase_nsa(l)
            if stop == "nsa":
                break
            self.phase_merge(l)
            if stop == "merge":
                break
            self.phase_ffn(l, last=(l == self.layers - 1))
        self.P.emit()
        return self.nc


def make_inputs(inputs, ncores=8):
    consts = host_consts()
    shared = {n: np.ascontiguousarray(np.asarray(inputs[n], dtype=np.float32)) for n, _ in PARAMS}
    shared.update(consts)
    x = np.asarray(inputs["x"], dtype=np.float32)
    maps = []
    for c in range(ncores):
        m = dict(shared)
        m["x"] = np.ascontiguousarray(x[c])
        maps.append(m)
    return maps


def kernel(**inputs):
    b = Builder()
    nc = b.build()
    maps = make_inputs(inputs)
    res = run_bass_kernel_spmd(nc, maps, core_ids=list(range(8)))
    return np.stack([np.asarray(r["out"], dtype=np.float32) for r in res.results], axis=0)
```

```python
import contextlib
import math
import numpy as np
import ml_dtypes
import concourse.bass as bass
import concourse.mybir as mybir
from concourse.bass_utils import run_bass_kernel_spmd

F32 = mybir.dt.float32
BF16 = mybir.dt.bfloat16
AF = mybir.ActivationFunctionType
ALU = mybir.AluOpType
AX = mybir.AxisListType

L = 4096
D = 1024
NT = 32
DEPTH = 2
ALPHA = (2 * DEPTH) ** 0.25
EPS = 1e-5
BIG = 30000.0
SB_BASE = 16512
SB_END = 229344


class Res:
    __slots__ = ("name", "w", "r")

    def __init__(self, name=""):
        self.name = name
        self.w = None
        self.r = []


class Prog:
    ENGS = ("pe", "act", "dve", "pool", "sp")
    NDMA = 8

    def __init__(self, nc):
        self.nc = nc
        self.ops = {e: [] for e in self.ENGS}
        self.cnt = {}
        self.seen = {e: {} for e in self.ENGS}
        self.dma_n = {e: 0 for e in self.ENGS}
        self.sem_keys = []
        for e in ("pe", "act", "dve", "pool"):
            self._mk(("c", e))
        for e in ("sp", "act", "pool"):
            for j in range(self.NDMA):
                self._mk(("d", e, j))
        self.nops = 0

    def _mk(self, key):
        self.cnt[key] = 0
        self.sem_keys.append(key)

    def _need(self, eng, waits, dep):
        if dep is None:
            return
        key, val = dep
        if self.seen[eng].get(key, 0) >= val:
            return
        waits[key] = max(waits.get(key, 0), val)

    def _deps(self, eng, reads, writes, self_ok=False):
        waits = {}
        for r in reads:
            self._need(eng, waits, r.w)
        for w in writes:
            self._need(eng, waits, w.w)
            for rd in w.r:
                self._need(eng, waits, rd)
        if self_ok:
            waits.pop(("c", eng), None)
        return waits

    def _commit(self, eng, waits, tok, reads, writes):
        for k, v in waits.items():
            self.seen[eng][k] = v
        for r in reads:
            r.r.append(tok)
        for w in writes:
            w.w = tok
            w.r = []
        self.nops += 1

    def op(self, eng, fn, reads=(), writes=(), self_ok=False):
        waits = self._deps(eng, reads, writes, self_ok)
        key = ("c", eng)
        self.cnt[key] += 1
        tok = (key, self.cnt[key])
        self.ops[eng].append((list(waits.items()), fn, key, 1))
        self._commit(eng, waits, tok, reads, writes)
        return tok

    def dma(self, q, fn, ndma, reads=(), writes=()):
        waits = self._deps(q, reads, writes)
        j = self.dma_n[q] % self.NDMA
        self.dma_n[q] += 1
        key = ("d", q, j)
        if self.cnt[key] > 0:
            self._need(q, waits, (key, self.cnt[key]))
        self.cnt[key] += 16 * ndma
        tok = (key, self.cnt[key])
        self.ops[q].append((list(waits.items()), fn, key, 16))
        self._commit(q, waits, tok, reads, writes)
        return tok

    def barrier(self):
        for e in self.ENGS:
            waits = {}
            for k in self.sem_keys:
                if self.cnt[k] > 0 and not (k[0] == "c" and k[1] == e):
                    self._need(e, waits, (k, self.cnt[k]))
            for k, v in waits.items():
                self.seen[e][k] = v
            if waits:
                self.ops[e].append((list(waits.items()), None, None, 0))

    def emit(self):
        nc = self.nc
        with contextlib.ExitStack() as st:
            sems = {}
            for k in self.sem_keys:
                sems[k] = st.enter_context(nc.semaphore("s_" + "_".join(str(x) for x in k)))
            block = st.enter_context(nc.Block())
            final = [(k, v) for k, v in self.cnt.items() if v > 0]

            def run(engname):
                def body(eng):
                    for waits, fn, key, inc in self.ops[engname]:
                        for k, v in waits:
                            eng.wait_ge(sems[k], v)
                        if fn is None:
                            continue
                        if inc == 16:
                            fn(eng, lambda ins: ins.then_inc(sems[key], 16))
                        else:
                            fn(eng).then_inc(sems[key], 1)
                    if engname == "sp":
                        for k, v in final:
                            eng.wait_ge(sems[k], v)
                return body

            block.tensor(run("pe"))
            block.scalar(run("act"))
            block.vector(run("dve"))
            block.gpsimd(run("pool"))
            block.sync(run("sp"))


PARAMS = [("ln_in_g", (D,)), ("ln_in_b", (D,)), ("w_in", (2, D, 5400)), ("w_pool", (2, 4, 128, 128)),
          ("pool_scale", (2, 512)), ("ssm_lam_re", (2, 32, 64)), ("ssm_lam_im", (2, 32, 64)),
          ("ssm_log_dt", (2, 32)), ("ssm_b_re", (2, 32, 64, 16)), ("ssm_b_im", (2, 32, 64, 16)),
          ("ssm_c_re", (2, 32, 16, 64)), ("ssm_c_im", (2, 32, 16, 64)), ("ssm_d", (2, 32, 16)),
          ("w_glu", (2, 512, 512)), ("b_glu", (2, 512)), ("cmp_pe_k", (2, 32, 64)), ("cmp_pe_v", (2, 32, 64)),
          ("cmp_wk1", (2, 2048, 128)), ("cmp_wk2", (2, 128, 64)), ("cmp_wv1", (2, 2048, 128)),
          ("cmp_wv2", (2, 128, 64)), ("w_up_pool", (2, 512, D)), ("w_up_ssm", (2, 512, D)),
          ("w_up_nsa", (2, 512, D)), ("w_out", (2, D, D)), ("ln1_g", (2, D)), ("ln1_b", (2, D)),
          ("w_ff1", (2, D, 4096)), ("w_ff2", (2, 4096, D)), ("ln2_g", (2, D)), ("ln2_b", (2, D))]


def host_consts():
    c = {}
    c["c_ident"] = np.eye(128, dtype=np.float32)
    pos = np.arange(L, dtype=np.float32)
    inv_freq = (np.float32(500000.0) ** (-np.arange(0, 16, 2, dtype=np.float32) / np.float32(16))).astype(np.float32)
    ang = (pos[:, None] * inv_freq[None, :]).astype(np.float32)
    c["c_rope"] = np.concatenate([np.cos(ang), np.sin(ang)], axis=1).astype(np.float32)
    eb = np.zeros((64, 4096), np.float32)
    for s in range(64):
        eb[s, s * 64:(s + 1) * 64] = 1.0
    c["c_ebig"] = eb.astype(ml_dtypes.bfloat16)
    cs = np.arange(256)[:, None] * 16
    ss = np.arange(64)[None, :] * 64
    ov = np.maximum(np.minimum(cs + 32, ss + 64) - np.maximum(cs, ss), 0) / 16.0
    ov[255:] = 0.0
    c["c_ov"] = ov.astype(np.float32)
    ic = np.zeros((4, 16), np.float32)
    for gi, w in enumerate((2, 4, 8, 16)):
        ic[gi] = 1.0 / np.minimum(np.arange(16) + 1, w)
    c["c_invcnt"] = ic.reshape(1, 64)
    return c


class Builder:
    def __init__(self, dbg=(), stop_after=None, layers=DEPTH, inject=(), skip=()):
        self.inject = set(inject)
        self.skip = set(skip)
        self.nc = nc = bass.Bass("TRN2", target_bir_lowering=False)
        self.P = Prog(nc)
        self.dbg = set(dbg)
        self.stop_after = stop_after
        self.layers = layers
        self.d = {}
        self.d["x"] = nc.dram_tensor("x", [L, D], F32, kind="ExternalInput").ap()
        for n, shp in PARAMS:
            self.d[n] = nc.dram_tensor(n, list(shp), F32, kind="ExternalInput").ap()
        for n, a in host_consts().items():
            dt = BF16 if a.dtype == ml_dtypes.bfloat16 else F32
            self.d[n] = nc.dram_tensor(n, list(a.shape), dt, kind="ExternalInput").ap()
        self.d["out"] = nc.dram_tensor("out", [L, D], F32, kind="ExternalOutput").ap()
        self.scr = {}
        self.sres = {}
        for n, shp, dt in [("XR", [L, D], F32), ("XT", [D, L], BF16), ("UPT", [512, L], F32), ("UST", [512, L], F32),
                           ("QT", [128, 4, L], BF16), ("KCT", [128, L], BF16), ("KST", [128, L], BF16),
                           ("KWT", [128, L], BF16), ("VCT", [128, L], BF16), ("VS", [L, 128], BF16),
                           ("VW", [L, 128], BF16), ("GN", [L, 24], F32), ("YPT", [512, L], BF16),
                           ("YST", [512, L], BF16), ("YNT", [512, L], BF16), ("XR1", [L, D], F32),
                           ("X1T", [D, L], BF16)]:
            kind = "ExternalOutput" if n in self.dbg else ("ExternalInput" if n in self.inject else "Internal")
            self.scr[n] = nc.dram_tensor("s_" + n, shp, dt, kind=kind).ap()
            self.sres[n] = Res(n)
        self.ps = [nc.alloc_psum_tensor("psb%d" % i, [128, 512], F32) for i in range(8)]
        self.psr = [Res("ps%d" % i) for i in range(8)]
        self.cur = SB_BASE
        self.uid = 0
        self.ident = self.sb([128, 128], F32)
        self.Rconst = Res("const")
        self.P.dma("sp", lambda e, inc: inc(e.dma_start(out=self.ident[:], in_=self.d["c_ident"][:, :])), 1,
                   writes=[self.Rconst])
        self.persist = self.cur

    def sb(self, shape, dt):
        nbytes = int(np.prod(shape[1:])) * (2 if dt == BF16 else 4)
        off = (self.cur + 63) // 64 * 64
        assert off + nbytes <= SB_END, ("SBUF overflow", off + nbytes - SB_END)
        self.cur = off + nbytes
        self.uid += 1
        return self.nc.alloc_sbuf_tensor_at("t%d" % self.uid, list(shape), dt, offset=off)

    def phase_end(self):
        self.P.barrier()
        self.cur = self.persist

    def load_cast(self, pieces, res, stage_cols=2048):
        P = self.P
        if not hasattr(self, "_lc"):
            self._lc = None
        stg = [self.sb([128, stage_cols], F32) for _ in range(3)]
        rs = [Res("stg%d" % i) for i in range(3)]
        for i, pc in enumerate(pieces):
            dst, src, shp = pc[0], pc[1], pc[2]
            p0 = pc[3] if len(pc) > 3 else 0
            j = i % 3
            n = int(np.prod(shp[1:]))
            assert n <= stage_cols
            p = shp[0]
            sview = stg[j][p0:p0 + p, 0:n]
            if len(shp) == 3:
                sview = sview.rearrange("p (a b) -> p a b", b=shp[2])
            P.dma("sp", (lambda sv, s: lambda e, inc: inc(e.dma_start(out=sv, in_=s)))(sview, src), 1, writes=[rs[j]])
            eng = "pool" if i % 2 == 0 else "dve"
            P.op(eng, (lambda dd, sv: lambda e: e.tensor_copy(out=dd, in_=sv))(dst, sview), reads=[rs[j]], writes=[res])

    def bcast_load(self, dst, src_row, res):
        self.P.dma("sp", lambda e, inc: inc(e.dma_start(out=dst, in_=src_row.partition_broadcast(128))), 1, writes=[res])

    def ln_tile(self, i, src, Rsrc, g_rep, b_rep, Rgb, dst_res_d, Rres, dst_T_d, RT, bufs):
        P = self.P
        st6, mv, rstd, xT, RxT, Rst = bufs
        P.op("dve", lambda e: e.bn_stats(out=st6[:, 0, :], in_=src[:, 0:512]), reads=[Rsrc], writes=[Rst])
        P.op("dve", lambda e: e.bn_stats(out=st6[:, 1, :], in_=src[:, 512:1024]), reads=[Rsrc, Rst], writes=[Rst])
        P.op("dve", lambda e: e.bn_aggr(out=mv[:], in_=st6[:]), reads=[Rst], writes=[Rst])
        P.op("dve", lambda e: e.tensor_scalar_add(out=rstd[:], in0=mv[:, 1:2], scalar1=EPS), reads=[Rst], writes=[Rst])
        P.op("act", lambda e: e.sqrt(out=rstd[:], in_=rstd[:]), reads=[Rst], writes=[Rst])
        P.op("dve", lambda e: e.reciprocal(out=rstd[:], in_=rstd[:]), reads=[Rst], writes=[Rst])
        P.op("dve", lambda e: e.tensor_scalar(out=src[:], in0=src[:], scalar1=mv[:, 0:1], scalar2=rstd[:, 0:1],
                                              op0=ALU.subtract, op1=ALU.mult), reads=[Rst, Rsrc], writes=[Rsrc])
        P.op("dve", lambda e: e.tensor_tensor(out=src[:], in0=src[:], in1=g_rep[:], op=ALU.mult), reads=[Rsrc, Rgb],
             writes=[Rsrc])
        P.op("dve", lambda e: e.tensor_tensor(out=src[:], in0=src[:], in1=b_rep[:], op=ALU.add), reads=[Rsrc, Rgb],
             writes=[Rsrc])
        P.dma("sp", lambda e, inc: inc(e.dma_start(out=dst_res_d[i * 128:(i + 1) * 128, :], in_=src[:])), 1,
              reads=[Rsrc], writes=[Rres])
        if dst_T_d is None:
            return
        for half in range(2):
            pb = 6 + half
            ps = self.ps[pb]

            def tr(e, half=half, ps=ps):
                ins = None
                for q in range(4):
                    dt = half * 4 + q
                    ins = e.transpose(out=ps[:, q * 128:(q + 1) * 128], in_=src[:, dt * 128:(dt + 1) * 128],
                                      identity=self.ident[:])
                return ins
            P.op("pe", tr, reads=[Rsrc, self.Rconst], writes=[self.psr[pb]])
            P.op("act", lambda e, half=half, ps=ps: e.copy(out=xT[:, half * 4:(half + 1) * 4, :],
                                                         in_=ps[:].rearrange("p (a b) -> p a b", b=128)),
                 reads=[self.psr[pb]], writes=[RxT])
        P.dma("sp", lambda e, inc: inc(e.dma_start(
            out=dst_T_d.rearrange("(a p) t -> p a t", p=128)[:, :, i * 128:(i + 1) * 128], in_=xT[:])), 1,
            reads=[RxT], writes=[RT])

    def ln_bufs(self):
        return (self.sb([128, 2, 6], F32), self.sb([128, 2], F32), self.sb([128, 1], F32),
                self.sb([128, 8, 128], BF16), Res("xT"), Res("lnst"))

    def phase_ln_in(self):
        P = self.P
        g_rep = self.sb([128, D], F32)
        b_rep = self.sb([128, D], F32)
        Rgb = Res("gb")
        self.bcast_load(g_rep[:], self.d["ln_in_g"].unsqueeze(0), Rgb)
        self.bcast_load(b_rep[:], self.d["ln_in_b"].unsqueeze(0), Rgb)
        xb = [self.sb([128, D], F32) for _ in range(2)]
        Rx = [Res("x0"), Res("x1")]
        bufs = [self.ln_bufs() for _ in range(2)]
        for i in range(NT):
            j = i % 2
            P.dma("sp", lambda e, inc, i=i, j=j: inc(e.dma_start(out=xb[j][:], in_=self.d["x"][i * 128:(i + 1) * 128, :])),
                  1, writes=[Rx[j]])
            self.ln_tile(i, xb[j], Rx[j], g_rep, b_rep, Rgb, self.scr["XR"], self.sres["XR"], self.scr["XT"],
                         self.sres["XT"], bufs[j])
        self.phase_end()

    def phase_proj(self, l):
        P = self.P
        w_in = self.d["w_in"][l].rearrange("(a p) c -> p a c", p=128)
        NC_ = 1024 + 1304
        WP = self.sb([128, 8, NC_], BF16)
        RW = Res("WP")
        plan = [(0, 0, 1024)]
        qorder = (0, 4, 1, 5, 2, 6, 3, 7)
        for j, h in enumerate(qorder):
            plan.append((1024 + j * 64, 1024 + h * 64, 64))
        kv0 = 1536
        base = 1024 + 512
        for j, s in enumerate((0, 2, 4, 1)):
            plan.append((base + j * 128, kv0 + s * 128, 128))
        base += 512
        for j, s in enumerate((3, 5)):
            plan.append((base + j * 128, kv0 + s * 128, 128))
        plan.append((base + 256, 2304, 24))
        pieces = []
        for dc, sc, n in plan:
            o = 0
            while o < n:
                m = min(256, n - o)
                pieces.append((WP[:, :, dc + o:dc + o + m], w_in[:, :, sc + o:sc + o + m], [128, 8, m]))
                o += m
        self.load_cast(pieces, RW)
        rope = self.sb([128, NT, 16], F32)
        Rrope = Res("rope")
        P.dma("sp", lambda e, inc: inc(e.dma_start(out=rope[:], in_=self.d["c_rope"].rearrange("(a p) c -> p a c", p=128))),
              1, writes=[Rrope])
        xt = [self.sb([128, 8, 512], BF16) for _ in range(2)]
        Rxt = [Res("xt0"), Res("xt1")]
        fst = [self.sb([128, 512], F32) for _ in range(2)]
        Rfst = [Res("f0"), Res("f1")]
        tm = [self.sb([128, 1304], F32) for _ in range(2)]
        Rtm = [Res("tm0"), Res("tm1")]
        t1 = self.sb([128, 14, 8], F32)
        t2 = self.sb([128, 14, 8], F32)
        t3 = self.sb([128, 14, 8], F32)
        Rt = Res("ropetmp")
        trs = [self.sb([128, 8, 128], BF16) for _ in range(2)]
        Rtrs = [Res("trs0"), Res("trs1")]
        vst = [self.sb([128, 256], BF16) for _ in range(2)]
        Rvst = [Res("vst0"), Res("vst1")]
        XTd = self.scr["XT"].rearrange("(a p) t -> p a t", p=128)
        S = self.scr
        SR = self.sres
        kcnt = 0
        for c in range(8):
            j = c % 2
            P.dma("sp", lambda e, inc, c=c, j=j: inc(e.dma_start(out=xt[j][:], in_=XTd[:, :, c * 512:(c + 1) * 512])), 1,
                  reads=[SR["XT"]], writes=[Rxt[j]])
            for ft in range(8):
                pb = ft % 2
                ps = self.ps[pb]

                def mm(e, ft=ft, ps=ps, j=j):
                    ins = None
                    for a in range(8):
                        ins = e.matmul(ps[:], lhsT=WP[:, a, ft * 128:(ft + 1) * 128], rhs=xt[j][:, a, :], start=(a == 0),
                                       stop=(a == 7))
                    return ins
                P.op("pe", mm, reads=[RW, Rxt[j]], writes=[self.psr[pb]])
                fj = ft % 2
                P.op("act", lambda e, ps=ps, fj=fj: e.copy(out=fst[fj][:], in_=ps[:]), reads=[self.psr[pb]],
                     writes=[Rfst[fj]])
                dst = (S["UPT"] if ft < 4 else S["UST"])
                dres = SR["UPT"] if ft < 4 else SR["UST"]
                r0 = (ft % 4) * 128
                P.dma("sp", lambda e, inc, dst=dst, r0=r0, c=c, fj=fj: inc(e.dma_start(
                    out=dst[r0:r0 + 128, c * 512:(c + 1) * 512], in_=fst[fj][:])), 1, reads=[Rfst[fj]], writes=[dres])
            for tt in range(4):
                i = c * 4 + tt
                tj = i % 2
                T = tm[tj]
                for ch, (c0, n) in enumerate(((0, 512), (512, 512), (1024, 280))):
                    pb = 2 + (kcnt % 3)
                    kcnt += 1
                    ps = self.ps[pb]

                    def mm2(e, ps=ps, j=j, tt=tt, c0=c0, n=n):
                        ins = None
                        for a in range(8):
                            ins = e.matmul(ps[:, 0:n], lhsT=xt[j][:, a, tt * 128:(tt + 1) * 128],
                                           rhs=WP[:, a, 1024 + c0:1024 + c0 + n], start=(a == 0), stop=(a == 7))
                        return ins
                    P.op("pe", mm2, reads=[RW, Rxt[j]], writes=[self.psr[pb]])
                    P.op("act", lambda e, ps=ps, T=T, c0=c0, n=n: e.copy(out=T[:, c0:c0 + n], in_=ps[:, 0:n]),
                         reads=[self.psr[pb]], writes=[Rtm[tj]])
                H3 = T[:, 0:896].rearrange("p (h d) -> p h d", d=64)
                x1 = H3[:, :, 0:8]
                x2 = H3[:, :, 8:16]
                cosb = rope[:, i, 0:8].unsqueeze(1).to_broadcast([128, 14, 8])
                sinb = rope[:, i, 8:16].unsqueeze(1).to_broadcast([128, 14, 8])
                P.op("dve", lambda e, x1=x1, sinb=sinb: e.tensor_tensor(out=t1[:], in0=x1, in1=sinb, op=ALU.mult),
                     reads=[Rtm[tj], Rrope], writes=[Rt])
                P.op("dve", lambda e, x2=x2, sinb=sinb: e.tensor_tensor(out=t2[:], in0=x2, in1=sinb, op=ALU.mult),
                     reads=[Rtm[tj], Rrope, Rt], writes=[Rt])
                P.op("dve", lambda e, x1=x1, cosb=cosb: e.tensor_tensor(out=x1, in0=x1, in1=cosb, op=ALU.mult),
                     reads=[Rtm[tj], Rrope, Rt], writes=[Rtm[tj]])
                P.op("dve", lambda e, x2=x2, cosb=cosb: e.tensor_tensor(out=t3[:], in0=x2, in1=cosb, op=ALU.mult),
                     reads=[Rtm[tj], Rrope, Rt], writes=[Rt])
                P.op("dve", lambda e, x1=x1: e.tensor_tensor(out=x1, in0=x1, in1=t2[:], op=ALU.subtract),
                     reads=[Rtm[tj], Rt], writes=[Rtm[tj]])
                P.op("dve", lambda e, x2=x2: e.tensor_tensor(out=x2, in0=t1[:], in1=t3[:], op=ALU.add),
                     reads=[Rtm[tj], Rt], writes=[Rtm[tj]])
                for half in range(2):
                    pb = 6 + half
                    ps = self.ps[pb]

                    def tr(e, half=half, ps=ps, T=T):
                        ins = None
                        for q in range(4):
                            a = half * 4 + q
                            ins = e.transpose(out=ps[:, q * 128:(q + 1) * 128], in_=T[:, a * 128:(a + 1) * 128],
                                              identity=self.ident[:])
                        return ins
                    P.op("pe", tr, reads=[Rtm[tj], self.Rconst], writes=[self.psr[pb]])
                    P.op("act", lambda e, half=half, ps=ps, tj=tj: e.copy(
                        out=trs[tj][:, half * 4:(half + 1) * 4, :], in_=ps[:].rearrange("p (a b) -> p a b", b=128)),
                        reads=[self.psr[pb]], writes=[Rtrs[tj]])
                t0 = i * 128
                P.dma("sp", lambda e, inc, tj=tj, t0=t0: inc(e.dma_start(out=S["QT"][:, :, t0:t0 + 128],
                                                                       in_=trs[tj][:, 0:4, :])), 1,
                      reads=[Rtrs[tj]], writes=[SR["QT"]])
                for a, nm in ((4, "KCT"), (5, "KST"), (6, "KWT"), (7, "VCT")):
                    P.dma("sp", lambda e, inc, tj=tj, t0=t0, a=a, nm=nm: inc(e.dma_start(
                        out=S[nm][:, t0:t0 + 128], in_=trs[tj][:, a, :])), 1, reads=[Rtrs[tj]], writes=[SR[nm]])
                P.op("pool", lambda e, tj=tj, T=T: e.tensor_copy(out=vst[tj][:], in_=T[:, 1024:1280]), reads=[Rtm[tj]],
                     writes=[Rvst[tj]])
                P.dma("sp", lambda e, inc, tj=tj, t0=t0: inc(e.dma_start(out=S["VS"][t0:t0 + 128, :], in_=vst[tj][:, 0:128])),
                      1, reads=[Rvst[tj]], writes=[SR["VS"]])
                P.dma("sp", lambda e, inc, tj=tj, t0=t0: inc(e.dma_start(out=S["VW"][t0:t0 + 128, :], in_=vst[tj][:, 128:256])),
                      1, reads=[Rvst[tj]], writes=[SR["VW"]])
                P.op("act", lambda e, T=T: e.activation(out=T[:, 1280:1304], in_=T[:, 1280:1304], func=AF.Sigmoid),
                     reads=[Rtm[tj]], writes=[Rtm[tj]])
                P.dma("sp", lambda e, inc, T=T, t0=t0: inc(e.dma_start(out=S["GN"][t0:t0 + 128, :], in_=T[:, 1280:1304])),
                      1, reads=[Rtm[tj]], writes=[SR["GN"]])
        self.phase_end()

    def phase_pool(self, l):
        P = self.P
        S, SR = self.scr, self.sres
        wp = self.sb([128, 4, 128], BF16)
        Rwp = Res("wpool")
        self.load_cast([(wp[:, g, :], self.d["w_pool"][l, g], [128, 128]) for g in range(4)], Rwp, stage_cols=128)
        psc = self.sb([128, 4], F32)
        Rpsc = Res("psc")
        P.dma("sp", lambda e, inc: inc(e.dma_start(out=psc[:], in_=self.d["pool_scale"][l].rearrange("(g p) -> p g", p=128),
                                                       allow_slow_non_contiguous=True)),
              1, writes=[Rpsc])
        icn = self.sb([128, 64], F32)
        P.dma("sp", lambda e, inc: inc(e.dma_start(out=icn[:], in_=self.d["c_invcnt"][0:1, :].partition_broadcast(128))),
              1, writes=[Rpsc])
        H = 16
        U = [self.sb([128, H + L], F32) for _ in range(2)]
        A = [self.sb([128, H + L], F32) for _ in range(2)]
        Bb = [self.sb([128, H + L], F32) for _ in range(2)]
        Z = [self.sb([128, L], BF16) for _ in range(2)]
        tmp16 = [self.sb([128, 16], F32) for _ in range(2)]
        yst = [self.sb([128, 512], BF16) for _ in range(2)]
        Ryst = [Res("y0"), Res("y1")]
        RU = [Res("U0"), Res("U1")]
        for j in range(2):
            eng = "dve" if j == 0 else "pool"
            for buf in (U[j], A[j], Bb[j]):
                P.op(eng, lambda e, buf=buf: e.memset(buf[:, 0:H], 0.0), writes=[RU[j]])
        for g in range(4):
            j = g % 2
            eng = "dve" if j == 0 else "pool"
            w = 2 << g
            P.dma("sp", lambda e, inc, g=g, j=j: inc(e.dma_start(out=U[j][:, H:], in_=S["UPT"][g * 128:(g + 1) * 128, :])), 1,
                  reads=[SR["UPT"]], writes=[RU[j]])
            src = U[j]
            dsts = [A[j], Bb[j]]
            for k in range(g + 1):
                sh = 1 << k
                dst = dsts[k % 2]
                P.op(eng, lambda e, dst=dst, src=src, sh=sh: e.tensor_tensor(out=dst[:, H:], in0=src[:, H:],
                                                                              in1=src[:, H - sh:H + L - sh], op=ALU.add),
                     reads=[RU[j]], writes=[RU[j]])
                src = dst
            P.op("dve", lambda e, src=src, j=j, w=w: e.scalar_tensor_tensor(out=Z[j][:], in0=src[:, H:], scalar=1.0 / w,
                                                                          in1=U[j][:, H:], op0=ALU.mult,
                                                                          op1=ALU.subtract), reads=[RU[j]], writes=[RU[j]])
            P.op(eng, lambda e, src=src, j=j, g=g: e.tensor_tensor(out=tmp16[j][:], in0=src[:, H:H + 16], in1=icn[:, g * 16:(g + 1) * 16],
                                                                   op=ALU.mult), reads=[RU[j], Rpsc], writes=[RU[j]])
            P.op(eng, lambda e, j=j: e.tensor_tensor(out=Z[j][:, 0:16], in0=tmp16[j][:], in1=U[j][:, H:H + 16],
                                                     op=ALU.subtract), reads=[RU[j]], writes=[RU[j]])
            for c in range(8):
                pb = c % 2
                ps = self.ps[pb]
                P.op("pe", lambda e, ps=ps, g=g, j=j, c=c: e.matmul(ps[:], lhsT=wp[:, g, :], rhs=Z[j][:, c * 512:(c + 1) * 512],
                                                                  start=True, stop=True), reads=[Rwp, RU[j]],
                     writes=[self.psr[pb]])
                yj = c % 2
                P.op("act", lambda e, ps=ps, yj=yj, g=g: e.activation(out=yst[yj][:], in_=ps[:], func=AF.Copy,
                                                                     scale=psc[:, g:g + 1]), reads=[self.psr[pb], Rpsc],
                     writes=[Ryst[yj]])
                P.dma("sp", lambda e, inc, g=g, c=c, yj=yj: inc(e.dma_start(
                    out=S["YPT"][g * 128:(g + 1) * 128, c * 512:(c + 1) * 512], in_=yst[yj][:])), 1, reads=[Ryst[yj]],
                    writes=[SR["YPT"]])
        self.phase_end()

    def cmul(self, eng, outr, outi, ar, ai, br, bi, t1, t2, Rin, Rout, Rt):
        P = self.P
        tt = lambda o, a, b, op: (lambda e: e.tensor_tensor(out=o, in0=a, in1=b, op=op))
        P.op(eng, tt(t1, ar, br, ALU.mult), reads=Rin, writes=[Rt])
        P.op(eng, tt(t2, ai, bi, ALU.mult), reads=Rin + [Rt], writes=[Rt])
        P.op(eng, tt(outr, t1, t2, ALU.subtract), reads=[Rt], writes=[Rout])
        P.op(eng, tt(t1, ar, bi, ALU.mult), reads=Rin + [Rt, Rout], writes=[Rt])
        P.op(eng, tt(t2, ai, br, ALU.mult), reads=Rin + [Rt], writes=[Rt])
        P.op(eng, tt(outi, t1, t2, ALU.add), reads=[Rt], writes=[Rout])

    def phase_s5(self, l):
        P = self.P
        S, SR = self.scr, self.sres
        T = 512
        sm = lambda: self.sb([128, 16], F32)
        lam_n = self.sb([16, 256], F32)
        Rp = Res("s5prep")
        P.dma("sp", lambda e, inc: inc(e.dma_start(out=lam_n[:, 0:128], in_=self.d["ssm_lam_re"][l].rearrange("(j g) p -> j (g p)", g=2))),
              1, writes=[Rp])
        P.dma("sp", lambda e, inc: inc(e.dma_start(out=lam_n[:, 128:256], in_=self.d["ssm_lam_im"][l].rearrange("(j g) p -> j (g p)", g=2))),
              1, writes=[Rp])
        lr, li, stp, rho, th, sn, cs, t1, t2, t3, kr, ki, nki, den = [sm() for _ in range(14)]
        P.op("pe", lambda e: (e.transpose(out=self.ps[7][:, 0:16], in_=lam_n[:, 0:128], identity=self.ident[0:16, 0:16]),
                              e.transpose(out=self.ps[7][:, 16:32], in_=lam_n[:, 128:256], identity=self.ident[0:16, 0:16]))[1],
             reads=[Rp, self.Rconst], writes=[self.psr[7]])
        P.op("dve", lambda e: e.tensor_copy(out=lr[:], in_=self.ps[7][:, 0:16]), reads=[self.psr[7]], writes=[Rp])
        P.op("dve", lambda e: e.tensor_copy(out=li[:], in_=self.ps[7][:, 16:32]), reads=[self.psr[7], Rp], writes=[Rp])
        ldt = self.d["ssm_log_dt"][l:l + 1, :].rearrange("o (j g) -> o j g", g=2)
        for gl in range(2):
            P.dma("sp", lambda e, inc, gl=gl: inc(e.dma_start(out=stp[gl * 64:(gl + 1) * 64, :],
                                                            in_=ldt[:, :, gl].partition_broadcast(64),
                                                            allow_slow_non_contiguous=True)), 1, reads=[Rp], writes=[Rp])
        P.op("act", lambda e: e.activation(out=stp[:], in_=stp[:], func=AF.Exp), reads=[Rp], writes=[Rp])
        V = lambda fn: P.op("dve", fn, reads=[Rp], writes=[Rp])
        V(lambda e: e.tensor_tensor(out=t1[:], in0=lr[:], in1=stp[:], op=ALU.mult))
        P.op("act", lambda e: e.activation(out=rho[:], in_=t1[:], func=AF.Exp), reads=[Rp], writes=[Rp])
        V(lambda e: e.tensor_tensor(out=th[:], in0=li[:], in1=stp[:], op=ALU.mult))
        TWO_PI = 2.0 * math.pi
        ni = self.sb([128, 16], mybir.dt.int32)
        nf = sm()

        def reduce_turns(dst, shift):
            V(lambda e: e.tensor_scalar(out=dst[:], in0=th[:], scalar1=1.0 / TWO_PI, scalar2=shift, op0=ALU.mult, op1=ALU.add))
            V(lambda e: e.tensor_copy(out=ni[:], in_=dst[:]))
            V(lambda e: e.tensor_copy(out=nf[:], in_=ni[:]))
            V(lambda e: e.tensor_tensor(out=dst[:], in0=dst[:], in1=nf[:], op=ALU.subtract))
            V(lambda e: e.tensor_single_scalar(out=nf[:], in_=dst[:], scalar=0.5, op=ALU.is_gt))
            V(lambda e: e.tensor_tensor(out=dst[:], in0=dst[:], in1=nf[:], op=ALU.subtract))
            V(lambda e: e.tensor_single_scalar(out=nf[:], in_=dst[:], scalar=-0.5, op=ALU.is_lt))
            V(lambda e: e.tensor_tensor(out=dst[:], in0=dst[:], in1=nf[:], op=ALU.add))
        reduce_turns(t1, 0.0)
        reduce_turns(t2, 0.25)
        P.op("act", lambda e: e.activation(out=sn[:], in_=t1[:], func=AF.Sin, scale=TWO_PI), reads=[Rp], writes=[Rp])
        P.op("act", lambda e: e.activation(out=cs[:], in_=t2[:], func=AF.Sin, scale=TWO_PI), reads=[Rp], writes=[Rp])
        V(lambda e: e.tensor_tensor(out=t1[:], in0=rho[:], in1=cs[:], op=ALU.mult))
        V(lambda e: e.tensor_scalar_add(out=t1[:], in0=t1[:], scalar1=-1.0))
        V(lambda e: e.tensor_tensor(out=t2[:], in0=rho[:], in1=sn[:], op=ALU.mult))
        V(lambda e: e.tensor_tensor(out=den[:], in0=lr[:], in1=lr[:], op=ALU.mult))
        V(lambda e: e.tensor_tensor(out=t3[:], in0=li[:], in1=li[:], op=ALU.mult))
        V(lambda e: e.tensor_tensor(out=den[:], in0=den[:], in1=t3[:], op=ALU.add))
        V(lambda e: e.reciprocal(out=den[:], in_=den[:]))
        V(lambda e: e.tensor_tensor(out=kr[:], in0=t1[:], in1=lr[:], op=ALU.mult))
        V(lambda e: e.tensor_tensor(out=t3[:], in0=t2[:], in1=li[:], op=ALU.mult))
        V(lambda e: e.tensor_tensor(out=kr[:], in0=kr[:], in1=t3[:], op=ALU.add))
        V(lambda e: e.tensor_tensor(out=kr[:], in0=kr[:], in1=den[:], op=ALU.mult))
        V(lambda e: e.tensor_tensor(out=ki[:], in0=t2[:], in1=lr[:], op=ALU.mult))
        V(lambda e: e.tensor_tensor(out=t3[:], in0=t1[:], in1=li[:], op=ALU.mult))
        V(lambda e: e.tensor_tensor(out=ki[:], in0=ki[:], in1=t3[:], op=ALU.subtract))
        V(lambda e: e.tensor_tensor(out=ki[:], in0=ki[:], in1=den[:], op=ALU.mult))
        V(lambda e: e.tensor_scalar_mul(out=nki[:], in0=ki[:], scalar1=-1.0))
        Er = self.sb([128, 16, T], F32)
        Ei = self.sb([128, 16, T], F32)
        ETr, ETi, emr, emi = sm(), sm(), sm(), sm()
        RE = Res("E")
        Rt = Res("ctmp")
        P.op("pool", lambda e: e.memset(Er[:, :, 0:1], 1.0), writes=[RE])
        P.op("pool", lambda e: e.memset(Ei[:, :, 0:1], 0.0), reads=[RE], writes=[RE])
        P.op("pool", lambda e: e.tensor_copy(out=Er[:, :, 1:2], in_=cs[:].unsqueeze(2)), reads=[Rp, RE], writes=[RE])
        P.op("pool", lambda e: e.tensor_copy(out=Ei[:, :, 1:2], in_=sn[:].unsqueeze(2)), reads=[Rp, RE], writes=[RE])
        LB = self.sb([128, 16, 2, 128], BF16)
        LC = self.sb([128, 16, 2, 128], BF16)
        RL = Res("LBC")
        ct = self.sb([128, 128], F32)
        dsk = self.sb([128, 4], F32)
        bgl = self.sb([128, 4], F32)
        Wg = self.sb([128, 4, 512], BF16)
        RWg = Res("wglu")
        wgl = self.d["w_glu"][l].rearrange("(a p) c -> p a c", p=128)
        self.load_cast([(Wg[:, :, 0:256], wgl[:, :, 0:256], [128, 4, 256]), (Wg[:, :, 256:512], wgl[:, :, 256:512], [128, 4, 256])],
                       RWg, stage_cols=1024)
        mark = self.cur
        big1 = self.sb([128, 16, 256], F32)
        big2 = self.sb([128, 16, 256], F32)
        m = 2
        while m < T:
            self.cmul("dve", emr[:], emi[:], Er[:, :, m - 1], Ei[:, :, m - 1], cs[:], sn[:], t1[:], t2[:], [RE, Rp], RE, Rt)
            bc = lambda a: a[:].unsqueeze(2).to_broadcast([128, 16, m])
            self.cmul("dve", Er[:, :, m:2 * m], Ei[:, :, m:2 * m], Er[:, :, 0:m], Ei[:, :, 0:m], bc(emr), bc(emi),
                      big1[:, :, 0:m], big2[:, :, 0:m], [RE], RE, Rt)
            m *= 2
        self.cmul("dve", ETr[:], ETi[:], Er[:, :, T - 1], Ei[:, :, T - 1], cs[:], sn[:], t1[:], t2[:], [RE, Rp], RE, Rt)
        Bp = [self.sb([128, 16, 128], F32) for _ in range(2)]
        Cp = [self.sb([128, 16, 128], F32) for _ in range(2)]
        Rpad = Res("pads")
        for t_ in Bp + Cp:
            P.op("pool", lambda e, t_=t_: e.memset(t_[:], 0.0), writes=[Rpad])
        for ri, (bn, cn) in enumerate((("ssm_b_re", "ssm_c_re"), ("ssm_b_im", "ssm_c_im"))):
            bsrc = self.d[bn][l].rearrange("(a q g) p c -> g q p a c", q=4, g=2)
            csrc = self.d[cn][l].rearrange("(a q g) c p -> g q c a p", q=4, g=2)
            for gl in range(2):
                for q in range(4):
                    P.dma("sp", lambda e, inc, ri=ri, gl=gl, q=q, bsrc=bsrc: inc(e.dma_start(
                        out=Bp[ri][gl * 64:(gl + 1) * 64, q:16:4, q * 32 + gl * 16:q * 32 + gl * 16 + 16], in_=bsrc[gl, q])), 1,
                        reads=[Rpad], writes=[Rpad])
                    P.dma("sp", lambda e, inc, ri=ri, gl=gl, q=q, csrc=csrc: inc(e.dma_start(
                        out=Cp[ri][q * 32 + gl * 16:q * 32 + gl * 16 + 16, q:16:4, gl * 64:(gl + 1) * 64], in_=csrc[gl, q])), 1,
                        reads=[Rpad], writes=[Rpad])
        for j in range(16):
            pb = 6 + j % 2
            ps = self.ps[pb]

            def tr(e, j=j, ps=ps):
                ins = None
                for q, src in enumerate((Bp[0], Bp[1], Cp[0], Cp[1])):
                    ins = e.transpose(out=ps[:, q * 128:(q + 1) * 128], in_=src[:, j, :], identity=self.ident[:])
                return ins
            P.op("pe", tr, reads=[Rpad, self.Rconst], writes=[self.psr[pb]])
            P.op("act", lambda e, j=j, ps=ps: e.copy(out=LB[:, j, :, :], in_=ps[:, 0:256].rearrange("p (a b) -> p a b", b=128)),
                 reads=[self.psr[pb]], writes=[RL])
            P.op("dve", lambda e, j=j, ps=ps: e.tensor_scalar(out=ct[:], in0=ps[:, 384:512], scalar1=ki[:, j:j + 1], scalar2=None,
                                                             op0=ALU.mult), reads=[self.psr[pb], Rp, RL], writes=[Rt])
            P.op("dve", lambda e, j=j, ps=ps: e.scalar_tensor_tensor(out=LC[:, j, 0, :], in0=ps[:, 256:384], scalar=kr[:, j:j + 1],
                                                                    in1=ct[:], op0=ALU.mult, op1=ALU.subtract),
                 reads=[self.psr[pb], Rp, Rt], writes=[RL])
            P.op("dve", lambda e, j=j, ps=ps: e.tensor_scalar(out=ct[:], in0=ps[:, 384:512], scalar1=kr[:, j:j + 1], scalar2=None,
                                                             op0=ALU.mult), reads=[self.psr[pb], Rp, RL], writes=[Rt])
            P.op("dve", lambda e, j=j, ps=ps: e.scalar_tensor_tensor(out=LC[:, j, 1, :], in0=ps[:, 256:384], scalar=nki[:, j:j + 1],
                                                                    in1=ct[:], op0=ALU.mult, op1=ALU.subtract),
                 reads=[self.psr[pb], Rp, Rt], writes=[RL])
        P.dma("sp", lambda e, inc: inc(e.dma_start(out=dsk[:], in_=self.d["ssm_d"][l].rearrange("g c -> (g c)").rearrange("(a p) -> p a", p=128),
                                                   allow_slow_non_contiguous=True)), 1, writes=[Rp])
        P.dma("sp", lambda e, inc: inc(e.dma_start(out=bgl[:], in_=self.d["b_glu"][l].rearrange("(a p) -> p a", p=128),
                                                   allow_slow_non_contiguous=True)), 1, writes=[Rp])
        self.P.barrier()
        self.cur = mark
        uf = [self.sb([128, 4, T], F32) for _ in range(2)]
        ub = [self.sb([128, 4, T], BF16) for _ in range(2)]
        Ruf = [Res("uf0"), Res("uf1")]
        Rub = [Res("ub0"), Res("ub1")]
        NB = 3
        m1 = [self.sb([128, T], F32) for _ in range(NB)]
        m2 = [self.sb([128, T], F32) for _ in range(NB)]
        m3 = [self.sb([128, T], F32) for _ in range(NB)]
        m4 = [self.sb([128, T], F32) for _ in range(NB)]
        gr = [self.sb([128, T], F32) for _ in range(NB)]
        gi = [self.sb([128, T], F32) for _ in range(NB)]
        Hr = [self.sb([128, T], BF16) for _ in range(NB)]
        Hi = [self.sb([128, T], BF16) for _ in range(NB)]
        Rm4 = [[Res("m%d_%d" % (i, q)) for q in range(4)] for i in range(NB)]
        Rgs = [[Res("gr%d" % i), Res("gi%d" % i)] for i in range(NB)]
        RHi = [Res("Hi%d" % i) for i in range(NB)]
        Rgl2 = Res("glast2")
        Rg = [Res("g%d" % i) for i in range(NB)]
        RH = [Res("H%d" % i) for i in range(NB)]
        glr, gli, gir, gii = sm(), sm(), sm(), sm()
        Rgl = Res("glast")
        Rgi = Res("ginit")
        P.op("pool", lambda e: e.memset(gir[:], 0.0), writes=[Rgi])
        P.op("pool", lambda e: e.memset(gii[:], 0.0), reads=[Rgi], writes=[Rgi])
        ysb = self.sb([128, T], F32)
        Rysb = Res("ysb")
        zf = self.sb([128, 4, T], F32)
        zb = self.sb([128, 4, T], BF16)
        Rzf, Rzb = Res("zf"), Res("zb")
        sgl = self.sb([128, T], F32)
        Rsgl = Res("sgl")
        ost = [self.sb([128, T], BF16) for _ in range(2)]
        Rost = [Res("ost0"), Res("ost1")]
        USTd = S["UST"].rearrange("(a p) t -> p a t", p=128)
        hm1, hm2, hm3, hm4 = m1, m2, m3, m4
        state = {"n": 0}

        def load_chunk(c):
            cj = c % 2
            P.dma("sp", lambda e, inc, c=c, cj=cj: inc(e.dma_start(out=uf[cj][:], in_=USTd[:, :, c * T:(c + 1) * T])), 1,
                  reads=[SR["UST"]], writes=[Ruf[cj]])
            P.op("pool", lambda e, cj=cj: e.tensor_copy(out=ub[cj][:], in_=uf[cj][:]), reads=[Ruf[cj]], writes=[Rub[cj]])

        def stageA(c, j):
            cj = c % 2
            kt = j // 4
            n = state["n"]
            state["n"] += 1
            b_ = n % NB
            pa, pbk = ((0, 1), (2, 3))[n % 2]
            P.op("pe", lambda e, j=j, kt=kt, cj=cj, pa=pa: e.matmul(self.ps[pa][:], lhsT=LB[:, j, 0, :], rhs=ub[cj][:, kt, :],
                                                                 start=True, stop=True), reads=[RL, Rub[cj]], writes=[self.psr[pa]])
            P.op("pe", lambda e, j=j, kt=kt, cj=cj, pbk=pbk: e.matmul(self.ps[pbk][:], lhsT=LB[:, j, 1, :], rhs=ub[cj][:, kt, :],
                                                                   start=True, stop=True), reads=[RL, Rub[cj]],
                 writes=[self.psr[pbk]])
            DV = lambda o, a, bb, op, rd, wr: P.op("dve", lambda e: e.tensor_tensor(out=o, in0=a, in1=bb, op=op), reads=rd, writes=wr)
            Pr_, Pi_ = self.ps[pa][:], self.ps[pbk][:]
            R1, R2, R3, R4 = Rm4[b_]
            DV(m3[b_][:], Pi_, Er[:, j, :], ALU.mult, [RE, self.psr[pbk]], [R3])
            DV(m4[b_][:], Pr_, Ei[:, j, :], ALU.mult, [RE, self.psr[pa]], [R4])
            DV(m1[b_][:], Pr_, Er[:, j, :], ALU.mult, [RE, self.psr[pa]], [R1])
            DV(m2[b_][:], Pi_, Ei[:, j, :], ALU.mult, [RE, self.psr[pbk]], [R2])
            DV(m3[b_][:], m3[b_][:], m4[b_][:], ALU.subtract, [R4], [R3])
            DV(m1[b_][:], m1[b_][:], m2[b_][:], ALU.add, [R2], [R1])
            P.op("dve", lambda e, b_=b_, j=j: e.tensor_tensor_scan(
                out=gi[b_][:], data0=rho[:, j:j + 1].to_broadcast([128, T]), data1=m3[b_][:], initial=gii[:, j:j + 1],
                op0=ALU.mult, op1=ALU.add), reads=[R3, Rp, Rgi], writes=[Rgs[b_][1]])
            P.op("dve", lambda e, b_=b_, j=j: e.tensor_tensor_scan(
                out=gr[b_][:], data0=rho[:, j:j + 1].to_broadcast([128, T]), data1=m1[b_][:], initial=gir[:, j:j + 1],
                op0=ALU.mult, op1=ALU.add), reads=[R1, Rp, Rgi], writes=[Rgs[b_][0]])
            return b_

        def stageB(c, j, b_):
            cj = c % 2
            kt = j // 4
            R1, R2, R3, R4 = Rm4[b_]
            Rgr, Rgi_ = Rgs[b_]
            P.op("pool", lambda e, b_=b_, j=j: e.tensor_copy(out=glr[:, j:j + 1], in_=gr[b_][:, T - 1:T]), reads=[Rgr], writes=[Rgl])
            P.op("pool", lambda e, b_=b_, j=j: e.tensor_copy(out=gli[:, j:j + 1], in_=gi[b_][:, T - 1:T]), reads=[Rgi_], writes=[Rgl2])
            PL = lambda o, a, bb, rd, wr: P.op("pool", lambda e: e.tensor_tensor(out=o, in0=a, in1=bb, op=ALU.mult), reads=rd, writes=wr)
            PL(hm2[b_][:], Ei[:, j, :], gi[b_][:], [RE, Rgi_], [R2])
            PL(hm3[b_][:], Er[:, j, :], gi[b_][:], [RE, Rgi_], [R3])
            PL(hm1[b_][:], Er[:, j, :], gr[b_][:], [RE, Rgr], [R1])
            PL(hm4[b_][:], Ei[:, j, :], gr[b_][:], [RE, Rgr], [R4])
            P.op("dve", lambda e, b_=b_: e.tensor_tensor(out=Hr[b_][:], in0=hm1[b_][:], in1=hm2[b_][:], op=ALU.subtract),
                 reads=[R1, R2], writes=[RH[b_]])
            P.op("dve", lambda e, b_=b_: e.tensor_tensor(out=Hi[b_][:], in0=hm3[b_][:], in1=hm4[b_][:], op=ALU.add),
                 reads=[R3, R4], writes=[RHi[b_]])
            py = 4 + kt % 2

            def mmy(e, j=j, b_=b_, py=py):
                e.matmul(self.ps[py][:], lhsT=LC[:, j, 0, :], rhs=Hr[b_][:], start=(j % 4 == 0), stop=False)
                return e.matmul(self.ps[py][:], lhsT=LC[:, j, 1, :], rhs=Hi[b_][:], start=False, stop=(j % 4 == 3))
            P.op("pe", mmy, reads=[RL, RH[b_], RHi[b_]], writes=[self.psr[py]], self_ok=(j % 4 != 0))
            if j % 4 == 3:
                P.op("dve", lambda e, kt=kt, cj=cj, py=py: e.scalar_tensor_tensor(
                    out=ysb[:], in0=uf[cj][:, kt, :], scalar=dsk[:, kt:kt + 1], in1=self.ps[py][:], op0=ALU.mult, op1=ALU.add),
                    reads=[self.psr[py], Ruf[cj], Rp], writes=[Rysb])
                P.op("act", lambda e, kt=kt: e.activation(out=zf[:, kt, :], in_=ysb[:], func=AF.Gelu_apprx_tanh),
                     reads=[Rysb], writes=[Rzf])
                P.op("act", lambda e, kt=kt: e.copy(out=zb[:, kt, :], in_=zf[:, kt, :]), reads=[Rzf], writes=[Rzb])

        load_chunk(0)
        for c in range(L // T):
            if c + 1 < L // T:
                load_chunk(c + 1)
            prev = None
            for j in range(16):
                b_ = stageA(c, j)
                if prev is not None:
                    stageB(c, prev[0], prev[1])
                prev = (j, b_)
            stageB(c, prev[0], prev[1])
            self.cmul("dve", gir[:], gii[:], ETr[:], ETi[:], glr[:], gli[:], t1[:], t2[:], [RE, Rgl, Rgl2], Rgi, Rt)
            for ot in range(4):
                def mmg(e, ot=ot):
                    ins = None
                    for a in range(4):
                        ins = e.matmul(self.ps[6][:], lhsT=Wg[:, a, ot * 128:(ot + 1) * 128], rhs=zb[:, a, :], start=(a == 0),
                                       stop=(a == 3))
                    return ins
                P.op("pe", mmg, reads=[RWg, Rzb], writes=[self.psr[6]])
                P.op("act", lambda e, ot=ot: e.activation(out=sgl[:], in_=self.ps[6][:], func=AF.Sigmoid, bias=bgl[:, ot:ot + 1]),
                     reads=[self.psr[6], Rp], writes=[Rsgl])
                oj = ot % 2
                P.op("dve", lambda e, ot=ot, oj=oj: e.tensor_tensor(out=ost[oj][:], in0=zf[:, ot, :], in1=sgl[:], op=ALU.mult),
                     reads=[Rzf, Rsgl], writes=[Rost[oj]])
                P.dma("sp", lambda e, inc, ot=ot, oj=oj, c=c: inc(e.dma_start(
                    out=S["YST"][ot * 128:(ot + 1) * 128, c * T:(c + 1) * T], in_=ost[oj][:])), 1, reads=[Rost[oj]],
                    writes=[SR["YST"]])
        self.phase_end()

    def phase_nsa(self, l):
        P = self.P
        S, SR = self.scr, self.sres
        ps, psr = self.ps, self.psr
        QT = self.sb([128, 4, L], BF16)
        KT = {n: self.sb([128, L], BF16) for n in ("KCT", "VCT")}
        RQ = Res("QTs")
        P.dma("sp", lambda e, inc: inc(e.dma_start(out=QT[:], in_=S["QT"][:, :, :])), 1, reads=[SR["QT"]], writes=[RQ])
        for n in KT:
            P.dma("sp", lambda e, inc, n=n: inc(e.dma_start(out=KT[n][:], in_=S[n][:, :])), 1, reads=[SR[n]], writes=[RQ])
        KWz = [self.sb([128, L], BF16) for _ in range(2)]
        KE = [self.sb([128, L], BF16) for _ in range(2)]
        for k in range(2):
            o = 1 - k
            P.op("pool", lambda e, k=k, o=o: e.memset(KWz[k][o * 64:(o + 1) * 64, :], 0.0), writes=[RQ])
            P.dma("sp", lambda e, inc, k=k: inc(e.dma_start(out=KWz[k][k * 64:(k + 1) * 64, :], in_=S["KWT"][k * 64:(k + 1) * 64, :])),
                  1, reads=[SR["KWT"], RQ], writes=[RQ])
            P.dma("sp", lambda e, inc, k=k: inc(e.dma_start(out=KE[k][k * 64:(k + 1) * 64, :], in_=S["KST"][k * 64:(k + 1) * 64, :])),
                  1, reads=[SR["KST"], RQ], writes=[RQ])
            P.dma("sp", lambda e, inc, k=k, o=o: inc(e.dma_start(out=KE[k][o * 64:(o + 1) * 64, :], in_=self.d["c_ebig"][:, :])),
                  1, reads=[RQ], writes=[RQ])
        Va = {n: self.sb([128, NT, 2, 65], BF16) for n in ("VS", "VW")}
        for n in Va:
            P.op("pool", lambda e, n=n: e.memset(Va[n][:, :, :, 64:65], 1.0), writes=[RQ])
            for k in range(2):
                P.dma("sp", lambda e, inc, n=n, k=k: inc(e.dma_start(
                    out=Va[n][:, :, k, 0:64], in_=S[n].rearrange("(a p) c -> p a c", p=128)[:, :, k * 64:(k + 1) * 64])), 1,
                    reads=[SR[n], RQ], writes=[RQ])
        G = self.sb([128, NT, 24], F32)
        P.dma("sp", lambda e, inc: inc(e.dma_start(out=G[:], in_=S["GN"].rearrange("(a p) c -> p a c", p=128))), 1,
              reads=[SR["GN"]], writes=[RQ])
        W1 = {t: self.sb([128, 32, 128], BF16) for t in "kv"}
        W2kd = self.sb([128, 128], BF16)
        W2v = self.sb([128, 64], BF16)
        RWc = Res("Wc")
        pieces = []
        for t, nm in (("k", "cmp_wk1"), ("v", "cmp_wv1")):
            src = self.d[nm][l].rearrange("(l d) h -> d l h", d=64)
            for half in range(2):
                for o in range(0, 32, 8):
                    pieces.append((W1[t][half * 64:(half + 1) * 64, o:o + 8, :], src[:, o:o + 8, :], [64, 8, 128], half * 64))
        pieces.append((W2kd[:, 0:64], self.d["cmp_wk2"][l], [128, 64]))
        pieces.append((W2kd[:, 64:128], self.d["cmp_wk2"][l], [128, 64]))
        pieces.append((W2v[:], self.d["cmp_wv2"][l], [128, 64]))
        self.load_cast(pieces, RWc, stage_cols=1024)
        pe_n = self.sb([32, 256], F32)
        Rpe = Res("pe")
        for ti, nm in enumerate(("cmp_pe_k", "cmp_pe_v")):
            for half in range(2):
                P.dma("sp", lambda e, inc, ti=ti, nm=nm, half=half: inc(e.dma_start(
                    out=pe_n[:, ti * 128 + half * 64:ti * 128 + half * 64 + 64], in_=self.d[nm][l])), 1, writes=[Rpe])
        peT = self.sb([128, 2, 32], BF16)
        P.op("pe", lambda e: (e.transpose(out=ps[7][:, 0:32], in_=pe_n[:, 0:128], identity=self.ident[0:32, 0:32]),
                              e.transpose(out=ps[7][:, 32:64], in_=pe_n[:, 128:256], identity=self.ident[0:32, 0:32]))[1],
             reads=[Rpe, self.Rconst], writes=[psr[7]])
        P.op("dve", lambda e: e.tensor_copy(out=peT[:], in_=ps[7][:, 0:64].rearrange("p (a b) -> p a b", b=32)), reads=[psr[7]],
             writes=[Rpe])
        cb = self.sb([128, 2], F32)
        for ti, t in enumerate("kv"):
            def mmb(e, ti=ti, t=t):
                ins = None
                for li_ in range(32):
                    ins = e.matmul(ps[7][:, 64 + ti:65 + ti], lhsT=W1[t][0:64, li_, :], rhs=peT[0:64, ti, li_:li_ + 1],
                                   start=(li_ == 0), stop=(li_ == 31))
                return ins
            P.op("pe", mmb, reads=[RWc, Rpe], writes=[psr[7]])
        P.op("dve", lambda e: e.tensor_copy(out=cb[:], in_=ps[7][:, 64:66]), reads=[psr[7]], writes=[Rpe])
        KcT = self.sb([128, 2, 256], BF16)
        rcmp = self.sb([128, 2, 2, 129], BF16)
        Rcmp = Res("cmpops")
        P.op("pool", lambda e: e.memset(KcT[:], 0.0), writes=[Rcmp])
        P.op("pool", lambda e: e.memset(rcmp[:], 0.0), reads=[Rcmp], writes=[Rcmp])
        P.op("pool", lambda e: e.memset(rcmp[:, :, :, 64:65], 1.0), reads=[Rcmp], writes=[Rcmp])
        ovf = self.sb([128, 2, 64], F32)
        P.dma("sp", lambda e, inc: inc(e.dma_start(out=ovf[:], in_=self.d["c_ov"].rearrange("(a p) s -> p a s", p=128))), 1,
              writes=[Rpe])
        for k in range(2):
            P.op("pool", lambda e, k=k: e.tensor_copy(out=rcmp[:, :, k, 65:129], in_=ovf[:]), reads=[Rpe, Rcmp], writes=[Rcmp])
        hid = [self.sb([128, 256], BF16) for _ in range(2)]
        Rhid = [Res("hid0"), Res("hid1")]
        for hj in range(2):
            P.op("pool", lambda e, hj=hj: e.memset(hid[hj][:], 0.0), writes=[Rhid[hj]])
        ci = 0
        for ti, (t, srcn) in enumerate((("k", "KCT"), ("v", "VCT"))):
            for k in range(2):
                hj = ci % 2
                pb = ci % 2
                ci += 1

                def mmh(e, t=t, srcn=srcn, k=k, pb=pb):
                    ins = None
                    for li_ in range(32):
                        ins = e.matmul(ps[pb][:, 0:255], lhsT=W1[t][k * 64:(k + 1) * 64, li_, :],
                                       rhs=KT[srcn][k * 64:(k + 1) * 64, li_:li_ + 4065:16], start=(li_ == 0), stop=(li_ == 31))
                    return ins
                P.op("pe", mmh, reads=[RWc, RQ], writes=[psr[pb]])
                P.op("act", lambda e, hj=hj, pb=pb, ti=ti: e.activation(out=hid[hj][:, 0:255], in_=ps[pb][:, 0:255],
                                                                      func=AF.Gelu_apprx_tanh, bias=cb[:, ti:ti + 1]),
                     reads=[psr[pb], Rpe], writes=[Rhid[hj]])
                if t == "k":
                    P.op("pe", lambda e, hj=hj: e.matmul(ps[2][:, 0:255], lhsT=W2kd[:], rhs=hid[hj][:, 0:255], start=True, stop=True),
                         reads=[RWc, Rhid[hj]], writes=[psr[2]])
                    P.op("dve", lambda e, k=k: e.tensor_copy(out=KcT[k * 64:(k + 1) * 64, k, 0:255],
                                                             in_=ps[2][k * 64:(k + 1) * 64, 0:255]), reads=[psr[2], Rcmp],
                         writes=[Rcmp])
                else:
                    for nt, rows in ((0, 128), (1, 127)):
                        P.op("pe", lambda e, hj=hj, nt=nt, rows=rows: e.matmul(
                            ps[3][0:rows, nt * 64:(nt + 1) * 64], lhsT=hid[hj][:, nt * 128:nt * 128 + rows], rhs=W2v[:], start=True,
                            stop=True), reads=[RWc, Rhid[hj]], writes=[psr[3]])
                        P.op("dve", lambda e, k=k, nt=nt, rows=rows: e.tensor_copy(out=rcmp[0:rows, nt, k, 0:64],
                                                                                 in_=ps[3][0:rows, nt * 64:(nt + 1) * 64]),
                             reads=[psr[3], Rcmp], writes=[Rcmp])
        import os
        if os.environ.get("NSA_UNITS") == "0":
            self.phase_end()
            return
        PT = [self.sb([128, 512], BF16) for _ in range(3)]
        RPT = [Res("PT%d" % i) for i in range(3)]
        yt = [self.sb([128, 512], F32) for _ in range(2)]
        Ryt = [Res("yt0"), Res("yt1")]
        yT = [self.sb([128, 4, 128], BF16) for _ in range(2)]
        RyT = [Res("yT0"), Res("yT1")]
        rsc = self.sb([128, 4], F32)
        wgt = self.sb([128, 4], F32)
        acc = self.sb([128, 64], F32)
        sc = self.sb([128, 64], F32)
        sc2 = self.sb([128, 64], F32)
        m8 = self.sb([128, 16], F32)
        biasf = self.sb([128, 128], F32)
        Rk = [self.sb([128, 4, 128], BF16) for _ in range(2)]
        Rpost = Res("post")
        Rbias = Res("biasf")
        RbT = [Res("R0"), Res("R1")]
        units = []
        deferred = []

        def flush_deferred():
            for f in deferred:
                f()
            deferred.clear()

        for qt in range(NT):
            for k in range(2):
                QTt = QT[:, :, qt * 128:(qt + 1) * 128]
                yj = qt % 2
                yacc = yt[yj][:, k * 256:(k + 1) * 256].rearrange("p (h d) -> p h d", d=64)
                gsl = lambda b, qt=qt, k=k: G[:, qt, k * 12 + b:k * 12 + 12:3]

                def post_cmp(qt=qt, k=k, yacc=yacc, gsl=gsl, yj=yj):
                    Dv = lambda fn, rd, wr: P.op("dve", fn, reads=rd, writes=wr)
                    for bi, b_ in enumerate((3, 4)):
                        Dv(lambda e, bi=bi, b_=b_: e.tensor_scalar(out=rsc[:, bi * 2:bi * 2 + 2], in0=ps[b_][:, 64:64 + 129 + 1:129],
                                                                    scalar1=1e-30, scalar2=None, op0=ALU.max), [psr[b_], Rpost], [Rpost])
                    Dv(lambda e: e.reciprocal(out=rsc[:], in_=rsc[:]), [Rpost], [Rpost])
                    for h in range(4):
                        b_ = 3 + h // 2
                        o = (h % 2) * 129
                        if h == 0:
                            Dv(lambda e, b_=b_, o=o: e.tensor_scalar(out=acc[:], in0=ps[b_][:, o + 65:o + 129], scalar1=rsc[:, 0:1],
                                                                     scalar2=None, op0=ALU.mult), [psr[b_], Rpost], [Rpost])
                        else:
                            Dv(lambda e, b_=b_, o=o, h=h: e.scalar_tensor_tensor(out=acc[:], in0=ps[b_][:, o + 65:o + 129],
                                                                                 scalar=rsc[:, h:h + 1], in1=acc[:], op0=ALU.mult,
                                                                                 op1=ALU.add), [psr[b_], Rpost], [Rpost])
                    Dv(lambda e: e.tensor_tensor(out=wgt[:], in0=rsc[:], in1=gsl(0), op=ALU.mult), [Rpost, RQ], [Rpost])
                    for h in range(4):
                        b_ = 3 + h // 2
                        o = (h % 2) * 129
                        Dv(lambda e, b_=b_, o=o, h=h: e.tensor_scalar(out=yacc[:, h, :], in0=ps[b_][:, o:o + 64], scalar1=wgt[:, h:h + 1],
                                                                       scalar2=None, op0=ALU.mult), [psr[b_], Rpost, Ryt[yj]], [Ryt[yj]])
                    Dv(lambda e: e.tensor_copy(out=sc[:], in_=acc[:]), [Rpost], [Rpost])
                    for e_ in range(2):
                        cur = 2 * qt + e_
                        rows = slice(e_ * 64, (e_ + 1) * 64)
                        if cur + 1 < 64:
                            Dv(lambda e, rows=rows, cur=cur: e.memset(sc[rows, cur + 1:64], -1e6), [Rpost], [Rpost])
                        Dv(lambda e, rows=rows: e.memset(sc[rows, 0:1], 1e6), [Rpost], [Rpost])
                        lo = max(cur - 1, 0)
                        Dv(lambda e, rows=rows, lo=lo, cur=cur: e.memset(sc[rows, lo:cur + 1], 1e6), [Rpost], [Rpost])
                    Dv(lambda e: e.max(out=m8[:, 0:8], in_=sc[:]), [Rpost], [Rpost])
                    Dv(lambda e: e.match_replace(out=sc2[:], in_to_replace=m8[:, 0:8], in_values=sc[:], imm_value=-1e30), [Rpost],
                       [Rpost])
                    Dv(lambda e: e.max(out=m8[:, 8:16], in_=sc2[:]), [Rpost], [Rpost])
                    for hh in range(2):
                        Dv(lambda e, hh=hh: e.tensor_scalar(out=biasf[:, hh * 64:(hh + 1) * 64], in0=sc[:], scalar1=m8[:, 15:16],
                                                            scalar2=-BIG, op0=ALU.is_lt, op1=ALU.mult), [Rpost, Rbias], [Rbias])

                def pre_sel(k=k, qt=qt):
                    flush_deferred()
                    o = 1 - k
                    P.op("pe", lambda e: e.transpose(out=ps[7][:, 0:128], in_=biasf[:], identity=self.ident[:]),
                         reads=[Rbias, self.Rconst], writes=[psr[7]])
                    P.op("pool", lambda e: e.tensor_copy(out=Rk[k][k * 64:(k + 1) * 64, :, :],
                                                         in_=QT[k * 64:(k + 1) * 64, :, qt * 128:(qt + 1) * 128]),
                         reads=[RQ], writes=[RbT[k]])
                    P.op("act", lambda e: e.copy(out=Rk[k][o * 64:(o + 1) * 64, :, :],
                                                 in_=ps[7][o * 64:(o + 1) * 64, 0:128].unsqueeze(1).to_broadcast([64, 4, 128])),
                         reads=[psr[7]], writes=[RbT[k]])

                def post_branch(bank, b, yacc=yacc, gsl=gsl, yj=yj):
                    def f():
                        Dv = lambda fn, rd, wr: P.op("dve", fn, reads=rd, writes=wr)
                        Dv(lambda e: e.reciprocal(out=rsc[:], in_=ps[bank][:, 64:64 + 3 * 65 + 1:65]), [psr[bank], Rpost], [Rpost])
                        Dv(lambda e: e.tensor_tensor(out=wgt[:], in0=rsc[:], in1=gsl(b), op=ALU.mult), [Rpost, RQ], [Rpost])
                        for h in range(4):
                            Dv(lambda e, h=h: e.scalar_tensor_tensor(out=yacc[:, h, :], in0=ps[bank][:, h * 65:h * 65 + 64],
                                                                     scalar=wgt[:, h:h + 1], in1=yacc[:, h, :], op0=ALU.mult,
                                                                     op1=ALU.add), [psr[bank], Rpost, Ryt[yj]], [Ryt[yj]])
                    return f

                def post_qt(qt=qt, yj=yj):
                    def tr_store():
                        def tr(e):
                            ins = None
                            for a in range(4):
                                ins = e.transpose(out=ps[7][:, a * 128:(a + 1) * 128], in_=yt[yj][:, a * 128:(a + 1) * 128],
                                                  identity=self.ident[:])
                            return ins
                        P.op("pe", tr, reads=[Ryt[yj], self.Rconst], writes=[psr[7]])
                        P.op("act", lambda e: e.copy(out=yT[yj][:], in_=ps[7][:].rearrange("p (a b) -> p a b", b=128)),
                             reads=[psr[7]], writes=[RyT[yj]])
                        P.dma("sp", lambda e, inc: inc(e.dma_start(
                            out=S["YNT"].rearrange("(a p) t -> p a t", p=128)[:, :, qt * 128:(qt + 1) * 128], in_=yT[yj][:])), 1,
                            reads=[RyT[yj]], writes=[SR["YNT"]])
                    return lambda: deferred.append(tr_store)

                nts = [0] if qt < 16 else [0, 1]
                for nt in nts:
                    rows = 128 if nt == 0 else 127
                    full = (128 * nt + rows - 1) <= 8 * qt - 2
                    mask = None if full else dict(pattern=[[0, 4], [1, 128]], base=128 * qt - 2048 * nt - 31, cm=-16)
                    units.append(dict(kind="cmp", rows=rows, lhsT=KcT[:, k, nt * 128:nt * 128 + rows], rhs=QTt,
                                      mask=mask, v=rcmp[0:rows, nt, k, :], first=(nt == 0), last=(nt == nts[-1]), pre=None,
                                      post=(post_cmp if nt == nts[-1] else None)))
                cmp_last_idx = len(units) - 1
                wk = list(range(max(0, qt - 4), qt + 1))
                for kt in wk:
                    mask = None
                    if kt == qt:
                        mask = dict(pattern=[[0, 4], [1, 128]], base=0, cm=-1)
                    elif kt == qt - 4:
                        mask = dict(pattern=[[0, 4], [-1, 128]], base=-1, cm=1)
                    units.append(dict(kind="win", rows=128, lhsT=KWz[k][:, kt * 128:(kt + 1) * 128], rhs=QTt,
                                      mask=mask, v=Va["VW"][:, kt, k, :], first=(kt == wk[0]), last=(kt == qt), pre=None,
                                      post=(post_branch(6, 2) if kt == qt else None), bank=6))
                for kt in range(qt + 1):
                    mask = dict(pattern=[[0, 4], [1, 128]], base=0, cm=-1) if kt == qt else None
                    posts = None
                    if kt == qt:
                        pb_ = post_branch(5, 1)
                        if k == 1:
                            pq = post_qt()
                            posts = (lambda pb_=pb_, pq=pq: (pb_(), post_qt_call(pq)))
                        else:
                            posts = pb_
                    units.append(dict(kind="sel", rows=128, lhsT=KE[k][:, kt * 128:(kt + 1) * 128], rhs=Rk[k][:],
                                      mask=mask, v=Va["VS"][:, kt, k, :], first=(kt == 0), last=(kt == qt),
                                      pre=(pre_sel if kt == 0 else None), post=posts, bank=5, k=k,
                                      after=(cmp_last_idx if kt == 0 else None)))

        def post_qt_call(pq):
            pq()

        def emit_qk(i):
            u = units[i]
            sb_ = i % 3
            if u["pre"] is not None:
                u["pre"]()
            rows = u["rows"]
            out3 = ps[sb_][0:rows, :].rearrange("p (a b) -> p a b", b=128)
            if u["kind"] == "sel":
                P.op("pe", lambda e, u=u, out3=out3: e.matmul(out3, lhsT=u["lhsT"], rhs=u["rhs"], start=True, stop=True),
                     reads=[RQ, RbT[u["k"]]], writes=[psr[sb_]])
            else:
                P.op("pe", lambda e, u=u, out3=out3: e.matmul(out3, lhsT=u["lhsT"], rhs=u["rhs"], start=True, stop=True),
                     reads=[RQ, Rcmp], writes=[psr[sb_]])

        def emit_rest(i):
            u = units[i]
            sb_ = i % 3
            rows = u["rows"]
            P.op("act", lambda e, sb_=sb_, rows=rows: e.activation(out=PT[sb_][0:rows, :], in_=ps[sb_][0:rows, :], func=AF.Exp,
                                                                  scale=0.125), reads=[psr[sb_]], writes=[RPT[sb_]])
            if u["mask"] is not None:
                mk = u["mask"]
                v3 = PT[sb_][0:rows, :].rearrange("p (a b) -> p a b", b=128)
                P.op("pool", lambda e, v3=v3, mk=mk: e.affine_select(out=v3, in_=v3, pattern=mk["pattern"], compare_op=ALU.is_ge,
                                                                    fill=0.0, base=mk["base"], channel_multiplier=mk["cm"]),
                     reads=[RPT[sb_]], writes=[RPT[sb_]])
            if u["kind"] == "cmp":
                def pv(e, u=u, sb_=sb_, rows=rows):
                    ins = None
                    for h in range(4):
                        b_ = 3 + h // 2
                        o = (h % 2) * 129
                        ins = e.matmul(ps[b_][:, o:o + 129], lhsT=PT[sb_][0:rows, h * 128:(h + 1) * 128], rhs=u["v"],
                                       start=(u["first"] and h % 2 == 0), stop=(u["last"] and h % 2 == 1))
                    return ins
                P.op("pe", pv, reads=[RPT[sb_], Rcmp], writes=[psr[3], psr[4]], self_ok=not u["first"])
            else:
                bank = u["bank"]

                def pv(e, u=u, sb_=sb_, bank=bank):
                    ins = None
                    for h in range(4):
                        ins = e.matmul(ps[bank][:, h * 65:(h + 1) * 65], lhsT=PT[sb_][:, h * 128:(h + 1) * 128], rhs=u["v"],
                                       start=(u["first"] and h == 0), stop=(u["last"] and h == 3))
                    return ins
                P.op("pe", pv, reads=[RPT[sb_], RQ], writes=[psr[bank]], self_ok=not u["first"])
            if u["post"] is not None:
                u["post"]()

        import os
        n = len(units)
        if os.environ.get("NSA_UNITS"):
            n = int(os.environ["NSA_UNITS"])
            deferred.clear()
        st_ = {"next": 1}

        def try_emit(limit, done_rest):
            while st_["next"] < n and st_["next"] <= limit:
                dep = units[st_["next"]].get("after")
                if dep is not None and dep > done_rest:
                    break
                emit_qk(st_["next"])
                st_["next"] += 1

        emit_qk(0)
        try_emit(2, -1)
        for i in range(n):
            assert st_["next"] > i
            emit_rest(i)
            try_emit(i + 3, i)
            if os.environ.get("NSA_UNITS") and i == n - 1:
                break
        flush_deferred()
        self.phase_end()

    def phase_merge(self, l):
        P = self.P
        S, SR = self.scr, self.sres
        w_in = self.d["w_in"][l].rearrange("(a p) c -> p a c", p=128)
        WU = self.sb([128, 12, D], BF16)
        WG = self.sb([128, 8, 3072], BF16)
        WO = self.sb([128, 8, D], BF16)
        RW = Res("WM")
        pieces = []
        for b, nm in enumerate(("w_up_pool", "w_up_ssm", "w_up_nsa")):
            wv = self.d[nm][l].rearrange("(a p) c -> p a c", p=128)
            for o in range(0, D, 512):
                pieces.append((WU[:, b * 4:(b + 1) * 4, o:o + 512], wv[:, :, o:o + 512], [128, 4, 512]))
        for o in range(0, 3072, 256):
            pieces.append((WG[:, :, o:o + 256], w_in[:, :, 2328 + o:2328 + o + 256], [128, 8, 256]))
        wo = self.d["w_out"][l].rearrange("(a p) c -> p a c", p=128)
        for o in range(0, D, 256):
            pieces.append((WO[:, :, o:o + 256], wo[:, :, o:o + 256], [128, 8, 256]))
        self.load_cast(pieces, RW)
        g_rep = self.sb([128, D], F32)
        b_rep = self.sb([128, D], F32)
        Rgb = Res("gb")
        self.bcast_load(g_rep[:], self.d["ln1_g"][l:l + 1, :], Rgb)
        self.bcast_load(b_rep[:], self.d["ln1_b"][l:l + 1, :], Rgb)
        xt = [self.sb([128, 8, 512], BF16) for _ in range(2)]
        Rxt = [Res("xt0"), Res("xt1")]
        yb = [self.sb([128, 12, 512], BF16) for _ in range(2)]
        Ryb = [Res("yb0"), Res("yb1")]
        mg = self.sb([128, 8, 512], BF16)
        Rmg = Res("mg")
        sg = [self.sb([128, 512], F32) for _ in range(2)]
        Rsg = [Res("sg0"), Res("sg1")]
        acc = self.sb([128, 512], F32)
        tmp = self.sb([128, 512], F32)
        Racc = Res("acc")
        Rtmp = Res("tmp")
        xr = [self.sb([128, D], F32) for _ in range(2)]
        Rxr = [Res("xr0"), Res("xr1")]
        bufs = [self.ln_bufs() for _ in range(2)]
        XTd = S["XT"].rearrange("(a p) t -> p a t", p=128)
        k = 0
        for c in range(8):
            j = c % 2
            P.dma("sp", lambda e, inc, c=c, j=j: inc(e.dma_start(out=xt[j][:], in_=XTd[:, :, c * 512:(c + 1) * 512])), 1,
                  reads=[SR["XT"]], writes=[Rxt[j]])
            for b, nm in enumerate(("YPT", "YST", "YNT")):
                P.dma("sp", lambda e, inc, c=c, j=j, b=b, nm=nm: inc(e.dma_start(
                    out=yb[j][:, b * 4:(b + 1) * 4, :],
                    in_=S[nm].rearrange("(a p) t -> p a t", p=128)[:, :, c * 512:(c + 1) * 512])), 1,
                    reads=[SR[nm]], writes=[Ryb[j]])
            for m in range(8):
                for b in range(3):
                    pu = k % 2
                    pg = 2 + k % 2
                    sj = k % 2
                    k += 1

                    def mmu(e, pu=pu, b=b, m=m, j=j):
                        ins = None
                        for a in range(4):
                            ins = e.matmul(self.ps[pu][:], lhsT=WU[:, b * 4 + a, m * 128:(m + 1) * 128], rhs=yb[j][:, b * 4 + a, :],
                                           start=(a == 0), stop=(a == 3))
                        return ins

                    def mmg(e, pg=pg, b=b, m=m, j=j):
                        ins = None
                        for a in range(8):
                            ins = e.matmul(self.ps[pg][:], lhsT=WG[:, a, b * 1024 + m * 128:b * 1024 + (m + 1) * 128],
                                           rhs=xt[j][:, a, :], start=(a == 0), stop=(a == 7))
                        return ins
                    P.op("pe", mmg, reads=[RW, Rxt[j]], writes=[self.psr[pg]])
                    P.op("pe", mmu, reads=[RW, Ryb[j]], writes=[self.psr[pu]])
                    P.op("act", lambda e, pg=pg, sj=sj: e.activation(out=sg[sj][:], in_=self.ps[pg][:], func=AF.Sigmoid),
                         reads=[self.psr[pg]], writes=[Rsg[sj]])
                    if b == 0:
                        P.op("dve", lambda e, pu=pu, sj=sj: e.tensor_tensor(out=acc[:], in0=self.ps[pu][:], in1=sg[sj][:],
                                                                            op=ALU.mult), reads=[self.psr[pu], Rsg[sj]],
                             writes=[Racc])
                    else:
                        P.op("dve", lambda e, pu=pu, sj=sj: e.tensor_tensor(out=tmp[:], in0=self.ps[pu][:], in1=sg[sj][:],
                                                                            op=ALU.mult), reads=[self.psr[pu], Rsg[sj]],
                             writes=[Rtmp])
                        if b == 1:
                            P.op("dve", lambda e: e.tensor_tensor(out=acc[:], in0=acc[:], in1=tmp[:], op=ALU.add),
                                 reads=[Rtmp, Racc], writes=[Racc])
                        else:
                            P.op("dve", lambda e, m=m: e.tensor_tensor(out=mg[:, m, :], in0=acc[:], in1=tmp[:], op=ALU.add),
                                 reads=[Rtmp, Racc], writes=[Rmg])
            for tt in range(4):
                i = c * 4 + tt
                xj = i % 2
                P.dma("sp", lambda e, inc, i=i, xj=xj: inc(e.dma_start(out=xr[xj][:], in_=S["XR"][i * 128:(i + 1) * 128, :])),
                      1, reads=[SR["XR"]], writes=[Rxr[xj]])
                for half in range(2):
                    pb = 4 + half

                    def mmo(e, pb=pb, tt=tt, half=half):
                        ins = None
                        for a in range(8):
                            ins = e.matmul(self.ps[pb][:], lhsT=mg[:, a, tt * 128:(tt + 1) * 128],
                                           rhs=WO[:, a, half * 512:(half + 1) * 512], start=(a == 0), stop=(a == 7))
                        return ins
                    P.op("pe", mmo, reads=[RW, Rmg], writes=[self.psr[pb]])
                    P.op("dve", lambda e, pb=pb, xj=xj, half=half: e.scalar_tensor_tensor(
                        out=xr[xj][:, half * 512:(half + 1) * 512], in0=xr[xj][:, half * 512:(half + 1) * 512], scalar=ALPHA,
                        in1=self.ps[pb][:], op0=ALU.mult, op1=ALU.add), reads=[self.psr[pb], Rxr[xj]], writes=[Rxr[xj]])
                self.ln_tile(i, xr[xj], Rxr[xj], g_rep, b_rep, Rgb, S["XR1"], SR["XR1"], S["X1T"], SR["X1T"], bufs[xj])
        self.phase_end()

    def phase_ffn(self, l, last):
        P = self.P
        S, SR = self.scr, self.sres
        W1 = self.sb([128, 8, 4096], BF16)
        W2 = self.sb([128, 32, D], BF16)
        RW = Res("WF")
        w1 = self.d["w_ff1"][l].rearrange("(a p) c -> p a c", p=128)
        w2 = self.d["w_ff2"][l].rearrange("(a p) c -> p a c", p=128)
        pieces = []
        for o in range(0, 4096, 128):
            pieces.append((W1[:, :, o:o + 128], w1[:, :, o:o + 128], [128, 8, 128]))
        for a in range(32):
            pieces.append((W2[:, a, :], w2[:, a, :], [128, D]))
        self.load_cast(pieces, RW, stage_cols=1024)
        g_rep = self.sb([128, D], F32)
        b_rep = self.sb([128, D], F32)
        Rgb = Res("gb")
        self.bcast_load(g_rep[:], self.d["ln2_g"][l:l + 1, :], Rgb)
        self.bcast_load(b_rep[:], self.d["ln2_b"][l:l + 1, :], Rgb)
        xt = [self.sb([128, 8, 512], BF16) for _ in range(1)]
        Rxt = [Res("xt0")]
        hT = self.sb([128, 32, 512], BF16)
        RhT = Res("hT")
        rl = [self.sb([128, 512], F32) for _ in range(2)]
        Rrl = [Res("rl0"), Res("rl1")]
        xr = [self.sb([128, D], F32) for _ in range(2)]
        Rxr = [Res("xr0"), Res("xr1")]
        bufs = [self.ln_bufs() for _ in range(2)]
        X1Td = S["X1T"].rearrange("(a p) t -> p a t", p=128)
        dst_res = self.d["out"] if last else S["XR"]
        Rdst = Res("outres") if last else SR["XR"]
        dst_T = None if last else S["XT"]
        k = 0
        for c in range(8):
            j = 0
            P.dma("sp", lambda e, inc, c=c, j=j: inc(e.dma_start(out=xt[j][:], in_=X1Td[:, :, c * 512:(c + 1) * 512])), 1,
                  reads=[SR["X1T"]], writes=[Rxt[j]])
            for f in range(32):
                pb = k % 4
                rj = k % 2
                k += 1

                def mm1(e, pb=pb, f=f, j=j):
                    ins = None
                    for a in range(8):
                        ins = e.matmul(self.ps[pb][:], lhsT=W1[:, a, f * 128:(f + 1) * 128], rhs=xt[j][:, a, :], start=(a == 0),
                                       stop=(a == 7))
                    return ins
                P.op("pe", mm1, reads=[RW, Rxt[j]], writes=[self.psr[pb]])
                P.op("act", lambda e, pb=pb, rj=rj: e.activation(out=rl[rj][:], in_=self.ps[pb][:], func=AF.Relu),
                     reads=[self.psr[pb]], writes=[Rrl[rj]])
                P.op("pool", lambda e, rj=rj, f=f: e.tensor_tensor(out=hT[:, f, :], in0=rl[rj][:], in1=rl[rj][:], op=ALU.mult),
                     reads=[Rrl[rj]], writes=[RhT])
            for tt in range(4):
                i = c * 4 + tt
                xj = i % 2
                P.dma("sp", lambda e, inc, i=i, xj=xj: inc(e.dma_start(out=xr[xj][:], in_=S["XR1"][i * 128:(i + 1) * 128, :])),
                      1, reads=[SR["XR1"]], writes=[Rxr[xj]])
                for half in range(2):
                    pb = 4 + half

                    def mm2(e, pb=pb, tt=tt, half=half):
                        ins = None
                        for f in range(32):
                            ins = e.matmul(self.ps[pb][:], lhsT=hT[:, f, tt * 128:(tt + 1) * 128],
                                           rhs=W2[:, f, half * 512:(half + 1) * 512], start=(f == 0), stop=(f == 31))
                        return ins
                    P.op("pe", mm2, reads=[RW, RhT], writes=[self.psr[pb]])
                    P.op("dve", lambda e, pb=pb, xj=xj, half=half: e.scalar_tensor_tensor(
                        out=xr[xj][:, half * 512:(half + 1) * 512], in0=xr[xj][:, half * 512:(half + 1) * 512], scalar=ALPHA,
                        in1=self.ps[pb][:], op0=ALU.mult, op1=ALU.add), reads=[self.psr[pb], Rxr[xj]], writes=[Rxr[xj]])
                self.ln_tile(i, xr[xj], Rxr[xj], g_rep, b_rep, Rgb, dst_res, Rdst, dst_T, SR["XT"], bufs[xj])
        self.phase_end()

    def build(self):
        self.phase_ln_in()
        for l in range(self.layers):
            stop = self.stop_after
            skip = self.skip
            if "proj" not in skip:
                self.phase_proj(l)
            if stop == "proj":
                break
            if "pool" not in skip:
                self.phase_pool(l)
            if stop == "pool":
                break
            if "s5" not in skip:
                self.phase_s5(l)
            if stop == "s5":
                break
            if "nsa" not in skip:
                self.phase_nsa(l)
            if stop == "nsa":
                break
            self.phase_merge(l)
            if stop == "merge":
                break
            self.phase_ffn(l, last=(l == self.layers - 1))
        self.P.emit()
        return self.nc


def make_inputs(inputs, ncores=8):
    consts = host_consts()
    shared = {n: np.ascontiguousarray(np.asarray(inputs[n], dtype=np.float32)) for n, _ in PARAMS}
    shared.update(consts)
    x = np.asarray(inputs["x"], dtype=np.float32)
    maps = []
    for c in range(ncores):
        m = dict(shared)
        m["x"] = np.ascontiguousarray(x[c])
        maps.append(m)
    return maps


def kernel(**inputs):
    b = Builder()
    nc = b.build()
    maps = make_inputs(inputs)
    res = run_bass_kernel_spmd(nc, maps, core_ids=list(range(8)))
    return np.stack([np.asarray(r["out"], dtype=np.float32) for r in res.results], axis=0)
```

```python
import contextlib
import math
import numpy as np
import ml_dtypes
import concourse.bass as bass
import concourse.mybir as mybir
from concourse.bass_utils import run_bass_kernel_spmd

F32 = mybir.dt.float32
BF16 = mybir.dt.bfloat16
AF = mybir.ActivationFunctionType
ALU = mybir.AluOpType
AX = mybir.AxisListType

L = 4096
D = 1024
NT = 32
DEPTH = 2
ALPHA = (2 * DEPTH) ** 0.25
EPS = 1e-5
BIG = 30000.0
SB_BASE = 16512
SB_END = 229344


class Res:
    __slots__ = ("name", "w", "r")

    def __init__(self, name=""):
        self.name = name
        self.w = None
        self.r = []


class Prog:
    ENGS = ("pe", "act", "dve", "pool", "sp")
    NDMA = 8

    def __init__(self, nc):
        self.nc = nc
        self.ops = {e: [] for e in self.ENGS}
        self.cnt = {}
        self.seen = {e: {} for e in self.ENGS}
        self.dma_n = {e: 0 for e in self.ENGS}
        self.sem_keys = []
        for e in ("pe", "act", "dve", "pool"):
            self._mk(("c", e))
        for e in ("sp", "act", "pool"):
            for j in range(self.NDMA):
                self._mk(("d", e, j))
        self.nops = 0

    def _mk(self, key):
        self.cnt[key] = 0
        self.sem_keys.append(key)

    def _need(self, eng, waits, dep):
        if dep is None:
            return
        key, val = dep
        if self.seen[eng].get(key, 0) >= val:
            return
        waits[key] = max(waits.get(key, 0), val)

    def _deps(self, eng, reads, writes, self_ok=False):
        waits = {}
        for r in reads:
            self._need(eng, waits, r.w)
        for w in writes:
            self._need(eng, waits, w.w)
            for rd in w.r:
                self._need(eng, waits, rd)
        if self_ok:
            waits.pop(("c", eng), None)
        return waits

    def _commit(self, eng, waits, tok, reads, writes):
        for k, v in waits.items():
            self.seen[eng][k] = v
        for r in reads:
            r.r.append(tok)
        for w in writes:
            w.w = tok
            w.r = []
        self.nops += 1

    def op(self, eng, fn, reads=(), writes=(), self_ok=False):
        waits = self._deps(eng, reads, writes, self_ok)
        key = ("c", eng)
        self.cnt[key] += 1
        tok = (key, self.cnt[key])
        self.ops[eng].append((list(waits.items()), fn, key, 1))
        self._commit(eng, waits, tok, reads, writes)
        return tok

    def dma(self, q, fn, ndma, reads=(), writes=()):
        waits = self._deps(q, reads, writes)
        j = self.dma_n[q] % self.NDMA
        self.dma_n[q] += 1
        key = ("d", q, j)
        if self.cnt[key] > 0:
            self._need(q, waits, (key, self.cnt[key]))
        self.cnt[key] += 16 * ndma
        tok = (key, self.cnt[key])
        self.ops[q].append((list(waits.items()), fn, key, 16))
        self._commit(q, waits, tok, reads, writes)
        return tok

    def barrier(self):
        for e in self.ENGS:
            waits = {}
            for k in self.sem_keys:
                if self.cnt[k] > 0 and not (k[0] == "c" and k[1] == e):
                    self._need(e, waits, (k, self.cnt[k]))
            for k, v in waits.items():
                self.seen[e][k] = v
            if waits:
                self.ops[e].append((list(waits.items()), None, None, 0))

    def emit(self):
        nc = self.nc
        with contextlib.ExitStack() as st:
            sems = {}
            for k in self.sem_keys:
                sems[k] = st.enter_context(nc.semaphore("s_" + "_".join(str(x) for x in k)))
            block = st.enter_context(nc.Block())
            final = [(k, v) for k, v in self.cnt.items() if v > 0]

            def run(engname):
                def body(eng):
                    for waits, fn, key, inc in self.ops[engname]:
                        for k, v in waits:
                            eng.wait_ge(sems[k], v)
                        if fn is None:
                            continue
                        if inc == 16:
                            fn(eng, lambda ins: ins.then_inc(sems[key], 16))
                        else:
                            fn(eng).then_inc(sems[key], 1)
                    if engname == "sp":
                        for k, v in final:
                            eng.wait_ge(sems[k], v)
                return body

            block.tensor(run("pe"))
            block.scalar(run("act"))
            block.vector(run("dve"))
            block.gpsimd(run("pool"))
            block.sync(run("sp"))


PARAMS = [("ln_in_g", (D,)), ("ln_in_b", (D,)), ("w_in", (2, D, 5400)), ("w_pool", (2, 4, 128, 128)),
          ("pool_scale", (2, 512)), ("ssm_lam_re", (2, 32, 64)), ("ssm_lam_im", (2, 32, 64)),
          ("ssm_log_dt", (2, 32)), ("ssm_b_re", (2, 32, 64, 16)), ("ssm_b_im", (2, 32, 64, 16)),
          ("ssm_c_re", (2, 32, 16, 64)), ("ssm_c_im", (2, 32, 16, 64)), ("ssm_d", (2, 32, 16)),
          ("w_glu", (2, 512, 512)), ("b_glu", (2, 512)), ("cmp_pe_k", (2, 32, 64)), ("cmp_pe_v", (2, 32, 64)),
          ("cmp_wk1", (2, 2048, 128)), ("cmp_wk2", (2, 128, 64)), ("cmp_wv1", (2, 2048, 128)),
          ("cmp_wv2", (2, 128, 64)), ("w_up_pool", (2, 512, D)), ("w_up_ssm", (2, 512, D)),
          ("w_up_nsa", (2, 512, D)), ("w_out", (2, D, D)), ("ln1_g", (2, D)), ("ln1_b", (2, D)),
          ("w_ff1", (2, D, 4096)), ("w_ff2", (2, 4096, D)), ("ln2_g", (2, D)), ("ln2_b", (2, D))]


def host_consts():
    c = {}
    c["c_ident"] = np.eye(128, dtype=np.float32)
    pos = np.arange(L, dtype=np.float32)
    inv_freq = (np.float32(500000.0) ** (-np.arange(0, 16, 2, dtype=np.float32) / np.float32(16))).astype(np.float32)
    ang = (pos[:, None] * inv_freq[None, :]).astype(np.float32)
    c["c_rope"] = np.concatenate([np.cos(ang), np.sin(ang)], axis=1).astype(np.float32)
    eb = np.zeros((64, 4096), np.float32)
    for s in range(64):
        eb[s, s * 64:(s + 1) * 64] = 1.0
    c["c_ebig"] = eb.astype(ml_dtypes.bfloat16)
    cs = np.arange(256)[:, None] * 16
    ss = np.arange(64)[None, :] * 64
    ov = np.maximum(np.minimum(cs + 32, ss + 64) - np.maximum(cs, ss), 0) / 16.0
    ov[255:] = 0.0
    c["c_ov"] = ov.astype(np.float32)
    ic = np.zeros((4, 16), np.float32)
    for gi, w in enumerate((2, 4, 8, 16)):
        ic[gi] = 1.0 / np.minimum(np.arange(16) + 1, w)
    c["c_invcnt"] = ic.reshape(1, 64)
    return c


class Builder:
    def __init__(self, dbg=(), stop_after=None, layers=DEPTH, inject=(), skip=()):
        self.inject = set(inject)
        self.skip = set(skip)
        self.nc = nc = bass.Bass("TRN2", target_bir_lowering=False)
        self.P = Prog(nc)
        self.dbg = set(dbg)
        self.stop_after = stop_after
        self.layers = layers
        self.d = {}
        self.d["x"] = nc.dram_tensor("x", [L, D], F32, kind="ExternalInput").ap()
        for n, shp in PARAMS:
            self.d[n] = nc.dram_tensor(n, list(shp), F32, kind="ExternalInput").ap()
        for n, a in host_consts().items():
            dt = BF16 if a.dtype == ml_dtypes.bfloat16 else F32
            self.d[n] = nc.dram_tensor(n, list(a.shape), dt, kind="ExternalInput").ap()
        self.d["out"] = nc.dram_tensor("out", [L, D], F32, kind="ExternalOutput").ap()
        self.scr = {}
        self.sres = {}
        for n, shp, dt in [("XR", [L, D], F32), ("XT", [D, L], BF16), ("UPT", [512, L], F32), ("UST", [512, L], F32),
                           ("QT", [128, 4, L], BF16), ("KCT", [128, L], BF16), ("KST", [128, L], BF16),
                           ("KWT", [128, L], BF16), ("VCT", [128, L], BF16), ("VS", [L, 128], BF16),
                           ("VW", [L, 128], BF16), ("GN", [L, 24], F32), ("YPT", [512, L], BF16),
                           ("YST", [512, L], BF16), ("YNT", [512, L], BF16), ("XR1", [L, D], F32),
                           ("X1T", [D, L], BF16)]:
            kind = "ExternalOutput" if n in self.dbg else ("ExternalInput" if n in self.inject else "Internal")
            self.scr[n] = nc.dram_tensor("s_" + n, shp, dt, kind=kind).ap()
            self.sres[n] = Res(n)
        self.ps = [nc.alloc_psum_tensor("psb%d" % i, [128, 512], F32) for i in range(8)]
        self.psr = [Res("ps%d" % i) for i in range(8)]
        self.cur = SB_BASE
        self.uid = 0
        self.ident = self.sb([128, 128], F32)
        self.Rconst = Res("const")
        self.P.dma("sp", lambda e, inc: inc(e.dma_start(out=self.ident[:], in_=self.d["c_ident"][:, :])), 1,
                   writes=[self.Rconst])
        self.persist = self.cur

    def sb(self, shape, dt):
        nbytes = int(np.prod(shape[1:])) * (2 if dt == BF16 else 4)
        off = (self.cur + 63) // 64 * 64
        assert off + nbytes <= SB_END, ("SBUF overflow", off + nbytes - SB_END)
        self.cur = off + nbytes
        self.uid += 1
        return self.nc.alloc_sbuf_tensor_at("t%d" % self.uid, list(shape), dt, offset=off)

    def phase_end(self):
        self.P.barrier()
        self.cur = self.persist

    def load_cast(self, pieces, res, stage_cols=2048):
        P = self.P
        if not hasattr(self, "_lc"):
            self._lc = None
        stg = [self.sb([128, stage_cols], F32) for _ in range(3)]
        rs = [Res("stg%d" % i) for i in range(3)]
        for i, pc in enumerate(pieces):
            dst, src, shp = pc[0], pc[1], pc[2]
            p0 = pc[3] if len(pc) > 3 else 0
            j = i % 3
            n = int(np.prod(shp[1:]))
            assert n <= stage_cols
            p = shp[0]
            sview = stg[j][p0:p0 + p, 0:n]
            if len(shp) == 3:
                sview = sview.rearrange("p (a b) -> p a b", b=shp[2])
            P.dma("sp", (lambda sv, s: lambda e, inc: inc(e.dma_start(out=sv, in_=s)))(sview, src), 1, writes=[rs[j]])
            if i % 2 == 0:
                P.op("act", (lambda dd, sv: lambda e: e.copy(out=dd, in_=sv))(dst, sview), reads=[rs[j]], writes=[res])
            else:
                P.op("dve", (lambda dd, sv: lambda e: e.tensor_copy(out=dd, in_=sv))(dst, sview), reads=[rs[j]], writes=[res])

    def bcast_load(self, dst, src_row, res):
        self.P.dma("sp", lambda e, inc: inc(e.dma_start(out=dst, in_=src_row.partition_broadcast(128))), 1, writes=[res])

    def ln_tile(self, i, src, Rsrc, g_rep, b_rep, Rgb, dst_res_d, Rres, dst_T_d, RT, bufs):
        P = self.P
        st6, mv, rstd, xT, RxT, Rst = bufs
        P.op("dve", lambda e: e.bn_stats(out=st6[:, 0, :], in_=src[:, 0:512]), reads=[Rsrc], writes=[Rst])
        P.op("dve", lambda e: e.bn_stats(out=st6[:, 1, :], in_=src[:, 512:1024]), reads=[Rsrc, Rst], writes=[Rst])
        P.op("dve", lambda e: e.bn_aggr(out=mv[:], in_=st6[:]), reads=[Rst], writes=[Rst])
        P.op("dve", lambda e: e.tensor_scalar_add(out=rstd[:], in0=mv[:, 1:2], scalar1=EPS), reads=[Rst], writes=[Rst])
        P.op("act", lambda e: e.sqrt(out=rstd[:], in_=rstd[:]), reads=[Rst], writes=[Rst])
        P.op("dve", lambda e: e.reciprocal(out=rstd[:], in_=rstd[:]), reads=[Rst], writes=[Rst])
        P.op("dve", lambda e: e.tensor_scalar(out=src[:], in0=src[:], scalar1=mv[:, 0:1], scalar2=rstd[:, 0:1],
                                              op0=ALU.subtract, op1=ALU.mult), reads=[Rst, Rsrc], writes=[Rsrc])
        P.op("dve", lambda e: e.tensor_tensor(out=src[:], in0=src[:], in1=g_rep[:], op=ALU.mult), reads=[Rsrc, Rgb],
             writes=[Rsrc])
        P.op("dve", lambda e: e.tensor_tensor(out=src[:], in0=src[:], in1=b_rep[:], op=ALU.add), reads=[Rsrc, Rgb],
             writes=[Rsrc])
        P.dma("sp", lambda e, inc: inc(e.dma_start(out=dst_res_d[i * 128:(i + 1) * 128, :], in_=src[:])), 1,
              reads=[Rsrc], writes=[Rres])
        if dst_T_d is None:
            return
        for half in range(2):
            pb = 6 + half
            ps = self.ps[pb]

            def tr(e, half=half, ps=ps):
                ins = None
                for q in range(4):
                    dt = half * 4 + q
                    ins = e.transpose(out=ps[:, q * 128:(q + 1) * 128], in_=src[:, dt * 128:(dt + 1) * 128],
                                      identity=self.ident[:])
                return ins
            P.op("pe", tr, reads=[Rsrc, self.Rconst], writes=[self.psr[pb]])
            P.op("act", lambda e, half=half, ps=ps: e.copy(out=xT[:, half * 4:(half + 1) * 4, :],
                                                         in_=ps[:].rearrange("p (a b) -> p a b", b=128)),
                 reads=[self.psr[pb]], writes=[RxT])
        P.dma("sp", lambda e, inc: inc(e.dma_start(
            out=dst_T_d.rearrange("(a p) t -> p a t", p=128)[:, :, i * 128:(i + 1) * 128], in_=xT[:])), 1,
            reads=[RxT], writes=[RT])

    def ln_bufs(self):
        return (self.sb([128, 2, 6], F32), self.sb([128, 2], F32), self.sb([128, 1], F32),
                self.sb([128, 8, 128], BF16), Res("xT"), Res("lnst"))

    def phase_ln_in(self):
        P = self.P
        g_rep = self.sb([128, D], F32)
        b_rep = self.sb([128, D], F32)
        Rgb = Res("gb")
        self.bcast_load(g_rep[:], self.d["ln_in_g"].unsqueeze(0), Rgb)
        self.bcast_load(b_rep[:], self.d["ln_in_b"].unsqueeze(0), Rgb)
        xb = [self.sb([128, D], F32) for _ in range(2)]
        Rx = [Res("x0"), Res("x1")]
        bufs = [self.ln_bufs() for _ in range(2)]
        for i in range(NT):
            j = i % 2
            P.dma("sp", lambda e, inc, i=i, j=j: inc(e.dma_start(out=xb[j][:], in_=self.d["x"][i * 128:(i + 1) * 128, :])),
                  1, writes=[Rx[j]])
            self.ln_tile(i, xb[j], Rx[j], g_rep, b_rep, Rgb, self.scr["XR"], self.sres["XR"], self.scr["XT"],
                         self.sres["XT"], bufs[j])
        self.phase_end()

    def phase_proj(self, l):
        P = self.P
        w_in = self.d["w_in"][l].rearrange("(a p) c -> p a c", p=128)
        NC_ = 1024 + 1304
        WP = self.sb([128, 8, NC_], BF16)
        RW = Res("WP")
        plan = [(0, 0, 1024)]
        qorder = (0, 4, 1, 5, 2, 6, 3, 7)
        for j, h in enumerate(qorder):
            plan.append((1024 + j * 64, 1024 + h * 64, 64))
        kv0 = 1536
        base = 1024 + 512
        for j, s in enumerate((0, 2, 4, 1)):
            plan.append((base + j * 128, kv0 + s * 128, 128))
        base += 512
        for j, s in enumerate((3, 5)):
            plan.append((base + j * 128, kv0 + s * 128, 128))
        plan.append((base + 256, 2304, 24))
        pieces = []
        for dc, sc, n in plan:
            o = 0
            while o < n:
                m = min(256, n - o)
                pieces.append((WP[:, :, dc + o:dc + o + m], w_in[:, :, sc + o:sc + o + m], [128, 8, m]))
                o += m
        self.load_cast(pieces, RW)
        rope = self.sb([128, NT, 16], F32)
        Rrope = Res("rope")
        P.dma("sp", lambda e, inc: inc(e.dma_start(out=rope[:], in_=self.d["c_rope"].rearrange("(a p) c -> p a c", p=128))),
              1, writes=[Rrope])
        xt = [self.sb([128, 8, 512], BF16) for _ in range(2)]
        Rxt = [Res("xt0"), Res("xt1")]
        fst = [self.sb([128, 512], F32) for _ in range(2)]
        Rfst = [Res("f0"), Res("f1")]
        tm = [self.sb([128, 1304], F32) for _ in range(2)]
        Rtm = [Res("tm0"), Res("tm1")]
        t1 = self.sb([128, 14, 8], F32)
        t2 = self.sb([128, 14, 8], F32)
        t3 = self.sb([128, 14, 8], F32)
        Rt = Res("ropetmp")
        trs = [self.sb([128, 8, 128], BF16) for _ in range(2)]
        Rtrs = [Res("trs0"), Res("trs1")]
        vst = [self.sb([128, 256], BF16) for _ in range(2)]
        Rvst = [Res("vst0"), Res("vst1")]
        XTd = self.scr["XT"].rearrange("(a p) t -> p a t", p=128)
        S = self.scr
        SR = self.sres
        kcnt = 0
        for c in range(8):
            j = c % 2
            P.dma("sp", lambda e, inc, c=c, j=j: inc(e.dma_start(out=xt[j][:], in_=XTd[:, :, c * 512:(c + 1) * 512])), 1,
                  reads=[SR["XT"]], writes=[Rxt[j]])
            for ft in range(8):
                pb = ft % 2
                ps = self.ps[pb]

                def mm(e, ft=ft, ps=ps, j=j):
                    ins = None
                    for a in range(8):
                        ins = e.matmul(ps[:], lhsT=WP[:, a, ft * 128:(ft + 1) * 128], rhs=xt[j][:, a, :], start=(a == 0),
                                       stop=(a == 7))
                    return ins
                P.op("pe", mm, reads=[RW, Rxt[j]], writes=[self.psr[pb]])
                fj = ft % 2
                P.op("act", lambda e, ps=ps, fj=fj: e.copy(out=fst[fj][:], in_=ps[:]), reads=[self.psr[pb]],
                     writes=[Rfst[fj]])
                dst = (S["UPT"] if ft < 4 else S["UST"])
                dres = SR["UPT"] if ft < 4 else SR["UST"]
                r0 = (ft % 4) * 128
                P.dma("sp", lambda e, inc, dst=dst, r0=r0, c=c, fj=fj: inc(e.dma_start(
                    out=dst[r0:r0 + 128, c * 512:(c + 1) * 512], in_=fst[fj][:])), 1, reads=[Rfst[fj]], writes=[dres])
            for tt in range(4):
                i = c * 4 + tt
                tj = i % 2
                T = tm[tj]
                for ch, (c0, n) in enumerate(((0, 512), (512, 512), (1024, 280))):
                    pb = 2 + (kcnt % 3)
                    kcnt += 1
                    ps = self.ps[pb]

                    def mm2(e, ps=ps, j=j, tt=tt, c0=c0, n=n):
                        ins = None
                        for a in range(8):
                            ins = e.matmul(ps[:, 0:n], lhsT=xt[j][:, a, tt * 128:(tt + 1) * 128],
                                           rhs=WP[:, a, 1024 + c0:1024 + c0 + n], start=(a == 0), stop=(a == 7))
                        return ins
                    P.op("pe", mm2, reads=[RW, Rxt[j]], writes=[self.psr[pb]])
                    P.op("act", lambda e, ps=ps, T=T, c0=c0, n=n: e.copy(out=T[:, c0:c0 + n], in_=ps[:, 0:n]),
                         reads=[self.psr[pb]], writes=[Rtm[tj]])
                H3 = T[:, 0:896].rearrange("p (h d) -> p h d", d=64)
                x1 = H3[:, :, 0:8]
                x2 = H3[:, :, 8:16]
                cosb = rope[:, i, 0:8].unsqueeze(1).to_broadcast([128, 14, 8])
                sinb = rope[:, i, 8:16].unsqueeze(1).to_broadcast([128, 14, 8])
                P.op("dve", lambda e, x1=x1, sinb=sinb: e.tensor_tensor(out=t1[:], in0=x1, in1=sinb, op=ALU.mult),
                     reads=[Rtm[tj], Rrope], writes=[Rt])
                P.op("dve", lambda e, x2=x2, sinb=sinb: e.tensor_tensor(out=t2[:], in0=x2, in1=sinb, op=ALU.mult),
                     reads=[Rtm[tj], Rrope, Rt], writes=[Rt])
                P.op("dve", lambda e, x1=x1, cosb=cosb: e.tensor_tensor(out=x1, in0=x1, in1=cosb, op=ALU.mult),
                     reads=[Rtm[tj], Rrope, Rt], writes=[Rtm[tj]])
                P.op("dve", lambda e, x2=x2, cosb=cosb: e.tensor_tensor(out=t3[:], in0=x2, in1=cosb, op=ALU.mult),
                     reads=[Rtm[tj], Rrope, Rt], writes=[Rt])
                P.op("dve", lambda e, x1=x1: e.tensor_tensor(out=x1, in0=x1, in1=t2[:], op=ALU.subtract),
                     reads=[Rtm[tj], Rt], writes=[Rtm[tj]])
                P.op("dve", lambda e, x2=x2: e.tensor_tensor(out=x2, in0=t1[:], in1=t3[:], op=ALU.add),
                     reads=[Rtm[tj], Rt], writes=[Rtm[tj]])
                for half in range(2):
                    pb = 6 + half
                    ps = self.ps[pb]

                    def tr(e, half=half, ps=ps, T=T):
                        ins = None
                        for q in range(4):
                            a = half * 4 + q
                            ins = e.transpose(out=ps[:, q * 128:(q + 1) * 128], in_=T[:, a * 128:(a + 1) * 128],
                                              identity=self.ident[:])
                        return ins
                    P.op("pe", tr, reads=[Rtm[tj], self.Rconst], writes=[self.psr[pb]])
                    P.op("act", lambda e, half=half, ps=ps, tj=tj: e.copy(
                        out=trs[tj][:, half * 4:(half + 1) * 4, :], in_=ps[:].rearrange("p (a b) -> p a b", b=128)),
                        reads=[self.psr[pb]], writes=[Rtrs[tj]])
                t0 = i * 128
                P.dma("sp", lambda e, inc, tj=tj, t0=t0: inc(e.dma_start(out=S["QT"][:, :, t0:t0 + 128],
                                                                       in_=trs[tj][:, 0:4, :])), 1,
                      reads=[Rtrs[tj]], writes=[SR["QT"]])
                for a, nm in ((4, "KCT"), (5, "KST"), (6, "KWT"), (7, "VCT")):
                    P.dma("sp", lambda e, inc, tj=tj, t0=t0, a=a, nm=nm: inc(e.dma_start(
                        out=S[nm][:, t0:t0 + 128], in_=trs[tj][:, a, :])), 1, reads=[Rtrs[tj]], writes=[SR[nm]])
                P.op("pool", lambda e, tj=tj, T=T: e.tensor_copy(out=vst[tj][:], in_=T[:, 1024:1280]), reads=[Rtm[tj]],
                     writes=[Rvst[tj]])
                P.dma("sp", lambda e, inc, tj=tj, t0=t0: inc(e.dma_start(out=S["VS"][t0:t0 + 128, :], in_=vst[tj][:, 0:128])),
                      1, reads=[Rvst[tj]], writes=[SR["VS"]])
                P.dma("sp", lambda e, inc, tj=tj, t0=t0: inc(e.dma_start(out=S["VW"][t0:t0 + 128, :], in_=vst[tj][:, 128:256])),
                      1, reads=[Rvst[tj]], writes=[SR["VW"]])
                P.op("act", lambda e, T=T: e.activation(out=T[:, 1280:1304], in_=T[:, 1280:1304], func=AF.Sigmoid),
                     reads=[Rtm[tj]], writes=[Rtm[tj]])
                P.dma("sp", lambda e, inc, T=T, t0=t0: inc(e.dma_start(out=S["GN"][t0:t0 + 128, :], in_=T[:, 1280:1304])),
                      1, reads=[Rtm[tj]], writes=[SR["GN"]])
        self.phase_end()

    def phase_pool(self, l):
        P = self.P
        S, SR = self.scr, self.sres
        wp = self.sb([128, 4, 128], BF16)
        Rwp = Res("wpool")
        self.load_cast([(wp[:, g, :], self.d["w_pool"][l, g], [128, 128]) for g in range(4)], Rwp, stage_cols=128)
        psc = self.sb([128, 4], F32)
        Rpsc = Res("psc")
        P.dma("sp", lambda e, inc: inc(e.dma_start(out=psc[:], in_=self.d["pool_scale"][l].rearrange("(g p) -> p g", p=128),
                                                       allow_slow_non_contiguous=True)),
              1, writes=[Rpsc])
        icn = self.sb([128, 64], F32)
        P.dma("sp", lambda e, inc: inc(e.dma_start(out=icn[:], in_=self.d["c_invcnt"][0:1, :].partition_broadcast(128))),
              1, writes=[Rpsc])
        H = 16
        U = [self.sb([128, H + L], F32) for _ in range(2)]
        A = [self.sb([128, H + L], F32) for _ in range(2)]
        Bb = [self.sb([128, H + L], F32) for _ in range(2)]
        Z = [self.sb([128, L], BF16) for _ in range(2)]
        tmp16 = [self.sb([128, 16], F32) for _ in range(2)]
        yst = [self.sb([128, 512], BF16) for _ in range(2)]
        Ryst = [Res("y0"), Res("y1")]
        RU = [Res("U0"), Res("U1")]
        for j in range(2):
            eng = "dve" if j == 0 else "pool"
            for buf in (U[j], A[j], Bb[j]):
                P.op(eng, lambda e, buf=buf: e.memset(buf[:, 0:H], 0.0), writes=[RU[j]])
        for g in range(4):
            j = g % 2
            eng = "dve" if j == 0 else "pool"
            w = 2 << g
            P.dma("sp", lambda e, inc, g=g, j=j: inc(e.dma_start(out=U[j][:, H:], in_=S["UPT"][g * 128:(g + 1) * 128, :])), 1,
                  reads=[SR["UPT"]], writes=[RU[j]])
            src = U[j]
            dsts = [A[j], Bb[j]]
            for k in range(g + 1):
                sh = 1 << k
                dst = dsts[k % 2]
                P.op(eng, lambda e, dst=dst, src=src, sh=sh: e.tensor_tensor(out=dst[:, H:], in0=src[:, H:],
                                                                              in1=src[:, H - sh:H + L - sh], op=ALU.add),
                     reads=[RU[j]], writes=[RU[j]])
                src = dst
            P.op("dve", lambda e, src=src, j=j, w=w: e.scalar_tensor_tensor(out=Z[j][:], in0=src[:, H:], scalar=1.0 / w,
                                                                          in1=U[j][:, H:], op0=ALU.mult,
                                                                          op1=ALU.subtract), reads=[RU[j]], writes=[RU[j]])
            P.op(eng, lambda e, src=src, j=j, g=g: e.tensor_tensor(out=tmp16[j][:], in0=src[:, H:H + 16], in1=icn[:, g * 16:(g + 1) * 16],
                                                                   op=ALU.mult), reads=[RU[j], Rpsc], writes=[RU[j]])
            P.op(eng, lambda e, j=j: e.tensor_tensor(out=Z[j][:, 0:16], in0=tmp16[j][:], in1=U[j][:, H:H + 16],
                                                     op=ALU.subtract), reads=[RU[j]], writes=[RU[j]])
            for c in range(8):
                pb = c % 2
                ps = self.ps[pb]
                P.op("pe", lambda e, ps=ps, g=g, j=j, c=c: e.matmul(ps[:], lhsT=wp[:, g, :], rhs=Z[j][:, c * 512:(c + 1) * 512],
                                                                  start=True, stop=True), reads=[Rwp, RU[j]],
                     writes=[self.psr[pb]])
                yj = c % 2
                P.op("act", lambda e, ps=ps, yj=yj, g=g: e.activation(out=yst[yj][:], in_=ps[:], func=AF.Copy,
                                                                     scale=psc[:, g:g + 1]), reads=[self.psr[pb], Rpsc],
                     writes=[Ryst[yj]])
                P.dma("sp", lambda e, inc, g=g, c=c, yj=yj: inc(e.dma_start(
                    out=S["YPT"][g * 128:(g + 1) * 128, c * 512:(c + 1) * 512], in_=yst[yj][:])), 1, reads=[Ryst[yj]],
                    writes=[SR["YPT"]])
        self.phase_end()

    def cmul(self, eng, outr, outi, ar, ai, br, bi, t1, t2, Rin, Rout, Rt):
        P = self.P
        tt = lambda o, a, b, op: (lambda e: e.tensor_tensor(out=o, in0=a, in1=b, op=op))
        P.op(eng, tt(t1, ar, br, ALU.mult), reads=Rin, writes=[Rt])
        P.op(eng, tt(t2, ai, bi, ALU.mult), reads=Rin + [Rt], writes=[Rt])
        P.op(eng, tt(outr, t1, t2, ALU.subtract), reads=[Rt], writes=[Rout])
        P.op(eng, tt(t1, ar, bi, ALU.mult), reads=Rin + [Rt, Rout], writes=[Rt])
        P.op(eng, tt(t2, ai, br, ALU.mult), reads=Rin + [Rt], writes=[Rt])
        P.op(eng, tt(outi, t1, t2, ALU.add), reads=[Rt], writes=[Rout])

    def phase_s5(self, l):
        P = self.P
        S, SR = self.scr, self.sres
        T = 512
        sm = lambda: self.sb([128, 16], F32)
        lam_n = self.sb([16, 256], F32)
        Rp = Res("s5prep")
        P.dma("sp", lambda e, inc: inc(e.dma_start(out=lam_n[:, 0:128], in_=self.d["ssm_lam_re"][l].rearrange("(j g) p -> j (g p)", g=2))),
              1, writes=[Rp])
        P.dma("sp", lambda e, inc: inc(e.dma_start(out=lam_n[:, 128:256], in_=self.d["ssm_lam_im"][l].rearrange("(j g) p -> j (g p)", g=2))),
              1, writes=[Rp])
        lr, li, stp, rho, th, sn, cs, t1, t2, t3, kr, ki, nki, den = [sm() for _ in range(14)]
        P.op("pe", lambda e: (e.transpose(out=self.ps[7][:, 0:16], in_=lam_n[:, 0:128], identity=self.ident[0:16, 0:16]),
                              e.transpose(out=self.ps[7][:, 16:32], in_=lam_n[:, 128:256], identity=self.ident[0:16, 0:16]))[1],
             reads=[Rp, self.Rconst], writes=[self.psr[7]])
        P.op("dve", lambda e: e.tensor_copy(out=lr[:], in_=self.ps[7][:, 0:16]), reads=[self.psr[7]], writes=[Rp])
        P.op("dve", lambda e: e.tensor_copy(out=li[:], in_=self.ps[7][:, 16:32]), reads=[self.psr[7], Rp], writes=[Rp])
        ldt = self.d["ssm_log_dt"][l:l + 1, :].rearrange("o (j g) -> o j g", g=2)
        for gl in range(2):
            P.dma("sp", lambda e, inc, gl=gl: inc(e.dma_start(out=stp[gl * 64:(gl + 1) * 64, :],
                                                            in_=ldt[:, :, gl].partition_broadcast(64),
                                                            allow_slow_non_contiguous=True)), 1, reads=[Rp], writes=[Rp])
        P.op("act", lambda e: e.activation(out=stp[:], in_=stp[:], func=AF.Exp), reads=[Rp], writes=[Rp])
        V = lambda fn: P.op("dve", fn, reads=[Rp], writes=[Rp])
        V(lambda e: e.tensor_tensor(out=t1[:], in0=lr[:], in1=stp[:], op=ALU.mult))
        P.op("act", lambda e: e.activation(out=rho[:], in_=t1[:], func=AF.Exp), reads=[Rp], writes=[Rp])
        V(lambda e: e.tensor_tensor(out=th[:], in0=li[:], in1=stp[:], op=ALU.mult))
        TWO_PI = 2.0 * math.pi
        ni = self.sb([128, 16], mybir.dt.int32)
        nf = sm()

        def reduce_turns(dst, shift):
            V(lambda e: e.tensor_scalar(out=dst[:], in0=th[:], scalar1=1.0 / TWO_PI, scalar2=shift, op0=ALU.mult, op1=ALU.add))
            V(lambda e: e.tensor_copy(out=ni[:], in_=dst[:]))
            V(lambda e: e.tensor_copy(out=nf[:], in_=ni[:]))
            V(lambda e: e.tensor_tensor(out=dst[:], in0=dst[:], in1=nf[:], op=ALU.subtract))
            V(lambda e: e.tensor_single_scalar(out=nf[:], in_=dst[:], scalar=0.5, op=ALU.is_gt))
            V(lambda e: e.tensor_tensor(out=dst[:], in0=dst[:], in1=nf[:], op=ALU.subtract))
            V(lambda e: e.tensor_single_scalar(out=nf[:], in_=dst[:], scalar=-0.5, op=ALU.is_lt))
            V(lambda e: e.tensor_tensor(out=dst[:], in0=dst[:], in1=nf[:], op=ALU.add))
        reduce_turns(t1, 0.0)
        reduce_turns(t2, 0.25)
        P.op("act", lambda e: e.activation(out=sn[:], in_=t1[:], func=AF.Sin, scale=TWO_PI), reads=[Rp], writes=[Rp])
        P.op("act", lambda e: e.activation(out=cs[:], in_=t2[:], func=AF.Sin, scale=TWO_PI), reads=[Rp], writes=[Rp])
        V(lambda e: e.tensor_tensor(out=t1[:], in0=rho[:], in1=cs[:], op=ALU.mult))
        V(lambda e: e.tensor_scalar_add(out=t1[:], in0=t1[:], scalar1=-1.0))
        V(lambda e: e.tensor_tensor(out=t2[:], in0=rho[:], in1=sn[:], op=ALU.mult))
        V(lambda e: e.tensor_tensor(out=den[:], in0=lr[:], in1=lr[:], op=ALU.mult))
        V(lambda e: e.tensor_tensor(out=t3[:], in0=li[:], in1=li[:], op=ALU.mult))
        V(lambda e: e.tensor_tensor(out=den[:], in0=den[:], in1=t3[:], op=ALU.add))
        V(lambda e: e.reciprocal(out=den[:], in_=den[:]))
        V(lambda e: e.tensor_tensor(out=kr[:], in0=t1[:], in1=lr[:], op=ALU.mult))
        V(lambda e: e.tensor_tensor(out=t3[:], in0=t2[:], in1=li[:], op=ALU.mult))
        V(lambda e: e.tensor_tensor(out=kr[:], in0=kr[:], in1=t3[:], op=ALU.add))
        V(lambda e: e.tensor_tensor(out=kr[:], in0=kr[:], in1=den[:], op=ALU.mult))
        V(lambda e: e.tensor_tensor(out=ki[:], in0=t2[:], in1=lr[:], op=ALU.mult))
        V(lambda e: e.tensor_tensor(out=t3[:], in0=t1[:], in1=li[:], op=ALU.mult))
        V(lambda e: e.tensor_tensor(out=ki[:], in0=ki[:], in1=t3[:], op=ALU.subtract))
        V(lambda e: e.tensor_tensor(out=ki[:], in0=ki[:], in1=den[:], op=ALU.mult))
        V(lambda e: e.tensor_scalar_mul(out=nki[:], in0=ki[:], scalar1=-1.0))
        Er = self.sb([128, 16, T], F32)
        Ei = self.sb([128, 16, T], F32)
        ETr, ETi, emr, emi = sm(), sm(), sm(), sm()
        RE = Res("E")
        Rt = Res("ctmp")
        P.op("pool", lambda e: e.memset(Er[:, :, 0:1], 1.0), writes=[RE])
        P.op("pool", lambda e: e.memset(Ei[:, :, 0:1], 0.0), reads=[RE], writes=[RE])
        P.op("pool", lambda e: e.tensor_copy(out=Er[:, :, 1:2], in_=cs[:].unsqueeze(2)), reads=[Rp, RE], writes=[RE])
        P.op("pool", lambda e: e.tensor_copy(out=Ei[:, :, 1:2], in_=sn[:].unsqueeze(2)), reads=[Rp, RE], writes=[RE])
        LB = self.sb([128, 16, 2, 128], BF16)
        LC = self.sb([128, 16, 2, 128], BF16)
        RL = Res("LBC")
        ct = self.sb([128, 128], F32)
        dsk = self.sb([128, 4], F32)
        bgl = self.sb([128, 4], F32)
        Wg = self.sb([128, 4, 512], BF16)
        RWg = Res("wglu")
        wgl = self.d["w_glu"][l].rearrange("(a p) c -> p a c", p=128)
        self.load_cast([(Wg[:, :, 0:256], wgl[:, :, 0:256], [128, 4, 256]), (Wg[:, :, 256:512], wgl[:, :, 256:512], [128, 4, 256])],
                       RWg, stage_cols=1024)
        mark = self.cur
        big1 = self.sb([128, 16, 256], F32)
        big2 = self.sb([128, 16, 256], F32)
        m = 2
        while m < T:
            self.cmul("dve", emr[:], emi[:], Er[:, :, m - 1], Ei[:, :, m - 1], cs[:], sn[:], t1[:], t2[:], [RE, Rp], RE, Rt)
            bc = lambda a: a[:].unsqueeze(2).to_broadcast([128, 16, m])
            self.cmul("dve", Er[:, :, m:2 * m], Ei[:, :, m:2 * m], Er[:, :, 0:m], Ei[:, :, 0:m], bc(emr), bc(emi),
                      big1[:, :, 0:m], big2[:, :, 0:m], [RE], RE, Rt)
            m *= 2
        self.cmul("dve", ETr[:], ETi[:], Er[:, :, T - 1], Ei[:, :, T - 1], cs[:], sn[:], t1[:], t2[:], [RE, Rp], RE, Rt)
        Bp = [self.sb([128, 16, 128], F32) for _ in range(2)]
        Cp = [self.sb([128, 16, 128], F32) for _ in range(2)]
        Rpad = Res("pads")
        for t_ in Bp + Cp:
            P.op("pool", lambda e, t_=t_: e.memset(t_[:], 0.0), writes=[Rpad])
        for ri, (bn, cn) in enumerate((("ssm_b_re", "ssm_c_re"), ("ssm_b_im", "ssm_c_im"))):
            bsrc = self.d[bn][l].rearrange("(a q g) p c -> g q p a c", q=4, g=2)
            csrc = self.d[cn][l].rearrange("(a q g) c p -> g q c a p", q=4, g=2)
            for gl in range(2):
                for q in range(4):
                    P.dma("sp", lambda e, inc, ri=ri, gl=gl, q=q, bsrc=bsrc: inc(e.dma_start(
                        out=Bp[ri][gl * 64:(gl + 1) * 64, q:16:4, q * 32 + gl * 16:q * 32 + gl * 16 + 16], in_=bsrc[gl, q])), 1,
                        reads=[Rpad], writes=[Rpad])
                    P.dma("sp", lambda e, inc, ri=ri, gl=gl, q=q, csrc=csrc: inc(e.dma_start(
                        out=Cp[ri][q * 32 + gl * 16:q * 32 + gl * 16 + 16, q:16:4, gl * 64:(gl + 1) * 64], in_=csrc[gl, q])), 1,
                        reads=[Rpad], writes=[Rpad])
        for j in range(16):
            pb = 6 + j % 2
            ps = self.ps[pb]

            def tr(e, j=j, ps=ps):
                ins = None
                for q, src in enumerate((Bp[0], Bp[1], Cp[0], Cp[1])):
                    ins = e.transpose(out=ps[:, q * 128:(q + 1) * 128], in_=src[:, j, :], identity=self.ident[:])
                return ins
            P.op("pe", tr, reads=[Rpad, self.Rconst], writes=[self.psr[pb]])
            P.op("act", lambda e, j=j, ps=ps: e.copy(out=LB[:, j, :, :], in_=ps[:, 0:256].rearrange("p (a b) -> p a b", b=128)),
                 reads=[self.psr[pb]], writes=[RL])
            P.op("dve", lambda e, j=j, ps=ps: e.tensor_scalar(out=ct[:], in0=ps[:, 384:512], scalar1=ki[:, j:j + 1], scalar2=None,
                                                             op0=ALU.mult), reads=[self.psr[pb], Rp, RL], writes=[Rt])
            P.op("dve", lambda e, j=j, ps=ps: e.scalar_tensor_tensor(out=LC[:, j, 0, :], in0=ps[:, 256:384], scalar=kr[:, j:j + 1],
                                                                    in1=ct[:], op0=ALU.mult, op1=ALU.subtract),
                 reads=[self.psr[pb], Rp, Rt], writes=[RL])
            P.op("dve", lambda e, j=j, ps=ps: e.tensor_scalar(out=ct[:], in0=ps[:, 384:512], scalar1=kr[:, j:j + 1], scalar2=None,
                                                             op0=ALU.mult), reads=[self.psr[pb], Rp, RL], writes=[Rt])
            P.op("dve", lambda e, j=j, ps=ps: e.scalar_tensor_tensor(out=LC[:, j, 1, :], in0=ps[:, 256:384], scalar=nki[:, j:j + 1],
                                                                    in1=ct[:], op0=ALU.mult, op1=ALU.subtract),
                 reads=[self.psr[pb], Rp, Rt], writes=[RL])
        P.dma("sp", lambda e, inc: inc(e.dma_start(out=dsk[:], in_=self.d["ssm_d"][l].rearrange("g c -> (g c)").rearrange("(a p) -> p a", p=128),
                                                   allow_slow_non_contiguous=True)), 1, writes=[Rp])
        P.dma("sp", lambda e, inc: inc(e.dma_start(out=bgl[:], in_=self.d["b_glu"][l].rearrange("(a p) -> p a", p=128),
                                                   allow_slow_non_contiguous=True)), 1, writes=[Rp])
        self.P.barrier()
        self.cur = mark
        uf = [self.sb([128, 4, T], F32) for _ in range(2)]
        ub = [self.sb([128, 4, T], BF16) for _ in range(2)]
        Ruf = [Res("uf0"), Res("uf1")]
        Rub = [Res("ub0"), Res("ub1")]
        NB = 3
        m1 = [self.sb([128, T], F32) for _ in range(NB)]
        m2 = [self.sb([128, T], F32) for _ in range(NB)]
        m3 = [self.sb([128, T], F32) for _ in range(NB)]
        m4 = [self.sb([128, T], F32) for _ in range(NB)]
        gr = [self.sb([128, T], F32) for _ in range(NB)]
        gi = [self.sb([128, T], F32) for _ in range(NB)]
        Hr = [self.sb([128, T], BF16) for _ in range(NB)]
        Hi = [self.sb([128, T], BF16) for _ in range(NB)]
        Rm4 = [[Res("m%d_%d" % (i, q)) for q in range(4)] for i in range(NB)]
        Rgs = [[Res("gr%d" % i), Res("gi%d" % i)] for i in range(NB)]
        RHi = [Res("Hi%d" % i) for i in range(NB)]
        Rgl2 = Res("glast2")
        Rg = [Res("g%d" % i) for i in range(NB)]
        RH = [Res("H%d" % i) for i in range(NB)]
        glr, gli, gir, gii = sm(), sm(), sm(), sm()
        Rgl = Res("glast")
        Rgi = Res("ginit")
        P.op("pool", lambda e: e.memset(gir[:], 0.0), writes=[Rgi])
        P.op("pool", lambda e: e.memset(gii[:], 0.0), reads=[Rgi], writes=[Rgi])
        ysb = self.sb([128, T], F32)
        Rysb = Res("ysb")
        zf = self.sb([128, 4, T], F32)
        zb = self.sb([128, 4, T], BF16)
        Rzf, Rzb = Res("zf"), Res("zb")
        sgl = self.sb([128, T], F32)
        Rsgl = Res("sgl")
        ost = [self.sb([128, T], BF16) for _ in range(2)]
        Rost = [Res("ost0"), Res("ost1")]
        USTd = S["UST"].rearrange("(a p) t -> p a t", p=128)
        hm1, hm2, hm3, hm4 = m1, m2, m3, m4
        state = {"n": 0}

        def load_chunk(c):
            cj = c % 2
            P.dma("sp", lambda e, inc, c=c, cj=cj: inc(e.dma_start(out=uf[cj][:], in_=USTd[:, :, c * T:(c + 1) * T])), 1,
                  reads=[SR["UST"]], writes=[Ruf[cj]])
            P.op("pool", lambda e, cj=cj: e.tensor_copy(out=ub[cj][:], in_=uf[cj][:]), reads=[Ruf[cj]], writes=[Rub[cj]])

        def stageA(c, j):
            cj = c % 2
            kt = j // 4
            n = state["n"]
            state["n"] += 1
            b_ = n % NB
            pa, pbk = ((0, 1), (2, 3))[n % 2]
            P.op("pe", lambda e, j=j, kt=kt, cj=cj, pa=pa: e.matmul(self.ps[pa][:], lhsT=LB[:, j, 0, :], rhs=ub[cj][:, kt, :],
                                                                 start=True, stop=True), reads=[RL, Rub[cj]], writes=[self.psr[pa]])
            P.op("pe", lambda e, j=j, kt=kt, cj=cj, pbk=pbk: e.matmul(self.ps[pbk][:], lhsT=LB[:, j, 1, :], rhs=ub[cj][:, kt, :],
                                                                   start=True, stop=True), reads=[RL, Rub[cj]],
                 writes=[self.psr[pbk]])
            DV = lambda o, a, bb, op, rd, wr: P.op("dve", lambda e: e.tensor_tensor(out=o, in0=a, in1=bb, op=op), reads=rd, writes=wr)
            Pr_, Pi_ = self.ps[pa][:], self.ps[pbk][:]
            R1, R2, R3, R4 = Rm4[b_]
            DV(m3[b_][:], Pi_, Er[:, j, :], ALU.mult, [RE, self.psr[pbk]], [R3])
            DV(m4[b_][:], Pr_, Ei[:, j, :], ALU.mult, [RE, self.psr[pa]], [R4])
            DV(m1[b_][:], Pr_, Er[:, j, :], ALU.mult, [RE, self.psr[pa]], [R1])
            DV(m2[b_][:], Pi_, Ei[:, j, :], ALU.mult, [RE, self.psr[pbk]], [R2])
            DV(m3[b_][:], m3[b_][:], m4[b_][:], ALU.subtract, [R4], [R3])
            DV(m1[b_][:], m1[b_][:], m2[b_][:], ALU.add, [R2], [R1])
            P.op("dve", lambda e, b_=b_, j=j: e.tensor_tensor_scan(
                out=gi[b_][:], data0=rho[:, j:j + 1].to_broadcast([128, T]), data1=m3[b_][:], initial=gii[:, j:j + 1],
                op0=ALU.mult, op1=ALU.add), reads=[R3, Rp, Rgi], writes=[Rgs[b_][1]])
            P.op("dve", lambda e, b_=b_, j=j: e.tensor_tensor_scan(
                out=gr[b_][:], data0=rho[:, j:j + 1].to_broadcast([128, T]), data1=m1[b_][:], initial=gir[:, j:j + 1],
                op0=ALU.mult, op1=ALU.add), reads=[R1, Rp, Rgi], writes=[Rgs[b_][0]])
            return b_

        def stageB(c, j, b_):
            cj = c % 2
            kt = j // 4
            R1, R2, R3, R4 = Rm4[b_]
            Rgr, Rgi_ = Rgs[b_]
            P.op("pool", lambda e, b_=b_, j=j: e.tensor_copy(out=glr[:, j:j + 1], in_=gr[b_][:, T - 1:T]), reads=[Rgr], writes=[Rgl])
            P.op("pool", lambda e, b_=b_, j=j: e.tensor_copy(out=gli[:, j:j + 1], in_=gi[b_][:, T - 1:T]), reads=[Rgi_], writes=[Rgl2])
            PL = lambda o, a, bb, rd, wr: P.op("pool", lambda e: e.tensor_tensor(out=o, in0=a, in1=bb, op=ALU.mult), reads=rd, writes=wr)
            PL(hm2[b_][:], Ei[:, j, :], gi[b_][:], [RE, Rgi_], [R2])
            PL(hm3[b_][:], Er[:, j, :], gi[b_][:], [RE, Rgi_], [R3])
            PL(hm1[b_][:], Er[:, j, :], gr[b_][:], [RE, Rgr], [R1])
            PL(hm4[b_][:], Ei[:, j, :], gr[b_][:], [RE, Rgr], [R4])
            P.op("dve", lambda e, b_=b_: e.tensor_tensor(out=Hr[b_][:], in0=hm1[b_][:], in1=hm2[b_][:], op=ALU.subtract),
                 reads=[R1, R2], writes=[RH[b_]])
            P.op("dve", lambda e, b_=b_: e.tensor_tensor(out=Hi[b_][:], in0=hm3[b_][:], in1=hm4[b_][:], op=ALU.add),
                 reads=[R3, R4], writes=[RHi[b_]])
            py = 4 + kt % 2

            def mmy(e, j=j, b_=b_, py=py):
                e.matmul(self.ps[py][:], lhsT=LC[:, j, 0, :], rhs=Hr[b_][:], start=(j % 4 == 0), stop=False)
                return e.matmul(self.ps[py][:], lhsT=LC[:, j, 1, :], rhs=Hi[b_][:], start=False, stop=(j % 4 == 3))
            P.op("pe", mmy, reads=[RL, RH[b_], RHi[b_]], writes=[self.psr[py]], self_ok=(j % 4 != 0))
            if j % 4 == 3:
                P.op("dve", lambda e, kt=kt, cj=cj, py=py: e.scalar_tensor_tensor(
                    out=ysb[:], in0=uf[cj][:, kt, :], scalar=dsk[:, kt:kt + 1], in1=self.ps[py][:], op0=ALU.mult, op1=ALU.add),
                    reads=[self.psr[py], Ruf[cj], Rp], writes=[Rysb])
                P.op("act", lambda e, kt=kt: e.activation(out=zf[:, kt, :], in_=ysb[:], func=AF.Gelu_apprx_tanh),
                     reads=[Rysb], writes=[Rzf])
                P.op("act", lambda e, kt=kt: e.copy(out=zb[:, kt, :], in_=zf[:, kt, :]), reads=[Rzf], writes=[Rzb])

        load_chunk(0)
        for c in range(L // T):
            if c + 1 < L // T:
                load_chunk(c + 1)
            prev = None
            for j in range(16):
                b_ = stageA(c, j)
                if prev is not None:
                    stageB(c, prev[0], prev[1])
                prev = (j, b_)
            stageB(c, prev[0], prev[1])
            self.cmul("dve", gir[:], gii[:], ETr[:], ETi[:], glr[:], gli[:], t1[:], t2[:], [RE, Rgl, Rgl2], Rgi, Rt)
            for ot in range(4):
                def mmg(e, ot=ot):
                    ins = None
                    for a in range(4):
                        ins = e.matmul(self.ps[6][:], lhsT=Wg[:, a, ot * 128:(ot + 1) * 128], rhs=zb[:, a, :], start=(a == 0),
                                       stop=(a == 3))
                    return ins
                P.op("pe", mmg, reads=[RWg, Rzb], writes=[self.psr[6]])
                P.op("act", lambda e, ot=ot: e.activation(out=sgl[:], in_=self.ps[6][:], func=AF.Sigmoid, bias=bgl[:, ot:ot + 1]),
                     reads=[self.psr[6], Rp], writes=[Rsgl])
                oj = ot % 2
                P.op("dve", lambda e, ot=ot, oj=oj: e.tensor_tensor(out=ost[oj][:], in0=zf[:, ot, :], in1=sgl[:], op=ALU.mult),
                     reads=[Rzf, Rsgl], writes=[Rost[oj]])
                P.dma("sp", lambda e, inc, ot=ot, oj=oj, c=c: inc(e.dma_start(
                    out=S["YST"][ot * 128:(ot + 1) * 128, c * T:(c + 1) * T], in_=ost[oj][:])), 1, reads=[Rost[oj]],
                    writes=[SR["YST"]])
        self.phase_end()

    def phase_nsa(self, l):
        P = self.P
        S, SR = self.scr, self.sres
        ps, psr = self.ps, self.psr
        QT = self.sb([128, 4, L], BF16)
        KT = {n: self.sb([128, L], BF16) for n in ("KCT", "VCT")}
        RQ = Res("QTs")
        P.dma("sp", lambda e, inc: inc(e.dma_start(out=QT[:], in_=S["QT"][:, :, :])), 1, reads=[SR["QT"]], writes=[RQ])
        for n in KT:
            P.dma("sp", lambda e, inc, n=n: inc(e.dma_start(out=KT[n][:], in_=S[n][:, :])), 1, reads=[SR[n]], writes=[RQ])
        KWz = [self.sb([128, L], BF16) for _ in range(2)]
        KE = [self.sb([128, L], BF16) for _ in range(2)]
        for k in range(2):
            o = 1 - k
            P.op("pool", lambda e, k=k, o=o: e.memset(KWz[k][o * 64:(o + 1) * 64, :], 0.0), writes=[RQ])
            P.dma("sp", lambda e, inc, k=k: inc(e.dma_start(out=KWz[k][k * 64:(k + 1) * 64, :], in_=S["KWT"][k * 64:(k + 1) * 64, :])),
                  1, reads=[SR["KWT"], RQ], writes=[RQ])
            P.dma("sp", lambda e, inc, k=k: inc(e.dma_start(out=KE[k][k * 64:(k + 1) * 64, :], in_=S["KST"][k * 64:(k + 1) * 64, :])),
                  1, reads=[SR["KST"], RQ], writes=[RQ])
            P.dma("sp", lambda e, inc, k=k, o=o: inc(e.dma_start(out=KE[k][o * 64:(o + 1) * 64, :], in_=self.d["c_ebig"][:, :])),
                  1, reads=[RQ], writes=[RQ])
        Va = {n: self.sb([128, NT, 2, 65], BF16) for n in ("VS", "VW")}
        for n in Va:
            P.op("pool", lambda e, n=n: e.memset(Va[n][:, :, :, 64:65], 1.0), writes=[RQ])
            for k in range(2):
                P.dma("sp", lambda e, inc, n=n, k=k: inc(e.dma_start(
                    out=Va[n][:, :, k, 0:64], in_=S[n].rearrange("(a p) c -> p a c", p=128)[:, :, k * 64:(k + 1) * 64])), 1,
                    reads=[SR[n], RQ], writes=[RQ])
        G = self.sb([128, NT, 24], F32)
        P.dma("sp", lambda e, inc: inc(e.dma_start(out=G[:], in_=S["GN"].rearrange("(a p) c -> p a c", p=128))), 1,
              reads=[SR["GN"]], writes=[RQ])
        W1 = {t: self.sb([128, 32, 128], BF16) for t in "kv"}
        W2kd = self.sb([128, 128], BF16)
        W2v = self.sb([128, 64], BF16)
        RWc = Res("Wc")
        pieces = []
        for t, nm in (("k", "cmp_wk1"), ("v", "cmp_wv1")):
            src = self.d[nm][l].rearrange("(l d) h -> d l h", d=64)
            for half in range(2):
                for o in range(0, 32, 8):
                    pieces.append((W1[t][half * 64:(half + 1) * 64, o:o + 8, :], src[:, o:o + 8, :], [64, 8, 128], half * 64))
        pieces.append((W2kd[:, 0:64], self.d["cmp_wk2"][l], [128, 64]))
        pieces.append((W2kd[:, 64:128], self.d["cmp_wk2"][l], [128, 64]))
        pieces.append((W2v[:], self.d["cmp_wv2"][l], [128, 64]))
        self.load_cast(pieces, RWc, stage_cols=1024)
        pe_n = self.sb([32, 256], F32)
        Rpe = Res("pe")
        for ti, nm in enumerate(("cmp_pe_k", "cmp_pe_v")):
            for half in range(2):
                P.dma("sp", lambda e, inc, ti=ti, nm=nm, half=half: inc(e.dma_start(
                    out=pe_n[:, ti * 128 + half * 64:ti * 128 + half * 64 + 64], in_=self.d[nm][l])), 1, writes=[Rpe])
        peT = self.sb([128, 2, 32], BF16)
        P.op("pe", lambda e: (e.transpose(out=ps[7][:, 0:32], in_=pe_n[:, 0:128], identity=self.ident[0:32, 0:32]),
                              e.transpose(out=ps[7][:, 32:64], in_=pe_n[:, 128:256], identity=self.ident[0:32, 0:32]))[1],
             reads=[Rpe, self.Rconst], writes=[psr[7]])
        P.op("dve", lambda e: e.tensor_copy(out=peT[:], in_=ps[7][:, 0:64].rearrange("p (a b) -> p a b", b=32)), reads=[psr[7]],
             writes=[Rpe])
        cb = self.sb([128, 2], F32)
        for ti, t in enumerate("kv"):
            def mmb(e, ti=ti, t=t):
                ins = None
                for li_ in range(32):
                    ins = e.matmul(ps[7][:, 64 + ti:65 + ti], lhsT=W1[t][0:64, li_, :], rhs=peT[0:64, ti, li_:li_ + 1],
                                   start=(li_ == 0), stop=(li_ == 31))
                return ins
            P.op("pe", mmb, reads=[RWc, Rpe], writes=[psr[7]])
        P.op("dve", lambda e: e.tensor_copy(out=cb[:], in_=ps[7][:, 64:66]), reads=[psr[7]], writes=[Rpe])
        KcT = self.sb([128, 2, 256], BF16)
        rcmp = self.sb([128, 2, 2, 129], BF16)
        Rcmp = Res("cmpops")
        P.op("pool", lambda e: e.memset(KcT[:], 0.0), writes=[Rcmp])
        P.op("pool", lambda e: e.memset(rcmp[:], 0.0), reads=[Rcmp], writes=[Rcmp])
        P.op("pool", lambda e: e.memset(rcmp[:, :, :, 64:65], 1.0), reads=[Rcmp], writes=[Rcmp])
        ovf = self.sb([128, 2, 64], F32)
        P.dma("sp", lambda e, inc: inc(e.dma_start(out=ovf[:], in_=self.d["c_ov"].rearrange("(a p) s -> p a s", p=128))), 1,
              writes=[Rpe])
        for k in range(2):
            P.op("pool", lambda e, k=k: e.tensor_copy(out=rcmp[:, :, k, 65:129], in_=ovf[:]), reads=[Rpe, Rcmp], writes=[Rcmp])
        hid = [self.sb([128, 256], BF16) for _ in range(2)]
        Rhid = [Res("hid0"), Res("hid1")]
        for hj in range(2):
            P.op("pool", lambda e, hj=hj: e.memset(hid[hj][:], 0.0), writes=[Rhid[hj]])
        ci = 0
        for ti, (t, srcn) in enumerate((("k", "KCT"), ("v", "VCT"))):
            for k in range(2):
                hj = ci % 2
                pb = ci % 2
                ci += 1

                def mmh(e, t=t, srcn=srcn, k=k, pb=pb):
                    ins = None
                    for li_ in range(32):
                        ins = e.matmul(ps[pb][:, 0:255], lhsT=W1[t][k * 64:(k + 1) * 64, li_, :],
                                       rhs=KT[srcn][k * 64:(k + 1) * 64, li_:li_ + 4065:16], start=(li_ == 0), stop=(li_ == 31))
                    return ins
                P.op("pe", mmh, reads=[RWc, RQ], writes=[psr[pb]])
                P.op("act", lambda e, hj=hj, pb=pb, ti=ti: e.activation(out=hid[hj][:, 0:255], in_=ps[pb][:, 0:255],
                                                                      func=AF.Gelu_apprx_tanh, bias=cb[:, ti:ti + 1]),
                     reads=[psr[pb], Rpe], writes=[Rhid[hj]])
                if t == "k":
                    P.op("pe", lambda e, hj=hj: e.matmul(ps[2][:, 0:255], lhsT=W2kd[:], rhs=hid[hj][:, 0:255], start=True, stop=True),
                         reads=[RWc, Rhid[hj]], writes=[psr[2]])
                    P.op("dve", lambda e, k=k: e.tensor_copy(out=KcT[k * 64:(k + 1) * 64, k, 0:255],
                                                             in_=ps[2][k * 64:(k + 1) * 64, 0:255]), reads=[psr[2], Rcmp],
                         writes=[Rcmp])
                else:
                    for nt, rows in ((0, 128), (1, 127)):
                        P.op("pe", lambda e, hj=hj, nt=nt, rows=rows: e.matmul(
                            ps[3][0:rows, nt * 64:(nt + 1) * 64], lhsT=hid[hj][:, nt * 128:nt * 128 + rows], rhs=W2v[:], start=True,
                            stop=True), reads=[RWc, Rhid[hj]], writes=[psr[3]])
                        P.op("dve", lambda e, k=k, nt=nt, rows=rows: e.tensor_copy(out=rcmp[0:rows, nt, k, 0:64],
                                                                                 in_=ps[3][0:rows, nt * 64:(nt + 1) * 64]),
                             reads=[psr[3], Rcmp], writes=[Rcmp])
        import os
        if os.environ.get("NSA_UNITS") == "0":
            self.phase_end()
            return
        PT = [self.sb([128, 512], BF16) for _ in range(3)]
        RPT = [Res("PT%d" % i) for i in range(3)]
        yt = [self.sb([128, 512], F32) for _ in range(2)]
        Ryt = [Res("yt0"), Res("yt1")]
        yT = [self.sb([128, 4, 128], BF16) for _ in range(2)]
        RyT = [Res("yT0"), Res("yT1")]
        rsc = self.sb([128, 4], F32)
        wgt = self.sb([128, 4], F32)
        acc = self.sb([128, 64], F32)
        sc = self.sb([128, 64], F32)
        sc2 = self.sb([128, 64], F32)
        m8 = self.sb([128, 16], F32)
        biasf = self.sb([128, 128], F32)
        Rk = [self.sb([128, 4, 128], BF16) for _ in range(2)]
        Rpost = Res("post")
        Rbias = Res("biasf")
        RbT = [Res("R0"), Res("R1")]
        units = []
        deferred = []

        def flush_deferred():
            for f in deferred:
                f()
            deferred.clear()

        for qt in range(NT):
            for k in range(2):
                QTt = QT[:, :, qt * 128:(qt + 1) * 128]
                yj = qt % 2
                yacc = yt[yj][:, k * 256:(k + 1) * 256].rearrange("p (h d) -> p h d", d=64)
                gsl = lambda b, qt=qt, k=k: G[:, qt, k * 12 + b:k * 12 + 12:3]

                def post_cmp(qt=qt, k=k, yacc=yacc, gsl=gsl, yj=yj):
                    Dv = lambda fn, rd, wr: P.op("dve", fn, reads=rd, writes=wr)
                    for bi, b_ in enumerate((3, 4)):
                        Dv(lambda e, bi=bi, b_=b_: e.tensor_scalar(out=rsc[:, bi * 2:bi * 2 + 2], in0=ps[b_][:, 64:64 + 129 + 1:129],
                                                                    scalar1=1e-30, scalar2=None, op0=ALU.max), [psr[b_], Rpost], [Rpost])
                    Dv(lambda e: e.reciprocal(out=rsc[:], in_=rsc[:]), [Rpost], [Rpost])
                    for h in range(4):
                        b_ = 3 + h // 2
                        o = (h % 2) * 129
                        if h == 0:
                            Dv(lambda e, b_=b_, o=o: e.tensor_scalar(out=acc[:], in0=ps[b_][:, o + 65:o + 129], scalar1=rsc[:, 0:1],
                                                                     scalar2=None, op0=ALU.mult), [psr[b_], Rpost], [Rpost])
                        else:
                            Dv(lambda e, b_=b_, o=o, h=h: e.scalar_tensor_tensor(out=acc[:], in0=ps[b_][:, o + 65:o + 129],
                                                                                 scalar=rsc[:, h:h + 1], in1=acc[:], op0=ALU.mult,
                                                                                 op1=ALU.add), [psr[b_], Rpost], [Rpost])
                    Dv(lambda e: e.tensor_tensor(out=wgt[:], in0=rsc[:], in1=gsl(0), op=ALU.mult), [Rpost, RQ], [Rpost])
                    for h in range(4):
                        b_ = 3 + h // 2
                        o = (h % 2) * 129
                        Dv(lambda e, b_=b_, o=o, h=h: e.tensor_scalar(out=yacc[:, h, :], in0=ps[b_][:, o:o + 64], scalar1=wgt[:, h:h + 1],
                                                                       scalar2=None, op0=ALU.mult), [psr[b_], Rpost, Ryt[yj]], [Ryt[yj]])
                    Dv(lambda e: e.tensor_copy(out=sc[:], in_=acc[:]), [Rpost], [Rpost])
                    for e_ in range(2):
                        cur = 2 * qt + e_
                        rows = slice(e_ * 64, (e_ + 1) * 64)
                        if cur + 1 < 64:
                            Dv(lambda e, rows=rows, cur=cur: e.memset(sc[rows, cur + 1:64], -1e6), [Rpost], [Rpost])
                        Dv(lambda e, rows=rows: e.memset(sc[rows, 0:1], 1e6), [Rpost], [Rpost])
                        lo = max(cur - 1, 0)
                        Dv(lambda e, rows=rows, lo=lo, cur=cur: e.memset(sc[rows, lo:cur + 1], 1e6), [Rpost], [Rpost])
                    Dv(lambda e: e.max(out=m8[:, 0:8], in_=sc[:]), [Rpost], [Rpost])
                    Dv(lambda e: e.match_replace(out=sc2[:], in_to_replace=m8[:, 0:8], in_values=sc[:], imm_value=-1e30), [Rpost],
                       [Rpost])
                    Dv(lambda e: e.max(out=m8[:, 8:16], in_=sc2[:]), [Rpost], [Rpost])
                    for hh in range(2):
                        Dv(lambda e, hh=hh: e.tensor_scalar(out=biasf[:, hh * 64:(hh + 1) * 64], in0=sc[:], scalar1=m8[:, 15:16],
                                                            scalar2=-BIG, op0=ALU.is_lt, op1=ALU.mult), [Rpost, Rbias], [Rbias])

                def pre_sel(k=k, qt=qt):
                    flush_deferred()
                    o = 1 - k
                    P.op("pe", lambda e: e.transpose(out=ps[7][:, 0:128], in_=biasf[:], identity=self.ident[:]),
                         reads=[Rbias, self.Rconst], writes=[psr[7]])
                    P.op("pool", lambda e: e.tensor_copy(out=Rk[k][k * 64:(k + 1) * 64, :, :],
                                                         in_=QT[k * 64:(k + 1) * 64, :, qt * 128:(qt + 1) * 128]),
                         reads=[RQ], writes=[RbT[k]])
                    P.op("act", lambda e: e.copy(out=Rk[k][o * 64:(o + 1) * 64, :, :],
                                                 in_=ps[7][o * 64:(o + 1) * 64, 0:128].unsqueeze(1).to_broadcast([64, 4, 128])),
                         reads=[psr[7]], writes=[RbT[k]])

                def post_branch(bank, b, yacc=yacc, gsl=gsl, yj=yj):
                    def f():
                        Dv = lambda fn, rd, wr: P.op("dve", fn, reads=rd, writes=wr)
                        Dv(lambda e: e.reciprocal(out=rsc[:], in_=ps[bank][:, 64:64 + 3 * 65 + 1:65]), [psr[bank], Rpost], [Rpost])
                        Dv(lambda e: e.tensor_tensor(out=wgt[:], in0=rsc[:], in1=gsl(b), op=ALU.mult), [Rpost, RQ], [Rpost])
                        for h in range(4):
                            Dv(lambda e, h=h: e.scalar_tensor_tensor(out=yacc[:, h, :], in0=ps[bank][:, h * 65:h * 65 + 64],
                                                                     scalar=wgt[:, h:h + 1], in1=yacc[:, h, :], op0=ALU.mult,
                                                                     op1=ALU.add), [psr[bank], Rpost, Ryt[yj]], [Ryt[yj]])
                    return f

                def post_qt(qt=qt, yj=yj):
                    def tr_store():
                        def tr(e):
                            ins = None
                            for a in range(4):
                                ins = e.transpose(out=ps[7][:, a * 128:(a + 1) * 128], in_=yt[yj][:, a * 128:(a + 1) * 128],
                                                  identity=self.ident[:])
                            return ins
                        P.op("pe", tr, reads=[Ryt[yj], self.Rconst], writes=[psr[7]])
                        P.op("act", lambda e: e.copy(out=yT[yj][:], in_=ps[7][:].rearrange("p (a b) -> p a b", b=128)),
                             reads=[psr[7]], writes=[RyT[yj]])
                        P.dma("sp", lambda e, inc: inc(e.dma_start(
                            out=S["YNT"].rearrange("(a p) t -> p a t", p=128)[:, :, qt * 128:(qt + 1) * 128], in_=yT[yj][:])), 1,
                            reads=[RyT[yj]], writes=[SR["YNT"]])
                    return lambda: deferred.append(tr_store)

                nts = [0] if qt < 16 else [0, 1]
                for nt in nts:
                    rows = 128 if nt == 0 else 127
                    full = (128 * nt + rows - 1) <= 8 * qt - 2
                    mask = None if full else dict(pattern=[[0, 4], [1, 128]], base=128 * qt - 2048 * nt - 31, cm=-16)
                    units.append(dict(kind="cmp", rows=rows, lhsT=KcT[:, k, nt * 128:nt * 128 + rows], rhs=QTt,
                                      mask=mask, v=rcmp[0:rows, nt, k, :], first=(nt == 0), last=(nt == nts[-1]), pre=None,
                                      post=(post_cmp if nt == nts[-1] else None)))
                cmp_last_idx = len(units) - 1
                wk = list(range(max(0, qt - 4), qt + 1))
                for kt in wk:
                    mask = None
                    if kt == qt:
                        mask = dict(pattern=[[0, 4], [1, 128]], base=0, cm=-1)
                    elif kt == qt - 4:
                        mask = dict(pattern=[[0, 4], [-1, 128]], base=-1, cm=1)
                    units.append(dict(kind="win", rows=128, lhsT=KWz[k][:, kt * 128:(kt + 1) * 128], rhs=QTt,
                                      mask=mask, v=Va["VW"][:, kt, k, :], first=(kt == wk[0]), last=(kt == qt), pre=None,
                                      post=(post_branch(6, 2) if kt == qt else None), bank=6))
                for kt in range(qt + 1):
                    mask = dict(pattern=[[0, 4], [1, 128]], base=0, cm=-1) if kt == qt else None
                    posts = None
                    if kt == qt:
                        pb_ = post_branch(5, 1)
                        if k == 1:
                            pq = post_qt()
                            posts = (lambda pb_=pb_, pq=pq: (pb_(), post_qt_call(pq)))
                        else:
                            posts = pb_
                    units.append(dict(kind="sel", rows=128, lhsT=KE[k][:, kt * 128:(kt + 1) * 128], rhs=Rk[k][:],
                                      mask=mask, v=Va["VS"][:, kt, k, :], first=(kt == 0), last=(kt == qt),
                                      pre=(pre_sel if kt == 0 else None), post=posts, bank=5, k=k,
                                      after=(cmp_last_idx if kt == 0 else None)))

        def post_qt_call(pq):
            pq()

        def emit_qk(i):
            u = units[i]
            sb_ = i % 3
            if u["pre"] is not None:
                u["pre"]()
            rows = u["rows"]
            out3 = ps[sb_][0:rows, :].rearrange("p (a b) -> p a b", b=128)
            if u["kind"] == "sel":
                P.op("pe", lambda e, u=u, out3=out3: e.matmul(out3, lhsT=u["lhsT"], rhs=u["rhs"], start=True, stop=True),
                     reads=[RQ, RbT[u["k"]]], writes=[psr[sb_]])
            else:
                P.op("pe", lambda e, u=u, out3=out3: e.matmul(out3, lhsT=u["lhsT"], rhs=u["rhs"], start=True, stop=True),
                     reads=[RQ, Rcmp], writes=[psr[sb_]])

        def emit_rest(i):
            u = units[i]
            sb_ = i % 3
            rows = u["rows"]
            P.op("act", lambda e, sb_=sb_, rows=rows: e.activation(out=PT[sb_][0:rows, :], in_=ps[sb_][0:rows, :], func=AF.Exp,
                                                                  scale=0.125), reads=[psr[sb_]], writes=[RPT[sb_]])
            if u["mask"] is not None:
                mk = u["mask"]
                v3 = PT[sb_][0:rows, :].rearrange("p (a b) -> p a b", b=128)
                P.op("pool", lambda e, v3=v3, mk=mk: e.affine_select(out=v3, in_=v3, pattern=mk["pattern"], compare_op=ALU.is_ge,
                                                                    fill=0.0, base=mk["base"], channel_multiplier=mk["cm"]),
                     reads=[RPT[sb_]], writes=[RPT[sb_]])
            if u["kind"] == "cmp":
                def pv(e, u=u, sb_=sb_, rows=rows):
                    ins = None
                    for h in range(4):
                        b_ = 3 + h // 2
                        o = (h % 2) * 129
                        ins = e.matmul(ps[b_][:, o:o + 129], lhsT=PT[sb_][0:rows, h * 128:(h + 1) * 128], rhs=u["v"],
                                       start=(u["first"] and h % 2 == 0), stop=(u["last"] and h % 2 == 1))
                    return ins
                P.op("pe", pv, reads=[RPT[sb_], Rcmp], writes=[psr[3], psr[4]], self_ok=not u["first"])
            else:
                bank = u["bank"]

                def pv(e, u=u, sb_=sb_, bank=bank):
                    ins = None
                    for h in range(4):
                        ins = e.matmul(ps[bank][:, h * 65:(h + 1) * 65], lhsT=PT[sb_][:, h * 128:(h + 1) * 128], rhs=u["v"],
                                       start=(u["first"] and h == 0), stop=(u["last"] and h == 3))
                    return ins
                P.op("pe", pv, reads=[RPT[sb_], RQ], writes=[psr[bank]], self_ok=not u["first"])
            if u["post"] is not None:
                u["post"]()

        import os
        n = len(units)
        if os.environ.get("NSA_UNITS"):
            n = int(os.environ["NSA_UNITS"])
            deferred.clear()
        st_ = {"next": 1}

        def try_emit(limit, done_rest):
            while st_["next"] < n and st_["next"] <= limit:
                dep = units[st_["next"]].get("after")
                if dep is not None and dep > done_rest:
                    break
                emit_qk(st_["next"])
                st_["next"] += 1

        emit_qk(0)
        try_emit(2, -1)
        for i in range(n):
            assert st_["next"] > i
            emit_rest(i)
            try_emit(i + 3, i)
            if os.environ.get("NSA_UNITS") and i == n - 1:
                break
        flush_deferred()
        self.phase_end()

    def phase_merge(self, l):
        P = self.P
        S, SR = self.scr, self.sres
        w_in = self.d["w_in"][l].rearrange("(a p) c -> p a c", p=128)
        WU = self.sb([128, 12, D], BF16)
        WG = self.sb([128, 8, 3072], BF16)
        WO = self.sb([128, 8, D], BF16)
        RW = Res("WM")
        pieces = []
        for b, nm in enumerate(("w_up_pool", "w_up_ssm", "w_up_nsa")):
            wv = self.d[nm][l].rearrange("(a p) c -> p a c", p=128)
            for o in range(0, D, 512):
                pieces.append((WU[:, b * 4:(b + 1) * 4, o:o + 512], wv[:, :, o:o + 512], [128, 4, 512]))
        for o in range(0, 3072, 256):
            pieces.append((WG[:, :, o:o + 256], w_in[:, :, 2328 + o:2328 + o + 256], [128, 8, 256]))
        wo = self.d["w_out"][l].rearrange("(a p) c -> p a c", p=128)
        for o in range(0, D, 256):
            pieces.append((WO[:, :, o:o + 256], wo[:, :, o:o + 256], [128, 8, 256]))
        self.load_cast(pieces, RW)
        g_rep = self.sb([128, D], F32)
        b_rep = self.sb([128, D], F32)
        Rgb = Res("gb")
        self.bcast_load(g_rep[:], self.d["ln1_g"][l:l + 1, :], Rgb)
        self.bcast_load(b_rep[:], self.d["ln1_b"][l:l + 1, :], Rgb)
        xt = [self.sb([128, 8, 512], BF16) for _ in range(2)]
        Rxt = [Res("xt0"), Res("xt1")]
        yb = [self.sb([128, 12, 512], BF16) for _ in range(2)]
        Ryb = [Res("yb0"), Res("yb1")]
        mg = self.sb([128, 8, 512], BF16)
        Rmg = Res("mg")
        sg = [self.sb([128, 512], F32) for _ in range(2)]
        Rsg = [Res("sg0"), Res("sg1")]
        acc = self.sb([128, 512], F32)
        tmp = self.sb([128, 512], F32)
        Racc = Res("acc")
        Rtmp = Res("tmp")
        xr = [self.sb([128, D], F32) for _ in range(2)]
        Rxr = [Res("xr0"), Res("xr1")]
        bufs = [self.ln_bufs() for _ in range(2)]
        XTd = S["XT"].rearrange("(a p) t -> p a t", p=128)
        k = 0
        for c in range(8):
            j = c % 2
            P.dma("sp", lambda e, inc, c=c, j=j: inc(e.dma_start(out=xt[j][:], in_=XTd[:, :, c * 512:(c + 1) * 512])), 1,
                  reads=[SR["XT"]], writes=[Rxt[j]])
            for b, nm in enumerate(("YPT", "YST", "YNT")):
                P.dma("sp", lambda e, inc, c=c, j=j, b=b, nm=nm: inc(e.dma_start(
                    out=yb[j][:, b * 4:(b + 1) * 4, :],
                    in_=S[nm].rearrange("(a p) t -> p a t", p=128)[:, :, c * 512:(c + 1) * 512])), 1,
                    reads=[SR[nm]], writes=[Ryb[j]])
            for m in range(8):
                for b in range(3):
                    pu = k % 2
                    pg = 2 + k % 2
                    sj = k % 2
                    k += 1

                    def mmu(e, pu=pu, b=b, m=m, j=j):
                        ins = None
                        for a in range(4):
                            ins = e.matmul(self.ps[pu][:], lhsT=WU[:, b * 4 + a, m * 128:(m + 1) * 128], rhs=yb[j][:, b * 4 + a, :],
                                           start=(a == 0), stop=(a == 3))
                        return ins

                    def mmg(e, pg=pg, b=b, m=m, j=j):
                        ins = None
                        for a in range(8):
                            ins = e.matmul(self.ps[pg][:], lhsT=WG[:, a, b * 1024 + m * 128:b * 1024 + (m + 1) * 128],
                                           rhs=xt[j][:, a, :], start=(a == 0), stop=(a == 7))
                        return ins
                    P.op("pe", mmg, reads=[RW, Rxt[j]], writes=[self.psr[pg]])
                    P.op("pe", mmu, reads=[RW, Ryb[j]], writes=[self.psr[pu]])
                    P.op("act", lambda e, pg=pg, sj=sj: e.activation(out=sg[sj][:], in_=self.ps[pg][:], func=AF.Sigmoid),
                         reads=[self.psr[pg]], writes=[Rsg[sj]])
                    if b == 0:
                        P.op("dve", lambda e, pu=pu, sj=sj: e.tensor_tensor(out=acc[:], in0=self.ps[pu][:], in1=sg[sj][:],
                                                                            op=ALU.mult), reads=[self.psr[pu], Rsg[sj]],
                             writes=[Racc])
                    else:
                        P.op("dve", lambda e, pu=pu, sj=sj: e.tensor_tensor(out=tmp[:], in0=self.ps[pu][:], in1=sg[sj][:],
                                                                            op=ALU.mult), reads=[self.psr[pu], Rsg[sj]],
                             writes=[Rtmp])
                        if b == 1:
                            P.op("dve", lambda e: e.tensor_tensor(out=acc[:], in0=acc[:], in1=tmp[:], op=ALU.add),
                                 reads=[Rtmp, Racc], writes=[Racc])
                        else:
                            P.op("dve", lambda e, m=m: e.tensor_tensor(out=mg[:, m, :], in0=acc[:], in1=tmp[:], op=ALU.add),
                                 reads=[Rtmp, Racc], writes=[Rmg])
            for tt in range(4):
                i = c * 4 + tt
                xj = i % 2
                P.dma("sp", lambda e, inc, i=i, xj=xj: inc(e.dma_start(out=xr[xj][:], in_=S["XR"][i * 128:(i + 1) * 128, :])),
                      1, reads=[SR["XR"]], writes=[Rxr[xj]])
                for half in range(2):
                    pb = 4 + half

                    def mmo(e, pb=pb, tt=tt, half=half):
                        ins = None
                        for a in range(8):
                            ins = e.matmul(self.ps[pb][:], lhsT=mg[:, a, tt * 128:(tt + 1) * 128],
                                           rhs=WO[:, a, half * 512:(half + 1) * 512], start=(a == 0), stop=(a == 7))
                        return ins
                    P.op("pe", mmo, reads=[RW, Rmg], writes=[self.psr[pb]])
                    P.op("dve", lambda e, pb=pb, xj=xj, half=half: e.scalar_tensor_tensor(
                        out=xr[xj][:, half * 512:(half + 1) * 512], in0=xr[xj][:, half * 512:(half + 1) * 512], scalar=ALPHA,
                        in1=self.ps[pb][:], op0=ALU.mult, op1=ALU.add), reads=[self.psr[pb], Rxr[xj]], writes=[Rxr[xj]])
                self.ln_tile(i, xr[xj], Rxr[xj], g_rep, b_rep, Rgb, S["XR1"], SR["XR1"], S["X1T"], SR["X1T"], bufs[xj])
        self.phase_end()

    def phase_ffn(self, l, last):
        P = self.P
        S, SR = self.scr, self.sres
        W1 = self.sb([128, 8, 4096], BF16)
        W2 = self.sb([128, 32, D], BF16)
        RW = Res("WF")
        w1 = self.d["w_ff1"][l].rearrange("(a p) c -> p a c", p=128)
        w2 = self.d["w_ff2"][l].rearrange("(a p) c -> p a c", p=128)
        pieces = []
        for o in range(0, 4096, 128):
            pieces.append((W1[:, :, o:o + 128], w1[:, :, o:o + 128], [128, 8, 128]))
        for a in range(32):
            pieces.append((W2[:, a, :], w2[:, a, :], [128, D]))
        self.load_cast(pieces, RW, stage_cols=1024)
        g_rep = self.sb([128, D], F32)
        b_rep = self.sb([128, D], F32)
        Rgb = Res("gb")
        self.bcast_load(g_rep[:], self.d["ln2_g"][l:l + 1, :], Rgb)
        self.bcast_load(b_rep[:], self.d["ln2_b"][l:l + 1, :], Rgb)
        xt = [self.sb([128, 8, 512], BF16) for _ in range(1)]
        Rxt = [Res("xt0")]
        hT = self.sb([128, 32, 512], BF16)
        RhT = Res("hT")
        rl = [self.sb([128, 512], F32) for _ in range(2)]
        Rrl = [Res("rl0"), Res("rl1")]
        xr = [self.sb([128, D], F32) for _ in range(2)]
        Rxr = [Res("xr0"), Res("xr1")]
        bufs = [self.ln_bufs() for _ in range(2)]
        X1Td = S["X1T"].rearrange("(a p) t -> p a t", p=128)
        dst_res = self.d["out"] if last else S["XR"]
        Rdst = Res("outres") if last else SR["XR"]
        dst_T = None if last else S["XT"]
        k = 0
        for c in range(8):
            j = 0
            P.dma("sp", lambda e, inc, c=c, j=j: inc(e.dma_start(out=xt[j][:], in_=X1Td[:, :, c * 512:(c + 1) * 512])), 1,
                  reads=[SR["X1T"]], writes=[Rxt[j]])
            for f in range(32):
                pb = k % 4
                rj = k % 2
                k += 1

                def mm1(e, pb=pb, f=f, j=j):
                    ins = None
                    for a in range(8):
                        ins = e.matmul(self.ps[pb][:], lhsT=W1[:, a, f * 128:(f + 1) * 128], rhs=xt[j][:, a, :], start=(a == 0),
                                       stop=(a == 7))
                    return ins
                P.op("pe", mm1, reads=[RW, Rxt[j]], writes=[self.psr[pb]])
                P.op("act", lambda e, pb=pb, rj=rj: e.activation(out=rl[rj][:], in_=self.ps[pb][:], func=AF.Relu),
                     reads=[self.psr[pb]], writes=[Rrl[rj]])
                P.op("pool", lambda e, rj=rj, f=f: e.tensor_tensor(out=hT[:, f, :], in0=rl[rj][:], in1=rl[rj][:], op=ALU.mult),
                     reads=[Rrl[rj]], writes=[RhT])
            for tt in range(4):
                i = c * 4 + tt
                xj = i % 2
                P.dma("sp", lambda e, inc, i=i, xj=xj: inc(e.dma_start(out=xr[xj][:], in_=S["XR1"][i * 128:(i + 1) * 128, :])),
                      1, reads=[SR["XR1"]], writes=[Rxr[xj]])
                for half in range(2):
                    pb = 4 + half

                    def mm2(e, pb=pb, tt=tt, half=half):
                        ins = None
                        for f in range(32):
                            ins = e.matmul(self.ps[pb][:], lhsT=hT[:, f, tt * 128:(tt + 1) * 128],
                                           rhs=W2[:, f, half * 512:(half + 1) * 512], start=(f == 0), stop=(f == 31))
                        return ins
                    P.op("pe", mm2, reads=[RW, RhT], writes=[self.psr[pb]])
                    P.op("dve", lambda e, pb=pb, xj=xj, half=half: e.scalar_tensor_tensor(
                        out=xr[xj][:, half * 512:(half + 1) * 512], in0=xr[xj][:, half * 512:(half + 1) * 512], scalar=ALPHA,
                        in1=self.ps[pb][:], op0=ALU.mult, op1=ALU.add), reads=[self.psr[pb], Rxr[xj]], writes=[Rxr[xj]])
                self.ln_tile(i, xr[xj], Rxr[xj], g_rep, b_rep, Rgb, dst_res, Rdst, dst_T, SR["XT"], bufs[xj])
        self.phase_end()

    def build(self):
        self.phase_ln_in()
        for l in range(self.layers):
            stop = self.stop_after
            skip = self.skip
            if "proj" not in skip:
                self.phase_proj(l)
            if stop == "proj":
                break
            if "pool" not in skip:
                self.phase_pool(l)
            if stop == "pool":
                break
            if "s5" not in skip:
                self.phase_s5(l)
            if stop == "s5":
                break
            if "nsa" not in skip:
                self.phase_nsa(l)
            if stop == "nsa":
                break
            self.phase_merge(l)
            if stop == "merge":
                break
            self.phase_ffn(l, last=(l == self.layers - 1))
        self.P.emit()
        return self.nc


def make_inputs(inputs, ncores=8):
    consts = host_consts()
    shared = {n: np.ascontiguousarray(np.asarray(inputs[n], dtype=np.float32)) for n, _ in PARAMS}
    shared.update(consts)
    x = np.asarray(inputs["x"], dtype=np.float32)
    maps = []
    for c in range(ncores):
        m = dict(shared)
        m["x"] = np.ascontiguousarray(x[c])
        maps.append(m)
    return maps


def kernel(**inputs):
    b = Builder()
    nc = b.build()
    maps = make_inputs(inputs)
    res = run_bass_kernel_spmd(nc, maps, core_ids=list(range(8)))
    return np.stack([np.asarray(r["out"], dtype=np.float32) for r in res.results], axis=0)
```
